# Optimizing a Trainium2 kernel written in Bass

```python
import math
import jax
import jax.numpy as jnp
from jax import lax
import numpy as np

D_MODEL = 1024
BATCH = 8
SEQ = 4096
DEPTH = 4

GRID_W = 64
CTX_LEN = 256
N_MOD = 9
RMS_EPS = 1e-6
GN_EPS = 1e-6
D_FF = 2816
NA_HEADS = 8
NA_HEAD_DIM = 64
NA_WIDTH = NA_HEADS * NA_HEAD_DIM
WIN_R = 8
WIN_C = 16
HY_WIDTH = D_MODEL - NA_WIDTH
HY_BANDS = 8
HY_EMB = 1 + 2 * HY_BANDS
HY_ORDER = 64
HY_TARGET = 1e-2
HY_FAST_PCT = 0.3
HY_SLOW_PCT = 1.5
EVEN_IN = 3 * NA_WIDTH + 3 * HY_WIDTH
EVEN_CAT = NA_WIDTH + HY_WIDTH
RET_HEADS = 4
RET_KEY_DIM = D_MODEL // RET_HEADS
RET_VAL_DIM = 2 * RET_KEY_DIM
RET_QK = RET_HEADS * RET_KEY_DIM
RET_V = RET_HEADS * RET_VAL_DIM
RET_IN = 2 * RET_QK + 2 * RET_V
RET_CHUNK = 128
ROPE_BASE = 10000.0

kernel_name = "hybrid_na_hyena_retention_dit"


def rms_norm(x, gain):
    xf = x.astype(jnp.float32)
    y = xf * lax.rsqrt(jnp.mean(xf * xf, axis=-1, keepdims=True) + RMS_EPS)
    return (y * gain).astype(x.dtype)


def modulate(x, gain, shift, scale):
    return rms_norm(x, gain) * (1 + scale) + shift


def swiglu(h, w_in, w_out):
    a, b = jnp.split(h @ w_in, 2, axis=-1)
    return (jax.nn.silu(a) * b) @ w_out


def split_heads(t, n_heads):
    b, l, _ = t.shape
    return t.reshape(b, l, n_heads, -1).transpose(0, 2, 1, 3)


def merge_heads(t):
    b, h, l, d = t.shape
    return t.transpose(0, 2, 1, 3).reshape(b, l, h * d)


def axial_rope_tables(length, head_dim):
    t = jnp.arange(length)
    n_freq = head_dim // 4
    inv = ROPE_BASE ** (-jnp.arange(n_freq, dtype=jnp.float32) / n_freq)
    ang_r = (t // GRID_W).astype(jnp.float32)[:, None] * inv
    ang_c = (t % GRID_W).astype(jnp.float32)[:, None] * inv
    ang = jnp.concatenate([ang_r, ang_c], axis=-1)
    return jnp.cos(ang), jnp.sin(ang)


def apply_axial_rope(x, cos, sin):
    nf = x.shape[-1] // 4
    xr1, xr2, xc1, xc2 = jnp.split(x.astype(jnp.float32), 4, axis=-1)
    cr, cc = cos[:, :nf], cos[:, nf:]
    sr, sc = sin[:, :nf], sin[:, nf:]
    out = jnp.concatenate([xr1 * cr - xr2 * sr, xr1 * sr + xr2 * cr,
                           xc1 * cc - xc2 * sc, xc1 * sc + xc2 * cc], axis=-1)
    return out.astype(x.dtype)


def qk_norm(t, gain):
    tf = t.astype(jnp.float32)
    return (tf * lax.rsqrt(jnp.mean(tf * tf, axis=-1, keepdims=True) + RMS_EPS) * gain).astype(t.dtype)


def dense_attention(q, k, v):
    s = jnp.einsum('bhqd,bhkd->bhqk', q, k).astype(jnp.float32) * (q.shape[-1] ** -0.5)
    p = jax.nn.softmax(s, axis=-1).astype(v.dtype)
    return jnp.einsum('bhqk,bhkd->bhqd', p, v)


def neighbourhood_attention(q, k, v, k_ctx, v_ctx, rpb):
    b, h, length, d = q.shape
    rows = length // GRID_W
    wr = min(WIN_R, rows)
    q = q.reshape(b, h, rows, GRID_W, d)
    k = k.reshape(b, h, rows, GRID_W, d)
    v = v.reshape(b, h, rows, GRID_W, d)
    col = jnp.arange(GRID_W)
    col_start = jnp.clip(col - WIN_C // 2, 0, GRID_W - WIN_C)
    col_idx = col_start[:, None] + jnp.arange(WIN_C)[None, :]
    dc_idx = col_idx - col[:, None] + (WIN_C - 1)
    scale = d ** -0.5
    n_win = wr * WIN_C

    def row_block(r):
        rs = jnp.clip(r - wr // 2, 0, rows - wr)
        dr_idx = rs + jnp.arange(wr) - r + (WIN_R - 1)
        bias = rpb[:, dr_idx[None, :, None], dc_idx[:, None, :]].astype(jnp.float32)
        q_r = lax.dynamic_index_in_dim(q, r, axis=2, keepdims=False)
        k_w = jnp.take(lax.dynamic_slice_in_dim(k, rs, wr, axis=2), col_idx, axis=3)
        v_w = jnp.take(lax.dynamic_slice_in_dim(v, rs, wr, axis=2), col_idx, axis=3)
        s_win = jnp.einsum('bhqd,bhrqjd->bhqrj', q_r, k_w).astype(jnp.float32) * scale + bias
        s_ctx = jnp.einsum('bhqd,bhcd->bhqc', q_r, k_ctx).astype(jnp.float32) * scale
        s = jnp.concatenate([s_win.reshape(b, h, GRID_W, n_win), s_ctx], axis=-1)
        p = jax.nn.softmax(s, axis=-1).astype(v.dtype)
        p_win = p[..., :n_win].reshape(b, h, GRID_W, wr, WIN_C)
        return (jnp.einsum('bhqrj,bhrqjd->bhqd', p_win, v_w)
                + jnp.einsum('bhqc,bhcd->bhqd', p[..., n_win:], v_ctx))

    out = lax.map(row_block, jnp.arange(rows))
    return out.transpose(1, 2, 0, 3, 4).reshape(b, h, length, d)


def short_conv(u, w, b):
    up = jnp.pad(u, ((0, 0), (1, 1), (0, 0)))
    return up[:, :-2] * w[0] + up[:, 1:-1] * w[1] + up[:, 2:] * w[2] + b


def hyena_filter(length, fw1, fb1, fw2, fb2, fw3, fb3, fw4, freq):
    t = jnp.linspace(0.0, 1.0, length, dtype=jnp.float32)[:, None]
    w = 2.0 * math.pi * jnp.arange(length, dtype=jnp.float32)[:, None] / length
    bands = jnp.linspace(1e-4, HY_BANDS - 1, HY_BANDS, dtype=jnp.float32)
    emb = jnp.concatenate([t, jnp.cos(bands * w), -jnp.sin(bands * w)], axis=-1)
    h = jnp.sin(freq * (emb @ fw1 + fb1))
    h = jnp.sin(freq * (h @ fw2 + fb2))
    h = jnp.sin(freq * (h @ fw3 + fb3))
    h = (h @ fw4).astype(jnp.float32)
    max_decay = math.log(HY_TARGET) / HY_FAST_PCT
    min_decay = math.log(HY_TARGET) / HY_SLOW_PCT
    deltas = jnp.linspace(min_decay, max_decay, HY_WIDTH, dtype=jnp.float32)
    window = jnp.exp(-t * jnp.abs(deltas))
    h_fwd = h[:, :HY_WIDTH] * window
    h_bwd = h[:, HY_WIDTH:] * window
    return jnp.concatenate([h_fwd[:1] + h_bwd[:1], h_fwd[1:],
                            jnp.zeros((1, HY_WIDTH), jnp.float32), h_bwd[1:][::-1]], axis=0)


def long_conv(u, filt):
    length = u.shape[1]
    uf = jnp.fft.rfft(u.astype(jnp.float32), n=2 * length, axis=1)
    kf = jnp.fft.rfft(filt, n=2 * length, axis=0)
    return jnp.fft.irfft(uf * kf[None], n=2 * length, axis=1)[:, :length]


def hyena_operator(u, conv_w, conv_b, filter_params, d_bias):
    uc = short_conv(u, conv_w, conv_b)
    x0, x1, v = jnp.split(uc, 3, axis=-1)
    z = v * x1
    filt = hyena_filter(u.shape[1], *filter_params)
    y = long_conv(z, filt).astype(z.dtype) + z * d_bias
    return y * x0


def na_hyena_mixer(h_lat, h_ctx, w_in, w_out, q_gain, k_gain, rpb, conv_w, conv_b,
                   filter_params, d_bias, ctx_out):
    p_lat = h_lat @ w_in
    p_ctx = h_ctx @ w_in
    qa_l, ka_l, va_l = [split_heads(t, NA_HEADS) for t in jnp.split(p_lat[..., :3 * NA_WIDTH], 3, axis=-1)]
    qa_c, ka_c, va_c = [split_heads(t, NA_HEADS) for t in jnp.split(p_ctx[..., :3 * NA_WIDTH], 3, axis=-1)]
    k_c = qk_norm(ka_c, k_gain)
    a_lat = merge_heads(neighbourhood_attention(qk_norm(qa_l, q_gain), qk_norm(ka_l, k_gain), va_l,
                                                k_c, va_c, rpb))
    b_lat = hyena_operator(p_lat[..., 3 * NA_WIDTH:], conv_w, conv_b, filter_params, d_bias)
    o_lat = jnp.concatenate([a_lat, b_lat], axis=-1) @ w_out
    if not ctx_out:
        return o_lat, None
    a_ctx = merge_heads(dense_attention(qk_norm(qa_c, q_gain), k_c, va_c))
    b_ctx = hyena_operator(p_ctx[..., 3 * NA_WIDTH:], conv_w, conv_b, filter_params, d_bias)
    o_ctx = jnp.concatenate([a_ctx, b_ctx], axis=-1) @ w_out
    return o_lat, o_ctx


def retention_scan(q, k, v, log_gamma, state0):
    b, h, length, dk = q.shape
    dv = v.shape[-1]
    n = length // RET_CHUNK

    def chunks(a):
        return a.astype(jnp.float32).reshape(b, h, n, RET_CHUNK, a.shape[-1]).transpose(2, 0, 1, 3, 4)

    j = jnp.arange(RET_CHUNK, dtype=jnp.float32)
    diff = j[:, None] - j[None, :]
    lg = log_gamma.astype(jnp.float32)
    dmask = jnp.where(diff[None] >= 0, jnp.exp(lg[:, None, None] * jnp.maximum(diff, 0.0)[None]), 0.0)
    xi = jnp.exp(lg[:, None] * (j + 1.0))
    zeta = jnp.exp(lg[:, None] * (RET_CHUNK - 1.0 - j))
    g_chunk = jnp.exp(lg * RET_CHUNK)

    def step(state, inp):
        qc, kc, vc = inp
        inner = jnp.einsum('bhid,bhjd->bhij', qc, kc) * dmask
        o = (jnp.einsum('bhij,bhje->bhie', inner, vc)
             + jnp.einsum('bhid,bhde->bhie', qc, state) * xi[..., None])
        state = state * g_chunk[:, None, None] + jnp.einsum('bhjd,bhje->bhde', kc * zeta[..., None], vc)
        return state, o

    state, o = lax.scan(step, state0, (chunks(q), chunks(k), chunks(v)))
    return o.transpose(1, 2, 0, 3, 4).reshape(b, h, length, dv), state


def retention_mixer(h_lat, h_ctx, w_in, w_out, logit_f, logit_b, rope_cos, rope_sin, ctx_out):
    def project(hh):
        q, k, v, g = jnp.split(hh @ w_in, [RET_QK, 2 * RET_QK, 2 * RET_QK + RET_V], axis=-1)
        return (split_heads(q, RET_HEADS), split_heads(k, RET_HEADS) * (RET_KEY_DIM ** -0.5),
                split_heads(v, RET_HEADS), g)

    def output(y, g, dtype):
        mu = jnp.mean(y, axis=-1, keepdims=True)
        var = jnp.mean(jnp.square(y - mu), axis=-1, keepdims=True)
        y = merge_heads((y - mu) * lax.rsqrt(var + GN_EPS)).astype(dtype)
        return (jax.nn.silu(g) * y) @ w_out

    q_l, k_l, v_l, g_l = project(h_lat)
    q_l = apply_axial_rope(q_l, rope_cos, rope_sin)
    k_l = apply_axial_rope(k_l, rope_cos, rope_sin)
    q_c, k_c, v_c, g_c = project(h_ctx)
    lg_f = jax.nn.log_sigmoid(logit_f.astype(jnp.float32))
    lg_b = jax.nn.log_sigmoid(logit_b.astype(jnp.float32))
    zero = jnp.zeros((h_lat.shape[0], RET_HEADS, RET_KEY_DIM, RET_VAL_DIM), jnp.float32)
    flip = lambda a: jnp.flip(a, axis=2)
    o_cf, s_f = retention_scan(q_c, k_c, v_c, lg_f, zero)
    o_cb, s_b = retention_scan(flip(q_c), flip(k_c), flip(v_c), lg_b, zero)
    o_lf, _ = retention_scan(q_l, k_l, v_l, lg_f, s_f)
    o_lb, _ = retention_scan(flip(q_l), flip(k_l), flip(v_l), lg_b, s_b)
    o_lat = output(o_lf + flip(o_lb), g_l, h_lat.dtype)
    if not ctx_out:
        return o_lat, None
    o_ctx = output(o_cf + flip(o_cb), g_c, h_ctx.dtype)
    return o_lat, o_ctx


def setup_inputs(seed: int = 0) -> dict:
    key = jax.random.key(seed)
    ks = iter(jax.random.split(key, 40))

    def nrm(shape, scale):
        return scale * jax.random.normal(next(ks), shape, jnp.float32)

    d = D_MODEL
    n_even = (DEPTH + 1) // 2
    n_odd = DEPTH // 2
    gamma = 1.0 - 2.0 ** (-5.0 - np.arange(RET_HEADS))
    logit0 = jnp.asarray(np.log(gamma / (1.0 - gamma)), jnp.float32)
    return {
        "x": nrm((BATCH, SEQ, d), 1.0),
        "c": nrm((BATCH, d), 1.0),
        "ctx": nrm((BATCH, CTX_LEN, d), 1.0),
        "c_ctx": nrm((d,), 1.0),
        "w_mod": nrm((DEPTH, d, N_MOD * d), 0.5 * d ** -0.5),
        "b_mod": nrm((DEPTH, N_MOD * d), 0.02),
        "norm_gain": 1.0 + nrm((DEPTH, 3, d), 0.05),
        "ffn_a_in": nrm((DEPTH, d, 2 * D_FF), d ** -0.5),
        "ffn_a_out": nrm((DEPTH, D_FF, d), D_FF ** -0.5),
        "ffn_b_in": nrm((DEPTH, d, 2 * D_FF), d ** -0.5),
        "ffn_b_out": nrm((DEPTH, D_FF, d), D_FF ** -0.5),
        "even_in": nrm((n_even, d, EVEN_IN), d ** -0.5),
        "even_out": nrm((n_even, EVEN_CAT, d), EVEN_CAT ** -0.5),
        "na_q_gain": 1.0 + nrm((n_even, NA_HEAD_DIM), 0.05),
        "na_k_gain": 1.0 + nrm((n_even, NA_HEAD_DIM), 0.05),
        "na_rpb": nrm((n_even, NA_HEADS, 2 * WIN_R - 1, 2 * WIN_C - 1), 0.1),
        "hy_conv_w": nrm((n_even, 3, 3 * HY_WIDTH), 3 ** -0.5),
        "hy_conv_b": nrm((n_even, 3 * HY_WIDTH), 0.02),
        "hy_fw1": nrm((n_even, HY_EMB, HY_ORDER), HY_EMB ** -0.5),
        "hy_fb1": nrm((n_even, HY_ORDER), 0.1),
        "hy_fw2": nrm((n_even, HY_ORDER, HY_ORDER), HY_ORDER ** -0.5),
        "hy_fb2": nrm((n_even, HY_ORDER), 0.1),
        "hy_fw3": nrm((n_even, HY_ORDER, HY_ORDER), HY_ORDER ** -0.5),
        "hy_fb3": nrm((n_even, HY_ORDER), 0.1),
        "hy_fw4": nrm((n_even, HY_ORDER, 2 * HY_WIDTH), 0.05 * HY_ORDER ** -0.5),
        "hy_freq": 1.0 + nrm((n_even, HY_ORDER), 0.1),
        "hy_bias": nrm((n_even, HY_WIDTH), 0.5),
        "ret_in": nrm((n_odd, d, RET_IN), d ** -0.5),
        "ret_out": nrm((n_odd, RET_V, d), RET_V ** -0.5),
        "ret_logit_f": logit0 + nrm((n_odd, RET_HEADS), 0.1),
        "ret_logit_b": logit0 + nrm((n_odd, RET_HEADS), 0.1),
    }


def reference(x, c, ctx, c_ctx, w_mod, b_mod, norm_gain, ffn_a_in, ffn_a_out, ffn_b_in, ffn_b_out,
              even_in, even_out, na_q_gain, na_k_gain, na_rpb, hy_conv_w, hy_conv_b,
              hy_fw1, hy_fb1, hy_fw2, hy_fb2, hy_fw3, hy_fb3, hy_fw4, hy_freq, hy_bias,
              ret_in, ret_out, ret_logit_f, ret_logit_b):
    b = x.shape[0]
    rope_cos, rope_sin = axial_rope_tables(x.shape[1], RET_KEY_DIM)
    x_lat, x_ctx = x, ctx
    for i in range(DEPTH):
        last = i == DEPTH - 1
        mod_l = (jax.nn.silu(c) @ w_mod[i] + b_mod[i]).reshape(b, N_MOD, 1, D_MODEL)
        mod_c = (jax.nn.silu(c_ctx) @ w_mod[i] + b_mod[i]).reshape(N_MOD, 1, D_MODEL)
        x_lat = x_lat + 0.5 * mod_l[:, 2] * swiglu(modulate(x_lat, norm_gain[i, 0], mod_l[:, 0], mod_l[:, 1]),
                                                 ffn_a_in[i], ffn_a_out[i])
        x_ctx = x_ctx + 0.5 * mod_c[2] * swiglu(modulate(x_ctx, norm_gain[i, 0], mod_c[0], mod_c[1]),
                                              ffn_a_in[i], ffn_a_out[i])
        h_l = modulate(x_lat, norm_gain[i, 1], mod_l[:, 3], mod_l[:, 4])
        h_c = modulate(x_ctx, norm_gain[i, 1], mod_c[3], mod_c[4])
        if i % 2 == 0:
            e = i // 2
            filter_params = (hy_fw1[e], hy_fb1[e], hy_fw2[e], hy_fb2[e], hy_fw3[e], hy_fb3[e], hy_fw4[e], hy_freq[e])
            o_l, o_c = na_hyena_mixer(h_l, h_c, even_in[e], even_out[e], na_q_gain[e], na_k_gain[e], na_rpb[e],
                                      hy_conv_w[e], hy_conv_b[e], filter_params, hy_bias[e], not last)
        else:
            o = i // 2
            o_l, o_c = retention_mixer(h_l, h_c, ret_in[o], ret_out[o], ret_logit_f[o], ret_logit_b[o],
                                       rope_cos, rope_sin, not last)
        x_lat = x_lat + mod_l[:, 5] * o_l
        x_lat = x_lat + 0.5 * mod_l[:, 8] * swiglu(modulate(x_lat, norm_gain[i, 2], mod_l[:, 6], mod_l[:, 7]),
                                                 ffn_b_in[i], ffn_b_out[i])
        if not last:
            x_ctx = x_ctx + mod_c[5] * o_c
            x_ctx = x_ctx + 0.5 * mod_c[8] * swiglu(modulate(x_ctx, norm_gain[i, 2], mod_c[6], mod_c[7]),
                                                  ffn_b_in[i], ffn_b_out[i])
    return x_lat
```

```python
import contextlib
import math
import numpy as np
import ml_dtypes
import concourse.bass as bass
import concourse.mybir as mybir
from concourse.bass_utils import run_bass_kernel_spmd

F32 = mybir.dt.float32
BF16 = mybir.dt.bfloat16
AF = mybir.ActivationFunctionType
ALU = mybir.AluOpType
AX = mybir.AxisListType

D = 1024
DFF = 2816
NMOD = 9
RMS_EPS = 1e-6
GN_EPS = 1e-6
GRID_W = 64
CTX = 256
SAME_ENGINE_SYNC = True


class Tok:
    __slots__ = ("w", "r", "name", "lane")

    def __init__(self, name=""):
        self.w = None
        self.r = {}
        self.name = name
        self.lane = None


class Lane:
    __slots__ = ("sem", "count", "sw")

    def __init__(self, sem):
        self.sem = sem
        self.count = 0
        self.sw = False


class T:
    def __init__(self, h, name):
        self.h = h
        self.tok = Tok(name)

    def __getitem__(self, idx):
        return self.h[idx]


class KB:
    def __init__(self):
        self.nc = bass.Bass("TRN2", target_bir_lowering=False)
        nc = self.nc
        self.es = contextlib.ExitStack()
        self.eng = dict(pe=nc.tensor, act=nc.scalar, dve=nc.vector, pool=nc.gpsimd, sp=nc.sync)
        self.sem = {e: self.es.enter_context(nc.semaphore("s_" + e)) for e in self.eng}
        self.cnt = {e: 0 for e in self.eng}
        self.seen = {e: {} for e in self.eng}
        self.free_lanes = []
        self.free_lanes_sw = []
        self.all_lanes = []
        self.nlanes = 0
        self.phase_stack = []
        self.ninst = 0
        self.uid = 0

    def _name(self, p):
        self.uid += 1
        return "%s_%d" % (p, self.uid)

    def sb(self, shape, dt, name="t"):
        st = self.phase_stack[-1][0] if self.phase_stack else self.es
        h = st.enter_context(self.nc.sbuf_tensor(self._name(name), list(shape), dt))
        t = T(h, name)
        if self.phase_stack:
            self.phase_stack[-1][1].append(t)
        return t

    def ps(self, shape, dt, name="p"):
        st = self.phase_stack[-1][0] if self.phase_stack else self.es
        h = st.enter_context(self.nc.psum_tensor(self._name(name), list(shape), dt))
        t = T(h, name)
        if self.phase_stack:
            self.phase_stack[-1][1].append(t)
        return t

    def pool_of(self, n, shape, dt, name, psum=False):
        return Ring([(self.ps if psum else self.sb)(shape, dt, name) for _ in range(n)])

    @contextlib.contextmanager
    def phase(self):
        st = contextlib.ExitStack()
        toks = []
        self.phase_stack.append((st, toks))
        try:
            yield
        finally:
            self.barrier()
            self.phase_stack.pop()
            for t in toks:
                if t.tok.lane is not None:
                    (self.free_lanes_sw if t.tok.lane.sw else self.free_lanes).append(t.tok.lane)
                    t.tok.lane = None
            st.close()

    def lane_of(self, tok, sw=False):
        if tok.lane is None:
            fl = self.free_lanes_sw if sw else self.free_lanes
            if fl:
                tok.lane = fl.pop()
            else:
                sem = self.es.enter_context(self.nc.semaphore("l_%d" % self.nlanes))
                self.nlanes += 1
                tok.lane = Lane(sem)
                tok.lane.sw = sw
                self.all_lanes.append(tok.lane)
        assert tok.lane.sw == sw, "token %s mixes SW and HW DMA queues" % tok.name
        return tok.lane

    def _waits(self, e, reads, writes):
        deps = {}

        def need(p):
            if p is None:
                return
            s, v = p
            k = id(s)
            if k not in deps or deps[k][1] < v:
                deps[k] = (s, v)

        for t in reads:
            need(t.w)
        for t in writes:
            need(t.w)
            for p in t.r.values():
                need(p)
        for k, (s, v) in deps.items():
            if s is self.sem[e] and (e == "pe" or e == "sp" or not SAME_ENGINE_SYNC):
                continue
            if self.seen[e].get(k, 0) >= v:
                continue
            self.eng[e].wait_ge(s, v)
            self.seen[e][k] = v
            self.ninst += 1

    @staticmethod
    def _toks(xs):
        out = []
        for x in xs:
            if x is None:
                continue
            out.append(x.tok if isinstance(x, T) else x)
        return out

    def op(self, e, emit, reads=(), writes=(), inc=True):
        reads = self._toks(reads)
        writes = self._toks(writes)
        self._waits(e, reads, writes)
        ins = emit(self.eng[e])
        self.ninst += 1
        if inc:
            self.cnt[e] += 1
            ins.then_inc(self.sem[e], 1)
            me = (self.sem[e], self.cnt[e])
        else:
            me = (self.sem[e], self.cnt[e] + 1)
        for t in reads:
            t.r[e] = me
        for t in writes:
            t.w = me
            t.r = {}
        return ins

    def dma(self, q, out, in_, reads=(), writes=(), lane_tok=None, **kw):
        reads = self._toks(reads)
        writes = self._toks(writes)
        lt = lane_tok.tok if isinstance(lane_tok, T) else lane_tok
        if lt is None:
            lt = writes[0] if writes else reads[0]
        lane = self.lane_of(lt, sw=(q == "pool"))
        self._waits(q, reads, writes)
        if q == "pool" and lane.count > 0 and self.seen[q].get(id(lane.sem), 0) < lane.count:
            self.eng[q].wait_ge(lane.sem, lane.count)
            self.seen[q][id(lane.sem)] = lane.count
        ins = self.eng[q].dma_start(out=out, in_=in_, **kw)
        self.ninst += 1
        lane.count += 16
        ins.then_inc(lane.sem, 16)
        me = (lane.sem, lane.count)
        for t in reads:
            t.r["dma%d" % id(lane)] = me
        for t in writes:
            t.w = me
            t.r = {}
        return ins

    def barrier(self):
        for e in self.eng:
            for f in self.eng:
                if f == e or self.cnt[f] == 0:
                    continue
                if self.seen[e].get(id(self.sem[f]), 0) >= self.cnt[f]:
                    continue
                self.eng[e].wait_ge(self.sem[f], self.cnt[f])
                self.seen[e][id(self.sem[f])] = self.cnt[f]
            for ln in self.all_lanes:
                if ln.count == 0 or self.seen[e].get(id(ln.sem), 0) >= ln.count:
                    continue
                self.eng[e].wait_ge(ln.sem, ln.count)
                self.seen[e][id(ln.sem)] = ln.count

    def mm(self, out, lhsT, rhs, start, stop, reads, writes, inc=None):
        if inc is None:
            inc = stop
        return self.op("pe", lambda g: g.matmul(out, lhsT, rhs, start=start, stop=stop), reads, writes, inc=inc)

    def tr(self, out, in_, ident, reads, writes, inc=True):
        return self.op("pe", lambda g: g.transpose(out, in_, ident), reads, writes, inc=inc)

    def act(self, out, in_, func, reads, writes, bias=None, scale=None, accum_out=None, e="act"):
        kw = {}
        if bias is not None:
            kw["bias"] = bias
        if scale is not None:
            kw["scale"] = scale
        if accum_out is not None:
            kw["accum_out"] = accum_out
        return self.op(e, lambda g: g.activation(out, in_, func, **kw), reads, writes)

    def ts(self, e, out, in0, s1, s2, op0, op1, reads, writes, accum_out=None):
        kw = {}
        if op1 is not None:
            kw["op1"] = op1
        if accum_out is not None:
            kw["accum_out"] = accum_out
        return self.op(e, lambda g: g.tensor_scalar(out, in0, s1, s2, op0, **kw), reads, writes)

    def tt(self, e, out, in0, in1, op, reads, writes):
        return self.op(e, lambda g: g.tensor_tensor(out, in0, in1, op), reads, writes)

    def stt(self, out, in0, scalar, in1, op0, op1, reads, writes, e="dve"):
        return self.op(e, lambda g: g.scalar_tensor_tensor(out, in0, scalar, in1, op0, op1), reads, writes)

    def copy(self, e, out, in_, reads, writes):
        if e == "act":
            return self.op(e, lambda g: g.copy(out, in_), reads, writes)
        return self.op(e, lambda g: g.tensor_copy(out, in_), reads, writes)

    def memset(self, e, ap, val, writes):
        return self.op(e, lambda g: g.memset(ap, val), (), writes)


class Ring:
    def __init__(self, items):
        self.items = items
        self.i = 0

    def get(self):
        t = self.items[self.i % len(self.items)]
        self.i += 1
        return t


class Cfg:
    def __init__(self, rows=64, depth=4, types=None):
        self.rows = rows
        self.L = rows * GRID_W
        self.depth = depth
        self.types = types if types is not None else [("even" if i % 2 == 0 else "ret") for i in range(depth)]
        self.layers = list(range(depth))


def bcast_rows(ap, n=128):
    return ap.partition_broadcast(n)


class Prog:
    def __init__(self, cfg, stages=None):
        self.cfg = cfg
        self.k = KB()
        self.nc = self.k.nc
        self.stages = stages
        nc = self.nc
        L = cfg.L
        nl = cfg.depth
        ne = (nl + 1) // 2
        no = nl // 2
        self.inp = {}

        def din(name, shape, dt=F32):
            self.inp[name] = nc.dram_tensor(name, list(shape), dt, kind="ExternalInput").ap()
            return self.inp[name]

        din("x", [L, D]); din("c", [1, D]); din("ctx", [CTX, D]); din("c_ctx", [1, D])
        din("w_mod", [nl, D, NMOD * D]); din("b_mod", [nl, NMOD * D]); din("norm_gain", [nl, 3, D])
        din("ffn_a_in", [nl, D, 2 * DFF]); din("ffn_a_out", [nl, DFF, D])
        din("ffn_b_in", [nl, D, 2 * DFF]); din("ffn_b_out", [nl, DFF, D])
        din("ident", [128, 128], BF16)
        self.out = nc.dram_tensor("out", [L, D], F32, kind="ExternalOutput").ap()
        self.xc = nc.dram_tensor("xc_scr", [CTX, D], F32, kind="ExternalOutput").ap()
        self.mod = nc.dram_tensor("mod_scr", [nl, 2, NMOD * D], F32).ap()
        if "ret" in cfg.types:
            self.ret_setup()
        if "even" in cfg.types:
            self.even_setup()

    def inp_add(self, name, shape, dt=F32):
        self.inp[name] = self.nc.dram_tensor(name, list(shape), dt, kind="ExternalInput").ap()
        return self.inp[name]

    def tile_tok0(self, xap):
        off = xap.offset // D
        return off if xap.tensor.name == self.out.tensor.name else self.cfg.L + off

    def consts(self):
        k = self.k
        self.ident = k.sb([128, 128], BF16, "ident")
        k.dma("sp", self.ident[:, :], self.inp["ident"], writes=[self.ident])
        self.epsb = k.sb([128, 1], F32, "eps")
        k.memset("dve", self.epsb[:, :], RMS_EPS, [self.epsb])

    def init_copy(self):
        k = self.k
        L = self.cfg.L
        self.t_xlat = Tok("xlat")
        self.t_xctx = Tok("xctx")
        nchunk = max(1, L // 1024)
        rows = L // nchunk
        for i in range(nchunk):
            k.dma("sp", self.out[i * rows:(i + 1) * rows, :], self.inp["x"][i * rows:(i + 1) * rows, :],
                  writes=[self.t_xlat])
        k.dma("sp", self.xc[:, :], self.inp["ctx"][:, :], writes=[self.t_xctx])
        k.barrier()

    def phase_mod(self, l):
        k = self.k
        with k.phase():
            cT = k.sb([128, 8, 2], F32, "cT")
            with self.nc.allow_non_contiguous_dma("tiny"):
                k.dma("sp", cT[:, :, 0], self.inp["c"].rearrange("o (c p) -> p (o c)", p=128), writes=[cT])
                k.dma("sp", cT[:, :, 1], self.inp["c_ctx"].rearrange("o (c p) -> p (o c)", p=128), writes=[cT])
            sc = k.sb([128, 8, 2], F32, "sc")
            k.act(sc[:, :, :], cT[:, :, :], AF.Silu, [cT], [sc])
            ones2 = k.sb([1, 2], F32, "ones2")
            k.memset("dve", ones2[:, :], 1.0, [ones2])
            brow = k.sb([1, NMOD * D], F32, "brow")
            k.dma("sp", brow[:, :], self.inp["b_mod"][l:l + 1, :], writes=[brow])
            wpool = k.pool_of(2, [128, 8, 512], F32, "wm")
            ppool = k.pool_of(2, [2, 512], F32, "pm", psum=True)
            spool = k.pool_of(2, [2, 512], F32, "sm")
            for n in range(NMOD * D // 512):
                w = wpool.get()
                k.dma("sp", w[:, :, :], self.inp["w_mod"][l, :, n * 512:(n + 1) * 512].rearrange("(c p) n -> p c n", p=128),
                      writes=[w])
                p = ppool.get()
                for c in range(8):
                    k.mm(p[:, :], sc[:, c, :], w[:, c, :], c == 0, False, [sc, w], [p])
                k.mm(p[:, :], ones2[:, :], brow[:, n * 512:(n + 1) * 512], False, True, [ones2, brow], [p])
                s = spool.get()
                k.copy("dve", s[:, :], p[:, :], [p], [s])
                k.dma("sp", self.mod[l, :, n * 512:(n + 1) * 512], s[:, :], reads=[s])

    def load_mod_vecs(self, l, nj, mv, half_gate):
        k = self.k
        G = k.sb([128, 8, 2], F32, "G")
        S = k.sb([128, 8, 2], F32, "S")
        gn = k.sb([128, 8], F32, "gn")
        gate = k.sb([128, 2, D], F32, "gate")
        with self.nc.allow_non_contiguous_dma("tiny"):
            k.dma("sp", gn[:, :], self.inp["norm_gain"][l, nj:nj + 1, :].rearrange("o (c p) -> p (o c)", p=128), writes=[gn])
            for r in range(2):
                k.dma("sp", S[:, :, r], self.mod[l, r:r + 1, mv * D:(mv + 1) * D].rearrange("o (c p) -> p (o c)", p=128), writes=[S])
                k.dma("sp", G[:, :, r], self.mod[l, r:r + 1, (mv + 1) * D:(mv + 2) * D].rearrange("o (c p) -> p (o c)", p=128), writes=[G])
                k.dma("sp", gate[:, r, :], bcast_rows(self.mod[l, r:r + 1, (mv + 2) * D:(mv + 3) * D]), writes=[gate])
        for r in range(2):
            k.stt(G[:, :, r], G[:, :, r], 1.0, gn[:, :], ALU.add, ALU.mult, [G, gn], [G])
        if half_gate:
            k.ts("dve", gate[:, :, :], gate[:, :, :], 0.5, None, ALU.mult, None, [gate], [gate])
        return G, S, gate

    def tiles(self, tsz):
        out = []
        for i in range(self.cfg.L // tsz):
            out.append((self.out[i * tsz:(i + 1) * tsz, :], tsz, 0))
        for i in range(max(1, CTX // tsz)):
            n = min(tsz, CTX)
            out.append((self.xc[i * n:(i + 1) * n, :], n, 1))
        return out

    def norm_mod_T(self, xt, ns, row, G, S, xs_pool, tp_pool, xnT, scr, ssq, rstd):
        k = self.k
        for s in range(ns):
            k.act(scr[:, :], xt[:, s, :], AF.Square, [xt], [scr, ssq], accum_out=ssq[:, s:s + 1])
        k.act(rstd[:, :ns], ssq[:, :ns], AF.Sqrt, [ssq, self.epsb], [rstd], bias=self.epsb[:, 0:1], scale=1.0 / D)
        k.op("dve", lambda g: g.reciprocal(rstd[:, :ns], rstd[:, :ns]), [rstd], [rstd])
        xs = xs_pool.get()
        for s in range(ns):
            k.ts("dve", xs[:, s, :], xt[:, s, :], rstd[:, s:s + 1], None, ALU.mult, None, [xt, rstd], [xs])
        for c in range(8):
            tp = tp_pool.get()
            for s in range(ns):
                k.tr(tp[:, s * 128:(s + 1) * 128], xs[:, s, c * 128:(c + 1) * 128], self.ident[:, :],
                     [xs, self.ident], [tp], inc=(s == ns - 1))
            k.act(xnT[:, c, :ns * 128], tp[:, :ns * 128], AF.Identity, [tp, G, S], [xnT],
                  bias=S[:, c, row:row + 1], scale=G[:, c, row:row + 1])

    def phase_ffn(self, l, which):
        k = self.k
        TS = 256
        NS = TS // 128
        win_d = self.inp["ffn_a_in" if which == 0 else "ffn_b_in"]
        wout_d = self.inp["ffn_a_out" if which == 0 else "ffn_b_out"]
        nj, mv = (0, 0) if which == 0 else (2, 6)
        NF = DFF // 128
        with k.phase():
            Win = [k.sb([128, 2 * DFF], BF16, "Win%d" % c) for c in range(8)]
            Wout = k.sb([128, NF, D], BF16, "Wout")
            for c in range(8):
                k.dma("pool", Win[c][:, :], win_d[l, c * 128:(c + 1) * 128, :], writes=[Win[c]])
            k.dma("pool", Wout[:, :, :], wout_d[l, :, :].rearrange("(j p) d -> p j d", p=128), writes=[Wout])
            G, S, gate = self.load_mod_vecs(l, nj, mv, True)
            xpool = k.pool_of(2, [128, NS, D], F32, "xt")
            xs_pool = k.pool_of(1, [128, NS, D], BF16, "xs")
            tp_pool = k.pool_of(2, [128, TS], BF16, "tp", psum=True)
            xnT = k.sb([128, 8, TS], BF16, "xnT")
            hT = k.sb([128, NF, TS], BF16, "hT")
            scr = k.sb([128, D], F32, "scr")
            ssq = k.sb([128, NS], F32, "ssq")
            rstd = k.sb([128, NS], F32, "rstd")
            pa_pool = k.pool_of(2, [128, TS], F32, "pa", psum=True)
            pb_pool = k.pool_of(2, [128, TS], F32, "pb", psum=True)
            sa_pool = k.pool_of(2, [128, TS], BF16, "sa")
            po_pool = k.pool_of(2, [128, 512], F32, "po", psum=True)
            for (xap, ntok, row) in self.tiles(TS):
                ns = ntok // 128
                xt = xpool.get()
                k.dma("sp", xt[:, :ns, :], xap.rearrange("(s p) d -> p s d", p=128), writes=[xt])
                self.norm_mod_T(xt, ns, row, G, S, xs_pool, tp_pool, xnT, scr, ssq, rstd)
                for j in range(NF):
                    pa = pa_pool.get()
                    pb = pb_pool.get()
                    for c in range(8):
                        k.mm(pa[:, :ntok], Win[c][:, j * 128:(j + 1) * 128], xnT[:, c, :ntok], c == 0, c == 7,
                             [Win[c], xnT], [pa])
                    for c in range(8):
                        k.mm(pb[:, :ntok], Win[c][:, DFF + j * 128:DFF + (j + 1) * 128], xnT[:, c, :ntok], c == 0, c == 7,
                             [Win[c], xnT], [pb])
                    sa = sa_pool.get()
                    k.act(sa[:, :ntok], pa[:, :ntok], AF.Silu, [pa], [sa])
                    k.tt("dve", hT[:, j, :ntok], sa[:, :ntok], pb[:, :ntok], ALU.mult, [sa, pb], [hT])
                for s in range(ns):
                    for hf in range(2):
                        po = po_pool.get()
                        for j in range(NF):
                            k.mm(po[:, :], hT[:, j, s * 128:(s + 1) * 128], Wout[:, j, hf * 512:(hf + 1) * 512],
                                 j == 0, j == NF - 1, [hT, Wout], [po])
                        sl = slice(hf * 512, (hf + 1) * 512)
                        k.tt("dve", scr[:, sl], po[:, :], gate[:, row, sl], ALU.mult, [po, gate], [scr])
                        k.tt("pool", xt[:, s, sl], xt[:, s, sl], scr[:, sl], ALU.add, [xt, scr], [xt])
                k.dma("sp", xap.rearrange("(s p) d -> p s d", p=128), xt[:, :ns, :], reads=[xt])

    def build(self):
        k = self.k
        self.consts()
        self.init_copy()
        for l in self.cfg.layers:
            self.phase_mod(l)
            self.phase_ffn(l, 0)
            if self.stages == "ffn_a":
                continue
            if self.stages != "ffn_only":
                if self.cfg.types[l] == "ret":
                    self.phase_ret(l)
                else:
                    self.phase_even(l)
            if self.stages == "mix":
                continue
            if self.stages == "mix_only" and False:
                continue
            self.phase_ffn(l, 1)
        k.barrier()
        return self.nc


RH = 4
RDK = 256
RDV = 512
RQK = RH * RDK
RV = RH * RDV
RIN = 2 * RQK + 2 * RV


def _ret_setup(self):
    nc = self.nc
    L = self.cfg.L
    NT = L + CTX
    NCH = NT // 128
    self.ret_in = self.inp_add("ret_in", [max(1, self.cfg.types.count("ret")), D, RIN])
    self.ret_out = self.inp_add("ret_out", [max(1, self.cfg.types.count("ret")), RV, D])
    self.ret_lf = self.inp_add("ret_logit_f", [max(1, self.cfg.types.count("ret")), RH])
    self.ret_lb = self.inp_add("ret_logit_b", [max(1, self.cfg.types.count("ret")), RH])
    self.rope = self.inp_add("rope_tab", [L, 2, 128])
    self.rconst = self.inp_add("ret_const", [128, 6, 128])
    self.qts = nc.dram_tensor("qts", [NCH, 128, 1024], BF16).ap()
    self.kts = nc.dram_tensor("kts", [NCH, 128, 1024], BF16).ap()
    self.ktok = nc.dram_tensor("ktok", [NT, RQK], BF16).ap()
    self.vtok = nc.dram_tensor("vtok", [NT, RV], BF16).ap()
    self.sgt = nc.dram_tensor("sgt", [NT, RV], BF16).ap()
    self.st = nc.dram_tensor("st", [2, NCH, RH, 128, 1024], BF16).ap()


def ret_consts_host():
    j = np.arange(128, dtype=np.float32)
    c = np.zeros((128, 6, 128), np.float32)
    diff = j[None, :] - j[:, None]
    c[:, 0, :] = np.maximum(diff, 0.0)
    c[:, 1, :] = np.maximum(-diff, 0.0)
    c[:, 2, :] = (diff >= 0).astype(np.float32) / 16.0
    c[:, 3, :] = (diff <= 0).astype(np.float32) / 16.0
    c[:, 4, :] = (j[None, :] + 1.0)
    c[:, 5, :] = (128.0 - j[None, :])
    return c


def _phase_ret(self, l):
    k = self.k
    nc = self.nc
    o = self.cfg.types[:l].count("ret")
    L = self.cfg.L
    NT = L + CTX
    NCH = NT // 128
    NCL = L // 128
    last = (l == self.cfg.depth - 1) and not getattr(self, "force_ctx_out", False)

    with k.phase():
        Wr = [k.sb([128, RIN], BF16, "Wr%d" % c) for c in range(8)]
        for c in range(8):
            k.dma("pool", Wr[c][:, :], self.ret_in[o, c * 128:(c + 1) * 128, :], writes=[Wr[c]])
        G, S, gate = self.load_mod_vecs(l, 1, 3, False)
        TS = 256
        xpool = k.pool_of(2, [128, 2, D], F32, "xt")
        xs_pool = k.pool_of(1, [128, 2, D], BF16, "xs")
        tp_pool = k.pool_of(2, [128, TS], BF16, "tp", psum=True)
        xnT = k.sb([128, 8, TS], BF16, "xnT")
        scr = k.sb([128, D], F32, "scr")
        ssq = k.sb([128, 2], F32, "ssq")
        rstd = k.sb([128, 2], F32, "rstd")
        pp = k.pool_of(3, [128, 512], F32, "pp", psum=True)
        tq_pool = k.pool_of(2, [128, 1024], BF16, "tq", psum=True)
        rope_pool = k.pool_of(2, [128, 2, 128], F32, "rope")
        qtok_pool = k.pool_of(2, [128, RQK], BF16, "qtok")
        ktok_pool = k.pool_of(2, [128, RQK], BF16, "ktok")
        v_pool = k.pool_of(2, [128, RV], BF16, "vst")
        g_pool = k.pool_of(2, [128, RV], BF16, "gst")
        qT_pool = k.pool_of(2, [128, 1024], BF16, "qTs")
        kT_pool = k.pool_of(2, [128, 1024], BF16, "kTs")
        tmp = [k.sb([128, 256], F32, "rt%d" % i) for i in range(4)]
        for (xap, ntok, row) in self.tiles(TS):
            xt = xpool.get()
            k.dma("sp", xt[:, :, :], xap.rearrange("(s p) d -> p s d", p=128), writes=[xt])
            self.norm_mod_T(xt, 2, row, G, S, xs_pool, tp_pool, xnT, scr, ssq, rstd)
            for s in range(2):
                tok0 = (self.tile_tok0(xap) + s * 128)
                ch = tok0 // 128
                if row == 0:
                    rp = rope_pool.get()
                    k.dma("sp", rp[:, :, :], self.rope[tok0:tok0 + 128, :, :], writes=[rp])
                qtok = qtok_pool.get()
                ktok = ktok_pool.get()
                vst = v_pool.get()
                gst = g_pool.get()
                for n in range(12):
                    p = pp.get()
                    for c in range(8):
                        k.mm(p[:, :], xnT[:, c, s * 128:(s + 1) * 128], Wr[c][:, n * 512:(n + 1) * 512], c == 0, c == 7,
                             [xnT, Wr[c]], [p])
                    if n < 4:
                        dst = qtok if n < 2 else ktok
                        cs = (n % 2) * 512
                        if row == 0:
                            pv = p[:, :].rearrange("p (h g f d) -> p h g f d", h=2, g=2, f=2)
                            dv = dst[:, cs:cs + 512].rearrange("p (h g f d) -> p h g f d", h=2, g=2, f=2)
                            ct = rp[:, 0, :].rearrange("p (g d) -> p g d", g=2).unsqueeze(1).broadcast_to([128, 2, 2, 64])
                            sn = rp[:, 1, :].rearrange("p (g d) -> p g d", g=2).unsqueeze(1).broadcast_to([128, 2, 2, 64])
                            tv = [t[:, :].rearrange("p (h g d) -> p h g d", h=2, g=2) for t in tmp]
                            k.tt("dve", tv[0], pv[:, :, :, 0, :], ct, ALU.mult, [p, rp], [tmp[0]])
                            k.tt("dve", tv[1], pv[:, :, :, 1, :], sn, ALU.mult, [p, rp], [tmp[1]])
                            k.tt("dve", tv[2], pv[:, :, :, 0, :], sn, ALU.mult, [p, rp], [tmp[2]])
                            k.tt("dve", tv[3], pv[:, :, :, 1, :], ct, ALU.mult, [p, rp], [tmp[3]])
                            k.tt("pool", dv[:, :, :, 0, :], tv[0], tv[1], ALU.subtract, [tmp[0], tmp[1]], [dst])
                            k.tt("pool", dv[:, :, :, 1, :], tv[2], tv[3], ALU.add, [tmp[2], tmp[3]], [dst])
                        else:
                            k.copy("act", dst[:, cs:cs + 512], p[:, :], [p], [dst])
                    elif n < 8:
                        k.copy("act", vst[:, (n - 4) * 512:(n - 3) * 512], p[:, :], [p], [vst])
                    else:
                        k.act(gst[:, (n - 8) * 512:(n - 7) * 512], p[:, :], AF.Silu, [p], [gst])
                for (src, dpool, dscr) in ((qtok, qT_pool, self.qts), (ktok, kT_pool, self.kts)):
                    tq = tq_pool.get()
                    for b in range(8):
                        k.tr(tq[:, b * 128:(b + 1) * 128], src[:, b * 128:(b + 1) * 128], self.ident[:, :],
                             [src, self.ident], [tq], inc=(b == 7))
                    dT = dpool.get()
                    k.copy("dve" if src is qtok else "act", dT[:, :], tq[:, :], [tq], [dT])
                    k.dma("sp", dscr[ch, :, :], dT[:, :], reads=[dT])
                k.dma("sp", self.ktok[tok0:tok0 + 128, :], ktok[:, :], reads=[ktok])
                k.dma("sp", self.vtok[tok0:tok0 + 128, :], vst[:, :], reads=[vst])
                k.dma("sp", self.sgt[tok0:tok0 + 128, :], gst[:, :], reads=[gst])

    with k.phase():
        rc = k.sb([128, 6, 128], F32, "rconst")
        k.dma("sp", rc[:, :, :], self.rconst, writes=[rc])
        lg = k.sb([128, 2 * RH], F32, "lg")
        k.dma("sp", lg[:, 0:RH], bcast_rows(self.ret_lf[o:o + 1, :]), writes=[lg])
        k.dma("sp", lg[:, RH:2 * RH], bcast_rows(self.ret_lb[o:o + 1, :]), writes=[lg])
        k.act(lg[:, :], lg[:, :], AF.Exp, [lg], [lg], scale=-1.0)
        k.act(lg[:, :], lg[:, :], AF.Ln, [lg], [lg], bias=1.0, scale=1.0)
        k.ts("dve", lg[:, :], lg[:, :], -1.0, None, ALU.mult, None, [lg], [lg])
        zeta = k.sb([128, 2 * RH], F32, "zeta")
        gch = k.sb([128, 2 * RH], F32, "gch")
        XI = k.sb([128, 2 * RH, 128], BF16, "XI")
        DcT = k.sb([128, RH, 128], BF16, "DcT")
        dtmp = k.sb([128, 2, 128], F32, "dtmp")
        for e in range(RH):
            f, b = e, RH + e
            k.act(XI[:, f, :], rc[:, 4, :], AF.Exp, [rc, lg], [XI], scale=lg[:, f:f + 1])
            k.act(XI[:, b, :], rc[:, 5, :], AF.Exp, [rc, lg], [XI], scale=lg[:, b:b + 1])
            k.act(dtmp[:, 0, :], rc[:, 0, :], AF.Exp, [rc, lg], [dtmp], scale=lg[:, f:f + 1])
            k.act(dtmp[:, 1, :], rc[:, 1, :], AF.Exp, [rc, lg], [dtmp], scale=lg[:, b:b + 1])
            k.tt("dve", dtmp[:, :, :], dtmp[:, :, :], rc[:, 2:4, :], ALU.mult, [dtmp, rc], [dtmp])
            k.tt("dve", DcT[:, e, :], dtmp[:, 0, :], dtmp[:, 1, :], ALU.add, [dtmp], [DcT])
        for e in range(RH):
            k.act(zeta[:, e:e + 1], rc[:, 0, 127:128], AF.Exp, [rc, lg], [zeta], scale=lg[:, e:e + 1])
            k.act(zeta[:, RH + e:RH + e + 1], rc[:, 1, 0:1], AF.Exp, [rc, lg], [zeta], scale=lg[:, RH + e:RH + e + 1])
        k.ts("dve", zeta[:, :], zeta[:, :], 1.0 / 16.0, None, ALU.mult, None, [zeta], [zeta])
        k.act(gch[:, :], lg[:, :], AF.Exp, [lg], [gch], scale=128.0)

        with k.phase():
            Sst = [[k.sb([128, 2, RDV], F32, "S%d%d" % (d, e)) for e in range(RH)] for d in range(2)]
            Sbf = [[k.pool_of(2, [128, 2 * RDV], BF16, "Sb%d%d" % (d, e)) for e in range(RH)] for d in range(2)]
            for d in range(2):
                for e in range(RH):
                    k.memset("pool", Sst[d][e][:, :, :], 0.0, [Sst[d][e]])
            kin = k.pool_of(4, [128, RQK], BF16, "kin")
            vin = k.pool_of(4, [128, RV], BF16, "vin")
            kz_pool = k.pool_of(4, [128, RDK], BF16, "kz")
            pd = k.pool_of(6, [128, RDV], F32, "pd", psum=True)
            order_f = [NCL, NCL + 1] + list(range(NCL))
            order_b = [NCL + 1, NCL] + list(range(NCL - 1, -1, -1))
            for step in range(NCH):
                for d, order in ((0, order_f), (1, order_b)):
                    ch = order[step]
                    kt = kin.get()
                    vt = vin.get()
                    k.dma("sp", kt[:, :], self.ktok[ch * 128:(ch + 1) * 128, :], writes=[kt])
                    k.dma("sp", vt[:, :], self.vtok[ch * 128:(ch + 1) * 128, :], writes=[vt])
                    for e in range(RH):
                        S_ = Sst[d][e]
                        sb_ = Sbf[d][e].get()
                        k.copy("act", sb_[:, :], S_[:, :, :].rearrange("p a b -> p (a b)"), [S_], [sb_])
                        k.dma("sp", self.st[d, ch, e, :, :], sb_[:, :], reads=[sb_])
                        if step == NCH - 1:
                            continue
                        kz = kz_pool.get()
                        k.ts("pool", kz[:, :], kt[:, e * RDK:(e + 1) * RDK], zeta[:, d * RH + e:d * RH + e + 1], None,
                             ALU.mult, None, [kt, zeta], [kz])
                        for dc in range(2):
                            p = pd.get()
                            k.mm(p[:, :], kz[:, dc * 128:(dc + 1) * 128], vt[:, e * RDV:(e + 1) * RDV], True, True, [kz, vt], [p])
                            k.stt(S_[:, dc, :], S_[:, dc, :], gch[:, d * RH + e:d * RH + e + 1], p[:, :], ALU.mult, ALU.add,
                                  [S_, gch, p], [S_])

        with k.phase():
            Wo = k.sb([128, 16, D], BF16, "Wo")
            k.dma("pool", Wo[:, :, :], self.ret_out[o, :, :].rearrange("(j p) d -> p j d", p=128), writes=[Wo])
            G, S, gate = self.load_mod_vecs(l, 1, 3, False)
            qT_pool = k.pool_of(2, [128, 8, 128], BF16, "qTc")
            kT_pool = k.pool_of(2, [128, 8, 128], BF16, "kTc")
            v_pool = k.pool_of(2, [128, RV], BF16, "vc")
            g_pool = k.pool_of(2, [128, RV], BF16, "gc")
            st_pool = k.pool_of(2, [128, 2, RH, 2, RDV], BF16, "stc")
            x_pool = k.pool_of(2, [128, D], F32, "xc")
            ps_pool = k.pool_of(2, [128, 128], F32, "psc", psum=True)
            po_pool = k.pool_of(2, [128, RDV], F32, "poc", psum=True)
            pt_pool = k.pool_of(2, [128, 1024], BF16, "ptc", psum=True)
            pr_pool = k.pool_of(2, [128, 512], F32, "prc", psum=True)
            in_pool = k.pool_of(2, [128, 128], BF16, "inT")
            qs_pool = k.pool_of(2, [128, 2, 2, 128], BF16, "qs")
            Y = k.sb([128, RV], BF16, "Y")
            YT = k.sb([128, 16, 128], BF16, "YT")
            stats = k.sb([128, RH, 6], F32, "stats")
            mv = k.sb([128, RH, 2], F32, "mv")
            rs = k.sb([128, RH], F32, "rs")
            nb = k.sb([128, RH], F32, "nb")
            yn = k.pool_of(2, [128, RDV], F32, "yn")
            scr2 = k.sb([128, D], F32, "scr2")
            gneps = k.sb([128, 1], F32, "gneps")
            k.memset("dve", gneps[:, :], GN_EPS, [gneps])
            chunks = list(range(NCL)) + ([] if last else [NCL, NCL + 1])
            for ch in chunks:
                row = 0 if ch < NCL else 1
                xap = self.out[ch * 128:(ch + 1) * 128, :] if row == 0 else self.xc[(ch - NCL) * 128:(ch - NCL + 1) * 128, :]
                qT = qT_pool.get(); kT = kT_pool.get(); vt = v_pool.get(); gt = g_pool.get(); stt_ = st_pool.get(); xt = x_pool.get()
                k.dma("sp", qT[:, :, :], self.qts[ch, :, :].rearrange("p (b t) -> p b t", b=8), writes=[qT])
                k.dma("sp", kT[:, :, :], self.kts[ch, :, :].rearrange("p (b t) -> p b t", b=8), writes=[kT])
                k.dma("sp", vt[:, :], self.vtok[ch * 128:(ch + 1) * 128, :], writes=[vt])
                k.dma("sp", gt[:, :], self.sgt[ch * 128:(ch + 1) * 128, :], writes=[gt])
                for d in range(2):
                    k.dma("sp", stt_[:, d, :, :, :], self.st[d, ch, :, :, :].rearrange("e p (a b) -> p e a b", a=2), writes=[stt_])
                k.dma("sp", xt[:, :], xap, writes=[xt])
                for e in range(RH):
                    ps = ps_pool.get()
                    for dc in range(2):
                        k.mm(ps[:, :], kT[:, 2 * e + dc, :], qT[:, 2 * e + dc, :], dc == 0, dc == 1, [kT, qT], [ps])
                    inT = in_pool.get()
                    k.tt("dve", inT[:, :], ps[:, :], DcT[:, e, :], ALU.mult, [ps, DcT], [inT])
                    qs = qs_pool.get()
                    for d in range(2):
                        k.tt("pool", qs[:, d, :, :], qT[:, 2 * e:2 * e + 2, :],
                             XI[:, d * RH + e, :].unsqueeze(1).broadcast_to([128, 2, 128]), ALU.mult, [qT, XI], [qs])
                    po = po_pool.get()
                    k.mm(po[:, :], inT[:, :], vt[:, e * RDV:(e + 1) * RDV], True, False, [inT, vt], [po])
                    for d in range(2):
                        for dc in range(2):
                            k.mm(po[:, :], qs[:, d, dc, :], stt_[:, d, e, dc, :], False, (d == 1 and dc == 1), [qs, stt_], [po])
                    k.op("dve", lambda g, e=e: g.bn_stats(stats[:, e, :], po[:, :]), [po], [stats])
                    k.op("dve", lambda g, e=e: g.bn_aggr(mv[:, e, :], stats[:, e, :]), [stats], [mv])
                    k.act(rs[:, e:e + 1], mv[:, e, 1:2], AF.Sqrt, [mv, gneps], [rs], bias=gneps[:, 0:1], scale=1.0)
                    k.op("dve", lambda g, e=e: g.reciprocal(rs[:, e:e + 1], rs[:, e:e + 1]), [rs], [rs])
                    k.stt(nb[:, e:e + 1], mv[:, e, 0:1], -1.0, rs[:, e:e + 1], ALU.mult, ALU.mult, [mv, rs], [nb])
                    y_ = yn.get()
                    k.act(y_[:, :], po[:, :], AF.Identity, [po, rs, nb], [y_], bias=nb[:, e:e + 1], scale=rs[:, e:e + 1])
                    k.tt("pool", Y[:, e * RDV:(e + 1) * RDV], y_[:, :], gt[:, e * RDV:(e + 1) * RDV], ALU.mult, [y_, gt], [Y])
                for hf in range(2):
                    pt = pt_pool.get()
                    for b in range(8):
                        bb = hf * 8 + b
                        k.tr(pt[:, b * 128:(b + 1) * 128], Y[:, bb * 128:(bb + 1) * 128], self.ident[:, :], [Y, self.ident], [pt],
                             inc=(b == 7))
                    k.copy("dve" if hf == 0 else "act", YT[:, hf * 8:(hf + 1) * 8, :].rearrange("p b t -> p (b t)"), pt[:, :], [pt], [YT])
                for hf in range(2):
                    pr = pr_pool.get()
                    for b in range(16):
                        k.mm(pr[:, :], YT[:, b, :], Wo[:, b, hf * 512:(hf + 1) * 512], b == 0, b == 15, [YT, Wo], [pr])
                    sl = slice(hf * 512, (hf + 1) * 512)
                    k.tt("dve", scr2[:, sl], pr[:, :], gate[:, row, sl], ALU.mult, [pr, gate], [scr2])
                    k.tt("pool", xt[:, sl], xt[:, sl], scr2[:, sl], ALU.add, [xt, scr2], [xt])
                k.dma("sp", xap, xt[:, :], reads=[xt])


Prog.ret_setup = _ret_setup
Prog.phase_ret = _phase_ret


def host_consts(rows):
    L = rows * GRID_W
    out = {}
    out["ident"] = np.eye(128, dtype=np.float32).astype(ml_dtypes.bfloat16)
    t = np.arange(L)
    nf = RDK // 4
    inv = (10000.0 ** (-np.arange(nf, dtype=np.float32) / nf)).astype(np.float32)
    ang = np.concatenate([(t // GRID_W).astype(np.float32)[:, None] * inv, (t % GRID_W).astype(np.float32)[:, None] * inv], axis=-1)
    rope = np.stack([np.cos(ang), np.sin(ang)], axis=1).astype(np.float32)
    out["rope_tab"] = rope
    out["ret_const"] = ret_consts_host()
    for nm, Ls in (("lat", L), ("ctx", CTX)):
        hc = hyena_consts_host(Ls)
        out["dft_" + nm] = hc["dft"]; out["emb_" + nm] = hc["emb"]; out["negt_" + nm] = hc["negt"]; out["wk_" + nm] = hc["wk"]
    max_decay = math.log(1e-2) / 0.3
    min_decay = math.log(1e-2) / 1.5
    out["absdelta"] = np.abs(np.linspace(min_decay, max_decay, HYW, dtype=np.float32))[None, :].astype(np.float32)
    return out


NAH = 8
NAD = 64
NAW = 512
HYW = 512
EIN = 3 * NAW + 3 * HYW
HY_EMB = 17
HY_ORDER = 64
I32 = mybir.dt.int32
TWO_PI = 2.0 * math.pi


def _even_setup(self):
    nc = self.nc
    L = self.cfg.L
    NT = L + CTX
    ne = max(1, self.cfg.types.count("even"))
    a = self.inp_add
    a("even_in", [ne, D, EIN]); a("even_out", [ne, D, D]); a("na_q_gain", [ne, NAD]); a("na_k_gain", [ne, NAD])
    a("na_tab", [ne, NAH, 128, 2, 16, 64])
    a("hy_conv_w", [ne, 3, 3 * HYW]); a("hy_conv_b", [ne, 3 * HYW])
    a("hy_fw1", [ne, HY_EMB, HY_ORDER]); a("hy_fb1", [ne, HY_ORDER]); a("hy_fw2", [ne, HY_ORDER, HY_ORDER]); a("hy_fb2", [ne, HY_ORDER])
    a("hy_fw3", [ne, HY_ORDER, HY_ORDER]); a("hy_fb3", [ne, HY_ORDER]); a("hy_fw4", [ne, HY_ORDER, 2 * HYW]); a("hy_freq", [ne, HY_ORDER])
    a("hy_bias", [ne, HYW])
    for nm, Ls in (("lat", L), ("ctx", CTX)):
        KC = Ls // 128 + 1
        a("dft_" + nm, [2, KC, 128, KC, 128], BF16)
        a("emb_" + nm, [HY_EMB, Ls])
        a("negt_" + nm, [128, Ls // 128])
        a("wk_" + nm, [128, KC, 2])
    a("absdelta", [1, HYW])
    self.qtn = nc.dram_tensor("qtn", [4, 128, NT], BF16).ap()
    self.ktn = nc.dram_tensor("ktn", [4, 128, NT], BF16).ap()
    self.vn = nc.dram_tensor("vn", [NT, NAW], BF16).ap()
    self.u_lat = nc.dram_tensor("u_lat", [L + 2, 3 * HYW], F32).ap()
    self.u_ctx = nc.dram_tensor("u_ctx", [CTX + 2, 3 * HYW], F32).ap()
    self.cat = nc.dram_tensor("cat", [NT, D], BF16).ap()
    self.x0z = nc.dram_tensor("x0z", [NT, 2, HYW], BF16).ap()


def na_tab_host(rpb, rows):
    ne = rpb.shape[0]
    tab = np.full((ne, NAH, 128, 2, 16, 64), -30000.0, np.float32)
    c = np.arange(64)
    cs = np.clip(c - 8, 0, 48)
    cp = np.arange(64)
    colvalid = (cp[:, None] >= cs[None, :]) & (cp[:, None] < cs[None, :] + 16)
    dcidx = np.clip(cp[:, None] - c[None, :] + 15, 0, 30)
    for rk in range(2):
        for jr in range(16):
            dr = rk + 7 - jr
            if abs(dr) > 7:
                continue
            g = rpb[:, :, dr + 7, :][:, :, dcidx]
            g = np.where(colvalid[None, None], g, np.float32(-30000.0))
            tab[:, :, rk * 64:(rk + 1) * 64, 0, jr, :] = g
            if -4 <= dr <= 3:
                tab[:, :, rk * 64:(rk + 1) * 64, 1, jr, :] = g
    return tab


def hyena_consts_host(Ls):
    KC = Ls // 128 + 1
    N = 2 * Ls
    nn = KC * 128
    a = np.arange(nn, dtype=np.int64)
    m = (a[:, None] * a[None, :]) % N
    ang = (2.0 * np.pi / N) * m.astype(np.float64)
    out = {}
    tabs = np.stack([np.cos(ang), np.sin(ang)]).astype(np.float32)
    t5 = tabs.reshape(2, KC, 128, KC, 128).transpose(0, 3, 2, 1, 4)
    out["dft"] = np.ascontiguousarray(t5).astype(ml_dtypes.bfloat16)
    t = np.linspace(0.0, 1.0, Ls, dtype=np.float32)[:, None]
    w = (2.0 * np.float32(math.pi) * np.arange(Ls, dtype=np.float32)[:, None] / np.float32(Ls)).astype(np.float32)
    bands = np.linspace(1e-4, 8 - 1, 8, dtype=np.float32)
    emb = np.concatenate([t, np.cos(bands * w), -np.sin(bands * w)], axis=-1).astype(np.float32)
    out["emb"] = np.ascontiguousarray(emb.T)
    out["negt"] = np.ascontiguousarray((-t[:, 0]).reshape(Ls // 128, 128).T).astype(np.float32)
    k = np.arange(nn)
    wk = np.where((k == 0) | (k == Ls), 1.0, 2.0) / N
    wk = np.where(k <= Ls, wk, 0.0).astype(np.float32)
    wk2 = np.stack([wk, -wk], axis=-1).reshape(KC, 128, 2).transpose(1, 0, 2)
    out["wk"] = np.ascontiguousarray(wk2).astype(np.float32)
    return out


def _phase_even(self, l):
    k = self.k
    e = self.cfg.types[:l].count("even")
    L = self.cfg.L
    NT = L + CTX
    rows = self.cfg.rows
    last = (l == self.cfg.depth - 1) and not getattr(self, "force_ctx_out", False)
    inp = self.inp

    with k.phase():
        We = [k.sb([128, EIN], BF16, "We%d" % c) for c in range(8)]
        for c in range(8):
            k.dma("pool", We[c][:, :], inp["even_in"][e, c * 128:(c + 1) * 128, :], writes=[We[c]])
        G, S, gate = self.load_mod_vecs(l, 1, 3, False)
        zrow = k.sb([1, 3 * HYW], F32, "zrow")
        k.memset("dve", zrow[:, :], 0.0, [zrow])
        for (ut, Ls) in ((self.u_lat, L), (self.u_ctx, CTX)):
            k.dma("sp", ut[0:1, :], zrow[:, :], reads=[zrow])
            k.dma("sp", ut[Ls + 1:Ls + 2, :], zrow[:, :], reads=[zrow])
        gq = k.sb([128, 2, NAD], F32, "gq")
        k.dma("sp", gq[:, 0, :], bcast_rows(inp["na_q_gain"][e:e + 1, :]), writes=[gq])
        k.dma("sp", gq[:, 1, :], bcast_rows(inp["na_k_gain"][e:e + 1, :]), writes=[gq])
        TS = 256
        xpool = k.pool_of(2, [128, 2, D], F32, "xt")
        xs_pool = k.pool_of(1, [128, 2, D], BF16, "xs")
        tp_pool = k.pool_of(2, [128, TS], BF16, "tp", psum=True)
        xnT = k.sb([128, 8, TS], BF16, "xnT")
        scr = k.sb([128, D], F32, "scr")
        ssq = k.sb([128, 2], F32, "ssq")
        rstd = k.sb([128, 2], F32, "rstd")
        pp = k.pool_of(3, [128, 512], F32, "pp", psum=True)
        tq_pool = k.pool_of(2, [128, 512], BF16, "tq", psum=True)
        sq = k.sb([128, 512], F32, "sq")
        hs = k.sb([128, 2, NAH], F32, "hs")
        qn = k.sb([128, 512], F32, "qn")
        qk_tok = k.pool_of(2, [128, 512], BF16, "qktok")
        qkT = k.pool_of(4, [128, 4, 128], BF16, "qkT")
        vst = k.pool_of(2, [128, NAW], BF16, "vst")
        ust = k.pool_of(2, [128, 3 * HYW], F32, "ust")
        for (xap, ntok, row) in self.tiles(TS):
            xt = xpool.get()
            k.dma("sp", xt[:, :, :], xap.rearrange("(s p) d -> p s d", p=128), writes=[xt])
            self.norm_mod_T(xt, 2, row, G, S, xs_pool, tp_pool, xnT, scr, ssq, rstd)
            for s in range(2):
                tok0 = self.tile_tok0(xap) + s * 128
                us = ust.get()
                for n in range(6):
                    p = pp.get()
                    for c in range(8):
                        k.mm(p[:, :], xnT[:, c, s * 128:(s + 1) * 128], We[c][:, n * 512:(n + 1) * 512], c == 0, c == 7,
                             [xnT, We[c]], [p])
                    if n < 2:
                        k.act(sq[:, :], p[:, :], AF.Square, [p], [sq])
                        k.op("dve", lambda g, n=n: g.tensor_reduce(hs[:, n, :], sq[:, :].rearrange("p (h d) -> p h d", h=NAH),
                                                                  AX.X, ALU.add), [sq], [hs])
                        k.act(hs[:, n, :], hs[:, n, :], AF.Sqrt, [hs, self.epsb], [hs], bias=self.epsb[:, 0:1], scale=1.0 / NAD)
                        k.op("dve", lambda g, n=n: g.reciprocal(hs[:, n, :], hs[:, n, :]), [hs], [hs])
                        k.tt("dve", qn[:, :].rearrange("p (h d) -> p h d", h=NAH), p[:, :].rearrange("p (h d) -> p h d", h=NAH),
                             hs[:, n, :].unsqueeze(2).broadcast_to([128, NAH, NAD]), ALU.mult, [p, hs], [qn])
                        qt = qk_tok.get()
                        k.tt("pool", qt[:, :].rearrange("p (h d) -> p h d", h=NAH), qn[:, :].rearrange("p (h d) -> p h d", h=NAH),
                             gq[:, n, :].unsqueeze(1).broadcast_to([128, NAH, NAD]), ALU.mult, [qn, gq], [qt])
                        tq = tq_pool.get()
                        for b in range(4):
                            k.tr(tq[:, b * 128:(b + 1) * 128], qt[:, b * 128:(b + 1) * 128], self.ident[:, :], [qt, self.ident], [tq],
                                 inc=(b == 3))
                        dT = qkT.get()
                        k.copy("act", dT[:, :, :].rearrange("p b t -> p (b t)"), tq[:, :], [tq], [dT])
                        dst = self.qtn if n == 0 else self.ktn
                        k.dma("sp", dst[:, :, tok0:tok0 + 128].rearrange("b p t -> p b t"), dT[:, :, :], reads=[dT])
                    elif n == 2:
                        v_ = vst.get()
                        k.copy("act", v_[:, :], p[:, :], [p], [v_])
                        k.dma("sp", self.vn[tok0:tok0 + 128, :], v_[:, :], reads=[v_])
                    else:
                        k.copy("act" if n % 2 else "dve", us[:, (n - 3) * 512:(n - 2) * 512], p[:, :], [p], [us])
                ut, t0 = (self.u_lat, tok0) if row == 0 else (self.u_ctx, tok0 - L)
                k.dma("sp", ut[1 + t0:1 + t0 + 128, :], us[:, :], reads=[us])

    self.hyena(l, e, "lat", L, self.u_lat, 0)
    if not last:
        self.hyena(l, e, "ctx", CTX, self.u_ctx, L)
    self.na_attention(l, e, last)
    with k.phase():
        Wo = k.sb([128, 8, D], BF16, "Weo")
        k.dma("pool", Wo[:, :, :], inp["even_out"][e, :, :].rearrange("(j p) d -> p j d", p=128), writes=[Wo])
        G, S, gate = self.load_mod_vecs(l, 1, 3, False)
        c_pool = k.pool_of(2, [128, D], BF16, "catc")
        x_pool = k.pool_of(2, [128, D], F32, "xo")
        pt_pool = k.pool_of(2, [128, 1024], BF16, "pto", psum=True)
        pr_pool = k.pool_of(2, [128, 512], F32, "pro", psum=True)
        cT_pool = k.pool_of(2, [128, 8, 128], BF16, "cT")
        scr2 = k.sb([128, D], F32, "scr2")
        nchunks = (L // 128) + (0 if last else CTX // 128)
        for ch in range(nchunks):
            row = 0 if ch < L // 128 else 1
            xap = self.out[ch * 128:(ch + 1) * 128, :] if row == 0 else self.xc[(ch - L // 128) * 128:(ch - L // 128 + 1) * 128, :]
            ct = c_pool.get(); xt = x_pool.get()
            k.dma("sp", ct[:, :], self.cat[ch * 128:(ch + 1) * 128, :], writes=[ct])
            k.dma("sp", xt[:, :], xap, writes=[xt])
            pt = pt_pool.get()
            for b in range(8):
                k.tr(pt[:, b * 128:(b + 1) * 128], ct[:, b * 128:(b + 1) * 128], self.ident[:, :], [ct, self.ident], [pt], inc=(b == 7))
            cT = cT_pool.get()
            k.copy("act", cT[:, :, :].rearrange("p b t -> p (b t)"), pt[:, :], [pt], [cT])
            for hf in range(2):
                pr = pr_pool.get()
                for b in range(8):
                    k.mm(pr[:, :], cT[:, b, :], Wo[:, b, hf * 512:(hf + 1) * 512], b == 0, b == 7, [cT, Wo], [pr])
                sl = slice(hf * 512, (hf + 1) * 512)
                k.tt("dve", scr2[:, sl], pr[:, :], gate[:, row, sl], ALU.mult, [pr, gate], [scr2])
                k.tt("pool", xt[:, sl], xt[:, sl], scr2[:, sl], ALU.add, [xt, scr2], [xt])
            k.dma("sp", xap, xt[:, :], reads=[xt])


def _hyena(self, l, e, nm, Ls, ut, tokbase):
    k = self.k
    inp = self.inp
    NCn = Ls // 128
    KC = NCn + 1
    dft = inp["dft_" + nm]
    with k.phase():
        cw = k.sb([128, 3, 3 * HYW], F32, "cw")
        cb = k.sb([128, 3 * HYW], F32, "cb")
        for tpi in range(3):
            k.dma("sp", cw[:, tpi, :], bcast_rows(inp["hy_conv_w"][e, tpi:tpi + 1, :]), writes=[cw])
        k.dma("sp", cb[:, :], bcast_rows(inp["hy_conv_b"][e:e + 1, :]), writes=[cb])
        upool = k.pool_of(2, [128, 3, 3 * HYW], F32, "uabc")
        t1 = k.pool_of(2, [128, 3 * HYW], F32, "sc1")
        t2 = k.pool_of(2, [128, 3 * HYW], F32, "sc2")
        xz = k.pool_of(2, [128, 2, HYW], BF16, "xz")
        for n in range(NCn):
            u = upool.get()
            for tpi in range(3):
                k.dma("sp", u[:, tpi, :], ut[n * 128 + tpi:n * 128 + tpi + 128, :], writes=[u])
            a_ = t1.get(); b2 = t2.get()
            k.tt("dve", a_[:, :], u[:, 0, :], cw[:, 0, :], ALU.mult, [u, cw], [a_])
            k.tt("pool", b2[:, :], u[:, 1, :], cw[:, 1, :], ALU.mult, [u, cw], [b2])
            k.tt("dve", a_[:, :], a_[:, :], b2[:, :], ALU.add, [a_, b2], [a_])
            k.tt("pool", b2[:, :], u[:, 2, :], cw[:, 2, :], ALU.mult, [u, cw], [b2])
            k.tt("pool", b2[:, :], b2[:, :], cb[:, :], ALU.add, [b2, cb], [b2])
            k.tt("dve", a_[:, :], a_[:, :], b2[:, :], ALU.add, [a_, b2], [a_])
            o_ = xz.get()
            k.copy("act", o_[:, 0, :], a_[:, 0:HYW], [a_], [o_])
            k.tt("dve", o_[:, 1, :], a_[:, HYW:2 * HYW], a_[:, 2 * HYW:3 * HYW], ALU.mult, [a_], [o_])
            k.dma("sp", self.x0z[tokbase + n * 128:tokbase + (n + 1) * 128, :, :], o_[:, :, :], reads=[o_])
    with k.phase():
        wk = k.sb([128, KC, 2], F32, "wk")
        k.dma("sp", wk[:, :, :], inp["wk_" + nm], writes=[wk])
        negt = k.sb([128, NCn], F32, "negt")
        k.dma("sp", negt[:, :], inp["negt_" + nm], writes=[negt])
        KK = k.sb([128, KC, 2, HYW], BF16, "KK")
        KKtok = [Tok("kk%d" % i) for i in range(KC)]
        with k.phase():
            Hf = k.sb([128, NCn, HYW], BF16, "Hf")
            Hb = k.sb([128, NCn, HYW], BF16, "Hb")
            with k.phase():
                CW = min(512, Ls)
                embp = k.pool_of(2, [HY_EMB, CW], F32, "emb")
                fw = [k.sb([HY_EMB, HY_ORDER], F32, "fw1"), k.sb([HY_ORDER, HY_ORDER], F32, "fw2"), k.sb([HY_ORDER, HY_ORDER], F32, "fw3")]
                fw4 = k.sb([HY_ORDER, 2 * HYW], F32, "fw4")
                fbT = k.sb([HY_ORDER, 4], F32, "fbT")
                k.dma("sp", fw[0][:, :], inp["hy_fw1"][e], writes=[fw[0]])
                k.dma("sp", fw[1][:, :], inp["hy_fw2"][e], writes=[fw[1]])
                k.dma("sp", fw[2][:, :], inp["hy_fw3"][e], writes=[fw[2]])
                k.dma("sp", fw4[:, :], inp["hy_fw4"][e], writes=[fw4])
                with self.nc.allow_non_contiguous_dma("tiny"):
                    for i, nmv in enumerate(("hy_fb1", "hy_fb2", "hy_fb3", "hy_freq")):
                        k.dma("sp", fbT[:, i:i + 1], inp[nmv][e:e + 1, :].rearrange("o f -> f o"), writes=[fbT])
                fbias = k.sb([HY_ORDER, 3], F32, "fbias")
                for i in range(3):
                    k.tt("dve", fbias[:, i:i + 1], fbT[:, i:i + 1], fbT[:, 3:4], ALU.mult, [fbT], [fbias])
                adl = k.sb([128, HYW], F32, "adl")
                k.dma("sp", adl[:, :], bcast_rows(inp["absdelta"]), writes=[adl])
                hcur = [k.sb([HY_ORDER, CW], F32, "hmlp%d" % i) for i in range(2)]
                pm = k.pool_of(2, [HY_ORDER, 512], F32, "pm", psum=True)
                pre = k.sb([HY_ORDER, 512], F32, "pre")
                nfl = k.sb([HY_ORDER, 512], F32, "nfl")
                nin = k.sb([HY_ORDER, 512], I32, "nin")
                win = k.pool_of(2, [128, HYW], F32, "win")
                ph = k.pool_of(2, [128, 512], F32, "ph", psum=True)
                for cc in range(Ls // CW):
                    em = embp.get()
                    k.dma("sp", em[:, :], inp["emb_" + nm][:, cc * CW:(cc + 1) * CW], writes=[em])
                    for layer in range(3):
                        src = em if layer == 0 else hcur[(layer - 1) % 2]
                        dst = hcur[layer % 2]
                        p = pm.get()
                        k.mm(p[:, :CW], fw[layer][:, :], src[:, :], True, True, [fw[layer], src], [p])
                        k.act(pre[:, :CW], p[:, :CW], AF.Identity, [p, fbT, fbias], [pre], bias=fbias[:, layer:layer + 1], scale=fbT[:, 3:4])
                        k.ts("dve", nfl[:, :CW], pre[:, :CW], 1.0 / TWO_PI, None, ALU.mult, None, [pre], [nfl])
                        k.copy("dve", nin[:, :CW], nfl[:, :CW], [nfl], [nin])
                        k.copy("dve", nfl[:, :CW], nin[:, :CW], [nin], [nfl])
                        k.stt(pre[:, :CW], nfl[:, :CW], -TWO_PI, pre[:, :CW], ALU.mult, ALU.add, [nfl, pre], [pre])
                        k.ts("dve", pre[:, :CW], pre[:, :CW], 3.1415925, -3.1415925, ALU.min, ALU.max, [pre], [pre])
                        k.act(dst[:, :], pre[:, :CW], AF.Sin, [pre], [dst])
                    h3 = hcur[0]
                    for sub in range(CW // 128):
                        n = cc * (CW // 128) + sub
                        w_ = win.get()
                        k.act(w_[:, :], adl[:, :], AF.Exp, [adl, negt], [w_], scale=negt[:, n:n + 1])
                        for hf, dstH in ((0, Hf), (1, Hb)):
                            p = ph.get()
                            k.mm(p[:, :], h3[:, sub * 128:(sub + 1) * 128], fw4[:, hf * HYW:(hf + 1) * HYW], True, True, [h3, fw4], [p])
                            k.tt("dve", dstH[:, n, :], p[:, :], w_[:, :], ALU.mult, [p, w_], [dstH])
            tabp = k.pool_of(2, [128, 2, NCn, 128], BF16, "tabF")
            pacc = [k.ps([128, 512], F32, "pF%d" % i) for i in range(4)]
            bsb = k.pool_of(2, [128, 2, HYW], F32, "bsb")
            for kc in range(KC):
                tb = tabp.get()
                for cs_ in range(2):
                    k.dma("sp", tb[:, cs_, :, :], dft[cs_, kc, :, 0:NCn, :], writes=[tb])
                for n in range(NCn):
                    for cs_ in range(2):
                        for hi, Hsrc in ((0, Hf), (1, Hb)):
                            k.mm(pacc[cs_ * 2 + hi][:, :], tb[:, cs_, n, :], Hsrc[:, n, :], n == 0, n == NCn - 1, [tb, Hsrc], [pacc[cs_ * 2 + hi]])
                Fc, Bc, Fs, Bs = pacc[0], pacc[1], pacc[2], pacc[3]
                b_ = bsb.get()
                k.act(b_[:, 0, :], Bc[:, :], AF.Identity, [Bc, wk], [b_], scale=wk[:, kc, 0:1])
                k.act(b_[:, 1, :], Bs[:, :], AF.Identity, [Bs, wk], [b_], scale=wk[:, kc, 0:1])
                k.stt(KK[:, kc, 0, :], Fc[:, :], wk[:, kc, 0:1], b_[:, 0, :], ALU.mult, ALU.add, [Fc, wk, b_], [KKtok[kc]])
                k.stt(KK[:, kc, 1, :], Fs[:, :], wk[:, kc, 1:2], b_[:, 1, :], ALU.mult, ALU.add, [Fs, wk, b_], [KKtok[kc]])
        with k.phase():
            z = k.sb([128, NCn, HYW], BF16, "z")
            k.dma("sp", z[:, :, :], self.x0z[tokbase:tokbase + Ls, 1, :].rearrange("(n p) c -> p n c", p=128), writes=[z])
            tabp = k.pool_of(2, [128, 2, NCn, 128], BF16, "tabZ")
            pz = [k.pool_of(2, [128, 512], F32, "pZ%d" % i, psum=True) for i in range(2)]
            tm = [k.pool_of(2, [128, HYW], F32, "tmz%d" % i) for i in range(4)]
            for kc in range(KC):
                tb = tabp.get()
                for cs_ in range(2):
                    k.dma("sp", tb[:, cs_, :, :], dft[cs_, kc, :, 0:NCn, :], writes=[tb])
                Zc = pz[0].get(); Zs = pz[1].get()
                for n in range(NCn):
                    k.mm(Zc[:, :], tb[:, 0, n, :], z[:, n, :], n == 0, n == NCn - 1, [tb, z], [Zc])
                    k.mm(Zs[:, :], tb[:, 1, n, :], z[:, n, :], n == 0, n == NCn - 1, [tb, z], [Zs])
                a1 = tm[0].get(); a2 = tm[1].get(); a3 = tm[2].get(); a4 = tm[3].get()
                kt = KKtok[kc]
                k.tt("dve", a1[:, :], Zc[:, :], KK[:, kc, 0, :], ALU.mult, [Zc, kt], [a1])
                k.tt("dve", a2[:, :], Zs[:, :], KK[:, kc, 1, :], ALU.mult, [Zs, kt], [a2])
                k.tt("dve", a3[:, :], Zs[:, :], KK[:, kc, 0, :], ALU.mult, [Zs, kt], [a3])
                k.tt("dve", a4[:, :], Zc[:, :], KK[:, kc, 1, :], ALU.mult, [Zc, kt], [a4])
                k.tt("pool", KK[:, kc, 0, :], a1[:, :], a2[:, :], ALU.add, [a1, a2], [kt])
                k.tt("pool", KK[:, kc, 1, :], a3[:, :], a4[:, :], ALU.subtract, [a3, a4], [kt])
        with k.phase():
            tabp = k.pool_of(2, [128, 2, KC, 128], BF16, "tabI")
            py = k.pool_of(2, [128, 512], F32, "pY", psum=True)
            db = k.sb([128, HYW], F32, "dbias")
            k.dma("sp", db[:, :], bcast_rows(inp["hy_bias"][e:e + 1, :]), writes=[db])
            e1 = k.pool_of(2, [128, HYW], F32, "e1")
            bo = k.pool_of(2, [128, HYW], BF16, "bo")
            xzp = k.pool_of(2, [128, 2, HYW], BF16, "xzi")
            for tc_ in range(NCn):
                tb = tabp.get()
                for cs_ in range(2):
                    k.dma("sp", tb[:, cs_, :, :], dft[cs_, tc_, :, :, :], writes=[tb])
                xz_ = xzp.get()
                k.dma("sp", xz_[:, :, :], self.x0z[tokbase + tc_ * 128:tokbase + (tc_ + 1) * 128, :, :], writes=[xz_])
                y = py.get()
                for kc in range(KC):
                    k.mm(y[:, :], tb[:, 0, kc, :], KK[:, kc, 0, :], kc == 0, False, [tb, KKtok[kc]], [y])
                    k.mm(y[:, :], tb[:, 1, kc, :], KK[:, kc, 1, :], False, kc == KC - 1, [tb, KKtok[kc]], [y])
                t_ = e1.get()
                k.tt("pool", t_[:, :], xz_[:, 1, :], db[:, :], ALU.mult, [xz_, db], [t_])
                k.tt("dve", t_[:, :], y[:, :], t_[:, :], ALU.add, [y, t_], [t_])
                o_ = bo.get()
                k.tt("dve", o_[:, :], t_[:, :], xz_[:, 0, :], ALU.mult, [t_, xz_], [o_])
                k.dma("sp", self.cat[tokbase + tc_ * 128:tokbase + (tc_ + 1) * 128, NAW:D], o_[:, :], reads=[o_])


Prog.even_setup = _even_setup
Prog.phase_even = _phase_even
Prog.hyena = _hyena


def _na_attention(self, l, e, last):
    k = self.k
    inp = self.inp
    L = self.cfg.L
    NT = L + CTX
    rows = self.cfg.rows
    nP = rows // 2
    NB = L // 128
    with k.phase():
        Ve = k.sb([128, NB + 2, NAH, NAD + 1], BF16, "Ve")
        Vo = k.sb([128, NB - 1, NAH, NAD + 1], BF16, "Vo")
        k.memset("pool", Ve[:, :, :, NAD:NAD + 1], 1.0, [Ve])
        k.memset("pool", Vo[:, :, :, NAD:NAD + 1], 1.0, [Vo])
        for b in range(NB + 2):
            k.dma("sp", Ve[:, b, :, 0:NAD], self.vn[b * 128:(b + 1) * 128, :].rearrange("p (h d) -> p h d", h=NAH), writes=[Ve])
        for b in range(NB - 1):
            k.dma("sp", Vo[:, b, :, 0:NAD], self.vn[64 + b * 128:64 + (b + 1) * 128, :].rearrange("p (h d) -> p h d", h=NAH), writes=[Vo])
        A = k.sb([128, NB + 2, NAW], BF16, "Aall")
        qpool = k.pool_of(2, [128, NT], BF16, "qTn")
        kpool = k.pool_of(2, [128, NT], BF16, "kTn")
        tst = k.pool_of(2, [128, 2, 16, 64], F32, "tst")
        ttp = k.pool_of(2, [128, 2, 16 * 64], BF16, "TT")
        ps_pool = k.pool_of(2, [128, 1024], F32, "psS", psum=True)
        pv_pool = k.pool_of(2, [128, NAD + 1], F32, "psV", psum=True)
        pt_pool = k.pool_of(3, [128, 7 * 128], BF16, "PT")
        rec = k.pool_of(4, [128, 1], F32, "rec")
        for h in range(NAH):
            hp, pb = h // 2, (h % 2) * 64
            if h % 2 == 0:
                qT = qpool.get(); kT = kpool.get()
                k.dma("sp", qT[:, :], self.qtn[hp, :, :], writes=[qT])
                k.dma("sp", kT[:, :], self.ktn[hp, :, :], writes=[kT])
            ts_ = tst.get()
            k.dma("sp", ts_[:, :, :, :], inp["na_tab"][e, h, :, :, :, :], writes=[ts_])
            TT = ttp.get()
            k.act(TT[:, :, :], ts_[:, :, :, :].rearrange("p v j c -> p v (j c)"), AF.Exp, [ts_], [TT])
            units = []
            for i in range(nP):
                r0 = 2 * i
                if i < 2:
                    al, var = [0, 2, 4, 6], 0
                elif i >= nP - 2:
                    al, var = [rows - 8, rows - 6, rows - 4, rows - 2], 0
                else:
                    al, var = [r0 - 4, r0 - 2, r0, r0 + 2, r0 + 4], 1
                units.append((r0 * 64, al, var, i))
            if not last:
                units.append((L, [], 0, NB))
                units.append((L + 128, [], 0, NB + 1))
            for (q0, al, var, ablk) in units:
                M = len(al)
                nb = M + 2
                ps = ps_pool.get()
                for b in range(nb):
                    if b < M:
                        a_ = al[M - 1 - b]
                        ks = a_ * 64
                    else:
                        ks = L + (b - M) * 128
                    k.mm(ps[:, b * 128:(b + 1) * 128], kT[pb:pb + 64, ks:ks + 128], qT[pb:pb + 64, q0:q0 + 128], True, True,
                         [kT, qT], [ps], inc=(b == nb - 1))
                PT = pt_pool.get()
                for b0 in range(0, nb, 4):
                    b1 = min(nb, b0 + 4)
                    k.act(PT[:, b0 * 128:b1 * 128], ps[:, b0 * 128:b1 * 128], AF.Exp, [ps], [PT], scale=NAD ** -0.5)
                if M > 0:
                    r0 = q0 // 64
                    base = 7 - (al[0] - r0) - 2 * (M - 1)
                    k.tt("dve", PT[:, 0:M * 128], PT[:, 0:M * 128], TT[:, var, base * 64:(base + 2 * M) * 64], ALU.mult, [PT, TT], [PT])
                pv = pv_pool.get()
                for b in range(nb):
                    if b < M:
                        a_ = al[M - 1 - b]
                        vt = Ve[:, a_ // 2, h, :] if a_ % 2 == 0 else Vo[:, (a_ - 1) // 2, h, :]
                        vtok = Ve if a_ % 2 == 0 else Vo
                    else:
                        vt = Ve[:, NB + (b - M), h, :]
                        vtok = Ve
                    k.mm(pv[:, :], PT[:, b * 128:(b + 1) * 128], vt, b == 0, b == nb - 1, [PT, vtok], [pv])
                rc = rec.get()
                k.op("dve", lambda g, rc=rc, pv=pv: g.reciprocal(rc[:, :], pv[:, NAD:NAD + 1]), [pv], [rc])
                k.act(A[:, ablk, h * NAD:(h + 1) * NAD], pv[:, 0:NAD], AF.Identity, [pv, rc], [A], scale=rc[:, 0:1])
        nblk = NB + (0 if last else 2)
        for b in range(nblk):
            k.dma("sp", self.cat[b * 128:(b + 1) * 128, 0:NAW], A[:, b, :], reads=[A])


Prog.na_attention = _na_attention


_WEIGHTS = ["w_mod", "b_mod", "norm_gain", "ffn_a_in", "ffn_a_out", "ffn_b_in", "ffn_b_out", "even_in", "even_out",
            "na_q_gain", "na_k_gain", "hy_conv_w", "hy_conv_b", "hy_fw1", "hy_fb1", "hy_fw2", "hy_fb2", "hy_fw3", "hy_fb3",
            "hy_fw4", "hy_freq", "hy_bias", "ret_in", "ret_out", "ret_logit_f", "ret_logit_b"]


def kernel(**inputs):
    rows = 64
    B = inputs["x"].shape[0]
    P = Prog(Cfg(rows=rows, depth=4))
    nc = P.build()
    shared = {n: np.ascontiguousarray(np.asarray(inputs[n], dtype=np.float32)) for n in _WEIGHTS}
    shared.update(host_consts(rows))
    shared["na_tab"] = na_tab_host(np.asarray(inputs["na_rpb"], dtype=np.float32), rows)
    shared["c_ctx"] = np.ascontiguousarray(np.asarray(inputs["c_ctx"], dtype=np.float32)[None, :])
    shared = {n: v for n, v in shared.items() if n in P.inp}
    in_maps = []
    for b in range(B):
        m = dict(shared)
        m["x"] = np.ascontiguousarray(inputs["x"][b], dtype=np.float32)
        m["c"] = np.ascontiguousarray(inputs["c"][b:b + 1], dtype=np.float32)
        m["ctx"] = np.ascontiguousarray(inputs["ctx"][b], dtype=np.float32)
        in_maps.append(m)
    res = run_bass_kernel_spmd(nc, in_maps, core_ids=list(range(B)))
    return np.stack([np.asarray(r["out"], dtype=np.float32) for r in res.results], axis=0)
```

```python
import contextlib
import math
import numpy as np
import ml_dtypes
import concourse.bass as bass
import concourse.mybir as mybir
from concourse.bass_utils import run_bass_kernel_spmd

F32 = mybir.dt.float32
BF16 = mybir.dt.bfloat16
AF = mybir.ActivationFunctionType
ALU = mybir.AluOpType
AX = mybir.AxisListType

D = 1024
DFF = 2816
NMOD = 9
RMS_EPS = 1e-6
GN_EPS = 1e-6
GRID_W = 64
CTX = 256
SAME_ENGINE_SYNC = True


class Tok:
    __slots__ = ("w", "r", "name", "lane")

    def __init__(self, name=""):
        self.w = None
        self.r = {}
        self.name = name
        self.lane = None


class Lane:
    __slots__ = ("sem", "count", "sw")

    def __init__(self, sem):
        self.sem = sem
        self.count = 0
        self.sw = False


class T:
    def __init__(self, h, name):
        self.h = h
        self.tok = Tok(name)

    def __getitem__(self, idx):
        return self.h[idx]


class KB:
    def __init__(self):
        self.nc = bass.Bass("TRN2", target_bir_lowering=False)
        nc = self.nc
        self.es = contextlib.ExitStack()
        self.eng = dict(pe=nc.tensor, act=nc.scalar, dve=nc.vector, pool=nc.gpsimd, sp=nc.sync)
        self.sem = {e: self.es.enter_context(nc.semaphore("s_" + e)) for e in self.eng}
        self.cnt = {e: 0 for e in self.eng}
        self.seen = {e: {} for e in self.eng}
        self.free_lanes = []
        self.free_lanes_sw = []
        self.all_lanes = []
        self.nlanes = 0
        self.phase_stack = []
        self.ninst = 0
        self.uid = 0

    def _name(self, p):
        self.uid += 1
        return "%s_%d" % (p, self.uid)

    def sb(self, shape, dt, name="t"):
        st = self.phase_stack[-1][0] if self.phase_stack else self.es
        h = st.enter_context(self.nc.sbuf_tensor(self._name(name), list(shape), dt))
        t = T(h, name)
        if self.phase_stack:
            self.phase_stack[-1][1].append(t)
        return t

    def ps(self, shape, dt, name="p"):
        st = self.phase_stack[-1][0] if self.phase_stack else self.es
        h = st.enter_context(self.nc.psum_tensor(self._name(name), list(shape), dt))
        t = T(h, name)
        if self.phase_stack:
            self.phase_stack[-1][1].append(t)
        return t

    def pool_of(self, n, shape, dt, name, psum=False):
        return Ring([(self.ps if psum else self.sb)(shape, dt, name) for _ in range(n)])

    @contextlib.contextmanager
    def phase(self):
        st = contextlib.ExitStack()
        toks = []
        self.phase_stack.append((st, toks))
        try:
            yield
        finally:
            self.barrier()
            self.phase_stack.pop()
            for t in toks:
                if t.tok.lane is not None:
                    (self.free_lanes_sw if t.tok.lane.sw else self.free_lanes).append(t.tok.lane)
                    t.tok.lane = None
            st.close()

    def lane_of(self, tok, sw=False):
        if tok.lane is None:
            fl = self.free_lanes_sw if sw else self.free_lanes
            if fl:
                tok.lane = fl.pop()
            else:
                sem = self.es.enter_context(self.nc.semaphore("l_%d" % self.nlanes))
                self.nlanes += 1
                tok.lane = Lane(sem)
                tok.lane.sw = sw
                self.all_lanes.append(tok.lane)
        assert tok.lane.sw == sw, "token %s mixes SW and HW DMA queues" % tok.name
        return tok.lane

    def _waits(self, e, reads, writes):
        deps = {}

        def need(p):
            if p is None:
                return
            s, v = p
            k = id(s)
            if k not in deps or deps[k][1] < v:
                deps[k] = (s, v)

        for t in reads:
            need(t.w)
        for t in writes:
            need(t.w)
            for p in t.r.values():
                need(p)
        for k, (s, v) in deps.items():
            if s is self.sem[e] and (e == "pe" or e == "sp" or not SAME_ENGINE_SYNC):
                continue
            if self.seen[e].get(k, 0) >= v:
                continue
            self.eng[e].wait_ge(s, v)
            self.seen[e][k] = v
            self.ninst += 1

    @staticmethod
    def _toks(xs):
        out = []
        for x in xs:
            if x is None:
                continue
            out.append(x.tok if isinstance(x, T) else x)
        return out

    def op(self, e, emit, reads=(), writes=(), inc=True):
        reads = self._toks(reads)
        writes = self._toks(writes)
        self._waits(e, reads, writes)
        ins = emit(self.eng[e])
        self.ninst += 1
        if inc:
            self.cnt[e] += 1
            ins.then_inc(self.sem[e], 1)
            me = (self.sem[e], self.cnt[e])
        else:
            me = (self.sem[e], self.cnt[e] + 1)
        for t in reads:
            t.r[e] = me
        for t in writes:
            t.w = me
            t.r = {}
        return ins

    def dma(self, q, out, in_, reads=(), writes=(), lane_tok=None, **kw):
        reads = self._toks(reads)
        writes = self._toks(writes)
        lt = lane_tok.tok if isinstance(lane_tok, T) else lane_tok
        if lt is None:
            lt = writes[0] if writes else reads[0]
        lane = self.lane_of(lt, sw=(q == "pool"))
        self._waits(q, reads, writes)
        if q == "pool" and lane.count > 0 and self.seen[q].get(id(lane.sem), 0) < lane.count:
            self.eng[q].wait_ge(lane.sem, lane.count)
            self.seen[q][id(lane.sem)] = lane.count
        ins = self.eng[q].dma_start(out=out, in_=in_, **kw)
        self.ninst += 1
        lane.count += 16
        ins.then_inc(lane.sem, 16)
        me = (lane.sem, lane.count)
        for t in reads:
            t.r["dma%d" % id(lane)] = me
        for t in writes:
            t.w = me
            t.r = {}
        return ins

    def barrier(self):
        for e in self.eng:
            for f in self.eng:
                if f == e or self.cnt[f] == 0:
                    continue
                if self.seen[e].get(id(self.sem[f]), 0) >= self.cnt[f]:
                    continue
                self.eng[e].wait_ge(self.sem[f], self.cnt[f])
                self.seen[e][id(self.sem[f])] = self.cnt[f]
            for ln in self.all_lanes:
                if ln.count == 0 or self.seen[e].get(id(ln.sem), 0) >= ln.count:
                    continue
                self.eng[e].wait_ge(ln.sem, ln.count)
                self.seen[e][id(ln.sem)] = ln.count

    def mm(self, out, lhsT, rhs, start, stop, reads, writes, inc=None):
        if inc is None:
            inc = stop
        return self.op("pe", lambda g: g.matmul(out, lhsT, rhs, start=start, stop=stop), reads, writes, inc=inc)

    def tr(self, out, in_, ident, reads, writes, inc=True):
        return self.op("pe", lambda g: g.transpose(out, in_, ident), reads, writes, inc=inc)

    def act(self, out, in_, func, reads, writes, bias=None, scale=None, accum_out=None, e="act"):
        kw = {}
        if bias is not None:
            kw["bias"] = bias
        if scale is not None:
            kw["scale"] = scale
        if accum_out is not None:
            kw["accum_out"] = accum_out
        return self.op(e, lambda g: g.activation(out, in_, func, **kw), reads, writes)

    def ts(self, e, out, in0, s1, s2, op0, op1, reads, writes, accum_out=None):
        kw = {}
        if op1 is not None:
            kw["op1"] = op1
        if accum_out is not None:
            kw["accum_out"] = accum_out
        return self.op(e, lambda g: g.tensor_scalar(out, in0, s1, s2, op0, **kw), reads, writes)

    def tt(self, e, out, in0, in1, op, reads, writes):
        return self.op(e, lambda g: g.tensor_tensor(out, in0, in1, op), reads, writes)

    def stt(self, out, in0, scalar, in1, op0, op1, reads, writes, e="dve"):
        return self.op(e, lambda g: g.scalar_tensor_tensor(out, in0, scalar, in1, op0, op1), reads, writes)

    def copy(self, e, out, in_, reads, writes):
        if e == "act":
            return self.op(e, lambda g: g.copy(out, in_), reads, writes)
        return self.op(e, lambda g: g.tensor_copy(out, in_), reads, writes)

    def memset(self, e, ap, val, writes):
        return self.op(e, lambda g: g.memset(ap, val), (), writes)


class Ring:
    def __init__(self, items):
        self.items = items
        self.i = 0

    def get(self):
        t = self.items[self.i % len(self.items)]
        self.i += 1
        return t


class Cfg:
    def __init__(self, rows=64, depth=4, types=None):
        self.rows = rows
        self.L = rows * GRID_W
        self.depth = depth
        self.types = types if types is not None else [("even" if i % 2 == 0 else "ret") for i in range(depth)]
        self.layers = list(range(depth))


def bcast_rows(ap, n=128):
    return ap.partition_broadcast(n)


class Prog:
    def __init__(self, cfg, stages=None):
        self.cfg = cfg
        self.k = KB()
        self.nc = self.k.nc
        self.stages = stages
        nc = self.nc
        L = cfg.L
        nl = cfg.depth
        ne = (nl + 1) // 2
        no = nl // 2
        self.inp = {}

        def din(name, shape, dt=F32):
            self.inp[name] = nc.dram_tensor(name, list(shape), dt, kind="ExternalInput").ap()
            return self.inp[name]

        din("x", [L, D]); din("c", [1, D]); din("ctx", [CTX, D]); din("c_ctx", [1, D])
        din("w_mod", [nl, D, NMOD * D]); din("b_mod", [nl, NMOD * D]); din("norm_gain", [nl, 3, D])
        din("ffn_a_in", [nl, D, 2 * DFF]); din("ffn_a_out", [nl, DFF, D])
        din("ffn_b_in", [nl, D, 2 * DFF]); din("ffn_b_out", [nl, DFF, D])
        din("ident", [128, 128], BF16)
        self.out = nc.dram_tensor("out", [L, D], F32, kind="ExternalOutput").ap()
        self.xc = nc.dram_tensor("xc_scr", [CTX, D], F32, kind="ExternalOutput").ap()
        self.mod = nc.dram_tensor("mod_scr", [nl, 2, NMOD * D], F32).ap()
        if "ret" in cfg.types:
            self.ret_setup()
        if "even" in cfg.types:
            self.even_setup()

    def inp_add(self, name, shape, dt=F32):
        self.inp[name] = self.nc.dram_tensor(name, list(shape), dt, kind="ExternalInput").ap()
        return self.inp[name]

    def tile_tok0(self, xap):
        off = xap.offset // D
        return off if xap.tensor.name == self.out.tensor.name else self.cfg.L + off

    def consts(self):
        k = self.k
        self.ident = k.sb([128, 128], BF16, "ident")
        k.dma("sp", self.ident[:, :], self.inp["ident"], writes=[self.ident])
        self.epsb = k.sb([128, 1], F32, "eps")
        k.memset("dve", self.epsb[:, :], RMS_EPS, [self.epsb])

    def init_copy(self):
        k = self.k
        L = self.cfg.L
        self.t_xlat = Tok("xlat")
        self.t_xctx = Tok("xctx")
        nchunk = max(1, L // 1024)
        rows = L // nchunk
        for i in range(nchunk):
            k.dma("sp", self.out[i * rows:(i + 1) * rows, :], self.inp["x"][i * rows:(i + 1) * rows, :],
                  writes=[self.t_xlat])
        k.dma("sp", self.xc[:, :], self.inp["ctx"][:, :], writes=[self.t_xctx])
        k.barrier()

    def phase_mod(self, l):
        k = self.k
        with k.phase():
            cT = k.sb([128, 8, 2], F32, "cT")
            with self.nc.allow_non_contiguous_dma("tiny"):
                k.dma("sp", cT[:, :, 0], self.inp["c"].rearrange("o (c p) -> p (o c)", p=128), writes=[cT])
                k.dma("sp", cT[:, :, 1], self.inp["c_ctx"].rearrange("o (c p) -> p (o c)", p=128), writes=[cT])
            sc = k.sb([128, 8, 2], F32, "sc")
            k.act(sc[:, :, :], cT[:, :, :], AF.Silu, [cT], [sc])
            ones2 = k.sb([1, 2], F32, "ones2")
            k.memset("dve", ones2[:, :], 1.0, [ones2])
            brow = k.sb([1, NMOD * D], F32, "brow")
            k.dma("sp", brow[:, :], self.inp["b_mod"][l:l + 1, :], writes=[brow])
            wpool = k.pool_of(3, [128, 8, 512], F32, "wm")
            ppool = k.pool_of(2, [2, 512], F32, "pm", psum=True)
            spool = k.pool_of(2, [2, 512], F32, "sm")
            for n in range(NMOD * D // 512):
                w = wpool.get()
                k.dma("sp", w[:, :, :], self.inp["w_mod"][l, :, n * 512:(n + 1) * 512].rearrange("(c p) n -> p c n", p=128),
                      writes=[w])
                p = ppool.get()
                for c in range(8):
                    k.mm(p[:, :], sc[:, c, :], w[:, c, :], c == 0, False, [sc, w], [p])
                k.mm(p[:, :], ones2[:, :], brow[:, n * 512:(n + 1) * 512], False, True, [ones2, brow], [p])
                s = spool.get()
                k.copy("dve", s[:, :], p[:, :], [p], [s])
                k.dma("sp", self.mod[l, :, n * 512:(n + 1) * 512], s[:, :], reads=[s])

    def load_mod_vecs(self, l, nj, mv, half_gate):
        k = self.k
        G = k.sb([128, 8, 2], F32, "G")
        S = k.sb([128, 8, 2], F32, "S")
        gn = k.sb([128, 8], F32, "gn")
        gate = k.sb([128, 2, D], F32, "gate")
        with self.nc.allow_non_contiguous_dma("tiny"):
            k.dma("sp", gn[:, :], self.inp["norm_gain"][l, nj:nj + 1, :].rearrange("o (c p) -> p (o c)", p=128), writes=[gn])
            for r in range(2):
                k.dma("sp", S[:, :, r], self.mod[l, r:r + 1, mv * D:(mv + 1) * D].rearrange("o (c p) -> p (o c)", p=128), writes=[S])
                k.dma("sp", G[:, :, r], self.mod[l, r:r + 1, (mv + 1) * D:(mv + 2) * D].rearrange("o (c p) -> p (o c)", p=128), writes=[G])
                k.dma("sp", gate[:, r, :], bcast_rows(self.mod[l, r:r + 1, (mv + 2) * D:(mv + 3) * D]), writes=[gate])
        for r in range(2):
            k.stt(G[:, :, r], G[:, :, r], 1.0, gn[:, :], ALU.add, ALU.mult, [G, gn], [G])
        if half_gate:
            k.ts("dve", gate[:, :, :], gate[:, :, :], 0.5, None, ALU.mult, None, [gate], [gate])
        return G, S, gate

    def tiles(self, tsz):
        out = []
        for i in range(self.cfg.L // tsz):
            out.append((self.out[i * tsz:(i + 1) * tsz, :], tsz, 0))
        for i in range(max(1, CTX // tsz)):
            n = min(tsz, CTX)
            out.append((self.xc[i * n:(i + 1) * n, :], n, 1))
        return out

    def norm_part(self, xt, ns, xs_pool, scr, ssq, rstd):
        k = self.k
        for s in range(ns):
            k.act(scr[:, :], xt[:, s, :], AF.Square, [xt], [scr, ssq], accum_out=ssq[:, s:s + 1])
        k.act(rstd[:, :ns], ssq[:, :ns], AF.Sqrt, [ssq, self.epsb], [rstd], bias=self.epsb[:, 0:1], scale=1.0 / D)
        k.op("dve", lambda g: g.reciprocal(rstd[:, :ns], rstd[:, :ns]), [rstd], [rstd])
        xs = xs_pool.get()
        for s in range(ns):
            k.ts("dve", xs[:, s, :], xt[:, s, :], rstd[:, s:s + 1], None, ALU.mult, None, [xt, rstd], [xs])
        return xs

    def transp_part(self, xs, ns, row, G, S, tp_pool, xnT):
        k = self.k
        for c in range(8):
            tp = tp_pool.get()
            for s in range(ns):
                k.tr(tp[:, s * 128:(s + 1) * 128], xs[:, s, c * 128:(c + 1) * 128], self.ident[:, :],
                     [xs, self.ident], [tp], inc=(s == ns - 1))
            k.act(xnT[:, c, :ns * 128], tp[:, :ns * 128], AF.Identity, [tp, G, S], [xnT],
                  bias=S[:, c, row:row + 1], scale=G[:, c, row:row + 1])

    def norm_mod_T(self, xt, ns, row, G, S, xs_pool, tp_pool, xnT, scr, ssq, rstd):
        xs = self.norm_part(xt, ns, xs_pool, scr, ssq, rstd)
        self.transp_part(xs, ns, row, G, S, tp_pool, xnT)

    def phase_ffn(self, l, which):
        k = self.k
        TS = 256
        NS = TS // 128
        win_d = self.inp["ffn_a_in" if which == 0 else "ffn_b_in"]
        wout_d = self.inp["ffn_a_out" if which == 0 else "ffn_b_out"]
        nj, mv = (0, 0) if which == 0 else (2, 6)
        NF = DFF // 128
        with k.phase():
            Win = [k.sb([128, 2 * DFF], BF16, "Win%d" % c) for c in range(8)]
            Wout = k.sb([128, NF, D], BF16, "Wout")
            for c in range(8):
                k.dma("pool", Win[c][:, :], win_d[l, c * 128:(c + 1) * 128, :], writes=[Win[c]])
            k.dma("pool", Wout[:, :, :], wout_d[l, :, :].rearrange("(j p) d -> p j d", p=128), writes=[Wout])
            G, S, gate = self.load_mod_vecs(l, nj, mv, True)
            xpool = k.pool_of(2, [128, NS, D], F32, "xt")
            xs_pool = k.pool_of(1, [128, NS, D], BF16, "xs")
            tp_pool = k.pool_of(2, [128, TS], BF16, "tp", psum=True)
            xnT = k.sb([128, 8, TS], BF16, "xnT")
            hT = k.sb([128, NF, TS], BF16, "hT")
            scr = k.sb([128, D], F32, "scr")
            ssq = k.sb([128, NS], F32, "ssq")
            rstd = k.sb([128, NS], F32, "rstd")
            pa_pool = k.pool_of(2, [128, TS], F32, "pa", psum=True)
            pb_pool = k.pool_of(2, [128, TS], F32, "pb", psum=True)
            sa_pool = k.pool_of(2, [128, TS], BF16, "sa")
            po_pool = k.pool_of(2, [128, 512], F32, "po", psum=True)
            sqj = k.sb([128, D], F32, "sqj")
            tl = self.tiles(TS)

            def load(i):
                xap, ntok, row = tl[i]
                xt = xpool.get()
                k.dma("sp", xt[:, :ntok // 128, :], xap.rearrange("(s p) d -> p s d", p=128), writes=[xt])
                return xt

            xts = {0: load(0)}
            xss = {0: self.norm_part(xts[0], tl[0][1] // 128, xs_pool, sqj, ssq, rstd)}
            self.transp_part(xss[0], tl[0][1] // 128, tl[0][2], G, S, tp_pool, xnT)
            if len(tl) > 1:
                xts[1] = load(1)
            for i, (xap, ntok, row) in enumerate(tl):
                ns = ntok // 128
                xt = xts.pop(i)
                for j in range(NF):
                    pa = pa_pool.get()
                    pb = pb_pool.get()
                    for c in range(8):
                        k.mm(pa[:, :ntok], Win[c][:, j * 128:(j + 1) * 128], xnT[:, c, :ntok], c == 0, c == 7,
                             [Win[c], xnT], [pa])
                    for c in range(8):
                        k.mm(pb[:, :ntok], Win[c][:, DFF + j * 128:DFF + (j + 1) * 128], xnT[:, c, :ntok], c == 0, c == 7,
                             [Win[c], xnT], [pb])
                    sa = sa_pool.get()
                    k.act(sa[:, :ntok], pa[:, :ntok], AF.Silu, [pa], [sa])
                    k.tt("dve", hT[:, j, :ntok], sa[:, :ntok], pb[:, :ntok], ALU.mult, [sa, pb], [hT])
                    if j == NF // 2 and i + 1 < len(tl):
                        xss[i + 1] = self.norm_part(xts[i + 1], tl[i + 1][1] // 128, xs_pool, sqj, ssq, rstd)
                if i + 1 < len(tl):
                    self.transp_part(xss.pop(i + 1), tl[i + 1][1] // 128, tl[i + 1][2], G, S, tp_pool, xnT)
                for s in range(ns):
                    for hf in range(2):
                        po = po_pool.get()
                        for j in range(NF):
                            k.mm(po[:, :], hT[:, j, s * 128:(s + 1) * 128], Wout[:, j, hf * 512:(hf + 1) * 512],
                                 j == 0, j == NF - 1, [hT, Wout], [po])
                        sl = slice(hf * 512, (hf + 1) * 512)
                        k.tt("dve", scr[:, sl], po[:, :], gate[:, row, sl], ALU.mult, [po, gate], [scr])
                        k.tt("pool", xt[:, s, sl], xt[:, s, sl], scr[:, sl], ALU.add, [xt, scr], [xt])
                k.dma("sp", xap.rearrange("(s p) d -> p s d", p=128), xt[:, :ns, :], reads=[xt])
                if i + 2 < len(tl):
                    xts[i + 2] = load(i + 2)

    def build(self):
        k = self.k
        self.consts()
        self.init_copy()
        for l in self.cfg.layers:
            self.phase_mod(l)
            self.phase_ffn(l, 0)
            if self.stages == "ffn_a":
                continue
            if self.stages != "ffn_only":
                if self.cfg.types[l] == "ret":
                    self.phase_ret(l)
                else:
                    self.phase_even(l)
            if self.stages == "mix":
                continue
            if self.stages == "mix_only" and False:
                continue
            self.phase_ffn(l, 1)
        k.barrier()
        return self.nc


RH = 4
RDK = 256
RDV = 512
RQK = RH * RDK
RV = RH * RDV
RIN = 2 * RQK + 2 * RV


def _ret_setup(self):
    nc = self.nc
    L = self.cfg.L
    NT = L + CTX
    NCH = NT // 128
    self.ret_in = self.inp_add("ret_in", [max(1, self.cfg.types.count("ret")), D, RIN])
    self.ret_out = self.inp_add("ret_out", [max(1, self.cfg.types.count("ret")), RV, D])
    self.ret_lf = self.inp_add("ret_logit_f", [max(1, self.cfg.types.count("ret")), RH])
    self.ret_lb = self.inp_add("ret_logit_b", [max(1, self.cfg.types.count("ret")), RH])
    self.rope = self.inp_add("rope_tab", [L, 2, 128])
    self.rconst = self.inp_add("ret_const", [128, 6, 128])
    self.qts = nc.dram_tensor("qts", [NCH, 128, 1024], BF16).ap()
    self.kts = nc.dram_tensor("kts", [NCH, 128, 1024], BF16).ap()
    self.ktok = nc.dram_tensor("ktok", [NT, RQK], BF16).ap()
    self.vtok = nc.dram_tensor("vtok", [NT, RV], BF16).ap()
    self.sgt = nc.dram_tensor("sgt", [NT, RV], BF16).ap()
    self.st = nc.dram_tensor("st", [2, NCH, RH, 128, 1024], BF16).ap()


def ret_consts_host():
    j = np.arange(128, dtype=np.float32)
    c = np.zeros((128, 6, 128), np.float32)
    diff = j[None, :] - j[:, None]
    c[:, 0, :] = np.maximum(diff, 0.0)
    c[:, 1, :] = np.maximum(-diff, 0.0)
    c[:, 2, :] = (diff >= 0).astype(np.float32) / 16.0
    c[:, 3, :] = (diff <= 0).astype(np.float32) / 16.0
    c[:, 4, :] = (j[None, :] + 1.0)
    c[:, 5, :] = (128.0 - j[None, :])
    return c


def _phase_ret(self, l):
    k = self.k
    nc = self.nc
    o = self.cfg.types[:l].count("ret")
    L = self.cfg.L
    NT = L + CTX
    NCH = NT // 128
    NCL = L // 128
    last = (l == self.cfg.depth - 1) and not getattr(self, "force_ctx_out", False)

    with k.phase():
        Wr = [k.sb([128, RIN], BF16, "Wr%d" % c) for c in range(8)]
        for c in range(8):
            k.dma("pool", Wr[c][:, :], self.ret_in[o, c * 128:(c + 1) * 128, :], writes=[Wr[c]])
        G, S, gate = self.load_mod_vecs(l, 1, 3, False)
        TS = 256
        xpool = k.pool_of(2, [128, 2, D], F32, "xt")
        xs_pool = k.pool_of(1, [128, 2, D], BF16, "xs")
        tp_pool = k.pool_of(2, [128, TS], BF16, "tp", psum=True)
        xnT = k.sb([128, 8, TS], BF16, "xnT")
        scr = k.sb([128, D], F32, "scr")
        ssq = k.sb([128, 2], F32, "ssq")
        rstd = k.sb([128, 2], F32, "rstd")
        pp = k.pool_of(3, [128, 512], F32, "pp", psum=True)
        tq_pool = k.pool_of(2, [128, 1024], BF16, "tq", psum=True)
        rope_pool = k.pool_of(2, [128, 2, 128], F32, "rope")
        qtok_pool = k.pool_of(2, [128, RQK], BF16, "qtok")
        ktok_pool = k.pool_of(2, [128, RQK], BF16, "ktok")
        v_pool = k.pool_of(2, [128, RV], BF16, "vst")
        g_pool = k.pool_of(2, [128, RV], BF16, "gst")
        qT_pool = k.pool_of(2, [128, 1024], BF16, "qTs")
        kT_pool = k.pool_of(2, [128, 1024], BF16, "kTs")
        tmp = [k.sb([128, 256], F32, "rt%d" % i) for i in range(4)]
        for (xap, ntok, row) in self.tiles(TS):
            xt = xpool.get()
            k.dma("sp", xt[:, :, :], xap.rearrange("(s p) d -> p s d", p=128), writes=[xt])
            self.norm_mod_T(xt, 2, row, G, S, xs_pool, tp_pool, xnT, scr, ssq, rstd)
            for s in range(2):
                tok0 = (self.tile_tok0(xap) + s * 128)
                ch = tok0 // 128
                if row == 0:
                    rp = rope_pool.get()
                    k.dma("sp", rp[:, :, :], self.rope[tok0:tok0 + 128, :, :], writes=[rp])
                qtok = qtok_pool.get()
                ktok = ktok_pool.get()
                vst = v_pool.get()
                gst = g_pool.get()
                for n in range(12):
                    p = pp.get()
                    for c in range(8):
                        k.mm(p[:, :], xnT[:, c, s * 128:(s + 1) * 128], Wr[c][:, n * 512:(n + 1) * 512], c == 0, c == 7,
                             [xnT, Wr[c]], [p])
                    if n < 4:
                        dst = qtok if n < 2 else ktok
                        cs = (n % 2) * 512
                        if row == 0:
                            pv = p[:, :].rearrange("p (h g f d) -> p h g f d", h=2, g=2, f=2)
                            dv = dst[:, cs:cs + 512].rearrange("p (h g f d) -> p h g f d", h=2, g=2, f=2)
                            ct = rp[:, 0, :].rearrange("p (g d) -> p g d", g=2).unsqueeze(1).broadcast_to([128, 2, 2, 64])
                            sn = rp[:, 1, :].rearrange("p (g d) -> p g d", g=2).unsqueeze(1).broadcast_to([128, 2, 2, 64])
                            tv = [t[:, :].rearrange("p (h g d) -> p h g d", h=2, g=2) for t in tmp]
                            k.tt("dve", tv[0], pv[:, :, :, 0, :], ct, ALU.mult, [p, rp], [tmp[0]])
                            k.tt("dve", tv[1], pv[:, :, :, 1, :], sn, ALU.mult, [p, rp], [tmp[1]])
                            k.tt("dve", tv[2], pv[:, :, :, 0, :], sn, ALU.mult, [p, rp], [tmp[2]])
                            k.tt("dve", tv[3], pv[:, :, :, 1, :], ct, ALU.mult, [p, rp], [tmp[3]])
                            k.tt("pool", dv[:, :, :, 0, :], tv[0], tv[1], ALU.subtract, [tmp[0], tmp[1]], [dst])
                            k.tt("pool", dv[:, :, :, 1, :], tv[2], tv[3], ALU.add, [tmp[2], tmp[3]], [dst])
                        else:
                            k.copy("act", dst[:, cs:cs + 512], p[:, :], [p], [dst])
                    elif n < 8:
                        k.copy("act", vst[:, (n - 4) * 512:(n - 3) * 512], p[:, :], [p], [vst])
                    else:
                        k.act(gst[:, (n - 8) * 512:(n - 7) * 512], p[:, :], AF.Silu, [p], [gst])
                for (src, dpool, dscr) in ((qtok, qT_pool, self.qts), (ktok, kT_pool, self.kts)):
                    tq = tq_pool.get()
                    for b in range(8):
                        k.tr(tq[:, b * 128:(b + 1) * 128], src[:, b * 128:(b + 1) * 128], self.ident[:, :],
                             [src, self.ident], [tq], inc=(b == 7))
                    dT = dpool.get()
                    k.copy("dve" if src is qtok else "act", dT[:, :], tq[:, :], [tq], [dT])
                    k.dma("sp", dscr[ch, :, :], dT[:, :], reads=[dT])
                k.dma("sp", self.ktok[tok0:tok0 + 128, :], ktok[:, :], reads=[ktok])
                k.dma("sp", self.vtok[tok0:tok0 + 128, :], vst[:, :], reads=[vst])
                k.dma("sp", self.sgt[tok0:tok0 + 128, :], gst[:, :], reads=[gst])

    with k.phase():
        rc = k.sb([128, 6, 128], F32, "rconst")
        k.dma("sp", rc[:, :, :], self.rconst, writes=[rc])
        lg = k.sb([128, 2 * RH], F32, "lg")
        k.dma("sp", lg[:, 0:RH], bcast_rows(self.ret_lf[o:o + 1, :]), writes=[lg])
        k.dma("sp", lg[:, RH:2 * RH], bcast_rows(self.ret_lb[o:o + 1, :]), writes=[lg])
        k.act(lg[:, :], lg[:, :], AF.Exp, [lg], [lg], scale=-1.0)
        k.act(lg[:, :], lg[:, :], AF.Ln, [lg], [lg], bias=1.0, scale=1.0)
        k.ts("dve", lg[:, :], lg[:, :], -1.0, None, ALU.mult, None, [lg], [lg])
        zeta = k.sb([128, 2 * RH], F32, "zeta")
        gch = k.sb([128, 2 * RH], F32, "gch")
        XI = k.sb([128, 2 * RH, 128], BF16, "XI")
        DcT = k.sb([128, RH, 128], BF16, "DcT")
        dtmp = k.sb([128, 2, 128], F32, "dtmp")
        for e in range(RH):
            f, b = e, RH + e
            k.act(XI[:, f, :], rc[:, 4, :], AF.Exp, [rc, lg], [XI], scale=lg[:, f:f + 1])
            k.act(XI[:, b, :], rc[:, 5, :], AF.Exp, [rc, lg], [XI], scale=lg[:, b:b + 1])
            k.act(dtmp[:, 0, :], rc[:, 0, :], AF.Exp, [rc, lg], [dtmp], scale=lg[:, f:f + 1])
            k.act(dtmp[:, 1, :], rc[:, 1, :], AF.Exp, [rc, lg], [dtmp], scale=lg[:, b:b + 1])
            k.tt("dve", dtmp[:, :, :], dtmp[:, :, :], rc[:, 2:4, :], ALU.mult, [dtmp, rc], [dtmp])
            k.tt("dve", DcT[:, e, :], dtmp[:, 0, :], dtmp[:, 1, :], ALU.add, [dtmp], [DcT])
        for e in range(RH):
            k.act(zeta[:, e:e + 1], rc[:, 0, 127:128], AF.Exp, [rc, lg], [zeta], scale=lg[:, e:e + 1])
            k.act(zeta[:, RH + e:RH + e + 1], rc[:, 1, 0:1], AF.Exp, [rc, lg], [zeta], scale=lg[:, RH + e:RH + e + 1])
        k.ts("dve", zeta[:, :], zeta[:, :], 1.0 / 16.0, None, ALU.mult, None, [zeta], [zeta])
        k.act(gch[:, :], lg[:, :], AF.Exp, [lg], [gch], scale=128.0)

        with k.phase():
            Sst = [[k.sb([128, 2, RDV], F32, "S%d%d" % (d, e)) for e in range(RH)] for d in range(2)]
            Sbf = [[k.pool_of(2, [128, 2 * RDV], BF16, "Sb%d%d" % (d, e)) for e in range(RH)] for d in range(2)]
            for d in range(2):
                for e in range(RH):
                    k.memset("pool", Sst[d][e][:, :, :], 0.0, [Sst[d][e]])
            kin = k.pool_of(4, [128, RQK], BF16, "kin")
            vin = k.pool_of(4, [128, RV], BF16, "vin")
            kz_pool = k.pool_of(3, [128, RQK], BF16, "kz")
            pd = k.pool_of(6, [128, RDV], F32, "pd", psum=True)
            order_f = [NCL, NCL + 1] + list(range(NCL))
            order_b = [NCL + 1, NCL] + list(range(NCL - 1, -1, -1))
            for step in range(NCH):
                for d, order in ((0, order_f), (1, order_b)):
                    ch = order[step]
                    kt = kin.get()
                    vt = vin.get()
                    k.dma("sp", kt[:, :], self.ktok[ch * 128:(ch + 1) * 128, :], writes=[kt])
                    k.dma("sp", vt[:, :], self.vtok[ch * 128:(ch + 1) * 128, :], writes=[vt])
                    if step < NCH - 1:
                        kz = kz_pool.get()
                        k.tt("dve", kz[:, :].rearrange("p (e x) -> p e x", e=RH), kt[:, :].rearrange("p (e x) -> p e x", e=RH),
                             zeta[:, d * RH:(d + 1) * RH].unsqueeze(2).broadcast_to([128, RH, RDK]), ALU.mult, [kt, zeta], [kz])
                    for e in range(RH):
                        S_ = Sst[d][e]
                        sb_ = Sbf[d][e].get()
                        k.copy("act", sb_[:, :], S_[:, :, :].rearrange("p a b -> p (a b)"), [S_], [sb_])
                        k.dma("sp", self.st[d, ch, e, :, :], sb_[:, :], reads=[sb_])
                        if step == NCH - 1:
                            continue
                        for dc in range(2):
                            p = pd.get()
                            k.mm(p[:, :], kz[:, e * RDK + dc * 128:e * RDK + (dc + 1) * 128], vt[:, e * RDV:(e + 1) * RDV], True, True, [kz, vt], [p])
                            k.stt(S_[:, dc, :], S_[:, dc, :], gch[:, d * RH + e:d * RH + e + 1], p[:, :], ALU.mult, ALU.add,
                                  [S_, gch, p], [S_])

        with k.phase():
            Wo = k.sb([128, 16, D], BF16, "Wo")
            k.dma("pool", Wo[:, :, :], self.ret_out[o, :, :].rearrange("(j p) d -> p j d", p=128), writes=[Wo])
            G, S, gate = self.load_mod_vecs(l, 1, 3, False)
            qT_pool = k.pool_of(2, [128, 8, 128], BF16, "qTc")
            kT_pool = k.pool_of(2, [128, 8, 128], BF16, "kTc")
            v_pool = k.pool_of(2, [128, RV], BF16, "vc")
            g_pool = k.pool_of(2, [128, RV], BF16, "gc")
            st_pool = k.pool_of(2, [128, 2, RH, 2, RDV], BF16, "stc")
            x_pool = k.pool_of(2, [128, D], F32, "xc")
            ps_pool = k.pool_of(2, [128, RH, 128], F32, "psc", psum=True)
            po = [k.ps([128, RDV], F32, "poc%d" % e) for e in range(RH)]
            pt_pool = k.pool_of(1, [128, 1024], BF16, "ptc", psum=True)
            pr_pool = k.pool_of(1, [128, 512], F32, "prc", psum=True)
            in_pool = k.pool_of(2, [128, RH, 128], BF16, "inT")
            qs_pool = k.pool_of(2, [128, 2, 2 * RH, 128], BF16, "qs")
            Y = k.sb([128, RV], BF16, "Y")
            YT = k.sb([128, 16, 128], BF16, "YT")
            stats = k.sb([128, RH, 6], F32, "stats")
            mv = k.sb([128, RH, 2], F32, "mv")
            rs = k.sb([128, RH], F32, "rs")
            nb = k.sb([128, RH], F32, "nb")
            yn = k.sb([128, RV], F32, "yn")
            scr2 = k.sb([128, D], F32, "scr2")
            gneps = k.sb([128, 1], F32, "gneps")
            k.memset("dve", gneps[:, :], GN_EPS, [gneps])
            chunks = list(range(NCL)) + ([] if last else [NCL, NCL + 1])
            for ch in chunks:
                row = 0 if ch < NCL else 1
                xap = self.out[ch * 128:(ch + 1) * 128, :] if row == 0 else self.xc[(ch - NCL) * 128:(ch - NCL + 1) * 128, :]
                qT = qT_pool.get(); kT = kT_pool.get(); vt = v_pool.get(); gt = g_pool.get(); stt_ = st_pool.get(); xt = x_pool.get()
                k.dma("sp", qT[:, :, :], self.qts[ch, :, :].rearrange("p (b t) -> p b t", b=8), writes=[qT])
                k.dma("sp", kT[:, :, :], self.kts[ch, :, :].rearrange("p (b t) -> p b t", b=8), writes=[kT])
                k.dma("sp", vt[:, :], self.vtok[ch * 128:(ch + 1) * 128, :], writes=[vt])
                k.dma("sp", gt[:, :], self.sgt[ch * 128:(ch + 1) * 128, :], writes=[gt])
                for d in range(2):
                    k.dma("sp", stt_[:, d, :, :, :], self.st[d, ch, :, :, :].rearrange("e p (a b) -> p e a b", a=2), writes=[stt_])
                k.dma("sp", xt[:, :], xap, writes=[xt])
                ps = ps_pool.get()
                for e in range(RH):
                    for dc in range(2):
                        k.mm(ps[:, e, :], kT[:, 2 * e + dc, :], qT[:, 2 * e + dc, :], dc == 0, dc == 1, [kT, qT], [ps],
                             inc=(e == RH - 1 and dc == 1))
                inT = in_pool.get()
                k.tt("dve", inT[:, :, :], ps[:, :, :], DcT[:, :, :], ALU.mult, [ps, DcT], [inT])
                qs = qs_pool.get()
                for d in range(2):
                    k.tt("pool", qs[:, d, :, :].rearrange("p (e c) t -> p e c t", e=RH), qT[:, :, :].rearrange("p (e c) t -> p e c t", e=RH),
                         XI[:, d * RH:(d + 1) * RH, :].unsqueeze(2).broadcast_to([128, RH, 2, 128]), ALU.mult, [qT, XI], [qs])
                for e in range(RH):
                    k.mm(po[e][:, :], inT[:, e, :], vt[:, e * RDV:(e + 1) * RDV], True, False, [inT, vt], [po[e]])
                    for d in range(2):
                        for dc in range(2):
                            k.mm(po[e][:, :], qs[:, d, 2 * e + dc, :], stt_[:, d, e, dc, :], False, (d == 1 and dc == 1), [qs, stt_], [po[e]])
                for e in range(RH):
                    k.op("dve", lambda g, e=e: g.bn_stats(stats[:, e, :], po[e][:, :]), [po[e]], [stats])
                for e in range(RH):
                    k.op("dve", lambda g, e=e: g.bn_aggr(mv[:, e, :], stats[:, e, :]), [stats], [mv])
                k.act(rs[:, :], mv[:, :, 1], AF.Sqrt, [mv, gneps], [rs], bias=gneps[:, 0:1], scale=1.0)
                k.op("dve", lambda g: g.reciprocal(rs[:, :], rs[:, :]), [rs], [rs])
                k.stt(nb[:, :], mv[:, :, 0], -1.0, rs[:, :], ALU.mult, ALU.mult, [mv, rs], [nb])
                for e in range(RH):
                    k.act(yn[:, e * RDV:(e + 1) * RDV], po[e][:, :], AF.Identity, [po[e], rs, nb], [yn], bias=nb[:, e:e + 1], scale=rs[:, e:e + 1])
                k.tt("pool", Y[:, :], yn[:, :], gt[:, :], ALU.mult, [yn, gt], [Y])
                for hf in range(2):
                    pt = pt_pool.get()
                    for b in range(8):
                        bb = hf * 8 + b
                        k.tr(pt[:, b * 128:(b + 1) * 128], Y[:, bb * 128:(bb + 1) * 128], self.ident[:, :], [Y, self.ident], [pt],
                             inc=(b == 7))
                    k.copy("dve" if hf == 0 else "act", YT[:, hf * 8:(hf + 1) * 8, :].rearrange("p b t -> p (b t)"), pt[:, :], [pt], [YT])
                for hf in range(2):
                    pr = pr_pool.get()
                    for b in range(16):
                        k.mm(pr[:, :], YT[:, b, :], Wo[:, b, hf * 512:(hf + 1) * 512], b == 0, b == 15, [YT, Wo], [pr])
                    sl = slice(hf * 512, (hf + 1) * 512)
                    k.tt("dve", scr2[:, sl], pr[:, :], gate[:, row, sl], ALU.mult, [pr, gate], [scr2])
                    k.tt("pool", xt[:, sl], xt[:, sl], scr2[:, sl], ALU.add, [xt, scr2], [xt])
                k.dma("sp", xap, xt[:, :], reads=[xt])


Prog.ret_setup = _ret_setup
Prog.phase_ret = _phase_ret


def host_consts(rows):
    L = rows * GRID_W
    out = {}
    out["ident"] = np.eye(128, dtype=np.float32).astype(ml_dtypes.bfloat16)
    t = np.arange(L)
    nf = RDK // 4
    inv = (10000.0 ** (-np.arange(nf, dtype=np.float32) / nf)).astype(np.float32)
    ang = np.concatenate([(t // GRID_W).astype(np.float32)[:, None] * inv, (t % GRID_W).astype(np.float32)[:, None] * inv], axis=-1)
    rope = np.stack([np.cos(ang), np.sin(ang)], axis=1).astype(np.float32)
    out["rope_tab"] = rope
    out["ret_const"] = ret_consts_host()
    for nm, Ls in (("lat", L), ("ctx", CTX)):
        hc = hyena_consts_host(Ls)
        out["dft_" + nm] = hc["dft"]; out["emb_" + nm] = hc["emb"]; out["negt_" + nm] = hc["negt"]; out["wk_" + nm] = hc["wk"]
    max_decay = math.log(1e-2) / 0.3
    min_decay = math.log(1e-2) / 1.5
    out["absdelta"] = np.abs(np.linspace(min_decay, max_decay, HYW, dtype=np.float32))[None, :].astype(np.float32)
    return out


NAH = 8
NAD = 64
NAW = 512
HYW = 512
EIN = 3 * NAW + 3 * HYW
HY_EMB = 17
HY_ORDER = 64
I32 = mybir.dt.int32
TWO_PI = 2.0 * math.pi


def _even_setup(self):
    nc = self.nc
    L = self.cfg.L
    NT = L + CTX
    ne = max(1, self.cfg.types.count("even"))
    a = self.inp_add
    a("even_in", [ne, D, EIN]); a("even_out", [ne, D, D]); a("na_q_gain", [ne, NAD]); a("na_k_gain", [ne, NAD])
    a("na_tab", [ne, NAH, 128, 2, 16, 64])
    a("hy_conv_w", [ne, 3, 3 * HYW]); a("hy_conv_b", [ne, 3 * HYW])
    a("hy_fw1", [ne, HY_EMB, HY_ORDER]); a("hy_fb1", [ne, HY_ORDER]); a("hy_fw2", [ne, HY_ORDER, HY_ORDER]); a("hy_fb2", [ne, HY_ORDER])
    a("hy_fw3", [ne, HY_ORDER, HY_ORDER]); a("hy_fb3", [ne, HY_ORDER]); a("hy_fw4", [ne, HY_ORDER, 2 * HYW]); a("hy_freq", [ne, HY_ORDER])
    a("hy_bias", [ne, HYW])
    for nm, Ls in (("lat", L), ("ctx", CTX)):
        KC = Ls // 128 + 1
        a("dft_" + nm, [2, KC, 128, KC, 128], BF16)
        a("emb_" + nm, [HY_EMB, Ls])
        a("negt_" + nm, [128, Ls // 128])
        a("wk_" + nm, [128, KC, 2])
    a("absdelta", [1, HYW])
    self.qtn = nc.dram_tensor("qtn", [4, 128, NT], BF16).ap()
    self.ktn = nc.dram_tensor("ktn", [4, 128, NT], BF16).ap()
    self.vn = nc.dram_tensor("vn", [NT, NAW], BF16).ap()
    self.u_lat = nc.dram_tensor("u_lat", [L + 2, 3 * HYW], F32).ap()
    self.u_ctx = nc.dram_tensor("u_ctx", [CTX + 2, 3 * HYW], F32).ap()
    self.cat = nc.dram_tensor("cat", [NT, D], BF16).ap()
    self.x0z = nc.dram_tensor("x0z", [NT, 2, HYW], BF16).ap()


def na_tab_host(rpb, rows):
    ne = rpb.shape[0]
    tab = np.full((ne, NAH, 128, 2, 16, 64), -30000.0, np.float32)
    c = np.arange(64)
    cs = np.clip(c - 8, 0, 48)
    cp = np.arange(64)
    colvalid = (cp[:, None] >= cs[None, :]) & (cp[:, None] < cs[None, :] + 16)
    dcidx = np.clip(cp[:, None] - c[None, :] + 15, 0, 30)
    for rk in range(2):
        for jr in range(16):
            dr = rk + 7 - jr
            if abs(dr) > 7:
                continue
            g = rpb[:, :, dr + 7, :][:, :, dcidx]
            g = np.where(colvalid[None, None], g, np.float32(-30000.0))
            tab[:, :, rk * 64:(rk + 1) * 64, 0, jr, :] = g
            if -4 <= dr <= 3:
                tab[:, :, rk * 64:(rk + 1) * 64, 1, jr, :] = g
    return tab


def hyena_consts_host(Ls):
    KC = Ls // 128 + 1
    N = 2 * Ls
    nn = KC * 128
    a = np.arange(nn, dtype=np.int64)
    m = (a[:, None] * a[None, :]) % N
    ang = (2.0 * np.pi / N) * m.astype(np.float64)
    out = {}
    tabs = np.stack([np.cos(ang), np.sin(ang)]).astype(np.float32)
    t5 = tabs.reshape(2, KC, 128, KC, 128).transpose(0, 3, 2, 1, 4)
    out["dft"] = np.ascontiguousarray(t5).astype(ml_dtypes.bfloat16)
    t = np.linspace(0.0, 1.0, Ls, dtype=np.float32)[:, None]
    w = (2.0 * np.float32(math.pi) * np.arange(Ls, dtype=np.float32)[:, None] / np.float32(Ls)).astype(np.float32)
    bands = np.linspace(1e-4, 8 - 1, 8, dtype=np.float32)
    emb = np.concatenate([t, np.cos(bands * w), -np.sin(bands * w)], axis=-1).astype(np.float32)
    out["emb"] = np.ascontiguousarray(emb.T)
    out["negt"] = np.ascontiguousarray((-t[:, 0]).reshape(Ls // 128, 128).T).astype(np.float32)
    k = np.arange(nn)
    wk = np.where((k == 0) | (k == Ls), 1.0, 2.0) / N
    wk = np.where(k <= Ls, wk, 0.0).astype(np.float32)
    wk2 = np.stack([wk, -wk], axis=-1).reshape(KC, 128, 2).transpose(1, 0, 2)
    out["wk"] = np.ascontiguousarray(wk2).astype(np.float32)
    return out


def _phase_even(self, l):
    k = self.k
    e = self.cfg.types[:l].count("even")
    L = self.cfg.L
    NT = L + CTX
    rows = self.cfg.rows
    last = (l == self.cfg.depth - 1) and not getattr(self, "force_ctx_out", False)
    inp = self.inp

    with k.phase():
        We = [k.sb([128, EIN], BF16, "We%d" % c) for c in range(8)]
        for c in range(8):
            k.dma("pool", We[c][:, :], inp["even_in"][e, c * 128:(c + 1) * 128, :], writes=[We[c]])
        G, S, gate = self.load_mod_vecs(l, 1, 3, False)
        zrow = k.sb([1, 3 * HYW], F32, "zrow")
        k.memset("dve", zrow[:, :], 0.0, [zrow])
        for (ut, Ls) in ((self.u_lat, L), (self.u_ctx, CTX)):
            k.dma("sp", ut[0:1, :], zrow[:, :], reads=[zrow])
            k.dma("sp", ut[Ls + 1:Ls + 2, :], zrow[:, :], reads=[zrow])
        gq = k.sb([128, 2, NAD], F32, "gq")
        k.dma("sp", gq[:, 0, :], bcast_rows(inp["na_q_gain"][e:e + 1, :]), writes=[gq])
        k.dma("sp", gq[:, 1, :], bcast_rows(inp["na_k_gain"][e:e + 1, :]), writes=[gq])
        TS = 256
        xpool = k.pool_of(2, [128, 2, D], F32, "xt")
        xs_pool = k.pool_of(1, [128, 2, D], BF16, "xs")
        tp_pool = k.pool_of(2, [128, TS], BF16, "tp", psum=True)
        xnT = k.sb([128, 8, TS], BF16, "xnT")
        scr = k.sb([128, D], F32, "scr")
        ssq = k.sb([128, 2], F32, "ssq")
        rstd = k.sb([128, 2], F32, "rstd")
        pp = k.pool_of(3, [128, 512], F32, "pp", psum=True)
        tq_pool = k.pool_of(2, [128, 512], BF16, "tq", psum=True)
        sq = k.sb([128, 512], F32, "sq")
        hs = k.sb([128, 2, NAH], F32, "hs")
        qn = k.sb([128, 512], F32, "qn")
        qk_tok = k.pool_of(2, [128, 512], BF16, "qktok")
        qkT = k.pool_of(4, [128, 4, 128], BF16, "qkT")
        vst = k.pool_of(2, [128, NAW], BF16, "vst")
        ust = k.pool_of(2, [128, 3 * HYW], F32, "ust")
        for (xap, ntok, row) in self.tiles(TS):
            xt = xpool.get()
            k.dma("sp", xt[:, :, :], xap.rearrange("(s p) d -> p s d", p=128), writes=[xt])
            self.norm_mod_T(xt, 2, row, G, S, xs_pool, tp_pool, xnT, scr, ssq, rstd)
            for s in range(2):
                tok0 = self.tile_tok0(xap) + s * 128
                us = ust.get()
                for n in range(6):
                    p = pp.get()
                    for c in range(8):
                        k.mm(p[:, :], xnT[:, c, s * 128:(s + 1) * 128], We[c][:, n * 512:(n + 1) * 512], c == 0, c == 7,
                             [xnT, We[c]], [p])
                    if n < 2:
                        k.act(sq[:, :], p[:, :], AF.Square, [p], [sq])
                        k.op("dve", lambda g, n=n: g.tensor_reduce(hs[:, n, :], sq[:, :].rearrange("p (h d) -> p h d", h=NAH),
                                                                  AX.X, ALU.add), [sq], [hs])
                        k.act(hs[:, n, :], hs[:, n, :], AF.Sqrt, [hs, self.epsb], [hs], bias=self.epsb[:, 0:1], scale=1.0 / NAD)
                        k.op("dve", lambda g, n=n: g.reciprocal(hs[:, n, :], hs[:, n, :]), [hs], [hs])
                        k.tt("dve", qn[:, :].rearrange("p (h d) -> p h d", h=NAH), p[:, :].rearrange("p (h d) -> p h d", h=NAH),
                             hs[:, n, :].unsqueeze(2).broadcast_to([128, NAH, NAD]), ALU.mult, [p, hs], [qn])
                        qt = qk_tok.get()
                        k.tt("pool", qt[:, :].rearrange("p (h d) -> p h d", h=NAH), qn[:, :].rearrange("p (h d) -> p h d", h=NAH),
                             gq[:, n, :].unsqueeze(1).broadcast_to([128, NAH, NAD]), ALU.mult, [qn, gq], [qt])
                        tq = tq_pool.get()
                        for b in range(4):
                            k.tr(tq[:, b * 128:(b + 1) * 128], qt[:, b * 128:(b + 1) * 128], self.ident[:, :], [qt, self.ident], [tq],
                                 inc=(b == 3))
                        dT = qkT.get()
                        k.copy("act", dT[:, :, :].rearrange("p b t -> p (b t)"), tq[:, :], [tq], [dT])
                        dst = self.qtn if n == 0 else self.ktn
                        k.dma("sp", dst[:, :, tok0:tok0 + 128].rearrange("b p t -> p b t"), dT[:, :, :], reads=[dT])
                    elif n == 2:
                        v_ = vst.get()
                        k.copy("act", v_[:, :], p[:, :], [p], [v_])
                        k.dma("sp", self.vn[tok0:tok0 + 128, :], v_[:, :], reads=[v_])
                    else:
                        k.copy("act" if n % 2 else "dve", us[:, (n - 3) * 512:(n - 2) * 512], p[:, :], [p], [us])
                ut, t0 = (self.u_lat, tok0) if row == 0 else (self.u_ctx, tok0 - L)
                k.dma("sp", ut[1 + t0:1 + t0 + 128, :], us[:, :], reads=[us])

    self.hyena(l, e, "lat", L, self.u_lat, 0)
    if not last:
        self.hyena(l, e, "ctx", CTX, self.u_ctx, L)
    self.na_attention(l, e, last)
    with k.phase():
        Wo = k.sb([128, 8, D], BF16, "Weo")
        k.dma("pool", Wo[:, :, :], inp["even_out"][e, :, :].rearrange("(j p) d -> p j d", p=128), writes=[Wo])
        G, S, gate = self.load_mod_vecs(l, 1, 3, False)
        c_pool = k.pool_of(2, [128, D], BF16, "catc")
        x_pool = k.pool_of(2, [128, D], F32, "xo")
        pt_pool = k.pool_of(2, [128, 1024], BF16, "pto", psum=True)
        pr_pool = k.pool_of(2, [128, 512], F32, "pro", psum=True)
        cT_pool = k.pool_of(2, [128, 8, 128], BF16, "cT")
        scr2 = k.sb([128, D], F32, "scr2")
        nchunks = (L // 128) + (0 if last else CTX // 128)
        for ch in range(nchunks):
            row = 0 if ch < L // 128 else 1
            xap = self.out[ch * 128:(ch + 1) * 128, :] if row == 0 else self.xc[(ch - L // 128) * 128:(ch - L // 128 + 1) * 128, :]
            ct = c_pool.get(); xt = x_pool.get()
            k.dma("sp", ct[:, :], self.cat[ch * 128:(ch + 1) * 128, :], writes=[ct])
            k.dma("sp", xt[:, :], xap, writes=[xt])
            pt = pt_pool.get()
            for b in range(8):
                k.tr(pt[:, b * 128:(b + 1) * 128], ct[:, b * 128:(b + 1) * 128], self.ident[:, :], [ct, self.ident], [pt], inc=(b == 7))
            cT = cT_pool.get()
            k.copy("act", cT[:, :, :].rearrange("p b t -> p (b t)"), pt[:, :], [pt], [cT])
            for hf in range(2):
                pr = pr_pool.get()
                for b in range(8):
                    k.mm(pr[:, :], cT[:, b, :], Wo[:, b, hf * 512:(hf + 1) * 512], b == 0, b == 7, [cT, Wo], [pr])
                sl = slice(hf * 512, (hf + 1) * 512)
                k.tt("dve", scr2[:, sl], pr[:, :], gate[:, row, sl], ALU.mult, [pr, gate], [scr2])
                k.tt("pool", xt[:, sl], xt[:, sl], scr2[:, sl], ALU.add, [xt, scr2], [xt])
            k.dma("sp", xap, xt[:, :], reads=[xt])


def _hyena(self, l, e, nm, Ls, ut, tokbase):
    k = self.k
    inp = self.inp
    NCn = Ls // 128
    KC = NCn + 1
    dft = inp["dft_" + nm]
    with k.phase():
        cw = k.sb([128, 3, 3 * HYW], F32, "cw")
        cb = k.sb([128, 3 * HYW], F32, "cb")
        for tpi in range(3):
            k.dma("sp", cw[:, tpi, :], bcast_rows(inp["hy_conv_w"][e, tpi:tpi + 1, :]), writes=[cw])
        k.dma("sp", cb[:, :], bcast_rows(inp["hy_conv_b"][e:e + 1, :]), writes=[cb])
        upool = k.pool_of(3, [128, 3, 3 * HYW], F32, "uabc")
        t1 = k.pool_of(3, [128, 3 * HYW], F32, "sc1")
        t2 = k.pool_of(3, [128, 3 * HYW], F32, "sc2")
        xz = k.pool_of(3, [128, 2, HYW], BF16, "xz")
        for n in range(NCn):
            u = upool.get()
            for tpi in range(3):
                k.dma("sp", u[:, tpi, :], ut[n * 128 + tpi:n * 128 + tpi + 128, :], writes=[u])
            a_ = t1.get(); b2 = t2.get()
            eg = "pool" if n % 3 == 2 else "dve"
            k.tt(eg, a_[:, :], u[:, 0, :], cw[:, 0, :], ALU.mult, [u, cw], [a_])
            k.tt(eg, b2[:, :], u[:, 1, :], cw[:, 1, :], ALU.mult, [u, cw], [b2])
            k.tt(eg, a_[:, :], a_[:, :], b2[:, :], ALU.add, [a_, b2], [a_])
            k.tt(eg, b2[:, :], u[:, 2, :], cw[:, 2, :], ALU.mult, [u, cw], [b2])
            k.tt(eg, a_[:, :], a_[:, :], b2[:, :], ALU.add, [a_, b2], [a_])
            k.tt(eg, a_[:, :], a_[:, :], cb[:, :], ALU.add, [a_, cb], [a_])
            o_ = xz.get()
            k.copy("act", o_[:, 0, :], a_[:, 0:HYW], [a_], [o_])
            k.tt(eg, o_[:, 1, :], a_[:, HYW:2 * HYW], a_[:, 2 * HYW:3 * HYW], ALU.mult, [a_], [o_])
            k.dma("sp", self.x0z[tokbase + n * 128:tokbase + (n + 1) * 128, :, :], o_[:, :, :], reads=[o_])
    with k.phase():
        wk = k.sb([128, KC, 2], F32, "wk")
        k.dma("sp", wk[:, :, :], inp["wk_" + nm], writes=[wk])
        negt = k.sb([128, NCn], F32, "negt")
        k.dma("sp", negt[:, :], inp["negt_" + nm], writes=[negt])
        KK = k.sb([128, KC, 2, HYW], BF16, "KK")
        KKtok = [Tok("kk%d" % i) for i in range(KC)]
        with k.phase():
            Hf = k.sb([128, NCn, HYW], BF16, "Hf")
            Hb = k.sb([128, NCn, HYW], BF16, "Hb")
            with k.phase():
                CW = min(512, Ls)
                embp = k.pool_of(2, [HY_EMB, CW], F32, "emb")
                fw = [k.sb([HY_EMB, HY_ORDER], F32, "fw1"), k.sb([HY_ORDER, HY_ORDER], F32, "fw2"), k.sb([HY_ORDER, HY_ORDER], F32, "fw3")]
                fw4 = k.sb([HY_ORDER, 2 * HYW], F32, "fw4")
                fbT = k.sb([HY_ORDER, 4], F32, "fbT")
                k.dma("sp", fw[0][:, :], inp["hy_fw1"][e], writes=[fw[0]])
                k.dma("sp", fw[1][:, :], inp["hy_fw2"][e], writes=[fw[1]])
                k.dma("sp", fw[2][:, :], inp["hy_fw3"][e], writes=[fw[2]])
                k.dma("sp", fw4[:, :], inp["hy_fw4"][e], writes=[fw4])
                with self.nc.allow_non_contiguous_dma("tiny"):
                    for i, nmv in enumerate(("hy_fb1", "hy_fb2", "hy_fb3", "hy_freq")):
                        k.dma("sp", fbT[:, i:i + 1], inp[nmv][e:e + 1, :].rearrange("o f -> f o"), writes=[fbT])
                fbias = k.sb([HY_ORDER, 3], F32, "fbias")
                for i in range(3):
                    k.tt("dve", fbias[:, i:i + 1], fbT[:, i:i + 1], fbT[:, 3:4], ALU.mult, [fbT], [fbias])
                adl = k.sb([128, HYW], F32, "adl")
                k.dma("sp", adl[:, :], bcast_rows(inp["absdelta"]), writes=[adl])
                hcur = [k.sb([HY_ORDER, CW], F32, "hmlp%d" % i) for i in range(2)]
                pm = k.pool_of(2, [HY_ORDER, 512], F32, "pm", psum=True)
                pre = k.sb([HY_ORDER, 512], F32, "pre")
                nfl = k.sb([HY_ORDER, 512], F32, "nfl")
                nin = k.sb([HY_ORDER, 512], I32, "nin")
                win = k.pool_of(2, [128, HYW], F32, "win")
                ph = k.pool_of(2, [128, 512], F32, "ph", psum=True)
                for cc in range(Ls // CW):
                    em = embp.get()
                    k.dma("sp", em[:, :], inp["emb_" + nm][:, cc * CW:(cc + 1) * CW], writes=[em])
                    for layer in range(3):
                        src = em if layer == 0 else hcur[(layer - 1) % 2]
                        dst = hcur[layer % 2]
                        p = pm.get()
                        k.mm(p[:, :CW], fw[layer][:, :], src[:, :], True, True, [fw[layer], src], [p])
                        k.act(pre[:, :CW], p[:, :CW], AF.Identity, [p, fbT, fbias], [pre], bias=fbias[:, layer:layer + 1], scale=fbT[:, 3:4])
                        k.ts("dve", nfl[:, :CW], pre[:, :CW], 1.0 / TWO_PI, None, ALU.mult, None, [pre], [nfl])
                        k.copy("dve", nin[:, :CW], nfl[:, :CW], [nfl], [nin])
                        k.copy("dve", nfl[:, :CW], nin[:, :CW], [nin], [nfl])
                        k.stt(pre[:, :CW], nfl[:, :CW], -TWO_PI, pre[:, :CW], ALU.mult, ALU.add, [nfl, pre], [pre])
                        k.ts("dve", pre[:, :CW], pre[:, :CW], 3.1415925, -3.1415925, ALU.min, ALU.max, [pre], [pre])
                        k.act(dst[:, :], pre[:, :CW], AF.Sin, [pre], [dst])
                    h3 = hcur[0]
                    for sub in range(CW // 128):
                        n = cc * (CW // 128) + sub
                        w_ = win.get()
                        k.act(w_[:, :], adl[:, :], AF.Exp, [adl, negt], [w_], scale=negt[:, n:n + 1])
                        for hf, dstH in ((0, Hf), (1, Hb)):
                            p = ph.get()
                            k.mm(p[:, :], h3[:, sub * 128:(sub + 1) * 128], fw4[:, hf * HYW:(hf + 1) * HYW], True, True, [h3, fw4], [p])
                            k.tt("dve", dstH[:, n, :], p[:, :], w_[:, :], ALU.mult, [p, w_], [dstH])
            tabp = k.pool_of(2, [128, 2, NCn, 128], BF16, "tabF")
            pacc = [k.ps([128, 512], F32, "pF%d" % i) for i in range(4)]
            bsb = k.pool_of(2, [128, 2, HYW], F32, "bsb")
            for kc in range(KC):
                tb = tabp.get()
                for cs_ in range(2):
                    k.dma("sp", tb[:, cs_, :, :], dft[cs_, kc, :, 0:NCn, :], writes=[tb])
                for n in range(NCn):
                    for cs_ in range(2):
                        for hi, Hsrc in ((0, Hf), (1, Hb)):
                            k.mm(pacc[cs_ * 2 + hi][:, :], tb[:, cs_, n, :], Hsrc[:, n, :], n == 0, n == NCn - 1, [tb, Hsrc], [pacc[cs_ * 2 + hi]])
                Fc, Bc, Fs, Bs = pacc[0], pacc[1], pacc[2], pacc[3]
                b_ = bsb.get()
                k.act(b_[:, 0, :], Bc[:, :], AF.Identity, [Bc, wk], [b_], scale=wk[:, kc, 0:1])
                k.act(b_[:, 1, :], Bs[:, :], AF.Identity, [Bs, wk], [b_], scale=wk[:, kc, 0:1])
                k.stt(KK[:, kc, 0, :], Fc[:, :], wk[:, kc, 0:1], b_[:, 0, :], ALU.mult, ALU.add, [Fc, wk, b_], [KKtok[kc]])
                k.stt(KK[:, kc, 1, :], Fs[:, :], wk[:, kc, 1:2], b_[:, 1, :], ALU.mult, ALU.add, [Fs, wk, b_], [KKtok[kc]])
        with k.phase():
            z = k.sb([128, NCn, HYW], BF16, "z")
            k.dma("sp", z[:, :, :], self.x0z[tokbase:tokbase + Ls, 1, :].rearrange("(n p) c -> p n c", p=128), writes=[z])
            tabp = k.pool_of(2, [128, 2, NCn, 128], BF16, "tabZ")
            pz = [k.pool_of(2, [128, 512], F32, "pZ%d" % i, psum=True) for i in range(2)]
            tm = [k.pool_of(2, [128, HYW], F32, "tmz%d" % i) for i in range(4)]
            for kc in range(KC):
                tb = tabp.get()
                for cs_ in range(2):
                    k.dma("sp", tb[:, cs_, :, :], dft[cs_, kc, :, 0:NCn, :], writes=[tb])
                Zc = pz[0].get(); Zs = pz[1].get()
                for n in range(NCn):
                    k.mm(Zc[:, :], tb[:, 0, n, :], z[:, n, :], n == 0, n == NCn - 1, [tb, z], [Zc])
                    k.mm(Zs[:, :], tb[:, 1, n, :], z[:, n, :], n == 0, n == NCn - 1, [tb, z], [Zs])
                a1 = tm[0].get(); a2 = tm[1].get(); a3 = tm[2].get(); a4 = tm[3].get()
                kt = KKtok[kc]
                k.tt("dve", a1[:, :], Zc[:, :], KK[:, kc, 0, :], ALU.mult, [Zc, kt], [a1])
                k.tt("dve", a2[:, :], Zs[:, :], KK[:, kc, 1, :], ALU.mult, [Zs, kt], [a2])
                k.tt("dve", a3[:, :], Zs[:, :], KK[:, kc, 0, :], ALU.mult, [Zs, kt], [a3])
                k.tt("dve", a4[:, :], Zc[:, :], KK[:, kc, 1, :], ALU.mult, [Zc, kt], [a4])
                k.tt("pool", KK[:, kc, 0, :], a1[:, :], a2[:, :], ALU.add, [a1, a2], [kt])
                k.tt("pool", KK[:, kc, 1, :], a3[:, :], a4[:, :], ALU.subtract, [a3, a4], [kt])
        with k.phase():
            tabp = k.pool_of(2, [128, 2, KC, 128], BF16, "tabI")
            py = k.pool_of(2, [128, 512], F32, "pY", psum=True)
            db = k.sb([128, HYW], F32, "dbias")
            k.dma("sp", db[:, :], bcast_rows(inp["hy_bias"][e:e + 1, :]), writes=[db])
            e1 = k.pool_of(2, [128, HYW], F32, "e1")
            bo = k.pool_of(2, [128, HYW], BF16, "bo")
            xzp = k.pool_of(2, [128, 2, HYW], BF16, "xzi")
            for tc_ in range(NCn):
                tb = tabp.get()
                for cs_ in range(2):
                    k.dma("sp", tb[:, cs_, :, :], dft[cs_, tc_, :, :, :], writes=[tb])
                xz_ = xzp.get()
                k.dma("sp", xz_[:, :, :], self.x0z[tokbase + tc_ * 128:tokbase + (tc_ + 1) * 128, :, :], writes=[xz_])
                y = py.get()
                for kc in range(KC):
                    k.mm(y[:, :], tb[:, 0, kc, :], KK[:, kc, 0, :], kc == 0, False, [tb, KKtok[kc]], [y])
                    k.mm(y[:, :], tb[:, 1, kc, :], KK[:, kc, 1, :], False, kc == KC - 1, [tb, KKtok[kc]], [y])
                t_ = e1.get()
                k.tt("pool", t_[:, :], xz_[:, 1, :], db[:, :], ALU.mult, [xz_, db], [t_])
                k.tt("dve", t_[:, :], y[:, :], t_[:, :], ALU.add, [y, t_], [t_])
                o_ = bo.get()
                k.tt("dve", o_[:, :], t_[:, :], xz_[:, 0, :], ALU.mult, [t_, xz_], [o_])
                k.dma("sp", self.cat[tokbase + tc_ * 128:tokbase + (tc_ + 1) * 128, NAW:D], o_[:, :], reads=[o_])


Prog.even_setup = _even_setup
Prog.phase_even = _phase_even
Prog.hyena = _hyena


def _na_attention(self, l, e, last):
    k = self.k
    inp = self.inp
    L = self.cfg.L
    NT = L + CTX
    rows = self.cfg.rows
    nP = rows // 2
    NB = L // 128
    with k.phase():
        Ve = k.sb([128, NB + 2, NAH, NAD + 1], BF16, "Ve")
        Vo = k.sb([128, NB - 1, NAH, NAD + 1], BF16, "Vo")
        k.memset("pool", Ve[:, :, :, NAD:NAD + 1], 1.0, [Ve])
        k.memset("pool", Vo[:, :, :, NAD:NAD + 1], 1.0, [Vo])
        for b in range(NB + 2):
            k.dma("sp", Ve[:, b, :, 0:NAD], self.vn[b * 128:(b + 1) * 128, :].rearrange("p (h d) -> p h d", h=NAH), writes=[Ve])
        for b in range(NB - 1):
            k.dma("sp", Vo[:, b, :, 0:NAD], self.vn[64 + b * 128:64 + (b + 1) * 128, :].rearrange("p (h d) -> p h d", h=NAH), writes=[Vo])
        A = k.sb([128, NB + 2, NAW], BF16, "Aall")
        qpool = k.pool_of(2, [128, NT], BF16, "qTn")
        kpool = k.pool_of(2, [128, NT], BF16, "kTn")
        tst = k.pool_of(2, [128, 2, 16, 64], F32, "tst")
        ttp = k.pool_of(2, [128, 2, 16 * 64], BF16, "TT")
        ps_pool = k.pool_of(2, [128, 1024], F32, "psS", psum=True)
        pv_pool = k.pool_of(2, [128, NAD + 1], F32, "psV", psum=True)
        pt_pool = k.pool_of(3, [128, 7 * 128], BF16, "PT")
        rec = k.pool_of(4, [128, 1], F32, "rec")
        for h in range(NAH):
            hp, pb = h // 2, (h % 2) * 64
            if h % 2 == 0:
                qT = qpool.get(); kT = kpool.get()
                k.dma("sp", qT[:, :], self.qtn[hp, :, :], writes=[qT])
                k.dma("sp", kT[:, :], self.ktn[hp, :, :], writes=[kT])
            ts_ = tst.get()
            k.dma("sp", ts_[:, :, :, :], inp["na_tab"][e, h, :, :, :, :], writes=[ts_])
            TT = ttp.get()
            k.act(TT[:, :, :], ts_[:, :, :, :].rearrange("p v j c -> p v (j c)"), AF.Exp, [ts_], [TT])
            units = []
            for i in range(nP):
                r0 = 2 * i
                if i < 2:
                    al, var = [0, 2, 4, 6], 0
                elif i >= nP - 2:
                    al, var = [rows - 8, rows - 6, rows - 4, rows - 2], 0
                else:
                    al, var = [r0 - 4, r0 - 2, r0, r0 + 2, r0 + 4], 1
                units.append((r0 * 64, al, var, i))
            if not last:
                units.append((L, [], 0, NB))
                units.append((L + 128, [], 0, NB + 1))
            for (q0, al, var, ablk) in units:
                M = len(al)
                nb = M + 2
                ps = ps_pool.get()
                for b in range(nb):
                    if b < M:
                        a_ = al[M - 1 - b]
                        ks = a_ * 64
                    else:
                        ks = L + (b - M) * 128
                    k.mm(ps[:, b * 128:(b + 1) * 128], kT[pb:pb + 64, ks:ks + 128], qT[pb:pb + 64, q0:q0 + 128], True, True,
                         [kT, qT], [ps], inc=(b == nb - 1))
                PT = pt_pool.get()
                for b0 in range(0, nb, 4):
                    b1 = min(nb, b0 + 4)
                    k.act(PT[:, b0 * 128:b1 * 128], ps[:, b0 * 128:b1 * 128], AF.Exp, [ps], [PT], scale=NAD ** -0.5)
                if M > 0:
                    r0 = q0 // 64
                    base = 7 - (al[0] - r0) - 2 * (M - 1)
                    k.tt("dve", PT[:, 0:M * 128], PT[:, 0:M * 128], TT[:, var, base * 64:(base + 2 * M) * 64], ALU.mult, [PT, TT], [PT])
                pv = pv_pool.get()
                for b in range(nb):
                    if b < M:
                        a_ = al[M - 1 - b]
                        vt = Ve[:, a_ // 2, h, :] if a_ % 2 == 0 else Vo[:, (a_ - 1) // 2, h, :]
                        vtok = Ve if a_ % 2 == 0 else Vo
                    else:
                        vt = Ve[:, NB + (b - M), h, :]
                        vtok = Ve
                    k.mm(pv[:, :], PT[:, b * 128:(b + 1) * 128], vt, b == 0, b == nb - 1, [PT, vtok], [pv])
                rc = rec.get()
                k.op("dve", lambda g, rc=rc, pv=pv: g.reciprocal(rc[:, :], pv[:, NAD:NAD + 1]), [pv], [rc])
                k.act(A[:, ablk, h * NAD:(h + 1) * NAD], pv[:, 0:NAD], AF.Identity, [pv, rc], [A], scale=rc[:, 0:1])
        nblk = NB + (0 if last else 2)
        for b in range(nblk):
            k.dma("sp", self.cat[b * 128:(b + 1) * 128, 0:NAW], A[:, b, :], reads=[A])


Prog.na_attention = _na_attention


_WEIGHTS = ["w_mod", "b_mod", "norm_gain", "ffn_a_in", "ffn_a_out", "ffn_b_in", "ffn_b_out", "even_in", "even_out",
            "na_q_gain", "na_k_gain", "hy_conv_w", "hy_conv_b", "hy_fw1", "hy_fb1", "hy_fw2", "hy_fb2", "hy_fw3", "hy_fb3",
            "hy_fw4", "hy_freq", "hy_bias", "ret_in", "ret_out", "ret_logit_f", "ret_logit_b"]


def kernel(**inputs):
    rows = 64
    B = inputs["x"].shape[0]
    P = Prog(Cfg(rows=rows, depth=4))
    nc = P.build()
    shared = {n: np.ascontiguousarray(np.asarray(inputs[n], dtype=np.float32)) for n in _WEIGHTS}
    shared.update(host_consts(rows))
    shared["na_tab"] = na_tab_host(np.asarray(inputs["na_rpb"], dtype=np.float32), rows)
    shared["c_ctx"] = np.ascontiguousarray(np.asarray(inputs["c_ctx"], dtype=np.float32)[None, :])
    shared = {n: v for n, v in shared.items() if n in P.inp}
    in_maps = []
    for b in range(B):
        m = dict(shared)
        m["x"] = np.ascontiguousarray(inputs["x"][b], dtype=np.float32)
        m["c"] = np.ascontiguousarray(inputs["c"][b:b + 1], dtype=np.float32)
        m["ctx"] = np.ascontiguousarray(inputs["ctx"][b], dtype=np.float32)
        in_maps.append(m)
    res = run_bass_kernel_spmd(nc, in_maps, core_ids=list(range(B)))
    return np.stack([np.asarray(r["out"], dtype=np.float32) for r in res.results], axis=0)
```

```python
import contextlib
import math
import numpy as np
import ml_dtypes
import concourse.bass as bass
import concourse.mybir as mybir
from concourse.bass_utils import run_bass_kernel_spmd

F32 = mybir.dt.float32
BF16 = mybir.dt.bfloat16
AF = mybir.ActivationFunctionType
ALU = mybir.AluOpType
AX = mybir.AxisListType

D = 1024
DFF = 2816
NMOD = 9
RMS_EPS = 1e-6
GN_EPS = 1e-6
GRID_W = 64
CTX = 256
SAME_ENGINE_SYNC = True


class Tok:
    __slots__ = ("w", "r", "name", "lane", "wb")

    def __init__(self, name=""):
        self.w = None
        self.r = {}
        self.name = name
        self.lane = None
        self.wb = False


class Lane:
    __slots__ = ("sem", "count", "sw")

    def __init__(self, sem):
        self.sem = sem
        self.count = 0
        self.sw = False


class T:
    def __init__(self, h, name):
        self.h = h
        self.tok = Tok(name)

    def __getitem__(self, idx):
        return self.h[idx]


class KB:
    def __init__(self):
        self.nc = bass.Bass("TRN2", target_bir_lowering=False)
        nc = self.nc
        self.es = contextlib.ExitStack()
        self.eng = dict(pe=nc.tensor, act=nc.scalar, dve=nc.vector, pool=nc.gpsimd, sp=nc.sync)
        self.sem = {e: self.es.enter_context(nc.semaphore("s_" + e)) for e in self.eng}
        self.cnt = {e: 0 for e in self.eng}
        self.seen = {e: {} for e in self.eng}
        self.free_lanes = []
        self.free_lanes_sw = []
        self.all_lanes = []
        self.nlanes = 0
        self.phase_stack = []
        self.ninst = 0
        self.uid = 0

    def _name(self, p):
        self.uid += 1
        return "%s_%d" % (p, self.uid)

    def sb(self, shape, dt, name="t"):
        st = self.phase_stack[-1][0] if self.phase_stack else self.es
        h = st.enter_context(self.nc.sbuf_tensor(self._name(name), list(shape), dt))
        t = T(h, name)
        if self.phase_stack:
            self.phase_stack[-1][1].append(t)
        return t

    def ps(self, shape, dt, name="p"):
        st = self.phase_stack[-1][0] if self.phase_stack else self.es
        h = st.enter_context(self.nc.psum_tensor(self._name(name), list(shape), dt))
        t = T(h, name)
        if self.phase_stack:
            self.phase_stack[-1][1].append(t)
        return t

    def pool_of(self, n, shape, dt, name, psum=False):
        return Ring([(self.ps if psum else self.sb)(shape, dt, name) for _ in range(n)])

    @contextlib.contextmanager
    def phase(self):
        st = contextlib.ExitStack()
        toks = []
        self.phase_stack.append((st, toks))
        try:
            yield
        finally:
            self.barrier()
            self.phase_stack.pop()
            for t in toks:
                if t.tok.lane is not None:
                    (self.free_lanes_sw if t.tok.lane.sw else self.free_lanes).append(t.tok.lane)
                    t.tok.lane = None
            st.close()

    def lane_of(self, tok, sw=False):
        if tok.lane is None:
            fl = self.free_lanes_sw if sw else self.free_lanes
            if fl:
                tok.lane = fl.pop()
            else:
                sem = self.es.enter_context(self.nc.semaphore("l_%d" % self.nlanes))
                self.nlanes += 1
                tok.lane = Lane(sem)
                tok.lane.sw = sw
                self.all_lanes.append(tok.lane)
        assert tok.lane.sw == sw, "token %s mixes SW and HW DMA queues" % tok.name
        return tok.lane

    def _waits(self, e, reads, writes, bulk=False, is_dma=False):
        deps = {}
        own = None if is_dma else self.sem.get(e)

        def need(p):
            if p is None:
                return
            s, v = p
            k = id(s)
            if k not in deps or deps[k][1] < v:
                deps[k] = (s, v)

        for t in reads:
            if t.w is not None and t.w[0] is own:
                if e == "pe" or not SAME_ENGINE_SYNC or (bulk and t.wb):
                    continue
            need(t.w)
        for t in writes:
            if not (t.w is not None and t.w[0] is own):
                need(t.w)
            for p in t.r.values():
                if p[0] is own:
                    continue
                need(p)
        for k, (s, v) in deps.items():
            if self.seen[e].get(k, 0) >= v:
                continue
            self.eng[e].wait_ge(s, v)
            self.seen[e][k] = v
            self.ninst += 1

    @staticmethod
    def _toks(xs):
        out = []
        for x in xs:
            if x is None:
                continue
            out.append(x.tok if isinstance(x, T) else x)
        return out

    def op(self, e, emit, reads=(), writes=(), inc=True, bulk=False):
        reads = self._toks(reads)
        writes = self._toks(writes)
        self._waits(e, reads, writes, bulk=bulk)
        ins = emit(self.eng[e])
        self.ninst += 1
        if inc:
            self.cnt[e] += 1
            ins.then_inc(self.sem[e], 1)
            me = (self.sem[e], self.cnt[e])
        else:
            me = (self.sem[e], self.cnt[e] + 1)
        for t in reads:
            t.r[e] = me
        for t in writes:
            t.w = me
            t.r = {}
            t.wb = bulk
        return ins

    def dma(self, q, out, in_, reads=(), writes=(), lane_tok=None, **kw):
        reads = self._toks(reads)
        writes = self._toks(writes)
        lt = lane_tok.tok if isinstance(lane_tok, T) else lane_tok
        if lt is None:
            lt = writes[0] if writes else reads[0]
        lane = self.lane_of(lt, sw=(q == "pool"))
        self._waits(q, reads, writes, is_dma=True)
        if q == "pool" and lane.count > 0 and self.seen[q].get(id(lane.sem), 0) < lane.count:
            self.eng[q].wait_ge(lane.sem, lane.count)
            self.seen[q][id(lane.sem)] = lane.count
        ins = self.eng[q].dma_start(out=out, in_=in_, **kw)
        self.ninst += 1
        lane.count += 16
        ins.then_inc(lane.sem, 16)
        me = (lane.sem, lane.count)
        for t in reads:
            t.r["dma%d" % id(lane)] = me
        for t in writes:
            t.w = me
            t.r = {}
            t.wb = False
        return ins

    def barrier(self):
        for e in self.eng:
            for f in self.eng:
                if f == e or self.cnt[f] == 0:
                    continue
                if self.seen[e].get(id(self.sem[f]), 0) >= self.cnt[f]:
                    continue
                self.eng[e].wait_ge(self.sem[f], self.cnt[f])
                self.seen[e][id(self.sem[f])] = self.cnt[f]
            for ln in self.all_lanes:
                if ln.count == 0 or self.seen[e].get(id(ln.sem), 0) >= ln.count:
                    continue
                self.eng[e].wait_ge(ln.sem, ln.count)
                self.seen[e][id(ln.sem)] = ln.count

    def mm(self, out, lhsT, rhs, start, stop, reads, writes, inc=None):
        if inc is None:
            inc = stop
        return self.op("pe", lambda g: g.matmul(out, lhsT, rhs, start=start, stop=stop), reads, writes, inc=inc)

    def tr(self, out, in_, ident, reads, writes, inc=True):
        return self.op("pe", lambda g: g.transpose(out, in_, ident), reads, writes, inc=inc)

    @staticmethod
    def _bulk(out):
        try:
            return out.free_size() >= 256
        except Exception:
            return False

    def act(self, out, in_, func, reads, writes, bias=None, scale=None, accum_out=None, e="act"):
        kw = {}
        if bias is not None:
            kw["bias"] = bias
        if scale is not None:
            kw["scale"] = scale
        if accum_out is not None:
            kw["accum_out"] = accum_out
        return self.op(e, lambda g: g.activation(out, in_, func, **kw), reads, writes,
                       bulk=(accum_out is None and self._bulk(out)))

    def ts(self, e, out, in0, s1, s2, op0, op1, reads, writes, accum_out=None):
        kw = {}
        if op1 is not None:
            kw["op1"] = op1
        if accum_out is not None:
            kw["accum_out"] = accum_out
        return self.op(e, lambda g: g.tensor_scalar(out, in0, s1, s2, op0, **kw), reads, writes,
                       bulk=(accum_out is None and self._bulk(out)))

    def tt(self, e, out, in0, in1, op, reads, writes):
        return self.op(e, lambda g: g.tensor_tensor(out, in0, in1, op), reads, writes, bulk=self._bulk(out))

    def stt(self, out, in0, scalar, in1, op0, op1, reads, writes, e="dve"):
        return self.op(e, lambda g: g.scalar_tensor_tensor(out, in0, scalar, in1, op0, op1), reads, writes, bulk=self._bulk(out))

    def copy(self, e, out, in_, reads, writes):
        if e == "act":
            return self.op(e, lambda g: g.copy(out, in_), reads, writes, bulk=self._bulk(out))
        return self.op(e, lambda g: g.tensor_copy(out, in_), reads, writes, bulk=self._bulk(out))

    def memset(self, e, ap, val, writes):
        return self.op(e, lambda g: g.memset(ap, val), (), writes, bulk=self._bulk(ap))


class Ring:
    def __init__(self, items):
        self.items = items
        self.i = 0

    def get(self):
        t = self.items[self.i % len(self.items)]
        self.i += 1
        return t


class Cfg:
    def __init__(self, rows=64, depth=4, types=None):
        self.rows = rows
        self.L = rows * GRID_W
        self.depth = depth
        self.types = types if types is not None else [("even" if i % 2 == 0 else "ret") for i in range(depth)]
        self.layers = list(range(depth))


def bcast_rows(ap, n=128):
    return ap.partition_broadcast(n)


class Prog:
    def __init__(self, cfg, stages=None):
        self.cfg = cfg
        self.k = KB()
        self.nc = self.k.nc
        self.stages = stages
        nc = self.nc
        L = cfg.L
        nl = cfg.depth
        ne = (nl + 1) // 2
        no = nl // 2
        self.inp = {}

        def din(name, shape, dt=F32):
            self.inp[name] = nc.dram_tensor(name, list(shape), dt, kind="ExternalInput").ap()
            return self.inp[name]

        din("x", [L, D]); din("c", [1, D]); din("ctx", [CTX, D]); din("c_ctx", [1, D])
        din("w_mod", [nl, D, NMOD * D]); din("b_mod", [nl, NMOD * D]); din("norm_gain", [nl, 3, D])
        din("ffn_a_in", [nl, D, 2 * DFF]); din("ffn_a_out", [nl, DFF, D])
        din("ffn_b_in", [nl, D, 2 * DFF]); din("ffn_b_out", [nl, DFF, D])
        din("ident", [128, 128], BF16)
        self.out = nc.dram_tensor("out", [L, D], F32, kind="ExternalOutput").ap()
        self.xc = nc.dram_tensor("xc_scr", [CTX, D], F32, kind="ExternalOutput").ap()
        self.mod = nc.dram_tensor("mod_scr", [nl, 2, NMOD * D], F32).ap()
        if "ret" in cfg.types:
            self.ret_setup()
        if "even" in cfg.types:
            self.even_setup()

    def inp_add(self, name, shape, dt=F32):
        self.inp[name] = self.nc.dram_tensor(name, list(shape), dt, kind="ExternalInput").ap()
        return self.inp[name]

    def tile_tok0(self, xap):
        off = xap.offset // D
        return off if xap.tensor.name == self.out.tensor.name else self.cfg.L + off

    def consts(self):
        k = self.k
        self.ident = k.sb([128, 128], BF16, "ident")
        k.dma("sp", self.ident[:, :], self.inp["ident"], writes=[self.ident])
        self.epsb = k.sb([128, 1], F32, "eps")
        k.memset("dve", self.epsb[:, :], RMS_EPS, [self.epsb])

    def init_copy(self):
        k = self.k
        L = self.cfg.L
        self.t_xlat = Tok("xlat")
        self.t_xctx = Tok("xctx")
        nchunk = max(1, L // 1024)
        rows = L // nchunk
        for i in range(nchunk):
            k.dma("sp", self.out[i * rows:(i + 1) * rows, :], self.inp["x"][i * rows:(i + 1) * rows, :],
                  writes=[self.t_xlat])
        k.dma("sp", self.xc[:, :], self.inp["ctx"][:, :], writes=[self.t_xctx])
        k.barrier()

    def phase_mod(self, l):
        k = self.k
        with k.phase():
            cT = k.sb([128, 8, 2], F32, "cT")
            with self.nc.allow_non_contiguous_dma("tiny"):
                k.dma("sp", cT[:, :, 0], self.inp["c"].rearrange("o (c p) -> p (o c)", p=128), writes=[cT])
                k.dma("sp", cT[:, :, 1], self.inp["c_ctx"].rearrange("o (c p) -> p (o c)", p=128), writes=[cT])
            sc = k.sb([128, 8, 2], F32, "sc")
            k.act(sc[:, :, :], cT[:, :, :], AF.Silu, [cT], [sc])
            ones2 = k.sb([1, 2], F32, "ones2")
            k.memset("dve", ones2[:, :], 1.0, [ones2])
            brow = k.sb([1, NMOD * D], F32, "brow")
            k.dma("sp", brow[:, :], self.inp["b_mod"][l:l + 1, :], writes=[brow])
            wpool = k.pool_of(3, [128, 8, 512], F32, "wm")
            ppool = k.pool_of(2, [2, 512], F32, "pm", psum=True)
            spool = k.pool_of(2, [2, 512], F32, "sm")
            for n in range(NMOD * D // 512):
                w = wpool.get()
                k.dma("sp", w[:, :, :], self.inp["w_mod"][l, :, n * 512:(n + 1) * 512].rearrange("(c p) n -> p c n", p=128),
                      writes=[w])
                p = ppool.get()
                for c in range(8):
                    k.mm(p[:, :], sc[:, c, :], w[:, c, :], c == 0, False, [sc, w], [p])
                k.mm(p[:, :], ones2[:, :], brow[:, n * 512:(n + 1) * 512], False, True, [ones2, brow], [p])
                s = spool.get()
                k.copy("dve", s[:, :], p[:, :], [p], [s])
                k.dma("sp", self.mod[l, :, n * 512:(n + 1) * 512], s[:, :], reads=[s])

    def load_mod_vecs(self, l, nj, mv, half_gate):
        k = self.k
        G = k.sb([128, 8, 2], F32, "G")
        S = k.sb([128, 8, 2], F32, "S")
        gn = k.sb([128, 8], F32, "gn")
        gate = k.sb([128, 2, D], F32, "gate")
        with self.nc.allow_non_contiguous_dma("tiny"):
            k.dma("sp", gn[:, :], self.inp["norm_gain"][l, nj:nj + 1, :].rearrange("o (c p) -> p (o c)", p=128), writes=[gn])
            for r in range(2):
                k.dma("sp", S[:, :, r], self.mod[l, r:r + 1, mv * D:(mv + 1) * D].rearrange("o (c p) -> p (o c)", p=128), writes=[S])
                k.dma("sp", G[:, :, r], self.mod[l, r:r + 1, (mv + 1) * D:(mv + 2) * D].rearrange("o (c p) -> p (o c)", p=128), writes=[G])
                k.dma("sp", gate[:, r, :], bcast_rows(self.mod[l, r:r + 1, (mv + 2) * D:(mv + 3) * D]), writes=[gate])
        for r in range(2):
            k.stt(G[:, :, r], G[:, :, r], 1.0, gn[:, :], ALU.add, ALU.mult, [G, gn], [G])
        if half_gate:
            k.ts("dve", gate[:, :, :], gate[:, :, :], 0.5, None, ALU.mult, None, [gate], [gate])
        return G, S, gate

    def tiles(self, tsz):
        out = []
        for i in range(self.cfg.L // tsz):
            out.append((self.out[i * tsz:(i + 1) * tsz, :], tsz, 0))
        for i in range(max(1, CTX // tsz)):
            n = min(tsz, CTX)
            out.append((self.xc[i * n:(i + 1) * n, :], n, 1))
        return out

    def norm_part(self, xt, ns, xs_pool, scr, ssq, rstd):
        k = self.k
        for s in range(ns):
            k.act(scr[:, :], xt[:, s, :], AF.Square, [xt], [scr, ssq], accum_out=ssq[:, s:s + 1])
        k.act(rstd[:, :ns], ssq[:, :ns], AF.Sqrt, [ssq, self.epsb], [rstd], bias=self.epsb[:, 0:1], scale=1.0 / D)
        k.op("dve", lambda g: g.reciprocal(rstd[:, :ns], rstd[:, :ns]), [rstd], [rstd])
        xs = xs_pool.get()
        for s in range(ns):
            k.ts("dve", xs[:, s, :], xt[:, s, :], rstd[:, s:s + 1], None, ALU.mult, None, [xt, rstd], [xs])
        return xs

    def transp_part(self, xs, ns, row, G, S, tp_pool, xnT):
        k = self.k
        for c in range(8):
            tp = tp_pool.get()
            for s in range(ns):
                k.tr(tp[:, s * 128:(s + 1) * 128], xs[:, s, c * 128:(c + 1) * 128], self.ident[:, :],
                     [xs, self.ident], [tp], inc=(s == ns - 1))
            k.act(xnT[:, c, :ns * 128], tp[:, :ns * 128], AF.Identity, [tp, G, S], [xnT],
                  bias=S[:, c, row:row + 1], scale=G[:, c, row:row + 1])

    def norm_mod_T(self, xt, ns, row, G, S, xs_pool, tp_pool, xnT, scr, ssq, rstd):
        xs = self.norm_part(xt, ns, xs_pool, scr, ssq, rstd)
        self.transp_part(xs, ns, row, G, S, tp_pool, xnT)

    def phase_ffn(self, l, which):
        k = self.k
        TS = 256
        NS = TS // 128
        win_d = self.inp["ffn_a_in" if which == 0 else "ffn_b_in"]
        wout_d = self.inp["ffn_a_out" if which == 0 else "ffn_b_out"]
        nj, mv = (0, 0) if which == 0 else (2, 6)
        NF = DFF // 128
        with k.phase():
            Win = [k.sb([128, 2 * DFF], BF16, "Win%d" % c) for c in range(8)]
            Wout = k.sb([128, NF, D], BF16, "Wout")
            for c in range(8):
                k.dma("pool", Win[c][:, :], win_d[l, c * 128:(c + 1) * 128, :], writes=[Win[c]])
            k.dma("pool", Wout[:, :, :], wout_d[l, :, :].rearrange("(j p) d -> p j d", p=128), writes=[Wout])
            G, S, gate = self.load_mod_vecs(l, nj, mv, True)
            xpool = k.pool_of(2, [128, NS, D], F32, "xt")
            xs_pool = k.pool_of(1, [128, NS, D], BF16, "xs")
            tp_pool = k.pool_of(2, [128, TS], BF16, "tp", psum=True)
            xnT = k.sb([128, 8, TS], BF16, "xnT")
            hT = k.sb([128, NF, TS], BF16, "hT")
            scr = k.sb([128, D], F32, "scr")
            ssq = k.sb([128, NS], F32, "ssq")
            rstd = k.sb([128, NS], F32, "rstd")
            pa_pool = k.pool_of(2, [128, TS], F32, "pa", psum=True)
            pb_pool = k.pool_of(2, [128, TS], F32, "pb", psum=True)
            sa_pool = k.pool_of(2, [128, TS], BF16, "sa")
            po_pool = k.pool_of(2, [128, 512], F32, "po", psum=True)
            sqj = k.sb([128, D], F32, "sqj")
            tl = self.tiles(TS)

            def load(i):
                xap, ntok, row = tl[i]
                xt = xpool.get()
                k.dma("sp", xt[:, :ntok // 128, :], xap.rearrange("(s p) d -> p s d", p=128), writes=[xt])
                return xt

            xts = {0: load(0)}
            xss = {0: self.norm_part(xts[0], tl[0][1] // 128, xs_pool, sqj, ssq, rstd)}
            self.transp_part(xss[0], tl[0][1] // 128, tl[0][2], G, S, tp_pool, xnT)
            if len(tl) > 1:
                xts[1] = load(1)
            for i, (xap, ntok, row) in enumerate(tl):
                ns = ntok // 128
                xt = xts.pop(i)
                for j in range(NF):
                    pa = pa_pool.get()
                    pb = pb_pool.get()
                    for c in range(8):
                        k.mm(pa[:, :ntok], Win[c][:, j * 128:(j + 1) * 128], xnT[:, c, :ntok], c == 0, c == 7,
                             [Win[c], xnT], [pa])
                    for c in range(8):
                        k.mm(pb[:, :ntok], Win[c][:, DFF + j * 128:DFF + (j + 1) * 128], xnT[:, c, :ntok], c == 0, c == 7,
                             [Win[c], xnT], [pb])
                    sa = sa_pool.get()
                    k.act(sa[:, :ntok], pa[:, :ntok], AF.Silu, [pa], [sa])
                    k.tt("dve", hT[:, j, :ntok], sa[:, :ntok], pb[:, :ntok], ALU.mult, [sa, pb], [hT])
                    if j == NF // 2 and i + 1 < len(tl):
                        xss[i + 1] = self.norm_part(xts[i + 1], tl[i + 1][1] // 128, xs_pool, sqj, ssq, rstd)
                if i + 1 < len(tl):
                    self.transp_part(xss.pop(i + 1), tl[i + 1][1] // 128, tl[i + 1][2], G, S, tp_pool, xnT)
                for s in range(ns):
                    for hf in range(2):
                        po = po_pool.get()
                        for j in range(NF):
                            k.mm(po[:, :], hT[:, j, s * 128:(s + 1) * 128], Wout[:, j, hf * 512:(hf + 1) * 512],
                                 j == 0, j == NF - 1, [hT, Wout], [po])
                        sl = slice(hf * 512, (hf + 1) * 512)
                        k.tt("dve", scr[:, sl], po[:, :], gate[:, row, sl], ALU.mult, [po, gate], [scr])
                        k.tt("pool", xt[:, s, sl], xt[:, s, sl], scr[:, sl], ALU.add, [xt, scr], [xt])
                k.dma("sp", xap.rearrange("(s p) d -> p s d", p=128), xt[:, :ns, :], reads=[xt])
                if i + 2 < len(tl):
                    xts[i + 2] = load(i + 2)

    def build(self):
        k = self.k
        self.consts()
        self.init_copy()
        for l in self.cfg.layers:
            self.phase_mod(l)
            self.phase_ffn(l, 0)
            if self.stages == "ffn_a":
                continue
            if self.stages != "ffn_only":
                if self.cfg.types[l] == "ret":
                    self.phase_ret(l)
                else:
                    self.phase_even(l)
            if self.stages == "mix":
                continue
            if self.stages == "mix_only" and False:
                continue
            self.phase_ffn(l, 1)
        k.barrier()
        return self.nc


RH = 4
RDK = 256
RDV = 512
RQK = RH * RDK
RV = RH * RDV
RIN = 2 * RQK + 2 * RV


def _ret_setup(self):
    nc = self.nc
    L = self.cfg.L
    NT = L + CTX
    NCH = NT // 128
    self.ret_in = self.inp_add("ret_in", [max(1, self.cfg.types.count("ret")), D, RIN])
    self.ret_out = self.inp_add("ret_out", [max(1, self.cfg.types.count("ret")), RV, D])
    self.ret_lf = self.inp_add("ret_logit_f", [max(1, self.cfg.types.count("ret")), RH])
    self.ret_lb = self.inp_add("ret_logit_b", [max(1, self.cfg.types.count("ret")), RH])
    self.rope = self.inp_add("rope_tab", [L, 2, 128])
    self.rconst = self.inp_add("ret_const", [128, 6, 128])
    self.qts = nc.dram_tensor("qts", [NCH, 128, 1024], BF16).ap()
    self.kts = nc.dram_tensor("kts", [NCH, 128, 1024], BF16).ap()
    self.ktok = nc.dram_tensor("ktok", [NT, RQK], BF16).ap()
    self.vtok = nc.dram_tensor("vtok", [NT, RV], BF16).ap()
    self.sgt = nc.dram_tensor("sgt", [NT, RV], BF16).ap()
    self.st = nc.dram_tensor("st", [2, NCH, RH, 128, 1024], BF16).ap()


def ret_consts_host():
    j = np.arange(128, dtype=np.float32)
    c = np.zeros((128, 6, 128), np.float32)
    diff = j[None, :] - j[:, None]
    c[:, 0, :] = np.maximum(diff, 0.0)
    c[:, 1, :] = np.maximum(-diff, 0.0)
    c[:, 2, :] = (diff >= 0).astype(np.float32) / 16.0
    c[:, 3, :] = (diff <= 0).astype(np.float32) / 16.0
    c[:, 4, :] = (j[None, :] + 1.0)
    c[:, 5, :] = (128.0 - j[None, :])
    return c


def _phase_ret(self, l):
    k = self.k
    nc = self.nc
    o = self.cfg.types[:l].count("ret")
    L = self.cfg.L
    NT = L + CTX
    NCH = NT // 128
    NCL = L // 128
    last = (l == self.cfg.depth - 1) and not getattr(self, "force_ctx_out", False)

    with k.phase():
        Wr = [k.sb([128, RIN], BF16, "Wr%d" % c) for c in range(8)]
        for c in range(8):
            k.dma("pool", Wr[c][:, :], self.ret_in[o, c * 128:(c + 1) * 128, :], writes=[Wr[c]])
        G, S, gate = self.load_mod_vecs(l, 1, 3, False)
        TS = 256
        xpool = k.pool_of(2, [128, 2, D], F32, "xt")
        xs_pool = k.pool_of(1, [128, 2, D], BF16, "xs")
        tp_pool = k.pool_of(2, [128, TS], BF16, "tp", psum=True)
        xnT = k.sb([128, 8, TS], BF16, "xnT")
        scr = k.sb([128, D], F32, "scr")
        ssq = k.sb([128, 2], F32, "ssq")
        rstd = k.sb([128, 2], F32, "rstd")
        pp = k.pool_of(3, [128, 512], F32, "pp", psum=True)
        tq_pool = k.pool_of(2, [128, 1024], BF16, "tq", psum=True)
        rope_pool = k.pool_of(2, [128, 2, 128], F32, "rope")
        qtok_pool = k.pool_of(2, [128, RQK], BF16, "qtok")
        ktok_pool = k.pool_of(2, [128, RQK], BF16, "ktok")
        v_pool = k.pool_of(2, [128, RV], BF16, "vst")
        g_pool = k.pool_of(2, [128, RV], BF16, "gst")
        qT_pool = k.pool_of(2, [128, 1024], BF16, "qTs")
        kT_pool = k.pool_of(2, [128, 1024], BF16, "kTs")
        tmp = [k.sb([128, 256], F32, "rt%d" % i) for i in range(4)]
        for (xap, ntok, row) in self.tiles(TS):
            xt = xpool.get()
            k.dma("sp", xt[:, :, :], xap.rearrange("(s p) d -> p s d", p=128), writes=[xt])
            self.norm_mod_T(xt, 2, row, G, S, xs_pool, tp_pool, xnT, scr, ssq, rstd)
            for s in range(2):
                tok0 = (self.tile_tok0(xap) + s * 128)
                ch = tok0 // 128
                if row == 0:
                    rp = rope_pool.get()
                    k.dma("sp", rp[:, :, :], self.rope[tok0:tok0 + 128, :, :], writes=[rp])
                qtok = qtok_pool.get()
                ktok = ktok_pool.get()
                vst = v_pool.get()
                gst = g_pool.get()
                for n in range(12):
                    p = pp.get()
                    for c in range(8):
                        k.mm(p[:, :], xnT[:, c, s * 128:(s + 1) * 128], Wr[c][:, n * 512:(n + 1) * 512], c == 0, c == 7,
                             [xnT, Wr[c]], [p])
                    if n < 4:
                        dst = qtok if n < 2 else ktok
                        cs = (n % 2) * 512
                        if row == 0:
                            pv = p[:, :].rearrange("p (h g f d) -> p h g f d", h=2, g=2, f=2)
                            dv = dst[:, cs:cs + 512].rearrange("p (h g f d) -> p h g f d", h=2, g=2, f=2)
                            ct = rp[:, 0, :].rearrange("p (g d) -> p g d", g=2).unsqueeze(1).broadcast_to([128, 2, 2, 64])
                            sn = rp[:, 1, :].rearrange("p (g d) -> p g d", g=2).unsqueeze(1).broadcast_to([128, 2, 2, 64])
                            tv = [t[:, :].rearrange("p (h g d) -> p h g d", h=2, g=2) for t in tmp]
                            k.tt("dve", tv[0], pv[:, :, :, 0, :], ct, ALU.mult, [p, rp], [tmp[0]])
                            k.tt("dve", tv[1], pv[:, :, :, 1, :], sn, ALU.mult, [p, rp], [tmp[1]])
                            k.tt("dve", tv[2], pv[:, :, :, 0, :], sn, ALU.mult, [p, rp], [tmp[2]])
                            k.tt("dve", tv[3], pv[:, :, :, 1, :], ct, ALU.mult, [p, rp], [tmp[3]])
                            k.tt("pool", dv[:, :, :, 0, :], tv[0], tv[1], ALU.subtract, [tmp[0], tmp[1]], [dst])
                            k.tt("pool", dv[:, :, :, 1, :], tv[2], tv[3], ALU.add, [tmp[2], tmp[3]], [dst])
                        else:
                            k.copy("act", dst[:, cs:cs + 512], p[:, :], [p], [dst])
                    elif n < 8:
                        k.copy("act", vst[:, (n - 4) * 512:(n - 3) * 512], p[:, :], [p], [vst])
                    else:
                        k.act(gst[:, (n - 8) * 512:(n - 7) * 512], p[:, :], AF.Silu, [p], [gst])
                for (src, dpool, dscr) in ((qtok, qT_pool, self.qts), (ktok, kT_pool, self.kts)):
                    tq = tq_pool.get()
                    for b in range(8):
                        k.tr(tq[:, b * 128:(b + 1) * 128], src[:, b * 128:(b + 1) * 128], self.ident[:, :],
                             [src, self.ident], [tq], inc=(b == 7))
                    dT = dpool.get()
                    k.copy("dve" if src is qtok else "act", dT[:, :], tq[:, :], [tq], [dT])
                    k.dma("sp", dscr[ch, :, :], dT[:, :], reads=[dT])
                k.dma("sp", self.ktok[tok0:tok0 + 128, :], ktok[:, :], reads=[ktok])
                k.dma("sp", self.vtok[tok0:tok0 + 128, :], vst[:, :], reads=[vst])
                k.dma("sp", self.sgt[tok0:tok0 + 128, :], gst[:, :], reads=[gst])

    with k.phase():
        rc = k.sb([128, 6, 128], F32, "rconst")
        k.dma("sp", rc[:, :, :], self.rconst, writes=[rc])
        lg = k.sb([128, 2 * RH], F32, "lg")
        k.dma("sp", lg[:, 0:RH], bcast_rows(self.ret_lf[o:o + 1, :]), writes=[lg])
        k.dma("sp", lg[:, RH:2 * RH], bcast_rows(self.ret_lb[o:o + 1, :]), writes=[lg])
        k.act(lg[:, :], lg[:, :], AF.Exp, [lg], [lg], scale=-1.0)
        k.act(lg[:, :], lg[:, :], AF.Ln, [lg], [lg], bias=1.0, scale=1.0)
        k.ts("dve", lg[:, :], lg[:, :], -1.0, None, ALU.mult, None, [lg], [lg])
        zeta = k.sb([128, 2 * RH], F32, "zeta")
        gch = k.sb([128, 2 * RH], F32, "gch")
        XI = k.sb([128, 2 * RH, 128], BF16, "XI")
        DcT = k.sb([128, RH, 128], BF16, "DcT")
        dtmp = k.sb([128, 2, 128], F32, "dtmp")
        for e in range(RH):
            f, b = e, RH + e
            k.act(XI[:, f, :], rc[:, 4, :], AF.Exp, [rc, lg], [XI], scale=lg[:, f:f + 1])
            k.act(XI[:, b, :], rc[:, 5, :], AF.Exp, [rc, lg], [XI], scale=lg[:, b:b + 1])
            k.act(dtmp[:, 0, :], rc[:, 0, :], AF.Exp, [rc, lg], [dtmp], scale=lg[:, f:f + 1])
            k.act(dtmp[:, 1, :], rc[:, 1, :], AF.Exp, [rc, lg], [dtmp], scale=lg[:, b:b + 1])
            k.tt("dve", dtmp[:, :, :], dtmp[:, :, :], rc[:, 2:4, :], ALU.mult, [dtmp, rc], [dtmp])
            k.tt("dve", DcT[:, e, :], dtmp[:, 0, :], dtmp[:, 1, :], ALU.add, [dtmp], [DcT])
        for e in range(RH):
            k.act(zeta[:, e:e + 1], rc[:, 0, 127:128], AF.Exp, [rc, lg], [zeta], scale=lg[:, e:e + 1])
            k.act(zeta[:, RH + e:RH + e + 1], rc[:, 1, 0:1], AF.Exp, [rc, lg], [zeta], scale=lg[:, RH + e:RH + e + 1])
        k.ts("dve", zeta[:, :], zeta[:, :], 1.0 / 16.0, None, ALU.mult, None, [zeta], [zeta])
        k.act(gch[:, :], lg[:, :], AF.Exp, [lg], [gch], scale=128.0)

        with k.phase():
            Sst = [[k.sb([128, 2, RDV], F32, "S%d%d" % (d, e)) for e in range(RH)] for d in range(2)]
            Sbf = [[k.pool_of(2, [128, 2 * RDV], BF16, "Sb%d%d" % (d, e)) for e in range(RH)] for d in range(2)]
            for d in range(2):
                for e in range(RH):
                    k.memset("pool", Sst[d][e][:, :, :], 0.0, [Sst[d][e]])
            kin = k.pool_of(4, [128, RQK], BF16, "kin")
            vin = k.pool_of(4, [128, RV], BF16, "vin")
            kz_pool = k.pool_of(3, [128, RQK], BF16, "kz")
            pd = k.pool_of(6, [128, RDV], F32, "pd", psum=True)
            order_f = [NCL, NCL + 1] + list(range(NCL))
            order_b = [NCL + 1, NCL] + list(range(NCL - 1, -1, -1))
            for step in range(NCH):
                for d, order in ((0, order_f), (1, order_b)):
                    ch = order[step]
                    kt = kin.get()
                    vt = vin.get()
                    k.dma("sp", kt[:, :], self.ktok[ch * 128:(ch + 1) * 128, :], writes=[kt])
                    k.dma("sp", vt[:, :], self.vtok[ch * 128:(ch + 1) * 128, :], writes=[vt])
                    if step < NCH - 1:
                        kz = kz_pool.get()
                        k.tt("dve", kz[:, :].rearrange("p (e x) -> p e x", e=RH), kt[:, :].rearrange("p (e x) -> p e x", e=RH),
                             zeta[:, d * RH:(d + 1) * RH].unsqueeze(2).broadcast_to([128, RH, RDK]), ALU.mult, [kt, zeta], [kz])
                    for e in range(RH):
                        S_ = Sst[d][e]
                        sb_ = Sbf[d][e].get()
                        k.copy("act", sb_[:, :], S_[:, :, :].rearrange("p a b -> p (a b)"), [S_], [sb_])
                        k.dma("sp", self.st[d, ch, e, :, :], sb_[:, :], reads=[sb_])
                        if step == NCH - 1:
                            continue
                        for dc in range(2):
                            p = pd.get()
                            k.mm(p[:, :], kz[:, e * RDK + dc * 128:e * RDK + (dc + 1) * 128], vt[:, e * RDV:(e + 1) * RDV], True, True, [kz, vt], [p])
                            k.stt(S_[:, dc, :], S_[:, dc, :], gch[:, d * RH + e:d * RH + e + 1], p[:, :], ALU.mult, ALU.add,
                                  [S_, gch, p], [S_])

        with k.phase():
            Wo = k.sb([128, 16, D], BF16, "Wo")
            k.dma("pool", Wo[:, :, :], self.ret_out[o, :, :].rearrange("(j p) d -> p j d", p=128), writes=[Wo])
            G, S, gate = self.load_mod_vecs(l, 1, 3, False)
            qT_pool = k.pool_of(2, [128, 8, 128], BF16, "qTc")
            kT_pool = k.pool_of(2, [128, 8, 128], BF16, "kTc")
            v_pool = k.pool_of(2, [128, RV], BF16, "vc")
            g_pool = k.pool_of(2, [128, RV], BF16, "gc")
            st_pool = k.pool_of(2, [128, 2, RH, 2, RDV], BF16, "stc")
            x_pool = k.pool_of(2, [128, D], F32, "xc")
            ps_pool = k.pool_of(2, [128, RH, 128], F32, "psc", psum=True)
            po = [k.ps([128, RDV], F32, "poc%d" % e) for e in range(RH)]
            pt_pool = k.pool_of(1, [128, 1024], BF16, "ptc", psum=True)
            pr_pool = k.pool_of(1, [128, 512], F32, "prc", psum=True)
            in_pool = k.pool_of(2, [128, RH, 128], BF16, "inT")
            qs_pool = k.pool_of(2, [128, 2, 2 * RH, 128], BF16, "qs")
            Y = k.sb([128, RV], BF16, "Y")
            YT = k.sb([128, 16, 128], BF16, "YT")
            stats = k.sb([128, RH, 6], F32, "stats")
            mv = k.sb([128, RH, 2], F32, "mv")
            rs = k.sb([128, RH], F32, "rs")
            nb = k.sb([128, RH], F32, "nb")
            yn = k.sb([128, RV], F32, "yn")
            scr2 = k.sb([128, D], F32, "scr2")
            gneps = k.sb([128, 1], F32, "gneps")
            k.memset("dve", gneps[:, :], GN_EPS, [gneps])
            chunks = list(range(NCL)) + ([] if last else [NCL, NCL + 1])
            for ch in chunks:
                row = 0 if ch < NCL else 1
                xap = self.out[ch * 128:(ch + 1) * 128, :] if row == 0 else self.xc[(ch - NCL) * 128:(ch - NCL + 1) * 128, :]
                qT = qT_pool.get(); kT = kT_pool.get(); vt = v_pool.get(); gt = g_pool.get(); stt_ = st_pool.get(); xt = x_pool.get()
                k.dma("sp", qT[:, :, :], self.qts[ch, :, :].rearrange("p (b t) -> p b t", b=8), writes=[qT])
                k.dma("sp", kT[:, :, :], self.kts[ch, :, :].rearrange("p (b t) -> p b t", b=8), writes=[kT])
                k.dma("sp", vt[:, :], self.vtok[ch * 128:(ch + 1) * 128, :], writes=[vt])
                k.dma("sp", gt[:, :], self.sgt[ch * 128:(ch + 1) * 128, :], writes=[gt])
                for d in range(2):
                    k.dma("sp", stt_[:, d, :, :, :], self.st[d, ch, :, :, :].rearrange("e p (a b) -> p e a b", a=2), writes=[stt_])
                k.dma("sp", xt[:, :], xap, writes=[xt])
                ps = ps_pool.get()
                for e in range(RH):
                    for dc in range(2):
                        k.mm(ps[:, e, :], kT[:, 2 * e + dc, :], qT[:, 2 * e + dc, :], dc == 0, dc == 1, [kT, qT], [ps],
                             inc=(e == RH - 1 and dc == 1))
                inT = in_pool.get()
                k.tt("dve", inT[:, :, :], ps[:, :, :], DcT[:, :, :], ALU.mult, [ps, DcT], [inT])
                qs = qs_pool.get()
                for d in range(2):
                    k.tt("pool", qs[:, d, :, :].rearrange("p (e c) t -> p e c t", e=RH), qT[:, :, :].rearrange("p (e c) t -> p e c t", e=RH),
                         XI[:, d * RH:(d + 1) * RH, :].unsqueeze(2).broadcast_to([128, RH, 2, 128]), ALU.mult, [qT, XI], [qs])
                for e in range(RH):
                    k.mm(po[e][:, :], inT[:, e, :], vt[:, e * RDV:(e + 1) * RDV], True, False, [inT, vt], [po[e]])
                    for d in range(2):
                        for dc in range(2):
                            k.mm(po[e][:, :], qs[:, d, 2 * e + dc, :], stt_[:, d, e, dc, :], False, (d == 1 and dc == 1), [qs, stt_], [po[e]])
                for e in range(RH):
                    k.op("dve", lambda g, e=e: g.bn_stats(stats[:, e, :], po[e][:, :]), [po[e]], [stats])
                for e in range(RH):
                    k.op("dve", lambda g, e=e: g.bn_aggr(mv[:, e, :], stats[:, e, :]), [stats], [mv])
                k.act(rs[:, :], mv[:, :, 1], AF.Sqrt, [mv, gneps], [rs], bias=gneps[:, 0:1], scale=1.0)
                k.op("dve", lambda g: g.reciprocal(rs[:, :], rs[:, :]), [rs], [rs])
                k.stt(nb[:, :], mv[:, :, 0], -1.0, rs[:, :], ALU.mult, ALU.mult, [mv, rs], [nb])
                for e in range(RH):
                    k.act(yn[:, e * RDV:(e + 1) * RDV], po[e][:, :], AF.Identity, [po[e], rs, nb], [yn], bias=nb[:, e:e + 1], scale=rs[:, e:e + 1])
                k.tt("pool", Y[:, :], yn[:, :], gt[:, :], ALU.mult, [yn, gt], [Y])
                for hf in range(2):
                    pt = pt_pool.get()
                    for b in range(8):
                        bb = hf * 8 + b
                        k.tr(pt[:, b * 128:(b + 1) * 128], Y[:, bb * 128:(bb + 1) * 128], self.ident[:, :], [Y, self.ident], [pt],
                             inc=(b == 7))
                    k.copy("dve" if hf == 0 else "act", YT[:, hf * 8:(hf + 1) * 8, :].rearrange("p b t -> p (b t)"), pt[:, :], [pt], [YT])
                for hf in range(2):
                    pr = pr_pool.get()
                    for b in range(16):
                        k.mm(pr[:, :], YT[:, b, :], Wo[:, b, hf * 512:(hf + 1) * 512], b == 0, b == 15, [YT, Wo], [pr])
                    sl = slice(hf * 512, (hf + 1) * 512)
                    k.tt("dve", scr2[:, sl], pr[:, :], gate[:, row, sl], ALU.mult, [pr, gate], [scr2])
                    k.tt("pool", xt[:, sl], xt[:, sl], scr2[:, sl], ALU.add, [xt, scr2], [xt])
                k.dma("sp", xap, xt[:, :], reads=[xt])


Prog.ret_setup = _ret_setup
Prog.phase_ret = _phase_ret


def host_consts(rows):
    L = rows * GRID_W
    out = {}
    out["ident"] = np.eye(128, dtype=np.float32).astype(ml_dtypes.bfloat16)
    t = np.arange(L)
    nf = RDK // 4
    inv = (10000.0 ** (-np.arange(nf, dtype=np.float32) / nf)).astype(np.float32)
    ang = np.concatenate([(t // GRID_W).astype(np.float32)[:, None] * inv, (t % GRID_W).astype(np.float32)[:, None] * inv], axis=-1)
    rope = np.stack([np.cos(ang), np.sin(ang)], axis=1).astype(np.float32)
    out["rope_tab"] = rope
    out["ret_const"] = ret_consts_host()
    for nm, Ls in (("lat", L), ("ctx", CTX)):
        hc = hyena_consts_host(Ls)
        out["dft_" + nm] = hc["dft"]; out["emb_" + nm] = hc["emb"]; out["negt_" + nm] = hc["negt"]; out["wk_" + nm] = hc["wk"]
    max_decay = math.log(1e-2) / 0.3
    min_decay = math.log(1e-2) / 1.5
    out["absdelta"] = np.abs(np.linspace(min_decay, max_decay, HYW, dtype=np.float32))[None, :].astype(np.float32)
    return out


NAH = 8
NAD = 64
NAW = 512
HYW = 512
EIN = 3 * NAW + 3 * HYW
HY_EMB = 17
HY_ORDER = 64
I32 = mybir.dt.int32
TWO_PI = 2.0 * math.pi


def _even_setup(self):
    nc = self.nc
    L = self.cfg.L
    NT = L + CTX
    ne = max(1, self.cfg.types.count("even"))
    a = self.inp_add
    a("even_in", [ne, D, EIN]); a("even_out", [ne, D, D]); a("na_q_gain", [ne, NAD]); a("na_k_gain", [ne, NAD])
    a("na_tab", [ne, NAH, 128, 2, 16, 64])
    a("hy_conv_w", [ne, 3, 3 * HYW]); a("hy_conv_b", [ne, 3 * HYW])
    a("hy_fw1", [ne, HY_EMB, HY_ORDER]); a("hy_fb1", [ne, HY_ORDER]); a("hy_fw2", [ne, HY_ORDER, HY_ORDER]); a("hy_fb2", [ne, HY_ORDER])
    a("hy_fw3", [ne, HY_ORDER, HY_ORDER]); a("hy_fb3", [ne, HY_ORDER]); a("hy_fw4", [ne, HY_ORDER, 2 * HYW]); a("hy_freq", [ne, HY_ORDER])
    a("hy_bias", [ne, HYW])
    for nm, Ls in (("lat", L), ("ctx", CTX)):
        KC = Ls // 128 + 1
        a("dft_" + nm, [2, KC, 128, KC, 128], BF16)
        a("emb_" + nm, [HY_EMB, Ls])
        a("negt_" + nm, [128, Ls // 128])
        a("wk_" + nm, [128, KC, 2])
    a("absdelta", [1, HYW])
    self.qtn = nc.dram_tensor("qtn", [4, 128, NT], BF16).ap()
    self.ktn = nc.dram_tensor("ktn", [4, 128, NT], BF16).ap()
    self.vn = nc.dram_tensor("vn", [NT, NAW], BF16).ap()
    self.u_lat = nc.dram_tensor("u_lat", [L + 2, 3 * HYW], F32).ap()
    self.u_ctx = nc.dram_tensor("u_ctx", [CTX + 2, 3 * HYW], F32).ap()
    self.cat = nc.dram_tensor("cat", [NT, D], BF16).ap()
    self.x0z = nc.dram_tensor("x0z", [NT, 2, HYW], BF16).ap()


def na_tab_host(rpb, rows):
    ne = rpb.shape[0]
    tab = np.full((ne, NAH, 128, 2, 16, 64), -30000.0, np.float32)
    c = np.arange(64)
    cs = np.clip(c - 8, 0, 48)
    cp = np.arange(64)
    colvalid = (cp[:, None] >= cs[None, :]) & (cp[:, None] < cs[None, :] + 16)
    dcidx = np.clip(cp[:, None] - c[None, :] + 15, 0, 30)
    for rk in range(2):
        for jr in range(16):
            dr = rk + 7 - jr
            if abs(dr) > 7:
                continue
            g = rpb[:, :, dr + 7, :][:, :, dcidx]
            g = np.where(colvalid[None, None], g, np.float32(-30000.0))
            tab[:, :, rk * 64:(rk + 1) * 64, 0, jr, :] = g
            if -4 <= dr <= 3:
                tab[:, :, rk * 64:(rk + 1) * 64, 1, jr, :] = g
    return tab


def hyena_consts_host(Ls):
    KC = Ls // 128 + 1
    N = 2 * Ls
    nn = KC * 128
    a = np.arange(nn, dtype=np.int64)
    m = (a[:, None] * a[None, :]) % N
    ang = (2.0 * np.pi / N) * m.astype(np.float64)
    out = {}
    tabs = np.stack([np.cos(ang), np.sin(ang)]).astype(np.float32)
    t5 = tabs.reshape(2, KC, 128, KC, 128).transpose(0, 3, 2, 1, 4)
    out["dft"] = np.ascontiguousarray(t5).astype(ml_dtypes.bfloat16)
    t = np.linspace(0.0, 1.0, Ls, dtype=np.float32)[:, None]
    w = (2.0 * np.float32(math.pi) * np.arange(Ls, dtype=np.float32)[:, None] / np.float32(Ls)).astype(np.float32)
    bands = np.linspace(1e-4, 8 - 1, 8, dtype=np.float32)
    emb = np.concatenate([t, np.cos(bands * w), -np.sin(bands * w)], axis=-1).astype(np.float32)
    out["emb"] = np.ascontiguousarray(emb.T)
    out["negt"] = np.ascontiguousarray((-t[:, 0]).reshape(Ls // 128, 128).T).astype(np.float32)
    k = np.arange(nn)
    wk = np.where((k == 0) | (k == Ls), 1.0, 2.0) / N
    wk = np.where(k <= Ls, wk, 0.0).astype(np.float32)
    wk2 = np.stack([wk, -wk], axis=-1).reshape(KC, 128, 2).transpose(1, 0, 2)
    out["wk"] = np.ascontiguousarray(wk2).astype(np.float32)
    return out


def _phase_even(self, l):
    k = self.k
    e = self.cfg.types[:l].count("even")
    L = self.cfg.L
    NT = L + CTX
    rows = self.cfg.rows
    last = (l == self.cfg.depth - 1) and not getattr(self, "force_ctx_out", False)
    inp = self.inp

    with k.phase():
        We = [k.sb([128, EIN], BF16, "We%d" % c) for c in range(8)]
        for c in range(8):
            k.dma("pool", We[c][:, :], inp["even_in"][e, c * 128:(c + 1) * 128, :], writes=[We[c]])
        G, S, gate = self.load_mod_vecs(l, 1, 3, False)
        zrow = k.sb([1, 3 * HYW], F32, "zrow")
        k.memset("dve", zrow[:, :], 0.0, [zrow])
        for (ut, Ls) in ((self.u_lat, L), (self.u_ctx, CTX)):
            k.dma("sp", ut[0:1, :], zrow[:, :], reads=[zrow])
            k.dma("sp", ut[Ls + 1:Ls + 2, :], zrow[:, :], reads=[zrow])
        gq = k.sb([128, 2, NAD], F32, "gq")
        k.dma("sp", gq[:, 0, :], bcast_rows(inp["na_q_gain"][e:e + 1, :]), writes=[gq])
        k.dma("sp", gq[:, 1, :], bcast_rows(inp["na_k_gain"][e:e + 1, :]), writes=[gq])
        TS = 256
        xpool = k.pool_of(2, [128, 2, D], F32, "xt")
        xs_pool = k.pool_of(1, [128, 2, D], BF16, "xs")
        tp_pool = k.pool_of(2, [128, TS], BF16, "tp", psum=True)
        xnT = k.sb([128, 8, TS], BF16, "xnT")
        scr = k.sb([128, D], F32, "scr")
        ssq = k.sb([128, 2], F32, "ssq")
        rstd = k.sb([128, 2], F32, "rstd")
        pp = k.pool_of(3, [128, 512], F32, "pp", psum=True)
        tq_pool = k.pool_of(2, [128, 512], BF16, "tq", psum=True)
        sq = k.sb([128, 512], F32, "sq")
        hs = k.sb([128, 2, NAH], F32, "hs")
        qn = k.sb([128, 512], F32, "qn")
        qk_tok = k.pool_of(2, [128, 512], BF16, "qktok")
        qkT = k.pool_of(4, [128, 4, 128], BF16, "qkT")
        vst = k.pool_of(2, [128, NAW], BF16, "vst")
        ust = k.pool_of(2, [128, 3 * HYW], F32, "ust")
        for (xap, ntok, row) in self.tiles(TS):
            xt = xpool.get()
            k.dma("sp", xt[:, :, :], xap.rearrange("(s p) d -> p s d", p=128), writes=[xt])
            self.norm_mod_T(xt, 2, row, G, S, xs_pool, tp_pool, xnT, scr, ssq, rstd)
            for s in range(2):
                tok0 = self.tile_tok0(xap) + s * 128
                us = ust.get()
                for n in range(6):
                    p = pp.get()
                    for c in range(8):
                        k.mm(p[:, :], xnT[:, c, s * 128:(s + 1) * 128], We[c][:, n * 512:(n + 1) * 512], c == 0, c == 7,
                             [xnT, We[c]], [p])
                    if n < 2:
                        k.act(sq[:, :], p[:, :], AF.Square, [p], [sq])
                        k.op("dve", lambda g, n=n: g.tensor_reduce(hs[:, n, :], sq[:, :].rearrange("p (h d) -> p h d", h=NAH),
                                                                  AX.X, ALU.add), [sq], [hs])
                        k.act(hs[:, n, :], hs[:, n, :], AF.Sqrt, [hs, self.epsb], [hs], bias=self.epsb[:, 0:1], scale=1.0 / NAD)
                        k.op("dve", lambda g, n=n: g.reciprocal(hs[:, n, :], hs[:, n, :]), [hs], [hs])
                        k.tt("dve", qn[:, :].rearrange("p (h d) -> p h d", h=NAH), p[:, :].rearrange("p (h d) -> p h d", h=NAH),
                             hs[:, n, :].unsqueeze(2).broadcast_to([128, NAH, NAD]), ALU.mult, [p, hs], [qn])
                        qt = qk_tok.get()
                        k.tt("pool", qt[:, :].rearrange("p (h d) -> p h d", h=NAH), qn[:, :].rearrange("p (h d) -> p h d", h=NAH),
                             gq[:, n, :].unsqueeze(1).broadcast_to([128, NAH, NAD]), ALU.mult, [qn, gq], [qt])
                        tq = tq_pool.get()
                        for b in range(4):
                            k.tr(tq[:, b * 128:(b + 1) * 128], qt[:, b * 128:(b + 1) * 128], self.ident[:, :], [qt, self.ident], [tq],
                                 inc=(b == 3))
                        dT = qkT.get()
                        k.copy("act", dT[:, :, :].rearrange("p b t -> p (b t)"), tq[:, :], [tq], [dT])
                        dst = self.qtn if n == 0 else self.ktn
                        k.dma("sp", dst[:, :, tok0:tok0 + 128].rearrange("b p t -> p b t"), dT[:, :, :], reads=[dT])
                    elif n == 2:
                        v_ = vst.get()
                        k.copy("act", v_[:, :], p[:, :], [p], [v_])
                        k.dma("sp", self.vn[tok0:tok0 + 128, :], v_[:, :], reads=[v_])
                    else:
                        k.copy("act" if n % 2 else "dve", us[:, (n - 3) * 512:(n - 2) * 512], p[:, :], [p], [us])
                ut, t0 = (self.u_lat, tok0) if row == 0 else (self.u_ctx, tok0 - L)
                k.dma("sp", ut[1 + t0:1 + t0 + 128, :], us[:, :], reads=[us])

    self.hyena(l, e, "lat", L, self.u_lat, 0)
    if not last:
        self.hyena(l, e, "ctx", CTX, self.u_ctx, L)
    self.na_attention(l, e, last)
    with k.phase():
        Wo = k.sb([128, 8, D], BF16, "Weo")
        k.dma("pool", Wo[:, :, :], inp["even_out"][e, :, :].rearrange("(j p) d -> p j d", p=128), writes=[Wo])
        G, S, gate = self.load_mod_vecs(l, 1, 3, False)
        c_pool = k.pool_of(2, [128, D], BF16, "catc")
        x_pool = k.pool_of(2, [128, D], F32, "xo")
        pt_pool = k.pool_of(2, [128, 1024], BF16, "pto", psum=True)
        pr_pool = k.pool_of(2, [128, 512], F32, "pro", psum=True)
        cT_pool = k.pool_of(2, [128, 8, 128], BF16, "cT")
        scr2 = k.sb([128, D], F32, "scr2")
        nchunks = (L // 128) + (0 if last else CTX // 128)
        for ch in range(nchunks):
            row = 0 if ch < L // 128 else 1
            xap = self.out[ch * 128:(ch + 1) * 128, :] if row == 0 else self.xc[(ch - L // 128) * 128:(ch - L // 128 + 1) * 128, :]
            ct = c_pool.get(); xt = x_pool.get()
            k.dma("sp", ct[:, :], self.cat[ch * 128:(ch + 1) * 128, :], writes=[ct])
            k.dma("sp", xt[:, :], xap, writes=[xt])
            pt = pt_pool.get()
            for b in range(8):
                k.tr(pt[:, b * 128:(b + 1) * 128], ct[:, b * 128:(b + 1) * 128], self.ident[:, :], [ct, self.ident], [pt], inc=(b == 7))
            cT = cT_pool.get()
            k.copy("act", cT[:, :, :].rearrange("p b t -> p (b t)"), pt[:, :], [pt], [cT])
            for hf in range(2):
                pr = pr_pool.get()
                for b in range(8):
                    k.mm(pr[:, :], cT[:, b, :], Wo[:, b, hf * 512:(hf + 1) * 512], b == 0, b == 7, [cT, Wo], [pr])
                sl = slice(hf * 512, (hf + 1) * 512)
                k.tt("dve", scr2[:, sl], pr[:, :], gate[:, row, sl], ALU.mult, [pr, gate], [scr2])
                k.tt("pool", xt[:, sl], xt[:, sl], scr2[:, sl], ALU.add, [xt, scr2], [xt])
            k.dma("sp", xap, xt[:, :], reads=[xt])


def _hyena(self, l, e, nm, Ls, ut, tokbase):
    k = self.k
    inp = self.inp
    NCn = Ls // 128
    KC = NCn + 1
    dft = inp["dft_" + nm]
    with k.phase():
        cw = k.sb([128, 3, 3 * HYW], F32, "cw")
        cb = k.sb([128, 3 * HYW], F32, "cb")
        for tpi in range(3):
            k.dma("sp", cw[:, tpi, :], bcast_rows(inp["hy_conv_w"][e, tpi:tpi + 1, :]), writes=[cw])
        k.dma("sp", cb[:, :], bcast_rows(inp["hy_conv_b"][e:e + 1, :]), writes=[cb])
        upool = k.pool_of(3, [128, 3, 3 * HYW], F32, "uabc")
        t1 = k.pool_of(3, [128, 3 * HYW], F32, "sc1")
        t2 = k.pool_of(3, [128, 3 * HYW], F32, "sc2")
        xz = k.pool_of(3, [128, 2, HYW], BF16, "xz")
        for n in range(NCn):
            u = upool.get()
            for tpi in range(3):
                k.dma("sp", u[:, tpi, :], ut[n * 128 + tpi:n * 128 + tpi + 128, :], writes=[u])
            a_ = t1.get(); b2 = t2.get()
            eg = "pool" if n % 3 == 2 else "dve"
            k.tt(eg, a_[:, :], u[:, 0, :], cw[:, 0, :], ALU.mult, [u, cw], [a_])
            k.tt(eg, b2[:, :], u[:, 1, :], cw[:, 1, :], ALU.mult, [u, cw], [b2])
            k.tt(eg, a_[:, :], a_[:, :], b2[:, :], ALU.add, [a_, b2], [a_])
            k.tt(eg, b2[:, :], u[:, 2, :], cw[:, 2, :], ALU.mult, [u, cw], [b2])
            k.tt(eg, a_[:, :], a_[:, :], b2[:, :], ALU.add, [a_, b2], [a_])
            k.tt(eg, a_[:, :], a_[:, :], cb[:, :], ALU.add, [a_, cb], [a_])
            o_ = xz.get()
            k.copy("act", o_[:, 0, :], a_[:, 0:HYW], [a_], [o_])
            k.tt(eg, o_[:, 1, :], a_[:, HYW:2 * HYW], a_[:, 2 * HYW:3 * HYW], ALU.mult, [a_], [o_])
            k.dma("sp", self.x0z[tokbase + n * 128:tokbase + (n + 1) * 128, :, :], o_[:, :, :], reads=[o_])
    with k.phase():
        wk = k.sb([128, KC, 2], F32, "wk")
        k.dma("sp", wk[:, :, :], inp["wk_" + nm], writes=[wk])
        negt = k.sb([128, NCn], F32, "negt")
        k.dma("sp", negt[:, :], inp["negt_" + nm], writes=[negt])
        KK = k.sb([128, KC, 2, HYW], BF16, "KK")
        KKtok = [Tok("kk%d" % i) for i in range(KC)]
        with k.phase():
            Hf = k.sb([128, NCn, HYW], BF16, "Hf")
            Hb = k.sb([128, NCn, HYW], BF16, "Hb")
            with k.phase():
                CW = min(512, Ls)
                embp = k.pool_of(2, [HY_EMB, CW], F32, "emb")
                fw = [k.sb([HY_EMB, HY_ORDER], F32, "fw1"), k.sb([HY_ORDER, HY_ORDER], F32, "fw2"), k.sb([HY_ORDER, HY_ORDER], F32, "fw3")]
                fw4 = k.sb([HY_ORDER, 2 * HYW], F32, "fw4")
                fbT = k.sb([HY_ORDER, 4], F32, "fbT")
                k.dma("sp", fw[0][:, :], inp["hy_fw1"][e], writes=[fw[0]])
                k.dma("sp", fw[1][:, :], inp["hy_fw2"][e], writes=[fw[1]])
                k.dma("sp", fw[2][:, :], inp["hy_fw3"][e], writes=[fw[2]])
                k.dma("sp", fw4[:, :], inp["hy_fw4"][e], writes=[fw4])
                with self.nc.allow_non_contiguous_dma("tiny"):
                    for i, nmv in enumerate(("hy_fb1", "hy_fb2", "hy_fb3", "hy_freq")):
                        k.dma("sp", fbT[:, i:i + 1], inp[nmv][e:e + 1, :].rearrange("o f -> f o"), writes=[fbT])
                fbias = k.sb([HY_ORDER, 3], F32, "fbias")
                for i in range(3):
                    k.tt("dve", fbias[:, i:i + 1], fbT[:, i:i + 1], fbT[:, 3:4], ALU.mult, [fbT], [fbias])
                adl = k.sb([128, HYW], F32, "adl")
                k.dma("sp", adl[:, :], bcast_rows(inp["absdelta"]), writes=[adl])
                hcur = [k.sb([HY_ORDER, CW], F32, "hmlp%d" % i) for i in range(2)]
                pm = k.pool_of(2, [HY_ORDER, 512], F32, "pm", psum=True)
                pre = k.sb([HY_ORDER, 512], F32, "pre")
                nfl = k.sb([HY_ORDER, 512], F32, "nfl")
                nin = k.sb([HY_ORDER, 512], I32, "nin")
                win = k.pool_of(2, [128, HYW], F32, "win")
                ph = k.pool_of(2, [128, 512], F32, "ph", psum=True)
                for cc in range(Ls // CW):
                    em = embp.get()
                    k.dma("sp", em[:, :], inp["emb_" + nm][:, cc * CW:(cc + 1) * CW], writes=[em])
                    for layer in range(3):
                        src = em if layer == 0 else hcur[(layer - 1) % 2]
                        dst = hcur[layer % 2]
                        p = pm.get()
                        k.mm(p[:, :CW], fw[layer][:, :], src[:, :], True, True, [fw[layer], src], [p])
                        k.act(pre[:, :CW], p[:, :CW], AF.Identity, [p, fbT, fbias], [pre], bias=fbias[:, layer:layer + 1], scale=fbT[:, 3:4])
                        k.ts("dve", nfl[:, :CW], pre[:, :CW], 1.0 / TWO_PI, None, ALU.mult, None, [pre], [nfl])
                        k.copy("dve", nin[:, :CW], nfl[:, :CW], [nfl], [nin])
                        k.copy("dve", nfl[:, :CW], nin[:, :CW], [nin], [nfl])
                        k.stt(pre[:, :CW], nfl[:, :CW], -TWO_PI, pre[:, :CW], ALU.mult, ALU.add, [nfl, pre], [pre])
                        k.ts("dve", pre[:, :CW], pre[:, :CW], 3.1415925, -3.1415925, ALU.min, ALU.max, [pre], [pre])
                        k.act(dst[:, :], pre[:, :CW], AF.Sin, [pre], [dst])
                    h3 = hcur[0]
                    for sub in range(CW // 128):
                        n = cc * (CW // 128) + sub
                        w_ = win.get()
                        k.act(w_[:, :], adl[:, :], AF.Exp, [adl, negt], [w_], scale=negt[:, n:n + 1])
                        for hf, dstH in ((0, Hf), (1, Hb)):
                            p = ph.get()
                            k.mm(p[:, :], h3[:, sub * 128:(sub + 1) * 128], fw4[:, hf * HYW:(hf + 1) * HYW], True, True, [h3, fw4], [p])
                            k.tt("dve", dstH[:, n, :], p[:, :], w_[:, :], ALU.mult, [p, w_], [dstH])
            tabp = k.pool_of(2, [128, 2, NCn, 128], BF16, "tabF")
            pacc = [k.ps([128, 512], F32, "pF%d" % i) for i in range(4)]
            bsb = k.pool_of(2, [128, 2, HYW], F32, "bsb")
            for kc in range(KC):
                tb = tabp.get()
                for cs_ in range(2):
                    k.dma("sp", tb[:, cs_, :, :], dft[cs_, kc, :, 0:NCn, :], writes=[tb])
                for n in range(NCn):
                    for cs_ in range(2):
                        for hi, Hsrc in ((0, Hf), (1, Hb)):
                            k.mm(pacc[cs_ * 2 + hi][:, :], tb[:, cs_, n, :], Hsrc[:, n, :], n == 0, n == NCn - 1, [tb, Hsrc], [pacc[cs_ * 2 + hi]])
                Fc, Bc, Fs, Bs = pacc[0], pacc[1], pacc[2], pacc[3]
                b_ = bsb.get()
                k.act(b_[:, 0, :], Bc[:, :], AF.Identity, [Bc, wk], [b_], scale=wk[:, kc, 0:1])
                k.act(b_[:, 1, :], Bs[:, :], AF.Identity, [Bs, wk], [b_], scale=wk[:, kc, 0:1])
                k.stt(KK[:, kc, 0, :], Fc[:, :], wk[:, kc, 0:1], b_[:, 0, :], ALU.mult, ALU.add, [Fc, wk, b_], [KKtok[kc]])
                k.stt(KK[:, kc, 1, :], Fs[:, :], wk[:, kc, 1:2], b_[:, 1, :], ALU.mult, ALU.add, [Fs, wk, b_], [KKtok[kc]])
        with k.phase():
            z = k.sb([128, NCn, HYW], BF16, "z")
            k.dma("sp", z[:, :, :], self.x0z[tokbase:tokbase + Ls, 1, :].rearrange("(n p) c -> p n c", p=128), writes=[z])
            tabp = k.pool_of(2, [128, 2, NCn, 128], BF16, "tabZ")
            pz = [k.pool_of(2, [128, 512], F32, "pZ%d" % i, psum=True) for i in range(2)]
            tm = [k.pool_of(2, [128, HYW], F32, "tmz%d" % i) for i in range(4)]
            for kc in range(KC):
                tb = tabp.get()
                for cs_ in range(2):
                    k.dma("sp", tb[:, cs_, :, :], dft[cs_, kc, :, 0:NCn, :], writes=[tb])
                Zc = pz[0].get(); Zs = pz[1].get()
                for n in range(NCn):
                    k.mm(Zc[:, :], tb[:, 0, n, :], z[:, n, :], n == 0, n == NCn - 1, [tb, z], [Zc])
                    k.mm(Zs[:, :], tb[:, 1, n, :], z[:, n, :], n == 0, n == NCn - 1, [tb, z], [Zs])
                a1 = tm[0].get(); a2 = tm[1].get(); a3 = tm[2].get(); a4 = tm[3].get()
                kt = KKtok[kc]
                k.tt("dve", a1[:, :], Zc[:, :], KK[:, kc, 0, :], ALU.mult, [Zc, kt], [a1])
                k.tt("dve", a2[:, :], Zs[:, :], KK[:, kc, 1, :], ALU.mult, [Zs, kt], [a2])
                k.tt("dve", a3[:, :], Zs[:, :], KK[:, kc, 0, :], ALU.mult, [Zs, kt], [a3])
                k.tt("dve", a4[:, :], Zc[:, :], KK[:, kc, 1, :], ALU.mult, [Zc, kt], [a4])
                k.tt("pool", KK[:, kc, 0, :], a1[:, :], a2[:, :], ALU.add, [a1, a2], [kt])
                k.tt("pool", KK[:, kc, 1, :], a3[:, :], a4[:, :], ALU.subtract, [a3, a4], [kt])
        with k.phase():
            tabp = k.pool_of(2, [128, 2, KC, 128], BF16, "tabI")
            py = k.pool_of(2, [128, 512], F32, "pY", psum=True)
            db = k.sb([128, HYW], F32, "dbias")
            k.dma("sp", db[:, :], bcast_rows(inp["hy_bias"][e:e + 1, :]), writes=[db])
            e1 = k.pool_of(2, [128, HYW], F32, "e1")
            bo = k.pool_of(2, [128, HYW], BF16, "bo")
            xzp = k.pool_of(2, [128, 2, HYW], BF16, "xzi")
            for tc_ in range(NCn):
                tb = tabp.get()
                for cs_ in range(2):
                    k.dma("sp", tb[:, cs_, :, :], dft[cs_, tc_, :, :, :], writes=[tb])
                xz_ = xzp.get()
                k.dma("sp", xz_[:, :, :], self.x0z[tokbase + tc_ * 128:tokbase + (tc_ + 1) * 128, :, :], writes=[xz_])
                y = py.get()
                for kc in range(KC):
                    k.mm(y[:, :], tb[:, 0, kc, :], KK[:, kc, 0, :], kc == 0, False, [tb, KKtok[kc]], [y])
                    k.mm(y[:, :], tb[:, 1, kc, :], KK[:, kc, 1, :], False, kc == KC - 1, [tb, KKtok[kc]], [y])
                t_ = e1.get()
                k.tt("pool", t_[:, :], xz_[:, 1, :], db[:, :], ALU.mult, [xz_, db], [t_])
                k.tt("dve", t_[:, :], y[:, :], t_[:, :], ALU.add, [y, t_], [t_])
                o_ = bo.get()
                k.tt("dve", o_[:, :], t_[:, :], xz_[:, 0, :], ALU.mult, [t_, xz_], [o_])
                k.dma("sp", self.cat[tokbase + tc_ * 128:tokbase + (tc_ + 1) * 128, NAW:D], o_[:, :], reads=[o_])


Prog.even_setup = _even_setup
Prog.phase_even = _phase_even
Prog.hyena = _hyena


def _na_attention(self, l, e, last):
    k = self.k
    inp = self.inp
    L = self.cfg.L
    NT = L + CTX
    rows = self.cfg.rows
    nP = rows // 2
    NB = L // 128
    with k.phase():
        Ve = k.sb([128, NB + 2, NAH, NAD + 1], BF16, "Ve")
        Vo = k.sb([128, NB - 1, NAH, NAD + 1], BF16, "Vo")
        k.memset("pool", Ve[:, :, :, NAD:NAD + 1], 1.0, [Ve])
        k.memset("pool", Vo[:, :, :, NAD:NAD + 1], 1.0, [Vo])
        for b in range(NB + 2):
            k.dma("sp", Ve[:, b, :, 0:NAD], self.vn[b * 128:(b + 1) * 128, :].rearrange("p (h d) -> p h d", h=NAH), writes=[Ve])
        for b in range(NB - 1):
            k.dma("sp", Vo[:, b, :, 0:NAD], self.vn[64 + b * 128:64 + (b + 1) * 128, :].rearrange("p (h d) -> p h d", h=NAH), writes=[Vo])
        A = k.sb([128, NB + 2, NAW], BF16, "Aall")
        qpool = k.pool_of(2, [128, NT], BF16, "qTn")
        kpool = k.pool_of(2, [128, NT], BF16, "kTn")
        tst = k.pool_of(2, [128, 2, 16, 64], F32, "tst")
        ttp = k.pool_of(2, [128, 2, 16 * 64], BF16, "TT")
        ps_pool = k.pool_of(2, [128, 1024], F32, "psS", psum=True)
        pv_pool = k.pool_of(2, [128, NAD + 1], F32, "psV", psum=True)
        pt_pool = k.pool_of(3, [128, 7 * 128], BF16, "PT")
        rec = k.pool_of(4, [128, 1], F32, "rec")
        for h in range(NAH):
            hp, pb = h // 2, (h % 2) * 64
            if h % 2 == 0:
                qT = qpool.get(); kT = kpool.get()
                k.dma("sp", qT[:, :], self.qtn[hp, :, :], writes=[qT])
                k.dma("sp", kT[:, :], self.ktn[hp, :, :], writes=[kT])
            ts_ = tst.get()
            k.dma("sp", ts_[:, :, :, :], inp["na_tab"][e, h, :, :, :, :], writes=[ts_])
            TT = ttp.get()
            k.act(TT[:, :, :], ts_[:, :, :, :].rearrange("p v j c -> p v (j c)"), AF.Exp, [ts_], [TT])
            units = []
            for i in range(nP):
                r0 = 2 * i
                if i < 2:
                    al, var = [0, 2, 4, 6], 0
                elif i >= nP - 2:
                    al, var = [rows - 8, rows - 6, rows - 4, rows - 2], 0
                else:
                    al, var = [r0 - 4, r0 - 2, r0, r0 + 2, r0 + 4], 1
                units.append((r0 * 64, al, var, i))
            if not last:
                units.append((L, [], 0, NB))
                units.append((L + 128, [], 0, NB + 1))
            for (q0, al, var, ablk) in units:
                M = len(al)
                nb = M + 2
                ps = ps_pool.get()
                for b in range(nb):
                    if b < M:
                        a_ = al[M - 1 - b]
                        ks = a_ * 64
                    else:
                        ks = L + (b - M) * 128
                    k.mm(ps[:, b * 128:(b + 1) * 128], kT[pb:pb + 64, ks:ks + 128], qT[pb:pb + 64, q0:q0 + 128], True, True,
                         [kT, qT], [ps], inc=(b == nb - 1))
                PT = pt_pool.get()
                for b0 in range(0, nb, 4):
                    b1 = min(nb, b0 + 4)
                    k.act(PT[:, b0 * 128:b1 * 128], ps[:, b0 * 128:b1 * 128], AF.Exp, [ps], [PT], scale=NAD ** -0.5)
                if M > 0:
                    r0 = q0 // 64
                    base = 7 - (al[0] - r0) - 2 * (M - 1)
                    k.tt("dve", PT[:, 0:M * 128], PT[:, 0:M * 128], TT[:, var, base * 64:(base + 2 * M) * 64], ALU.mult, [PT, TT], [PT])
                pv = pv_pool.get()
                for b in range(nb):
                    if b < M:
                        a_ = al[M - 1 - b]
                        vt = Ve[:, a_ // 2, h, :] if a_ % 2 == 0 else Vo[:, (a_ - 1) // 2, h, :]
                        vtok = Ve if a_ % 2 == 0 else Vo
                    else:
                        vt = Ve[:, NB + (b - M), h, :]
                        vtok = Ve
                    k.mm(pv[:, :], PT[:, b * 128:(b + 1) * 128], vt, b == 0, b == nb - 1, [PT, vtok], [pv])
                rc = rec.get()
                k.op("dve", lambda g, rc=rc, pv=pv: g.reciprocal(rc[:, :], pv[:, NAD:NAD + 1]), [pv], [rc])
                k.act(A[:, ablk, h * NAD:(h + 1) * NAD], pv[:, 0:NAD], AF.Identity, [pv, rc], [A], scale=rc[:, 0:1])
        nblk = NB + (0 if last else 2)
        for b in range(nblk):
            k.dma("sp", self.cat[b * 128:(b + 1) * 128, 0:NAW], A[:, b, :], reads=[A])


Prog.na_attention = _na_attention


_WEIGHTS = ["w_mod", "b_mod", "norm_gain", "ffn_a_in", "ffn_a_out", "ffn_b_in", "ffn_b_out", "even_in", "even_out",
            "na_q_gain", "na_k_gain", "hy_conv_w", "hy_conv_b", "hy_fw1", "hy_fb1", "hy_fw2", "hy_fb2", "hy_fw3", "hy_fb3",
            "hy_fw4", "hy_freq", "hy_bias", "ret_in", "ret_out", "ret_logit_f", "ret_logit_b"]


def kernel(**inputs):
    rows = 64
    B = inputs["x"].shape[0]
    P = Prog(Cfg(rows=rows, depth=4))
    nc = P.build()
    shared = {n: np.ascontiguousarray(np.asarray(inputs[n], dtype=np.float32)) for n in _WEIGHTS}
    shared.update(host_consts(rows))
    shared["na_tab"] = na_tab_host(np.asarray(inputs["na_rpb"], dtype=np.float32), rows)
    shared["c_ctx"] = np.ascontiguousarray(np.asarray(inputs["c_ctx"], dtype=np.float32)[None, :])
    shared = {n: v for n, v in shared.items() if n in P.inp}
    in_maps = []
    for b in range(B):
        m = dict(shared)
        m["x"] = np.ascontiguousarray(inputs["x"][b], dtype=np.float32)
        m["c"] = np.ascontiguousarray(inputs["c"][b:b + 1], dtype=np.float32)
        m["ctx"] = np.ascontiguousarray(inputs["ctx"][b], dtype=np.float32)
        in_maps.append(m)
    res = run_bass_kernel_spmd(nc, in_maps, core_ids=list(range(B)))
    return np.stack([np.asarray(r["out"], dtype=np.float32) for r in res.results], axis=0)
```

```python
import contextlib
import math
import numpy as np
import ml_dtypes
import concourse.bass as bass
import concourse.mybir as mybir
from concourse.bass_utils import run_bass_kernel_spmd

F32 = mybir.dt.float32
BF16 = mybir.dt.bfloat16
AF = mybir.ActivationFunctionType
ALU = mybir.AluOpType
AX = mybir.AxisListType

D = 1024
DFF = 2816
NMOD = 9
RMS_EPS = 1e-6
GN_EPS = 1e-6
GRID_W = 64
CTX = 256
SAME_ENGINE_SYNC = True


class Tok:
    __slots__ = ("w", "r", "name", "lane", "wb")

    def __init__(self, name=""):
        self.w = None
        self.r = {}
        self.name = name
        self.lane = None
        self.wb = False


class Lane:
    __slots__ = ("sem", "count", "sw")

    def __init__(self, sem):
        self.sem = sem
        self.count = 0
        self.sw = False


class T:
    def __init__(self, h, name):
        self.h = h
        self.tok = Tok(name)

    def __getitem__(self, idx):
        return self.h[idx]


class KB:
    def __init__(self):
        self.nc = bass.Bass("TRN2", target_bir_lowering=False)
        nc = self.nc
        self.es = contextlib.ExitStack()
        self.eng = dict(pe=nc.tensor, act=nc.scalar, dve=nc.vector, pool=nc.gpsimd, sp=nc.sync)
        self.sem = {e: self.es.enter_context(nc.semaphore("s_" + e)) for e in self.eng}
        self.cnt = {e: 0 for e in self.eng}
        self.seen = {e: {} for e in self.eng}
        self.free_lanes = []
        self.free_lanes_sw = []
        self.all_lanes = []
        self.nlanes = 0
        self.phase_stack = []
        self.ninst = 0
        self.uid = 0

    def _name(self, p):
        self.uid += 1
        return "%s_%d" % (p, self.uid)

    def sb(self, shape, dt, name="t"):
        st = self.phase_stack[-1][0] if self.phase_stack else self.es
        h = st.enter_context(self.nc.sbuf_tensor(self._name(name), list(shape), dt))
        t = T(h, name)
        if self.phase_stack:
            self.phase_stack[-1][1].append(t)
        return t

    def ps(self, shape, dt, name="p"):
        st = self.phase_stack[-1][0] if self.phase_stack else self.es
        h = st.enter_context(self.nc.psum_tensor(self._name(name), list(shape), dt))
        t = T(h, name)
        if self.phase_stack:
            self.phase_stack[-1][1].append(t)
        return t

    def pool_of(self, n, shape, dt, name, psum=False):
        return Ring([(self.ps if psum else self.sb)(shape, dt, name) for _ in range(n)])

    @contextlib.contextmanager
    def phase(self):
        st = contextlib.ExitStack()
        toks = []
        self.phase_stack.append((st, toks))
        try:
            yield
        finally:
            self.barrier()
            self.phase_stack.pop()
            for t in toks:
                if t.tok.lane is not None:
                    (self.free_lanes_sw if t.tok.lane.sw else self.free_lanes).append(t.tok.lane)
                    t.tok.lane = None
            st.close()

    def lane_of(self, tok, sw=False):
        if tok.lane is None:
            fl = self.free_lanes_sw if sw else self.free_lanes
            if fl:
                tok.lane = fl.pop()
            else:
                sem = self.es.enter_context(self.nc.semaphore("l_%d" % self.nlanes))
                self.nlanes += 1
                tok.lane = Lane(sem)
                tok.lane.sw = sw
                self.all_lanes.append(tok.lane)
        assert tok.lane.sw == sw, "token %s mixes SW and HW DMA queues" % tok.name
        return tok.lane

    def _waits(self, e, reads, writes, bulk=False, is_dma=False, skip_sem=None):
        deps = {}
        own = None if is_dma else self.sem.get(e)

        def need(p):
            if p is None:
                return
            s, v = p
            k = id(s)
            if k not in deps or deps[k][1] < v:
                deps[k] = (s, v)

        for t in reads:
            if t.w is not None and t.w[0] is own:
                if e == "pe" or not SAME_ENGINE_SYNC or (bulk and t.wb):
                    continue
            need(t.w)
        for t in writes:
            if not (t.w is not None and (t.w[0] is own or (skip_sem is not None and t.w[0] is skip_sem))):
                need(t.w)
            for p in t.r.values():
                if p[0] is own:
                    continue
                need(p)
        for k, (s, v) in deps.items():
            if self.seen[e].get(k, 0) >= v:
                continue
            self.eng[e].wait_ge(s, v)
            self.seen[e][k] = v
            self.ninst += 1

    @staticmethod
    def _toks(xs):
        out = []
        for x in xs:
            if x is None:
                continue
            out.append(x.tok if isinstance(x, T) else x)
        return out

    def op(self, e, emit, reads=(), writes=(), inc=True, bulk=False):
        reads = self._toks(reads)
        writes = self._toks(writes)
        self._waits(e, reads, writes, bulk=bulk)
        ins = emit(self.eng[e])
        self.ninst += 1
        if inc:
            self.cnt[e] += 1
            ins.then_inc(self.sem[e], 1)
            me = (self.sem[e], self.cnt[e])
        else:
            me = (self.sem[e], self.cnt[e] + 1)
        for t in reads:
            t.r[e] = me
        for t in writes:
            t.w = me
            t.r = {}
            t.wb = bulk
        return ins

    def dma(self, q, out, in_, reads=(), writes=(), lane_tok=None, **kw):
        reads = self._toks(reads)
        writes = self._toks(writes)
        lt = lane_tok.tok if isinstance(lane_tok, T) else lane_tok
        if lt is None:
            lt = writes[0] if writes else reads[0]
        lane = self.lane_of(lt, sw=(q == "pool"))
        self._waits(q, reads, writes, is_dma=True, skip_sem=lane.sem)
        if q == "pool" and lane.count > 0 and self.seen[q].get(id(lane.sem), 0) < lane.count:
            self.eng[q].wait_ge(lane.sem, lane.count)
            self.seen[q][id(lane.sem)] = lane.count
        ins = self.eng[q].dma_start(out=out, in_=in_, **kw)
        self.ninst += 1
        lane.count += 16
        ins.then_inc(lane.sem, 16)
        me = (lane.sem, lane.count)
        for t in reads:
            t.r["dma%d" % id(lane)] = me
        for t in writes:
            t.w = me
            t.r = {}
            t.wb = False
        return ins

    def barrier(self):
        for e in self.eng:
            for f in self.eng:
                if f == e or self.cnt[f] == 0:
                    continue
                if self.seen[e].get(id(self.sem[f]), 0) >= self.cnt[f]:
                    continue
                self.eng[e].wait_ge(self.sem[f], self.cnt[f])
                self.seen[e][id(self.sem[f])] = self.cnt[f]
            for ln in self.all_lanes:
                if ln.count == 0 or self.seen[e].get(id(ln.sem), 0) >= ln.count:
                    continue
                self.eng[e].wait_ge(ln.sem, ln.count)
                self.seen[e][id(ln.sem)] = ln.count

    def mm(self, out, lhsT, rhs, start, stop, reads, writes, inc=None):
        if inc is None:
            inc = stop
        return self.op("pe", lambda g: g.matmul(out, lhsT, rhs, start=start, stop=stop), reads, writes, inc=inc)

    def tr(self, out, in_, ident, reads, writes, inc=True):
        return self.op("pe", lambda g: g.transpose(out, in_, ident), reads, writes, inc=inc)

    @staticmethod
    def _bulk(out):
        try:
            return out.free_size() >= 256
        except Exception:
            return False

    def act(self, out, in_, func, reads, writes, bias=None, scale=None, accum_out=None, e="act"):
        kw = {}
        if bias is not None:
            kw["bias"] = bias
        if scale is not None:
            kw["scale"] = scale
        if accum_out is not None:
            kw["accum_out"] = accum_out
        return self.op(e, lambda g: g.activation(out, in_, func, **kw), reads, writes,
                       bulk=(accum_out is None and self._bulk(out)))

    def ts(self, e, out, in0, s1, s2, op0, op1, reads, writes, accum_out=None):
        kw = {}
        if op1 is not None:
            kw["op1"] = op1
        if accum_out is not None:
            kw["accum_out"] = accum_out
        return self.op(e, lambda g: g.tensor_scalar(out, in0, s1, s2, op0, **kw), reads, writes,
                       bulk=(accum_out is None and self._bulk(out)))

    def tt(self, e, out, in0, in1, op, reads, writes):
        return self.op(e, lambda g: g.tensor_tensor(out, in0, in1, op), reads, writes, bulk=self._bulk(out))

    def stt(self, out, in0, scalar, in1, op0, op1, reads, writes, e="dve"):
        return self.op(e, lambda g: g.scalar_tensor_tensor(out, in0, scalar, in1, op0, op1), reads, writes, bulk=self._bulk(out))

    def copy(self, e, out, in_, reads, writes):
        if e == "act":
            return self.op(e, lambda g: g.copy(out, in_), reads, writes, bulk=self._bulk(out))
        return self.op(e, lambda g: g.tensor_copy(out, in_), reads, writes, bulk=self._bulk(out))

    def memset(self, e, ap, val, writes):
        return self.op(e, lambda g: g.memset(ap, val), (), writes, bulk=self._bulk(ap))


class Ring:
    def __init__(self, items):
        self.items = items
        self.i = 0

    def get(self):
        t = self.items[self.i % len(self.items)]
        self.i += 1
        return t


class Cfg:
    def __init__(self, rows=64, depth=4, types=None):
        self.rows = rows
        self.L = rows * GRID_W
        self.depth = depth
        self.types = types if types is not None else [("even" if i % 2 == 0 else "ret") for i in range(depth)]
        self.layers = list(range(depth))


def bcast_rows(ap, n=128):
    return ap.partition_broadcast(n)


class Prog:
    def __init__(self, cfg, stages=None):
        self.cfg = cfg
        self.k = KB()
        self.nc = self.k.nc
        self.stages = stages
        nc = self.nc
        L = cfg.L
        nl = cfg.depth
        ne = (nl + 1) // 2
        no = nl // 2
        self.inp = {}

        def din(name, shape, dt=F32):
            self.inp[name] = nc.dram_tensor(name, list(shape), dt, kind="ExternalInput").ap()
            return self.inp[name]

        din("x", [L, D]); din("c", [1, D]); din("ctx", [CTX, D]); din("c_ctx", [1, D])
        din("w_mod", [nl, D, NMOD * D]); din("b_mod", [nl, NMOD * D]); din("norm_gain", [nl, 3, D])
        din("ffn_a_in", [nl, D, 2 * DFF]); din("ffn_a_out", [nl, DFF, D])
        din("ffn_b_in", [nl, D, 2 * DFF]); din("ffn_b_out", [nl, DFF, D])
        din("ident", [128, 128], BF16)
        self.out = nc.dram_tensor("out", [L, D], F32, kind="ExternalOutput").ap()
        self.xc = nc.dram_tensor("xc_scr", [CTX, D], F32, kind="ExternalOutput").ap()
        self.mod = nc.dram_tensor("mod_scr", [nl, 2, NMOD * D], F32).ap()
        if "ret" in cfg.types:
            self.ret_setup()
        if "even" in cfg.types:
            self.even_setup()

    def inp_add(self, name, shape, dt=F32):
        self.inp[name] = self.nc.dram_tensor(name, list(shape), dt, kind="ExternalInput").ap()
        return self.inp[name]

    def tile_tok0(self, xap):
        off = xap.offset // D
        return off if xap.tensor.name == self.out.tensor.name else self.cfg.L + off

    def consts(self):
        k = self.k
        self.ident = k.sb([128, 128], BF16, "ident")
        k.dma("sp", self.ident[:, :], self.inp["ident"], writes=[self.ident])
        self.epsb = k.sb([128, 1], F32, "eps")
        k.memset("dve", self.epsb[:, :], RMS_EPS, [self.epsb])

    def init_copy(self):
        k = self.k
        L = self.cfg.L
        self.t_xlat = Tok("xlat")
        self.t_xctx = Tok("xctx")
        nchunk = max(1, L // 1024)
        rows = L // nchunk
        for i in range(nchunk):
            k.dma("sp", self.out[i * rows:(i + 1) * rows, :], self.inp["x"][i * rows:(i + 1) * rows, :],
                  writes=[self.t_xlat])
        k.dma("sp", self.xc[:, :], self.inp["ctx"][:, :], writes=[self.t_xctx])
        k.barrier()

    def phase_mod(self, l):
        k = self.k
        with k.phase():
            cT = k.sb([128, 8, 2], F32, "cT")
            with self.nc.allow_non_contiguous_dma("tiny"):
                k.dma("sp", cT[:, :, 0], self.inp["c"].rearrange("o (c p) -> p (o c)", p=128), writes=[cT])
                k.dma("sp", cT[:, :, 1], self.inp["c_ctx"].rearrange("o (c p) -> p (o c)", p=128), writes=[cT])
            sc = k.sb([128, 8, 2], F32, "sc")
            k.act(sc[:, :, :], cT[:, :, :], AF.Silu, [cT], [sc])
            ones2 = k.sb([1, 2], F32, "ones2")
            k.memset("dve", ones2[:, :], 1.0, [ones2])
            brow = k.sb([1, NMOD * D], F32, "brow")
            k.dma("sp", brow[:, :], self.inp["b_mod"][l:l + 1, :], writes=[brow])
            wpool = k.pool_of(3, [128, 8, 512], F32, "wm")
            ppool = k.pool_of(2, [2, 512], F32, "pm", psum=True)
            spool = k.pool_of(2, [2, 512], F32, "sm")
            for n in range(NMOD * D // 512):
                w = wpool.get()
                k.dma("sp", w[:, :, :], self.inp["w_mod"][l, :, n * 512:(n + 1) * 512].rearrange("(c p) n -> p c n", p=128),
                      writes=[w])
                p = ppool.get()
                for c in range(8):
                    k.mm(p[:, :], sc[:, c, :], w[:, c, :], c == 0, False, [sc, w], [p])
                k.mm(p[:, :], ones2[:, :], brow[:, n * 512:(n + 1) * 512], False, True, [ones2, brow], [p])
                s = spool.get()
                k.copy("dve", s[:, :], p[:, :], [p], [s])
                k.dma("sp", self.mod[l, :, n * 512:(n + 1) * 512], s[:, :], reads=[s])

    def load_mod_vecs(self, l, nj, mv, half_gate):
        k = self.k
        G = k.sb([128, 8, 2], F32, "G")
        S = k.sb([128, 8, 2], F32, "S")
        gn = k.sb([128, 8], F32, "gn")
        gate = k.sb([128, 2, D], F32, "gate")
        with self.nc.allow_non_contiguous_dma("tiny"):
            k.dma("sp", gn[:, :], self.inp["norm_gain"][l, nj:nj + 1, :].rearrange("o (c p) -> p (o c)", p=128), writes=[gn])
            for r in range(2):
                k.dma("sp", S[:, :, r], self.mod[l, r:r + 1, mv * D:(mv + 1) * D].rearrange("o (c p) -> p (o c)", p=128), writes=[S])
                k.dma("sp", G[:, :, r], self.mod[l, r:r + 1, (mv + 1) * D:(mv + 2) * D].rearrange("o (c p) -> p (o c)", p=128), writes=[G])
                k.dma("sp", gate[:, r, :], bcast_rows(self.mod[l, r:r + 1, (mv + 2) * D:(mv + 3) * D]), writes=[gate])
        for r in range(2):
            k.stt(G[:, :, r], G[:, :, r], 1.0, gn[:, :], ALU.add, ALU.mult, [G, gn], [G])
        if half_gate:
            k.ts("dve", gate[:, :, :], gate[:, :, :], 0.5, None, ALU.mult, None, [gate], [gate])
        return G, S, gate

    def tiles(self, tsz):
        out = []
        for i in range(self.cfg.L // tsz):
            out.append((self.out[i * tsz:(i + 1) * tsz, :], tsz, 0))
        for i in range(max(1, CTX // tsz)):
            n = min(tsz, CTX)
            out.append((self.xc[i * n:(i + 1) * n, :], n, 1))
        return out

    def norm_part(self, xt, ns, xs_pool, scr, ssq, rstd):
        k = self.k
        for s in range(ns):
            k.act(scr[:, :], xt[:, s, :], AF.Square, [xt], [scr, ssq], accum_out=ssq[:, s:s + 1])
        k.act(rstd[:, :ns], ssq[:, :ns], AF.Sqrt, [ssq, self.epsb], [rstd], bias=self.epsb[:, 0:1], scale=1.0 / D)
        k.op("dve", lambda g: g.reciprocal(rstd[:, :ns], rstd[:, :ns]), [rstd], [rstd])
        xs = xs_pool.get()
        for s in range(ns):
            k.ts("dve", xs[:, s, :], xt[:, s, :], rstd[:, s:s + 1], None, ALU.mult, None, [xt, rstd], [xs])
        return xs

    def transp_part(self, xs, ns, row, G, S, tp_pool, xnT):
        k = self.k
        for c in range(8):
            tp = tp_pool.get()
            for s in range(ns):
                k.tr(tp[:, s * 128:(s + 1) * 128], xs[:, s, c * 128:(c + 1) * 128], self.ident[:, :],
                     [xs, self.ident], [tp], inc=(s == ns - 1))
            k.act(xnT[:, c, :ns * 128], tp[:, :ns * 128], AF.Identity, [tp, G, S], [xnT],
                  bias=S[:, c, row:row + 1], scale=G[:, c, row:row + 1])

    def norm_mod_T(self, xt, ns, row, G, S, xs_pool, tp_pool, xnT, scr, ssq, rstd):
        xs = self.norm_part(xt, ns, xs_pool, scr, ssq, rstd)
        self.transp_part(xs, ns, row, G, S, tp_pool, xnT)

    def phase_ffn(self, l, which):
        k = self.k
        TS = 256
        NS = TS // 128
        win_d = self.inp["ffn_a_in" if which == 0 else "ffn_b_in"]
        wout_d = self.inp["ffn_a_out" if which == 0 else "ffn_b_out"]
        nj, mv = (0, 0) if which == 0 else (2, 6)
        NF = DFF // 128
        with k.phase():
            Win = [k.sb([128, 2 * DFF], BF16, "Win%d" % c) for c in range(8)]
            Wout = k.sb([128, NF, D], BF16, "Wout")
            for c in range(8):
                k.dma("pool", Win[c][:, :], win_d[l, c * 128:(c + 1) * 128, :], writes=[Win[c]])
            k.dma("pool", Wout[:, :, :], wout_d[l, :, :].rearrange("(j p) d -> p j d", p=128), writes=[Wout])
            G, S, gate = self.load_mod_vecs(l, nj, mv, True)
            xpool = k.pool_of(2, [128, NS, D], F32, "xt")
            xs_pool = k.pool_of(1, [128, NS, D], BF16, "xs")
            tp_pool = k.pool_of(2, [128, TS], BF16, "tp", psum=True)
            xnT = k.sb([128, 8, TS], BF16, "xnT")
            hT = k.sb([128, NF, TS], BF16, "hT")
            scr = k.sb([128, D], F32, "scr")
            ssq = k.sb([128, NS], F32, "ssq")
            rstd = k.sb([128, NS], F32, "rstd")
            pa_pool = k.pool_of(2, [128, TS], F32, "pa", psum=True)
            pb_pool = k.pool_of(2, [128, TS], F32, "pb", psum=True)
            sa_pool = k.pool_of(2, [128, TS], BF16, "sa")
            po_pool = k.pool_of(2, [128, 512], F32, "po", psum=True)
            sqj = k.sb([128, D], F32, "sqj")
            tl = self.tiles(TS)

            def load(i):
                xap, ntok, row = tl[i]
                xt = xpool.get()
                k.dma("sp", xt[:, :ntok // 128, :], xap.rearrange("(s p) d -> p s d", p=128), writes=[xt])
                return xt

            xts = {0: load(0)}
            xss = {0: self.norm_part(xts[0], tl[0][1] // 128, xs_pool, sqj, ssq, rstd)}
            self.transp_part(xss[0], tl[0][1] // 128, tl[0][2], G, S, tp_pool, xnT)
            if len(tl) > 1:
                xts[1] = load(1)
            for i, (xap, ntok, row) in enumerate(tl):
                ns = ntok // 128
                xt = xts.pop(i)
                for j in range(NF):
                    pa = pa_pool.get()
                    pb = pb_pool.get()
                    for c in range(8):
                        k.mm(pa[:, :ntok], Win[c][:, j * 128:(j + 1) * 128], xnT[:, c, :ntok], c == 0, c == 7,
                             [Win[c], xnT], [pa])
                    for c in range(8):
                        k.mm(pb[:, :ntok], Win[c][:, DFF + j * 128:DFF + (j + 1) * 128], xnT[:, c, :ntok], c == 0, c == 7,
                             [Win[c], xnT], [pb])
                    sa = sa_pool.get()
                    k.act(sa[:, :ntok], pa[:, :ntok], AF.Silu, [pa], [sa])
                    k.tt("dve", hT[:, j, :ntok], sa[:, :ntok], pb[:, :ntok], ALU.mult, [sa, pb], [hT])
                    if j == NF // 2 and i + 1 < len(tl):
                        xss[i + 1] = self.norm_part(xts[i + 1], tl[i + 1][1] // 128, xs_pool, sqj, ssq, rstd)
                if i + 1 < len(tl):
                    self.transp_part(xss.pop(i + 1), tl[i + 1][1] // 128, tl[i + 1][2], G, S, tp_pool, xnT)
                for s in range(ns):
                    for hf in range(2):
                        po = po_pool.get()
                        for j in range(NF):
                            k.mm(po[:, :], hT[:, j, s * 128:(s + 1) * 128], Wout[:, j, hf * 512:(hf + 1) * 512],
                                 j == 0, j == NF - 1, [hT, Wout], [po])
                        sl = slice(hf * 512, (hf + 1) * 512)
                        k.tt("dve", scr[:, sl], po[:, :], gate[:, row, sl], ALU.mult, [po, gate], [scr])
                        k.tt("pool", xt[:, s, sl], xt[:, s, sl], scr[:, sl], ALU.add, [xt, scr], [xt])
                k.dma("sp", xap.rearrange("(s p) d -> p s d", p=128), xt[:, :ns, :], reads=[xt])
                if i + 2 < len(tl):
                    xts[i + 2] = load(i + 2)

    def build(self):
        k = self.k
        self.consts()
        self.init_copy()
        for l in self.cfg.layers:
            self.phase_mod(l)
            self.phase_ffn(l, 0)
            if self.stages == "ffn_a":
                continue
            if self.stages != "ffn_only":
                if self.cfg.types[l] == "ret":
                    self.phase_ret(l)
                else:
                    self.phase_even(l)
            if self.stages == "mix":
                continue
            if self.stages == "mix_only" and False:
                continue
            self.phase_ffn(l, 1)
        k.barrier()
        return self.nc


RH = 4
RDK = 256
RDV = 512
RQK = RH * RDK
RV = RH * RDV
RIN = 2 * RQK + 2 * RV


def _ret_setup(self):
    nc = self.nc
    L = self.cfg.L
    NT = L + CTX
    NCH = NT // 128
    self.ret_in = self.inp_add("ret_in", [max(1, self.cfg.types.count("ret")), D, RIN])
    self.ret_out = self.inp_add("ret_out", [max(1, self.cfg.types.count("ret")), RV, D])
    self.ret_lf = self.inp_add("ret_logit_f", [max(1, self.cfg.types.count("ret")), RH])
    self.ret_lb = self.inp_add("ret_logit_b", [max(1, self.cfg.types.count("ret")), RH])
    self.rope = self.inp_add("rope_tab", [L, 2, 128])
    self.rconst = self.inp_add("ret_const", [128, 6, 128])
    self.qts = nc.dram_tensor("qts", [NCH, 128, 1024], BF16).ap()
    self.kts = nc.dram_tensor("kts", [NCH, 128, 1024], BF16).ap()
    self.ktok = nc.dram_tensor("ktok", [NT, RQK], BF16).ap()
    self.vtok = nc.dram_tensor("vtok", [NT, RV], BF16).ap()
    self.sgt = nc.dram_tensor("sgt", [NT, RV], BF16).ap()
    self.st = nc.dram_tensor("st", [2, NCH, RH, 128, 1024], BF16).ap()


def ret_consts_host():
    j = np.arange(128, dtype=np.float32)
    c = np.zeros((128, 6, 128), np.float32)
    diff = j[None, :] - j[:, None]
    c[:, 0, :] = np.maximum(diff, 0.0)
    c[:, 1, :] = np.maximum(-diff, 0.0)
    c[:, 2, :] = (diff >= 0).astype(np.float32) / 16.0
    c[:, 3, :] = (diff <= 0).astype(np.float32) / 16.0
    c[:, 4, :] = (j[None, :] + 1.0)
    c[:, 5, :] = (128.0 - j[None, :])
    return c


def _phase_ret(self, l):
    k = self.k
    nc = self.nc
    o = self.cfg.types[:l].count("ret")
    L = self.cfg.L
    NT = L + CTX
    NCH = NT // 128
    NCL = L // 128
    last = (l == self.cfg.depth - 1) and not getattr(self, "force_ctx_out", False)

    with k.phase():
        Wr = [k.sb([128, RIN], BF16, "Wr%d" % c) for c in range(8)]
        for c in range(8):
            k.dma("pool", Wr[c][:, :], self.ret_in[o, c * 128:(c + 1) * 128, :], writes=[Wr[c]])
        G, S, gate = self.load_mod_vecs(l, 1, 3, False)
        TS = 256
        xpool = k.pool_of(2, [128, 2, D], F32, "xt")
        xs_pool = k.pool_of(1, [128, 2, D], BF16, "xs")
        tp_pool = k.pool_of(2, [128, TS], BF16, "tp", psum=True)
        xnT = k.sb([128, 8, TS], BF16, "xnT")
        scr = k.sb([128, D], F32, "scr")
        ssq = k.sb([128, 2], F32, "ssq")
        rstd = k.sb([128, 2], F32, "rstd")
        pp = k.pool_of(3, [128, 512], F32, "pp", psum=True)
        tq_pool = k.pool_of(2, [128, 1024], BF16, "tq", psum=True)
        rope_pool = k.pool_of(2, [128, 2, 128], F32, "rope")
        qtok_pool = k.pool_of(2, [128, RQK], BF16, "qtok")
        ktok_pool = k.pool_of(2, [128, RQK], BF16, "ktok")
        v_pool = k.pool_of(2, [128, RV], BF16, "vst")
        g_pool = k.pool_of(2, [128, RV], BF16, "gst")
        qT_pool = k.pool_of(2, [128, 1024], BF16, "qTs")
        kT_pool = k.pool_of(2, [128, 1024], BF16, "kTs")
        tmp = [k.sb([128, 256], F32, "rt%d" % i) for i in range(4)]
        for (xap, ntok, row) in self.tiles(TS):
            xt = xpool.get()
            k.dma("sp", xt[:, :, :], xap.rearrange("(s p) d -> p s d", p=128), writes=[xt])
            self.norm_mod_T(xt, 2, row, G, S, xs_pool, tp_pool, xnT, scr, ssq, rstd)
            for s in range(2):
                tok0 = (self.tile_tok0(xap) + s * 128)
                ch = tok0 // 128
                if row == 0:
                    rp = rope_pool.get()
                    k.dma("sp", rp[:, :, :], self.rope[tok0:tok0 + 128, :, :], writes=[rp])
                qtok = qtok_pool.get()
                ktok = ktok_pool.get()
                vst = v_pool.get()
                gst = g_pool.get()
                for n in range(12):
                    p = pp.get()
                    for c in range(8):
                        k.mm(p[:, :], xnT[:, c, s * 128:(s + 1) * 128], Wr[c][:, n * 512:(n + 1) * 512], c == 0, c == 7,
                             [xnT, Wr[c]], [p])
                    if n < 4:
                        dst = qtok if n < 2 else ktok
                        cs = (n % 2) * 512
                        if row == 0:
                            pv = p[:, :].rearrange("p (h g f d) -> p h g f d", h=2, g=2, f=2)
                            dv = dst[:, cs:cs + 512].rearrange("p (h g f d) -> p h g f d", h=2, g=2, f=2)
                            ct = rp[:, 0, :].rearrange("p (g d) -> p g d", g=2).unsqueeze(1).broadcast_to([128, 2, 2, 64])
                            sn = rp[:, 1, :].rearrange("p (g d) -> p g d", g=2).unsqueeze(1).broadcast_to([128, 2, 2, 64])
                            tv = [t[:, :].rearrange("p (h g d) -> p h g d", h=2, g=2) for t in tmp]
                            k.tt("dve", tv[0], pv[:, :, :, 0, :], ct, ALU.mult, [p, rp], [tmp[0]])
                            k.tt("dve", tv[1], pv[:, :, :, 1, :], sn, ALU.mult, [p, rp], [tmp[1]])
                            k.tt("dve", tv[2], pv[:, :, :, 0, :], sn, ALU.mult, [p, rp], [tmp[2]])
                            k.tt("dve", tv[3], pv[:, :, :, 1, :], ct, ALU.mult, [p, rp], [tmp[3]])
                            k.tt("pool", dv[:, :, :, 0, :], tv[0], tv[1], ALU.subtract, [tmp[0], tmp[1]], [dst])
                            k.tt("pool", dv[:, :, :, 1, :], tv[2], tv[3], ALU.add, [tmp[2], tmp[3]], [dst])
                        else:
                            k.copy("act", dst[:, cs:cs + 512], p[:, :], [p], [dst])
                    elif n < 8:
                        k.copy("act", vst[:, (n - 4) * 512:(n - 3) * 512], p[:, :], [p], [vst])
                    else:
                        k.act(gst[:, (n - 8) * 512:(n - 7) * 512], p[:, :], AF.Silu, [p], [gst])
                for (src, dpool, dscr) in ((qtok, qT_pool, self.qts), (ktok, kT_pool, self.kts)):
                    tq = tq_pool.get()
                    for b in range(8):
                        k.tr(tq[:, b * 128:(b + 1) * 128], src[:, b * 128:(b + 1) * 128], self.ident[:, :],
                             [src, self.ident], [tq], inc=(b == 7))
                    dT = dpool.get()
                    k.copy("dve" if src is qtok else "act", dT[:, :], tq[:, :], [tq], [dT])
                    k.dma("sp", dscr[ch, :, :], dT[:, :], reads=[dT])
                k.dma("sp", self.ktok[tok0:tok0 + 128, :], ktok[:, :], reads=[ktok])
                k.dma("sp", self.vtok[tok0:tok0 + 128, :], vst[:, :], reads=[vst])
                k.dma("sp", self.sgt[tok0:tok0 + 128, :], gst[:, :], reads=[gst])

    with k.phase():
        rc = k.sb([128, 6, 128], F32, "rconst")
        k.dma("sp", rc[:, :, :], self.rconst, writes=[rc])
        lg = k.sb([128, 2 * RH], F32, "lg")
        k.dma("sp", lg[:, 0:RH], bcast_rows(self.ret_lf[o:o + 1, :]), writes=[lg])
        k.dma("sp", lg[:, RH:2 * RH], bcast_rows(self.ret_lb[o:o + 1, :]), writes=[lg])
        k.act(lg[:, :], lg[:, :], AF.Exp, [lg], [lg], scale=-1.0)
        k.act(lg[:, :], lg[:, :], AF.Ln, [lg], [lg], bias=1.0, scale=1.0)
        k.ts("dve", lg[:, :], lg[:, :], -1.0, None, ALU.mult, None, [lg], [lg])
        zeta = k.sb([128, 2 * RH], F32, "zeta")
        gch = k.sb([128, 2 * RH], F32, "gch")
        XI = k.sb([128, 2 * RH, 128], BF16, "XI")
        DcT = k.sb([128, RH, 128], BF16, "DcT")
        dtmp = k.sb([128, 2, 128], F32, "dtmp")
        for e in range(RH):
            f, b = e, RH + e
            k.act(XI[:, f, :], rc[:, 4, :], AF.Exp, [rc, lg], [XI], scale=lg[:, f:f + 1])
            k.act(XI[:, b, :], rc[:, 5, :], AF.Exp, [rc, lg], [XI], scale=lg[:, b:b + 1])
            k.act(dtmp[:, 0, :], rc[:, 0, :], AF.Exp, [rc, lg], [dtmp], scale=lg[:, f:f + 1])
            k.act(dtmp[:, 1, :], rc[:, 1, :], AF.Exp, [rc, lg], [dtmp], scale=lg[:, b:b + 1])
            k.tt("dve", dtmp[:, :, :], dtmp[:, :, :], rc[:, 2:4, :], ALU.mult, [dtmp, rc], [dtmp])
            k.tt("dve", DcT[:, e, :], dtmp[:, 0, :], dtmp[:, 1, :], ALU.add, [dtmp], [DcT])
        for e in range(RH):
            k.act(zeta[:, e:e + 1], rc[:, 0, 127:128], AF.Exp, [rc, lg], [zeta], scale=lg[:, e:e + 1])
            k.act(zeta[:, RH + e:RH + e + 1], rc[:, 1, 0:1], AF.Exp, [rc, lg], [zeta], scale=lg[:, RH + e:RH + e + 1])
        k.ts("dve", zeta[:, :], zeta[:, :], 1.0 / 16.0, None, ALU.mult, None, [zeta], [zeta])
        k.act(gch[:, :], lg[:, :], AF.Exp, [lg], [gch], scale=128.0)

        with k.phase():
            Sst = [[k.sb([128, 2, RDV], F32, "S%d%d" % (d, e)) for e in range(RH)] for d in range(2)]
            Sbf = [[k.pool_of(2, [128, 2 * RDV], BF16, "Sb%d%d" % (d, e)) for e in range(RH)] for d in range(2)]
            for d in range(2):
                for e in range(RH):
                    k.memset("pool", Sst[d][e][:, :, :], 0.0, [Sst[d][e]])
            kin = k.pool_of(4, [128, RQK], BF16, "kin")
            vin = k.pool_of(4, [128, RV], BF16, "vin")
            kz_pool = k.pool_of(3, [128, RQK], BF16, "kz")
            pd = k.pool_of(6, [128, RDV], F32, "pd", psum=True)
            order_f = [NCL, NCL + 1] + list(range(NCL))
            order_b = [NCL + 1, NCL] + list(range(NCL - 1, -1, -1))
            for step in range(NCH):
                for d, order in ((0, order_f), (1, order_b)):
                    ch = order[step]
                    kt = kin.get()
                    vt = vin.get()
                    k.dma("sp", kt[:, :], self.ktok[ch * 128:(ch + 1) * 128, :], writes=[kt])
                    k.dma("sp", vt[:, :], self.vtok[ch * 128:(ch + 1) * 128, :], writes=[vt])
                    if step < NCH - 1:
                        kz = kz_pool.get()
                        k.tt("dve", kz[:, :].rearrange("p (e x) -> p e x", e=RH), kt[:, :].rearrange("p (e x) -> p e x", e=RH),
                             zeta[:, d * RH:(d + 1) * RH].unsqueeze(2).broadcast_to([128, RH, RDK]), ALU.mult, [kt, zeta], [kz])
                    for e in range(RH):
                        S_ = Sst[d][e]
                        sb_ = Sbf[d][e].get()
                        k.copy("act", sb_[:, :], S_[:, :, :].rearrange("p a b -> p (a b)"), [S_], [sb_])
                        k.dma("sp", self.st[d, ch, e, :, :], sb_[:, :], reads=[sb_])
                        if step == NCH - 1:
                            continue
                        for dc in range(2):
                            p = pd.get()
                            k.mm(p[:, :], kz[:, e * RDK + dc * 128:e * RDK + (dc + 1) * 128], vt[:, e * RDV:(e + 1) * RDV], True, True, [kz, vt], [p])
                            k.stt(S_[:, dc, :], S_[:, dc, :], gch[:, d * RH + e:d * RH + e + 1], p[:, :], ALU.mult, ALU.add,
                                  [S_, gch, p], [S_])

        with k.phase():
            Wo = k.sb([128, 16, D], BF16, "Wo")
            k.dma("pool", Wo[:, :, :], self.ret_out[o, :, :].rearrange("(j p) d -> p j d", p=128), writes=[Wo])
            G, S, gate = self.load_mod_vecs(l, 1, 3, False)
            qT_pool = k.pool_of(2, [128, 8, 128], BF16, "qTc")
            kT_pool = k.pool_of(2, [128, 8, 128], BF16, "kTc")
            v_pool = k.pool_of(2, [128, RV], BF16, "vc")
            g_pool = k.pool_of(2, [128, RV], BF16, "gc")
            st_pool = k.pool_of(2, [128, 2, RH, 2, RDV], BF16, "stc")
            x_pool = k.pool_of(2, [128, D], F32, "xc")
            ps_pool = k.pool_of(2, [128, RH, 128], F32, "psc", psum=True)
            po = [k.ps([128, RDV], F32, "poc%d" % e) for e in range(RH)]
            pt_pool = k.pool_of(1, [128, 1024], BF16, "ptc", psum=True)
            pr_pool = k.pool_of(1, [128, 512], F32, "prc", psum=True)
            in_pool = k.pool_of(2, [128, RH, 128], BF16, "inT")
            qs_pool = k.pool_of(2, [128, 2, 2 * RH, 128], BF16, "qs")
            Y = k.sb([128, RV], BF16, "Y")
            YT = k.sb([128, 16, 128], BF16, "YT")
            stats = k.sb([128, RH, 6], F32, "stats")
            mv = k.sb([128, RH, 2], F32, "mv")
            rs = k.sb([128, RH], F32, "rs")
            nb = k.sb([128, RH], F32, "nb")
            yn = k.sb([128, RV], F32, "yn")
            scr2 = k.sb([128, D], F32, "scr2")
            gneps = k.sb([128, 1], F32, "gneps")
            k.memset("dve", gneps[:, :], GN_EPS, [gneps])
            chunks = list(range(NCL)) + ([] if last else [NCL, NCL + 1])
            for ch in chunks:
                row = 0 if ch < NCL else 1
                xap = self.out[ch * 128:(ch + 1) * 128, :] if row == 0 else self.xc[(ch - NCL) * 128:(ch - NCL + 1) * 128, :]
                qT = qT_pool.get(); kT = kT_pool.get(); vt = v_pool.get(); gt = g_pool.get(); stt_ = st_pool.get(); xt = x_pool.get()
                k.dma("sp", qT[:, :, :], self.qts[ch, :, :].rearrange("p (b t) -> p b t", b=8), writes=[qT])
                k.dma("sp", kT[:, :, :], self.kts[ch, :, :].rearrange("p (b t) -> p b t", b=8), writes=[kT])
                k.dma("sp", vt[:, :], self.vtok[ch * 128:(ch + 1) * 128, :], writes=[vt])
                k.dma("sp", gt[:, :], self.sgt[ch * 128:(ch + 1) * 128, :], writes=[gt])
                for d in range(2):
                    k.dma("sp", stt_[:, d, :, :, :], self.st[d, ch, :, :, :].rearrange("e p (a b) -> p e a b", a=2), writes=[stt_])
                k.dma("sp", xt[:, :], xap, writes=[xt])
                ps = ps_pool.get()
                for e in range(RH):
                    for dc in range(2):
                        k.mm(ps[:, e, :], kT[:, 2 * e + dc, :], qT[:, 2 * e + dc, :], dc == 0, dc == 1, [kT, qT], [ps],
                             inc=(e == RH - 1 and dc == 1))
                inT = in_pool.get()
                k.tt("dve", inT[:, :, :], ps[:, :, :], DcT[:, :, :], ALU.mult, [ps, DcT], [inT])
                qs = qs_pool.get()
                for d in range(2):
                    k.tt("pool", qs[:, d, :, :].rearrange("p (e c) t -> p e c t", e=RH), qT[:, :, :].rearrange("p (e c) t -> p e c t", e=RH),
                         XI[:, d * RH:(d + 1) * RH, :].unsqueeze(2).broadcast_to([128, RH, 2, 128]), ALU.mult, [qT, XI], [qs])
                for e in range(RH):
                    k.mm(po[e][:, :], inT[:, e, :], vt[:, e * RDV:(e + 1) * RDV], True, False, [inT, vt], [po[e]])
                    for d in range(2):
                        for dc in range(2):
                            k.mm(po[e][:, :], qs[:, d, 2 * e + dc, :], stt_[:, d, e, dc, :], False, (d == 1 and dc == 1), [qs, stt_], [po[e]])
                for e in range(RH):
                    k.op("dve", lambda g, e=e: g.bn_stats(stats[:, e, :], po[e][:, :]), [po[e]], [stats])
                for e in range(RH):
                    k.op("dve", lambda g, e=e: g.bn_aggr(mv[:, e, :], stats[:, e, :]), [stats], [mv])
                k.act(rs[:, :], mv[:, :, 1], AF.Sqrt, [mv, gneps], [rs], bias=gneps[:, 0:1], scale=1.0)
                k.op("dve", lambda g: g.reciprocal(rs[:, :], rs[:, :]), [rs], [rs])
                k.stt(nb[:, :], mv[:, :, 0], -1.0, rs[:, :], ALU.mult, ALU.mult, [mv, rs], [nb])
                for e in range(RH):
                    k.act(yn[:, e * RDV:(e + 1) * RDV], po[e][:, :], AF.Identity, [po[e], rs, nb], [yn], bias=nb[:, e:e + 1], scale=rs[:, e:e + 1])
                k.tt("pool", Y[:, :], yn[:, :], gt[:, :], ALU.mult, [yn, gt], [Y])
                for hf in range(2):
                    pt = pt_pool.get()
                    for b in range(8):
                        bb = hf * 8 + b
                        k.tr(pt[:, b * 128:(b + 1) * 128], Y[:, bb * 128:(bb + 1) * 128], self.ident[:, :], [Y, self.ident], [pt],
                             inc=(b == 7))
                    k.copy("dve" if hf == 0 else "act", YT[:, hf * 8:(hf + 1) * 8, :].rearrange("p b t -> p (b t)"), pt[:, :], [pt], [YT])
                for hf in range(2):
                    pr = pr_pool.get()
                    for b in range(16):
                        k.mm(pr[:, :], YT[:, b, :], Wo[:, b, hf * 512:(hf + 1) * 512], b == 0, b == 15, [YT, Wo], [pr])
                    sl = slice(hf * 512, (hf + 1) * 512)
                    k.tt("dve", scr2[:, sl], pr[:, :], gate[:, row, sl], ALU.mult, [pr, gate], [scr2])
                    k.tt("pool", xt[:, sl], xt[:, sl], scr2[:, sl], ALU.add, [xt, scr2], [xt])
                k.dma("sp", xap, xt[:, :], reads=[xt])


Prog.ret_setup = _ret_setup
Prog.phase_ret = _phase_ret


def host_consts(rows):
    L = rows * GRID_W
    out = {}
    out["ident"] = np.eye(128, dtype=np.float32).astype(ml_dtypes.bfloat16)
    t = np.arange(L)
    nf = RDK // 4
    inv = (10000.0 ** (-np.arange(nf, dtype=np.float32) / nf)).astype(np.float32)
    ang = np.concatenate([(t // GRID_W).astype(np.float32)[:, None] * inv, (t % GRID_W).astype(np.float32)[:, None] * inv], axis=-1)
    rope = np.stack([np.cos(ang), np.sin(ang)], axis=1).astype(np.float32)
    out["rope_tab"] = rope
    out["ret_const"] = ret_consts_host()
    for nm, Ls in (("lat", L), ("ctx", CTX)):
        hc = hyena_consts_host(Ls)
        out["dft_" + nm] = hc["dft"]; out["emb_" + nm] = hc["emb"]; out["negt_" + nm] = hc["negt"]; out["wk_" + nm] = hc["wk"]
    max_decay = math.log(1e-2) / 0.3
    min_decay = math.log(1e-2) / 1.5
    out["absdelta"] = np.abs(np.linspace(min_decay, max_decay, HYW, dtype=np.float32))[None, :].astype(np.float32)
    return out


NAH = 8
NAD = 64
NAW = 512
HYW = 512
EIN = 3 * NAW + 3 * HYW
HY_EMB = 17
HY_ORDER = 64
I32 = mybir.dt.int32
TWO_PI = 2.0 * math.pi


def _even_setup(self):
    nc = self.nc
    L = self.cfg.L
    NT = L + CTX
    ne = max(1, self.cfg.types.count("even"))
    a = self.inp_add
    a("even_in", [ne, D, EIN]); a("even_out", [ne, D, D]); a("na_q_gain", [ne, NAD]); a("na_k_gain", [ne, NAD])
    a("na_tab", [ne, NAH, 128, 2, 16, 64])
    a("hy_conv_w", [ne, 3, 3 * HYW]); a("hy_conv_b", [ne, 3 * HYW])
    a("hy_fw1", [ne, HY_EMB, HY_ORDER]); a("hy_fb1", [ne, HY_ORDER]); a("hy_fw2", [ne, HY_ORDER, HY_ORDER]); a("hy_fb2", [ne, HY_ORDER])
    a("hy_fw3", [ne, HY_ORDER, HY_ORDER]); a("hy_fb3", [ne, HY_ORDER]); a("hy_fw4", [ne, HY_ORDER, 2 * HYW]); a("hy_freq", [ne, HY_ORDER])
    a("hy_bias", [ne, HYW])
    for nm, Ls in (("lat", L), ("ctx", CTX)):
        KC = Ls // 128 + 1
        a("dft_" + nm, [2, KC, 128, KC, 128], BF16)
        a("emb_" + nm, [HY_EMB, Ls])
        a("negt_" + nm, [128, Ls // 128])
        a("wk_" + nm, [128, KC, 2])
    a("absdelta", [1, HYW])
    self.qtn = nc.dram_tensor("qtn", [4, 128, NT], BF16).ap()
    self.ktn = nc.dram_tensor("ktn", [4, 128, NT], BF16).ap()
    self.vn = nc.dram_tensor("vn", [NT, NAW], BF16).ap()
    self.u_lat = nc.dram_tensor("u_lat", [L + 2, 3 * HYW], F32).ap()
    self.u_ctx = nc.dram_tensor("u_ctx", [CTX + 2, 3 * HYW], F32).ap()
    self.cat = nc.dram_tensor("cat", [NT, D], BF16).ap()
    self.x0z = nc.dram_tensor("x0z", [NT, 2, HYW], BF16).ap()


def na_tab_host(rpb, rows):
    ne = rpb.shape[0]
    tab = np.full((ne, NAH, 128, 2, 16, 64), -30000.0, np.float32)
    c = np.arange(64)
    cs = np.clip(c - 8, 0, 48)
    cp = np.arange(64)
    colvalid = (cp[:, None] >= cs[None, :]) & (cp[:, None] < cs[None, :] + 16)
    dcidx = np.clip(cp[:, None] - c[None, :] + 15, 0, 30)
    for rk in range(2):
        for jr in range(16):
            dr = rk + 7 - jr
            if abs(dr) > 7:
                continue
            g = rpb[:, :, dr + 7, :][:, :, dcidx]
            g = np.where(colvalid[None, None], g, np.float32(-30000.0))
            tab[:, :, rk * 64:(rk + 1) * 64, 0, jr, :] = g
            if -4 <= dr <= 3:
                tab[:, :, rk * 64:(rk + 1) * 64, 1, jr, :] = g
    return tab


def hyena_consts_host(Ls):
    KC = Ls // 128 + 1
    N = 2 * Ls
    nn = KC * 128
    a = np.arange(nn, dtype=np.int64)
    m = (a[:, None] * a[None, :]) % N
    ang = (2.0 * np.pi / N) * m.astype(np.float64)
    out = {}
    tabs = np.stack([np.cos(ang), np.sin(ang)]).astype(np.float32)
    t5 = tabs.reshape(2, KC, 128, KC, 128).transpose(0, 3, 2, 1, 4)
    out["dft"] = np.ascontiguousarray(t5).astype(ml_dtypes.bfloat16)
    t = np.linspace(0.0, 1.0, Ls, dtype=np.float32)[:, None]
    w = (2.0 * np.float32(math.pi) * np.arange(Ls, dtype=np.float32)[:, None] / np.float32(Ls)).astype(np.float32)
    bands = np.linspace(1e-4, 8 - 1, 8, dtype=np.float32)
    emb = np.concatenate([t, np.cos(bands * w), -np.sin(bands * w)], axis=-1).astype(np.float32)
    out["emb"] = np.ascontiguousarray(emb.T)
    out["negt"] = np.ascontiguousarray((-t[:, 0]).reshape(Ls // 128, 128).T).astype(np.float32)
    k = np.arange(nn)
    wk = np.where((k == 0) | (k == Ls), 1.0, 2.0) / N
    wk = np.where(k <= Ls, wk, 0.0).astype(np.float32)
    wk2 = np.stack([wk, -wk], axis=-1).reshape(KC, 128, 2).transpose(1, 0, 2)
    out["wk"] = np.ascontiguousarray(wk2).astype(np.float32)
    return out


def _phase_even(self, l):
    k = self.k
    e = self.cfg.types[:l].count("even")
    L = self.cfg.L
    NT = L + CTX
    rows = self.cfg.rows
    last = (l == self.cfg.depth - 1) and not getattr(self, "force_ctx_out", False)
    inp = self.inp

    with k.phase():
        We = [k.sb([128, EIN], BF16, "We%d" % c) for c in range(8)]
        for c in range(8):
            k.dma("pool", We[c][:, :], inp["even_in"][e, c * 128:(c + 1) * 128, :], writes=[We[c]])
        G, S, gate = self.load_mod_vecs(l, 1, 3, False)
        zrow = k.sb([1, 3 * HYW], F32, "zrow")
        k.memset("dve", zrow[:, :], 0.0, [zrow])
        for (ut, Ls) in ((self.u_lat, L), (self.u_ctx, CTX)):
            k.dma("sp", ut[0:1, :], zrow[:, :], reads=[zrow])
            k.dma("sp", ut[Ls + 1:Ls + 2, :], zrow[:, :], reads=[zrow])
        gq = k.sb([128, 2, NAD], F32, "gq")
        k.dma("sp", gq[:, 0, :], bcast_rows(inp["na_q_gain"][e:e + 1, :]), writes=[gq])
        k.dma("sp", gq[:, 1, :], bcast_rows(inp["na_k_gain"][e:e + 1, :]), writes=[gq])
        TS = 256
        xpool = k.pool_of(2, [128, 2, D], F32, "xt")
        xs_pool = k.pool_of(1, [128, 2, D], BF16, "xs")
        tp_pool = k.pool_of(2, [128, TS], BF16, "tp", psum=True)
        xnT = k.sb([128, 8, TS], BF16, "xnT")
        scr = k.sb([128, D], F32, "scr")
        ssq = k.sb([128, 2], F32, "ssq")
        rstd = k.sb([128, 2], F32, "rstd")
        pp = k.pool_of(3, [128, 512], F32, "pp", psum=True)
        tq_pool = k.pool_of(2, [128, 512], BF16, "tq", psum=True)
        sq = k.sb([128, 512], F32, "sq")
        hs = k.sb([128, 2, NAH], F32, "hs")
        qn = k.sb([128, 512], F32, "qn")
        qk_tok = k.pool_of(2, [128, 512], BF16, "qktok")
        qkT = k.pool_of(4, [128, 4, 128], BF16, "qkT")
        vst = k.pool_of(2, [128, NAW], BF16, "vst")
        ust = k.pool_of(2, [128, 3 * HYW], F32, "ust")
        for (xap, ntok, row) in self.tiles(TS):
            xt = xpool.get()
            k.dma("sp", xt[:, :, :], xap.rearrange("(s p) d -> p s d", p=128), writes=[xt])
            self.norm_mod_T(xt, 2, row, G, S, xs_pool, tp_pool, xnT, scr, ssq, rstd)
            for s in range(2):
                tok0 = self.tile_tok0(xap) + s * 128
                us = ust.get()
                for n in range(6):
                    p = pp.get()
                    for c in range(8):
                        k.mm(p[:, :], xnT[:, c, s * 128:(s + 1) * 128], We[c][:, n * 512:(n + 1) * 512], c == 0, c == 7,
                             [xnT, We[c]], [p])
                    if n < 2:
                        k.act(sq[:, :], p[:, :], AF.Square, [p], [sq])
                        k.op("dve", lambda g, n=n: g.tensor_reduce(hs[:, n, :], sq[:, :].rearrange("p (h d) -> p h d", h=NAH),
                                                                  AX.X, ALU.add), [sq], [hs])
                        k.act(hs[:, n, :], hs[:, n, :], AF.Sqrt, [hs, self.epsb], [hs], bias=self.epsb[:, 0:1], scale=1.0 / NAD)
                        k.op("dve", lambda g, n=n: g.reciprocal(hs[:, n, :], hs[:, n, :]), [hs], [hs])
                        k.tt("dve", qn[:, :].rearrange("p (h d) -> p h d", h=NAH), p[:, :].rearrange("p (h d) -> p h d", h=NAH),
                             hs[:, n, :].unsqueeze(2).broadcast_to([128, NAH, NAD]), ALU.mult, [p, hs], [qn])
                        qt = qk_tok.get()
                        k.tt("pool", qt[:, :].rearrange("p (h d) -> p h d", h=NAH), qn[:, :].rearrange("p (h d) -> p h d", h=NAH),
                             gq[:, n, :].unsqueeze(1).broadcast_to([128, NAH, NAD]), ALU.mult, [qn, gq], [qt])
                        tq = tq_pool.get()
                        for b in range(4):
                            k.tr(tq[:, b * 128:(b + 1) * 128], qt[:, b * 128:(b + 1) * 128], self.ident[:, :], [qt, self.ident], [tq],
                                 inc=(b == 3))
                        dT = qkT.get()
                        k.copy("act", dT[:, :, :].rearrange("p b t -> p (b t)"), tq[:, :], [tq], [dT])
                        dst = self.qtn if n == 0 else self.ktn
                        k.dma("sp", dst[:, :, tok0:tok0 + 128].rearrange("b p t -> p b t"), dT[:, :, :], reads=[dT])
                    elif n == 2:
                        v_ = vst.get()
                        k.copy("act", v_[:, :], p[:, :], [p], [v_])
                        k.dma("sp", self.vn[tok0:tok0 + 128, :], v_[:, :], reads=[v_])
                    else:
                        k.copy("act" if n % 2 else "dve", us[:, (n - 3) * 512:(n - 2) * 512], p[:, :], [p], [us])
                ut, t0 = (self.u_lat, tok0) if row == 0 else (self.u_ctx, tok0 - L)
                k.dma("sp", ut[1 + t0:1 + t0 + 128, :], us[:, :], reads=[us])

    self.hyena(l, e, "lat", L, self.u_lat, 0)
    if not last:
        self.hyena(l, e, "ctx", CTX, self.u_ctx, L)
    self.na_attention(l, e, last)
    with k.phase():
        Wo = k.sb([128, 8, D], BF16, "Weo")
        k.dma("pool", Wo[:, :, :], inp["even_out"][e, :, :].rearrange("(j p) d -> p j d", p=128), writes=[Wo])
        G, S, gate = self.load_mod_vecs(l, 1, 3, False)
        c_pool = k.pool_of(2, [128, D], BF16, "catc")
        x_pool = k.pool_of(2, [128, D], F32, "xo")
        pt_pool = k.pool_of(2, [128, 1024], BF16, "pto", psum=True)
        pr_pool = k.pool_of(2, [128, 512], F32, "pro", psum=True)
        cT_pool = k.pool_of(2, [128, 8, 128], BF16, "cT")
        scr2 = k.sb([128, D], F32, "scr2")
        nchunks = (L // 128) + (0 if last else CTX // 128)
        for ch in range(nchunks):
            row = 0 if ch < L // 128 else 1
            xap = self.out[ch * 128:(ch + 1) * 128, :] if row == 0 else self.xc[(ch - L // 128) * 128:(ch - L // 128 + 1) * 128, :]
            ct = c_pool.get(); xt = x_pool.get()
            k.dma("sp", ct[:, :], self.cat[ch * 128:(ch + 1) * 128, :], writes=[ct])
            k.dma("sp", xt[:, :], xap, writes=[xt])
            pt = pt_pool.get()
            for b in range(8):
                k.tr(pt[:, b * 128:(b + 1) * 128], ct[:, b * 128:(b + 1) * 128], self.ident[:, :], [ct, self.ident], [pt], inc=(b == 7))
            cT = cT_pool.get()
            k.copy("act", cT[:, :, :].rearrange("p b t -> p (b t)"), pt[:, :], [pt], [cT])
            for hf in range(2):
                pr = pr_pool.get()
                for b in range(8):
                    k.mm(pr[:, :], cT[:, b, :], Wo[:, b, hf * 512:(hf + 1) * 512], b == 0, b == 7, [cT, Wo], [pr])
                sl = slice(hf * 512, (hf + 1) * 512)
                k.tt("dve", scr2[:, sl], pr[:, :], gate[:, row, sl], ALU.mult, [pr, gate], [scr2])
                k.tt("pool", xt[:, sl], xt[:, sl], scr2[:, sl], ALU.add, [xt, scr2], [xt])
            k.dma("sp", xap, xt[:, :], reads=[xt])


def _hyena(self, l, e, nm, Ls, ut, tokbase):
    k = self.k
    inp = self.inp
    NCn = Ls // 128
    KC = NCn + 1
    dft = inp["dft_" + nm]
    with k.phase():
        cw = k.sb([128, 3, 3 * HYW], F32, "cw")
        cb = k.sb([128, 3 * HYW], F32, "cb")
        for tpi in range(3):
            k.dma("sp", cw[:, tpi, :], bcast_rows(inp["hy_conv_w"][e, tpi:tpi + 1, :]), writes=[cw])
        k.dma("sp", cb[:, :], bcast_rows(inp["hy_conv_b"][e:e + 1, :]), writes=[cb])
        upool = k.pool_of(3, [128, 3, 3 * HYW], F32, "uabc")
        t1 = k.pool_of(3, [128, 3 * HYW], F32, "sc1")
        t2 = k.pool_of(3, [128, 3 * HYW], F32, "sc2")
        xz = k.pool_of(3, [128, 2, HYW], BF16, "xz")
        for n in range(NCn):
            u = upool.get()
            for tpi in range(3):
                k.dma("sp", u[:, tpi, :], ut[n * 128 + tpi:n * 128 + tpi + 128, :], writes=[u])
            a_ = t1.get(); b2 = t2.get()
            eg = "pool" if n % 3 == 2 else "dve"
            k.tt(eg, a_[:, :], u[:, 0, :], cw[:, 0, :], ALU.mult, [u, cw], [a_])
            k.tt(eg, b2[:, :], u[:, 1, :], cw[:, 1, :], ALU.mult, [u, cw], [b2])
            k.tt(eg, a_[:, :], a_[:, :], b2[:, :], ALU.add, [a_, b2], [a_])
            k.tt(eg, b2[:, :], u[:, 2, :], cw[:, 2, :], ALU.mult, [u, cw], [b2])
            k.tt(eg, a_[:, :], a_[:, :], b2[:, :], ALU.add, [a_, b2], [a_])
            k.tt(eg, a_[:, :], a_[:, :], cb[:, :], ALU.add, [a_, cb], [a_])
            o_ = xz.get()
            k.copy("act", o_[:, 0, :], a_[:, 0:HYW], [a_], [o_])
            k.tt(eg, o_[:, 1, :], a_[:, HYW:2 * HYW], a_[:, 2 * HYW:3 * HYW], ALU.mult, [a_], [o_])
            k.dma("sp", self.x0z[tokbase + n * 128:tokbase + (n + 1) * 128, :, :], o_[:, :, :], reads=[o_])
    with k.phase():
        wk = k.sb([128, KC, 2], F32, "wk")
        k.dma("sp", wk[:, :, :], inp["wk_" + nm], writes=[wk])
        negt = k.sb([128, NCn], F32, "negt")
        k.dma("sp", negt[:, :], inp["negt_" + nm], writes=[negt])
        KK = k.sb([128, KC, 2, HYW], BF16, "KK")
        KKtok = [Tok("kk%d" % i) for i in range(KC)]
        with k.phase():
            Hf = k.sb([128, NCn, HYW], BF16, "Hf")
            Hb = k.sb([128, NCn, HYW], BF16, "Hb")
            with k.phase():
                CW = min(512, Ls)
                embp = k.pool_of(2, [HY_EMB, CW], F32, "emb")
                fw = [k.sb([HY_EMB, HY_ORDER], F32, "fw1"), k.sb([HY_ORDER, HY_ORDER], F32, "fw2"), k.sb([HY_ORDER, HY_ORDER], F32, "fw3")]
                fw4 = k.sb([HY_ORDER, 2 * HYW], F32, "fw4")
                fbT = k.sb([HY_ORDER, 4], F32, "fbT")
                k.dma("sp", fw[0][:, :], inp["hy_fw1"][e], writes=[fw[0]])
                k.dma("sp", fw[1][:, :], inp["hy_fw2"][e], writes=[fw[1]])
                k.dma("sp", fw[2][:, :], inp["hy_fw3"][e], writes=[fw[2]])
                k.dma("sp", fw4[:, :], inp["hy_fw4"][e], writes=[fw4])
                with self.nc.allow_non_contiguous_dma("tiny"):
                    for i, nmv in enumerate(("hy_fb1", "hy_fb2", "hy_fb3", "hy_freq")):
                        k.dma("sp", fbT[:, i:i + 1], inp[nmv][e:e + 1, :].rearrange("o f -> f o"), writes=[fbT])
                fbias = k.sb([HY_ORDER, 3], F32, "fbias")
                for i in range(3):
                    k.tt("dve", fbias[:, i:i + 1], fbT[:, i:i + 1], fbT[:, 3:4], ALU.mult, [fbT], [fbias])
                adl = k.sb([128, HYW], F32, "adl")
                k.dma("sp", adl[:, :], bcast_rows(inp["absdelta"]), writes=[adl])
                hcur = [k.sb([HY_ORDER, CW], F32, "hmlp%d" % i) for i in range(2)]
                pm = k.pool_of(2, [HY_ORDER, 512], F32, "pm", psum=True)
                pre = k.sb([HY_ORDER, 512], F32, "pre")
                nfl = k.sb([HY_ORDER, 512], F32, "nfl")
                nin = k.sb([HY_ORDER, 512], I32, "nin")
                win = k.pool_of(2, [128, HYW], F32, "win")
                ph = k.pool_of(2, [128, 512], F32, "ph", psum=True)
                for cc in range(Ls // CW):
                    em = embp.get()
                    k.dma("sp", em[:, :], inp["emb_" + nm][:, cc * CW:(cc + 1) * CW], writes=[em])
                    for layer in range(3):
                        src = em if layer == 0 else hcur[(layer - 1) % 2]
                        dst = hcur[layer % 2]
                        p = pm.get()
                        k.mm(p[:, :CW], fw[layer][:, :], src[:, :], True, True, [fw[layer], src], [p])
                        k.act(pre[:, :CW], p[:, :CW], AF.Identity, [p, fbT, fbias], [pre], bias=fbias[:, layer:layer + 1], scale=fbT[:, 3:4])
                        k.ts("dve", nfl[:, :CW], pre[:, :CW], 1.0 / TWO_PI, None, ALU.mult, None, [pre], [nfl])
                        k.copy("dve", nin[:, :CW], nfl[:, :CW], [nfl], [nin])
                        k.copy("dve", nfl[:, :CW], nin[:, :CW], [nin], [nfl])
                        k.stt(pre[:, :CW], nfl[:, :CW], -TWO_PI, pre[:, :CW], ALU.mult, ALU.add, [nfl, pre], [pre])
                        k.ts("dve", pre[:, :CW], pre[:, :CW], 3.1415925, -3.1415925, ALU.min, ALU.max, [pre], [pre])
                        k.act(dst[:, :], pre[:, :CW], AF.Sin, [pre], [dst])
                    h3 = hcur[0]
                    for sub in range(CW // 128):
                        n = cc * (CW // 128) + sub
                        w_ = win.get()
                        k.act(w_[:, :], adl[:, :], AF.Exp, [adl, negt], [w_], scale=negt[:, n:n + 1])
                        for hf, dstH in ((0, Hf), (1, Hb)):
                            p = ph.get()
                            k.mm(p[:, :], h3[:, sub * 128:(sub + 1) * 128], fw4[:, hf * HYW:(hf + 1) * HYW], True, True, [h3, fw4], [p])
                            k.tt("dve", dstH[:, n, :], p[:, :], w_[:, :], ALU.mult, [p, w_], [dstH])
            tabp = k.pool_of(2, [128, 2, NCn, 128], BF16, "tabF")
            pacc = [k.ps([128, 512], F32, "pF%d" % i) for i in range(4)]
            bsb = k.pool_of(2, [128, 2, HYW], F32, "bsb")
            for kc in range(KC):
                tb = tabp.get()
                for cs_ in range(2):
                    k.dma("sp", tb[:, cs_, :, :], dft[cs_, kc, :, 0:NCn, :], writes=[tb])
                for n in range(NCn):
                    for cs_ in range(2):
                        for hi, Hsrc in ((0, Hf), (1, Hb)):
                            k.mm(pacc[cs_ * 2 + hi][:, :], tb[:, cs_, n, :], Hsrc[:, n, :], n == 0, n == NCn - 1, [tb, Hsrc], [pacc[cs_ * 2 + hi]])
                Fc, Bc, Fs, Bs = pacc[0], pacc[1], pacc[2], pacc[3]
                b_ = bsb.get()
                k.act(b_[:, 0, :], Bc[:, :], AF.Identity, [Bc, wk], [b_], scale=wk[:, kc, 0:1])
                k.act(b_[:, 1, :], Bs[:, :], AF.Identity, [Bs, wk], [b_], scale=wk[:, kc, 0:1])
                k.stt(KK[:, kc, 0, :], Fc[:, :], wk[:, kc, 0:1], b_[:, 0, :], ALU.mult, ALU.add, [Fc, wk, b_], [KKtok[kc]])
                k.stt(KK[:, kc, 1, :], Fs[:, :], wk[:, kc, 1:2], b_[:, 1, :], ALU.mult, ALU.add, [Fs, wk, b_], [KKtok[kc]])
        with k.phase():
            z = k.sb([128, NCn, HYW], BF16, "z")
            k.dma("sp", z[:, :, :], self.x0z[tokbase:tokbase + Ls, 1, :].rearrange("(n p) c -> p n c", p=128), writes=[z])
            tabp = k.pool_of(2, [128, 2, NCn, 128], BF16, "tabZ")
            pz = [k.pool_of(2, [128, 512], F32, "pZ%d" % i, psum=True) for i in range(2)]
            tm = [k.pool_of(2, [128, HYW], F32, "tmz%d" % i) for i in range(4)]
            for kc in range(KC):
                tb = tabp.get()
                for cs_ in range(2):
                    k.dma("sp", tb[:, cs_, :, :], dft[cs_, kc, :, 0:NCn, :], writes=[tb])
                Zc = pz[0].get(); Zs = pz[1].get()
                for n in range(NCn):
                    k.mm(Zc[:, :], tb[:, 0, n, :], z[:, n, :], n == 0, n == NCn - 1, [tb, z], [Zc])
                    k.mm(Zs[:, :], tb[:, 1, n, :], z[:, n, :], n == 0, n == NCn - 1, [tb, z], [Zs])
                a1 = tm[0].get(); a2 = tm[1].get(); a3 = tm[2].get(); a4 = tm[3].get()
                kt = KKtok[kc]
                k.tt("dve", a1[:, :], Zc[:, :], KK[:, kc, 0, :], ALU.mult, [Zc, kt], [a1])
                k.tt("dve", a2[:, :], Zs[:, :], KK[:, kc, 1, :], ALU.mult, [Zs, kt], [a2])
                k.tt("dve", a3[:, :], Zs[:, :], KK[:, kc, 0, :], ALU.mult, [Zs, kt], [a3])
                k.tt("dve", a4[:, :], Zc[:, :], KK[:, kc, 1, :], ALU.mult, [Zc, kt], [a4])
                k.tt("pool", KK[:, kc, 0, :], a1[:, :], a2[:, :], ALU.add, [a1, a2], [kt])
                k.tt("pool", KK[:, kc, 1, :], a3[:, :], a4[:, :], ALU.subtract, [a3, a4], [kt])
        with k.phase():
            tabp = k.pool_of(2, [128, 2, KC, 128], BF16, "tabI")
            py = k.pool_of(2, [128, 512], F32, "pY", psum=True)
            db = k.sb([128, HYW], F32, "dbias")
            k.dma("sp", db[:, :], bcast_rows(inp["hy_bias"][e:e + 1, :]), writes=[db])
            e1 = k.pool_of(2, [128, HYW], F32, "e1")
            bo = k.pool_of(2, [128, HYW], BF16, "bo")
            xzp = k.pool_of(2, [128, 2, HYW], BF16, "xzi")
            for tc_ in range(NCn):
                tb = tabp.get()
                for cs_ in range(2):
                    k.dma("sp", tb[:, cs_, :, :], dft[cs_, tc_, :, :, :], writes=[tb])
                xz_ = xzp.get()
                k.dma("sp", xz_[:, :, :], self.x0z[tokbase + tc_ * 128:tokbase + (tc_ + 1) * 128, :, :], writes=[xz_])
                y = py.get()
                for kc in range(KC):
                    k.mm(y[:, :], tb[:, 0, kc, :], KK[:, kc, 0, :], kc == 0, False, [tb, KKtok[kc]], [y])
                    k.mm(y[:, :], tb[:, 1, kc, :], KK[:, kc, 1, :], False, kc == KC - 1, [tb, KKtok[kc]], [y])
                t_ = e1.get()
                k.tt("pool", t_[:, :], xz_[:, 1, :], db[:, :], ALU.mult, [xz_, db], [t_])
                k.tt("dve", t_[:, :], y[:, :], t_[:, :], ALU.add, [y, t_], [t_])
                o_ = bo.get()
                k.tt("dve", o_[:, :], t_[:, :], xz_[:, 0, :], ALU.mult, [t_, xz_], [o_])
                k.dma("sp", self.cat[tokbase + tc_ * 128:tokbase + (tc_ + 1) * 128, NAW:D], o_[:, :], reads=[o_])


Prog.even_setup = _even_setup
Prog.phase_even = _phase_even
Prog.hyena = _hyena


def _na_attention(self, l, e, last):
    k = self.k
    inp = self.inp
    L = self.cfg.L
    NT = L + CTX
    rows = self.cfg.rows
    nP = rows // 2
    NB = L // 128
    with k.phase():
        Ve = k.sb([128, NB + 2, NAH, NAD + 1], BF16, "Ve")
        Vo = k.sb([128, NB - 1, NAH, NAD + 1], BF16, "Vo")
        k.memset("pool", Ve[:, :, :, NAD:NAD + 1], 1.0, [Ve])
        k.memset("pool", Vo[:, :, :, NAD:NAD + 1], 1.0, [Vo])
        for b in range(NB + 2):
            k.dma("sp", Ve[:, b, :, 0:NAD], self.vn[b * 128:(b + 1) * 128, :].rearrange("p (h d) -> p h d", h=NAH), writes=[Ve])
        for b in range(NB - 1):
            k.dma("sp", Vo[:, b, :, 0:NAD], self.vn[64 + b * 128:64 + (b + 1) * 128, :].rearrange("p (h d) -> p h d", h=NAH), writes=[Vo])
        A = k.sb([128, NB + 2, NAW], BF16, "Aall")
        qpool = k.pool_of(2, [128, NT], BF16, "qTn")
        kpool = k.pool_of(2, [128, NT], BF16, "kTn")
        tst = k.pool_of(2, [128, 2, 16, 64], F32, "tst")
        ttp = k.pool_of(2, [128, 2, 16 * 64], BF16, "TT")
        ps_pool = k.pool_of(2, [128, 1024], F32, "psS", psum=True)
        pv_pool = k.pool_of(2, [128, NAD + 1], F32, "psV", psum=True)
        pt_pool = k.pool_of(3, [128, 7 * 128], BF16, "PT")
        rec = k.pool_of(4, [128, 1], F32, "rec")
        for h in range(NAH):
            hp, pb = h // 2, (h % 2) * 64
            if h % 2 == 0:
                qT = qpool.get(); kT = kpool.get()
                k.dma("sp", qT[:, :], self.qtn[hp, :, :], writes=[qT])
                k.dma("sp", kT[:, :], self.ktn[hp, :, :], writes=[kT])
            ts_ = tst.get()
            k.dma("sp", ts_[:, :, :, :], inp["na_tab"][e, h, :, :, :, :], writes=[ts_])
            TT = ttp.get()
            k.act(TT[:, :, :], ts_[:, :, :, :].rearrange("p v j c -> p v (j c)"), AF.Exp, [ts_], [TT])
            units = []
            for i in range(nP):
                r0 = 2 * i
                if i < 2:
                    al, var = [0, 2, 4, 6], 0
                elif i >= nP - 2:
                    al, var = [rows - 8, rows - 6, rows - 4, rows - 2], 0
                else:
                    al, var = [r0 - 4, r0 - 2, r0, r0 + 2, r0 + 4], 1
                units.append((r0 * 64, al, var, i))
            if not last:
                units.append((L, [], 0, NB))
                units.append((L + 128, [], 0, NB + 1))
            for (q0, al, var, ablk) in units:
                M = len(al)
                nb = M + 2
                ps = ps_pool.get()
                for b in range(nb):
                    if b < M:
                        a_ = al[M - 1 - b]
                        ks = a_ * 64
                    else:
                        ks = L + (b - M) * 128
                    k.mm(ps[:, b * 128:(b + 1) * 128], kT[pb:pb + 64, ks:ks + 128], qT[pb:pb + 64, q0:q0 + 128], True, True,
                         [kT, qT], [ps], inc=(b == nb - 1))
                PT = pt_pool.get()
                for b0 in range(0, nb, 4):
                    b1 = min(nb, b0 + 4)
                    k.act(PT[:, b0 * 128:b1 * 128], ps[:, b0 * 128:b1 * 128], AF.Exp, [ps], [PT], scale=NAD ** -0.5)
                if M > 0:
                    r0 = q0 // 64
                    base = 7 - (al[0] - r0) - 2 * (M - 1)
                    k.tt("dve", PT[:, 0:M * 128], PT[:, 0:M * 128], TT[:, var, base * 64:(base + 2 * M) * 64], ALU.mult, [PT, TT], [PT])
                pv = pv_pool.get()
                for b in range(nb):
                    if b < M:
                        a_ = al[M - 1 - b]
                        vt = Ve[:, a_ // 2, h, :] if a_ % 2 == 0 else Vo[:, (a_ - 1) // 2, h, :]
                        vtok = Ve if a_ % 2 == 0 else Vo
                    else:
                        vt = Ve[:, NB + (b - M), h, :]
                        vtok = Ve
                    k.mm(pv[:, :], PT[:, b * 128:(b + 1) * 128], vt, b == 0, b == nb - 1, [PT, vtok], [pv])
                rc = rec.get()
                k.op("dve", lambda g, rc=rc, pv=pv: g.reciprocal(rc[:, :], pv[:, NAD:NAD + 1]), [pv], [rc])
                k.act(A[:, ablk, h * NAD:(h + 1) * NAD], pv[:, 0:NAD], AF.Identity, [pv, rc], [A], scale=rc[:, 0:1])
        nblk = NB + (0 if last else 2)
        for b in range(nblk):
            k.dma("sp", self.cat[b * 128:(b + 1) * 128, 0:NAW], A[:, b, :], reads=[A])


Prog.na_attention = _na_attention


_WEIGHTS = ["w_mod", "b_mod", "norm_gain", "ffn_a_in", "ffn_a_out", "ffn_b_in", "ffn_b_out", "even_in", "even_out",
            "na_q_gain", "na_k_gain", "hy_conv_w", "hy_conv_b", "hy_fw1", "hy_fb1", "hy_fw2", "hy_fb2", "hy_fw3", "hy_fb3",
            "hy_fw4", "hy_freq", "hy_bias", "ret_in", "ret_out", "ret_logit_f", "ret_logit_b"]


def kernel(**inputs):
    rows = 64
    B = inputs["x"].shape[0]
    P = Prog(Cfg(rows=rows, depth=4))
    nc = P.build()
    shared = {n: np.ascontiguousarray(np.asarray(inputs[n], dtype=np.float32)) for n in _WEIGHTS}
    shared.update(host_consts(rows))
    shared["na_tab"] = na_tab_host(np.asarray(inputs["na_rpb"], dtype=np.float32), rows)
    shared["c_ctx"] = np.ascontiguousarray(np.asarray(inputs["c_ctx"], dtype=np.float32)[None, :])
    shared = {n: v for n, v in shared.items() if n in P.inp}
    in_maps = []
    for b in range(B):
        m = dict(shared)
        m["x"] = np.ascontiguousarray(inputs["x"][b], dtype=np.float32)
        m["c"] = np.ascontiguousarray(inputs["c"][b:b + 1], dtype=np.float32)
        m["ctx"] = np.ascontiguousarray(inputs["ctx"][b], dtype=np.float32)
        in_maps.append(m)
    res = run_bass_kernel_spmd(nc, in_maps, core_ids=list(range(B)))
    return np.stack([np.asarray(r["out"], dtype=np.float32) for r in res.results], axis=0)
```

```python
import contextlib
import math
import numpy as np
import ml_dtypes
import concourse.bass as bass
import concourse.mybir as mybir
from concourse.bass_utils import run_bass_kernel_spmd

F32 = mybir.dt.float32
BF16 = mybir.dt.bfloat16
AF = mybir.ActivationFunctionType
ALU = mybir.AluOpType
AX = mybir.AxisListType

D = 1024
DFF = 2816
NMOD = 9
RMS_EPS = 1e-6
GN_EPS = 1e-6
GRID_W = 64
CTX = 256
SAME_ENGINE_SYNC = True


class Tok:
    __slots__ = ("w", "r", "name", "lane", "wb")

    def __init__(self, name=""):
        self.w = None
        self.r = {}
        self.name = name
        self.lane = None
        self.wb = False


class Lane:
    __slots__ = ("sem", "count", "sw")

    def __init__(self, sem):
        self.sem = sem
        self.count = 0
        self.sw = False


class T:
    def __init__(self, h, name):
        self.h = h
        self.tok = Tok(name)

    def __getitem__(self, idx):
        return self.h[idx]


class KB:
    def __init__(self):
        self.nc = bass.Bass("TRN2", target_bir_lowering=False)
        nc = self.nc
        self.es = contextlib.ExitStack()
        self.eng = dict(pe=nc.tensor, act=nc.scalar, dve=nc.vector, pool=nc.gpsimd, sp=nc.sync)
        self.sem = {e: self.es.enter_context(nc.semaphore("s_" + e)) for e in self.eng}
        self.cnt = {e: 0 for e in self.eng}
        self.seen = {e: {} for e in self.eng}
        self.free_lanes = []
        self.free_lanes_sw = []
        self.all_lanes = []
        self.nlanes = 0
        self.phase_stack = []
        self.ninst = 0
        self.uid = 0
        self.pending = []
        self.loads_since = 0

    def _name(self, p):
        self.uid += 1
        return "%s_%d" % (p, self.uid)

    def sb(self, shape, dt, name="t"):
        st = self.phase_stack[-1][0] if self.phase_stack else self.es
        h = st.enter_context(self.nc.sbuf_tensor(self._name(name), list(shape), dt))
        t = T(h, name)
        if self.phase_stack:
            self.phase_stack[-1][1].append(t)
        return t

    def ps(self, shape, dt, name="p"):
        st = self.phase_stack[-1][0] if self.phase_stack else self.es
        h = st.enter_context(self.nc.psum_tensor(self._name(name), list(shape), dt))
        t = T(h, name)
        if self.phase_stack:
            self.phase_stack[-1][1].append(t)
        return t

    def pool_of(self, n, shape, dt, name, psum=False):
        return Ring([(self.ps if psum else self.sb)(shape, dt, name) for _ in range(n)])

    @contextlib.contextmanager
    def phase(self):
        st = contextlib.ExitStack()
        toks = []
        self.phase_stack.append((st, toks))
        try:
            yield
        finally:
            self.barrier()
            self.phase_stack.pop()
            for t in toks:
                if t.tok.lane is not None:
                    (self.free_lanes_sw if t.tok.lane.sw else self.free_lanes).append(t.tok.lane)
                    t.tok.lane = None
            st.close()

    def lane_of(self, tok, sw=False):
        if tok.lane is None:
            fl = self.free_lanes_sw if sw else self.free_lanes
            if fl:
                tok.lane = fl.pop()
            else:
                sem = self.es.enter_context(self.nc.semaphore("l_%d" % self.nlanes))
                self.nlanes += 1
                tok.lane = Lane(sem)
                tok.lane.sw = sw
                self.all_lanes.append(tok.lane)
        assert tok.lane.sw == sw, "token %s mixes SW and HW DMA queues" % tok.name
        return tok.lane

    def _waits(self, e, reads, writes, bulk=False, is_dma=False, skip_sem=None):
        deps = {}
        own = None if is_dma else self.sem.get(e)

        def need(p):
            if p is None:
                return
            s, v = p
            k = id(s)
            if k not in deps or deps[k][1] < v:
                deps[k] = (s, v)

        for t in reads:
            if t.w is not None and t.w[0] is own:
                if e == "pe" or not SAME_ENGINE_SYNC or (bulk and t.wb):
                    continue
            need(t.w)
        for t in writes:
            if not (t.w is not None and (t.w[0] is own or (skip_sem is not None and t.w[0] is skip_sem))):
                need(t.w)
            for p in t.r.values():
                if p[0] is own:
                    continue
                need(p)
        for k, (s, v) in deps.items():
            if self.seen[e].get(k, 0) >= v:
                continue
            self.eng[e].wait_ge(s, v)
            self.seen[e][k] = v
            self.ninst += 1

    @staticmethod
    def _toks(xs):
        out = []
        for x in xs:
            if x is None:
                continue
            out.append(x.tok if isinstance(x, T) else x)
        return out

    def flush_stores(self):
        pend, self.pending = self.pending, []
        for (q, out, in_, reads, kw) in pend:
            self._dma_now(q, out, in_, reads, (), None, kw)
        self.loads_since = 0

    def _maybe_flush(self, writes, is_compute):
        if not self.pending:
            return
        if is_compute and self.loads_since > 0:
            self.flush_stores()
            return
        for (q, out, in_, reads, kw) in self.pending:
            for t in reads:
                if t in writes:
                    self.flush_stores()
                    return

    def op(self, e, emit, reads=(), writes=(), inc=True, bulk=False):
        reads = self._toks(reads)
        writes = self._toks(writes)
        self._maybe_flush(writes, e != "pe" or True)
        self._waits(e, reads, writes, bulk=bulk)
        ins = emit(self.eng[e])
        self.ninst += 1
        if inc:
            self.cnt[e] += 1
            ins.then_inc(self.sem[e], 1)
            me = (self.sem[e], self.cnt[e])
        else:
            me = (self.sem[e], self.cnt[e] + 1)
        for t in reads:
            t.r[e] = me
        for t in writes:
            t.w = me
            t.r = {}
            t.wb = bulk
        return ins

    def dma(self, q, out, in_, reads=(), writes=(), lane_tok=None, **kw):
        reads = self._toks(reads)
        writes = self._toks(writes)
        if not writes and reads and lane_tok is None and q == "sp":
            if len(self.pending) >= 6:
                self.flush_stores()
            self.pending.append((q, out, in_, reads, kw))
            return None
        self._maybe_flush(writes, False)
        self.loads_since += 1
        return self._dma_now(q, out, in_, reads, writes, lane_tok, kw)

    def _dma_now(self, q, out, in_, reads, writes, lane_tok, kw):
        lt = lane_tok.tok if isinstance(lane_tok, T) else lane_tok
        if lt is None:
            lt = writes[0] if writes else reads[0]
        lane = self.lane_of(lt, sw=(q == "pool"))
        self._waits(q, reads, writes, is_dma=True, skip_sem=lane.sem)
        if q == "pool" and lane.count > 0 and self.seen[q].get(id(lane.sem), 0) < lane.count:
            self.eng[q].wait_ge(lane.sem, lane.count)
            self.seen[q][id(lane.sem)] = lane.count
        ins = self.eng[q].dma_start(out=out, in_=in_, **kw)
        self.ninst += 1
        lane.count += 16
        ins.then_inc(lane.sem, 16)
        me = (lane.sem, lane.count)
        for t in reads:
            t.r["dma%d" % id(lane)] = me
        for t in writes:
            t.w = me
            t.r = {}
            t.wb = False
        return ins

    def barrier(self):
        self.flush_stores()
        for e in self.eng:
            for f in self.eng:
                if f == e or self.cnt[f] == 0:
                    continue
                if self.seen[e].get(id(self.sem[f]), 0) >= self.cnt[f]:
                    continue
                self.eng[e].wait_ge(self.sem[f], self.cnt[f])
                self.seen[e][id(self.sem[f])] = self.cnt[f]
            for ln in self.all_lanes:
                if ln.count == 0 or self.seen[e].get(id(ln.sem), 0) >= ln.count:
                    continue
                self.eng[e].wait_ge(ln.sem, ln.count)
                self.seen[e][id(ln.sem)] = ln.count

    def mm(self, out, lhsT, rhs, start, stop, reads, writes, inc=None):
        if inc is None:
            inc = stop
        return self.op("pe", lambda g: g.matmul(out, lhsT, rhs, start=start, stop=stop), reads, writes, inc=inc)

    def tr(self, out, in_, ident, reads, writes, inc=True):
        return self.op("pe", lambda g: g.transpose(out, in_, ident), reads, writes, inc=inc)

    @staticmethod
    def _bulk(out):
        try:
            return out.free_size() >= 256
        except Exception:
            return False

    def act(self, out, in_, func, reads, writes, bias=None, scale=None, accum_out=None, e="act"):
        kw = {}
        if bias is not None:
            kw["bias"] = bias
        if scale is not None:
            kw["scale"] = scale
        if accum_out is not None:
            kw["accum_out"] = accum_out
        return self.op(e, lambda g: g.activation(out, in_, func, **kw), reads, writes,
                       bulk=(accum_out is None and self._bulk(out)))

    def ts(self, e, out, in0, s1, s2, op0, op1, reads, writes, accum_out=None):
        kw = {}
        if op1 is not None:
            kw["op1"] = op1
        if accum_out is not None:
            kw["accum_out"] = accum_out
        return self.op(e, lambda g: g.tensor_scalar(out, in0, s1, s2, op0, **kw), reads, writes,
                       bulk=(accum_out is None and self._bulk(out)))

    def tt(self, e, out, in0, in1, op, reads, writes):
        return self.op(e, lambda g: g.tensor_tensor(out, in0, in1, op), reads, writes, bulk=self._bulk(out))

    def stt(self, out, in0, scalar, in1, op0, op1, reads, writes, e="dve"):
        return self.op(e, lambda g: g.scalar_tensor_tensor(out, in0, scalar, in1, op0, op1), reads, writes, bulk=self._bulk(out))

    def copy(self, e, out, in_, reads, writes):
        if e == "act":
            return self.op(e, lambda g: g.copy(out, in_), reads, writes, bulk=self._bulk(out))
        return self.op(e, lambda g: g.tensor_copy(out, in_), reads, writes, bulk=self._bulk(out))

    def memset(self, e, ap, val, writes):
        return self.op(e, lambda g: g.memset(ap, val), (), writes, bulk=self._bulk(ap))


class Ring:
    def __init__(self, items):
        self.items = items
        self.i = 0

    def get(self):
        t = self.items[self.i % len(self.items)]
        self.i += 1
        return t


class Cfg:
    def __init__(self, rows=64, depth=4, types=None):
        self.rows = rows
        self.L = rows * GRID_W
        self.depth = depth
        self.types = types if types is not None else [("even" if i % 2 == 0 else "ret") for i in range(depth)]
        self.layers = list(range(depth))


def bcast_rows(ap, n=128):
    return ap.partition_broadcast(n)


class Prog:
    def __init__(self, cfg, stages=None):
        self.cfg = cfg
        self.k = KB()
        self.nc = self.k.nc
        self.stages = stages
        nc = self.nc
        L = cfg.L
        nl = cfg.depth
        ne = (nl + 1) // 2
        no = nl // 2
        self.inp = {}

        def din(name, shape, dt=F32):
            self.inp[name] = nc.dram_tensor(name, list(shape), dt, kind="ExternalInput").ap()
            return self.inp[name]

        din("x", [L, D]); din("c", [1, D]); din("ctx", [CTX, D]); din("c_ctx", [1, D])
        din("w_mod", [nl, D, NMOD * D]); din("b_mod", [nl, NMOD * D]); din("norm_gain", [nl, 3, D])
        din("ffn_a_in", [nl, D, 2 * DFF]); din("ffn_a_out", [nl, DFF, D])
        din("ffn_b_in", [nl, D, 2 * DFF]); din("ffn_b_out", [nl, DFF, D])
        din("ident", [128, 128], BF16)
        self.out = nc.dram_tensor("out", [L, D], F32, kind="ExternalOutput").ap()
        self.xc = nc.dram_tensor("xc_scr", [CTX, D], F32, kind="ExternalOutput").ap()
        self.mod = nc.dram_tensor("mod_scr", [nl, 2, NMOD * D], F32).ap()
        if "ret" in cfg.types:
            self.ret_setup()
        if "even" in cfg.types:
            self.even_setup()

    def inp_add(self, name, shape, dt=F32):
        self.inp[name] = self.nc.dram_tensor(name, list(shape), dt, kind="ExternalInput").ap()
        return self.inp[name]

    def tile_tok0(self, xap):
        off = xap.offset // D
        return off if xap.tensor.name == self.out.tensor.name else self.cfg.L + off

    def consts(self):
        k = self.k
        self.ident = k.sb([128, 128], BF16, "ident")
        k.dma("sp", self.ident[:, :], self.inp["ident"], writes=[self.ident])
        self.epsb = k.sb([128, 1], F32, "eps")
        k.memset("dve", self.epsb[:, :], RMS_EPS, [self.epsb])

    def init_copy(self):
        k = self.k
        L = self.cfg.L
        self.t_xlat = Tok("xlat")
        self.t_xctx = Tok("xctx")
        nchunk = max(1, L // 1024)
        rows = L // nchunk
        for i in range(nchunk):
            k.dma("sp", self.out[i * rows:(i + 1) * rows, :], self.inp["x"][i * rows:(i + 1) * rows, :],
                  writes=[self.t_xlat])
        k.dma("sp", self.xc[:, :], self.inp["ctx"][:, :], writes=[self.t_xctx])
        k.barrier()

    def phase_mod(self, l):
        k = self.k
        with k.phase():
            cT = k.sb([128, 8, 2], F32, "cT")
            with self.nc.allow_non_contiguous_dma("tiny"):
                k.dma("sp", cT[:, :, 0], self.inp["c"].rearrange("o (c p) -> p (o c)", p=128), writes=[cT])
                k.dma("sp", cT[:, :, 1], self.inp["c_ctx"].rearrange("o (c p) -> p (o c)", p=128), writes=[cT])
            sc = k.sb([128, 8, 2], F32, "sc")
            k.act(sc[:, :, :], cT[:, :, :], AF.Silu, [cT], [sc])
            ones2 = k.sb([1, 2], F32, "ones2")
            k.memset("dve", ones2[:, :], 1.0, [ones2])
            brow = k.sb([1, NMOD * D], F32, "brow")
            k.dma("sp", brow[:, :], self.inp["b_mod"][l:l + 1, :], writes=[brow])
            wpool = k.pool_of(3, [128, 8, 512], F32, "wm")
            ppool = k.pool_of(2, [2, 512], F32, "pm", psum=True)
            spool = k.pool_of(2, [2, 512], F32, "sm")
            for n in range(NMOD * D // 512):
                w = wpool.get()
                k.dma("sp", w[:, :, :], self.inp["w_mod"][l, :, n * 512:(n + 1) * 512].rearrange("(c p) n -> p c n", p=128),
                      writes=[w])
                p = ppool.get()
                for c in range(8):
                    k.mm(p[:, :], sc[:, c, :], w[:, c, :], c == 0, False, [sc, w], [p])
                k.mm(p[:, :], ones2[:, :], brow[:, n * 512:(n + 1) * 512], False, True, [ones2, brow], [p])
                s = spool.get()
                k.copy("dve", s[:, :], p[:, :], [p], [s])
                k.dma("sp", self.mod[l, :, n * 512:(n + 1) * 512], s[:, :], reads=[s])

    def load_mod_vecs(self, l, nj, mv, half_gate):
        k = self.k
        G = k.sb([128, 8, 2], F32, "G")
        S = k.sb([128, 8, 2], F32, "S")
        gn = k.sb([128, 8], F32, "gn")
        gate = k.sb([128, 2, D], F32, "gate")
        with self.nc.allow_non_contiguous_dma("tiny"):
            k.dma("sp", gn[:, :], self.inp["norm_gain"][l, nj:nj + 1, :].rearrange("o (c p) -> p (o c)", p=128), writes=[gn])
            for r in range(2):
                k.dma("sp", S[:, :, r], self.mod[l, r:r + 1, mv * D:(mv + 1) * D].rearrange("o (c p) -> p (o c)", p=128), writes=[S])
                k.dma("sp", G[:, :, r], self.mod[l, r:r + 1, (mv + 1) * D:(mv + 2) * D].rearrange("o (c p) -> p (o c)", p=128), writes=[G])
                k.dma("sp", gate[:, r, :], bcast_rows(self.mod[l, r:r + 1, (mv + 2) * D:(mv + 3) * D]), writes=[gate])
        for r in range(2):
            k.stt(G[:, :, r], G[:, :, r], 1.0, gn[:, :], ALU.add, ALU.mult, [G, gn], [G])
        if half_gate:
            k.ts("dve", gate[:, :, :], gate[:, :, :], 0.5, None, ALU.mult, None, [gate], [gate])
        return G, S, gate

    def tiles(self, tsz):
        out = []
        for i in range(self.cfg.L // tsz):
            out.append((self.out[i * tsz:(i + 1) * tsz, :], tsz, 0))
        for i in range(max(1, CTX // tsz)):
            n = min(tsz, CTX)
            out.append((self.xc[i * n:(i + 1) * n, :], n, 1))
        return out

    def norm_part(self, xt, ns, xs_pool, scr, ssq, rstd):
        k = self.k
        for s in range(ns):
            k.act(scr[:, :], xt[:, s, :], AF.Square, [xt], [scr, ssq], accum_out=ssq[:, s:s + 1])
        k.act(rstd[:, :ns], ssq[:, :ns], AF.Sqrt, [ssq, self.epsb], [rstd], bias=self.epsb[:, 0:1], scale=1.0 / D)
        k.op("dve", lambda g: g.reciprocal(rstd[:, :ns], rstd[:, :ns]), [rstd], [rstd])
        xs = xs_pool.get()
        for s in range(ns):
            k.ts("dve", xs[:, s, :], xt[:, s, :], rstd[:, s:s + 1], None, ALU.mult, None, [xt, rstd], [xs])
        return xs

    def transp_part(self, xs, ns, row, G, S, tp_pool, xnT):
        k = self.k
        for c in range(8):
            tp = tp_pool.get()
            for s in range(ns):
                k.tr(tp[:, s * 128:(s + 1) * 128], xs[:, s, c * 128:(c + 1) * 128], self.ident[:, :],
                     [xs, self.ident], [tp], inc=(s == ns - 1))
            k.act(xnT[:, c, :ns * 128], tp[:, :ns * 128], AF.Identity, [tp, G, S], [xnT],
                  bias=S[:, c, row:row + 1], scale=G[:, c, row:row + 1])

    def norm_mod_T(self, xt, ns, row, G, S, xs_pool, tp_pool, xnT, scr, ssq, rstd):
        xs = self.norm_part(xt, ns, xs_pool, scr, ssq, rstd)
        self.transp_part(xs, ns, row, G, S, tp_pool, xnT)

    def phase_ffn(self, l, which):
        k = self.k
        TS = 256
        NS = TS // 128
        win_d = self.inp["ffn_a_in" if which == 0 else "ffn_b_in"]
        wout_d = self.inp["ffn_a_out" if which == 0 else "ffn_b_out"]
        nj, mv = (0, 0) if which == 0 else (2, 6)
        NF = DFF // 128
        with k.phase():
            Win = [k.sb([128, 2 * DFF], BF16, "Win%d" % c) for c in range(8)]
            Wout = k.sb([128, NF, D], BF16, "Wout")
            for c in range(8):
                k.dma("pool", Win[c][:, :], win_d[l, c * 128:(c + 1) * 128, :], writes=[Win[c]])
            k.dma("pool", Wout[:, :, :], wout_d[l, :, :].rearrange("(j p) d -> p j d", p=128), writes=[Wout])
            G, S, gate = self.load_mod_vecs(l, nj, mv, True)
            xpool = k.pool_of(2, [128, NS, D], F32, "xt")
            xs_pool = k.pool_of(1, [128, NS, D], BF16, "xs")
            tp_pool = k.pool_of(2, [128, TS], BF16, "tp", psum=True)
            xnT = k.sb([128, 8, TS], BF16, "xnT")
            hT = k.sb([128, NF, TS], BF16, "hT")
            scr = k.sb([128, D], F32, "scr")
            ssq = k.sb([128, NS], F32, "ssq")
            rstd = k.sb([128, NS], F32, "rstd")
            pa_pool = k.pool_of(2, [128, TS], F32, "pa", psum=True)
            pb_pool = k.pool_of(2, [128, TS], F32, "pb", psum=True)
            sa_pool = k.pool_of(2, [128, TS], BF16, "sa")
            po_pool = k.pool_of(2, [128, 512], F32, "po", psum=True)
            sqj = k.sb([128, D], F32, "sqj")
            tl = self.tiles(TS)

            def load(i):
                xap, ntok, row = tl[i]
                xt = xpool.get()
                k.dma("sp", xt[:, :ntok // 128, :], xap.rearrange("(s p) d -> p s d", p=128), writes=[xt])
                return xt

            xts = {0: load(0)}
            xss = {0: self.norm_part(xts[0], tl[0][1] // 128, xs_pool, sqj, ssq, rstd)}
            self.transp_part(xss[0], tl[0][1] // 128, tl[0][2], G, S, tp_pool, xnT)
            if len(tl) > 1:
                xts[1] = load(1)
            for i, (xap, ntok, row) in enumerate(tl):
                ns = ntok // 128
                xt = xts.pop(i)
                for j in range(NF):
                    pa = pa_pool.get()
                    pb = pb_pool.get()
                    for c in range(8):
                        k.mm(pa[:, :ntok], Win[c][:, j * 128:(j + 1) * 128], xnT[:, c, :ntok], c == 0, c == 7,
                             [Win[c], xnT], [pa])
                    for c in range(8):
                        k.mm(pb[:, :ntok], Win[c][:, DFF + j * 128:DFF + (j + 1) * 128], xnT[:, c, :ntok], c == 0, c == 7,
                             [Win[c], xnT], [pb])
                    sa = sa_pool.get()
                    k.act(sa[:, :ntok], pa[:, :ntok], AF.Silu, [pa], [sa])
                    k.tt("dve", hT[:, j, :ntok], sa[:, :ntok], pb[:, :ntok], ALU.mult, [sa, pb], [hT])
                    if j == NF // 2 and i + 1 < len(tl):
                        xss[i + 1] = self.norm_part(xts[i + 1], tl[i + 1][1] // 128, xs_pool, sqj, ssq, rstd)
                if i + 1 < len(tl):
                    self.transp_part(xss.pop(i + 1), tl[i + 1][1] // 128, tl[i + 1][2], G, S, tp_pool, xnT)
                for s in range(ns):
                    for hf in range(2):
                        po = po_pool.get()
                        for j in range(NF):
                            k.mm(po[:, :], hT[:, j, s * 128:(s + 1) * 128], Wout[:, j, hf * 512:(hf + 1) * 512],
                                 j == 0, j == NF - 1, [hT, Wout], [po])
                        sl = slice(hf * 512, (hf + 1) * 512)
                        k.tt("dve", scr[:, sl], po[:, :], gate[:, row, sl], ALU.mult, [po, gate], [scr])
                        k.tt("pool", xt[:, s, sl], xt[:, s, sl], scr[:, sl], ALU.add, [xt, scr], [xt])
                k.dma("sp", xap.rearrange("(s p) d -> p s d", p=128), xt[:, :ns, :], reads=[xt])
                if i + 2 < len(tl):
                    xts[i + 2] = load(i + 2)

    def build(self):
        k = self.k
        self.consts()
        self.init_copy()
        for l in self.cfg.layers:
            self.phase_mod(l)
            self.phase_ffn(l, 0)
            if self.stages == "ffn_a":
                continue
            if self.stages != "ffn_only":
                if self.cfg.types[l] == "ret":
                    self.phase_ret(l)
                else:
                    self.phase_even(l)
            if self.stages == "mix":
                continue
            if self.stages == "mix_only" and False:
                continue
            self.phase_ffn(l, 1)
        k.barrier()
        return self.nc


RH = 4
RDK = 256
RDV = 512
RQK = RH * RDK
RV = RH * RDV
RIN = 2 * RQK + 2 * RV


def _ret_setup(self):
    nc = self.nc
    L = self.cfg.L
    NT = L + CTX
    NCH = NT // 128
    self.ret_in = self.inp_add("ret_in", [max(1, self.cfg.types.count("ret")), D, RIN])
    self.ret_out = self.inp_add("ret_out", [max(1, self.cfg.types.count("ret")), RV, D])
    self.ret_lf = self.inp_add("ret_logit_f", [max(1, self.cfg.types.count("ret")), RH])
    self.ret_lb = self.inp_add("ret_logit_b", [max(1, self.cfg.types.count("ret")), RH])
    self.rope = self.inp_add("rope_tab", [L, 2, 128])
    self.rconst = self.inp_add("ret_const", [128, 6, 128])
    self.qts = nc.dram_tensor("qts", [NCH, 128, 1024], BF16).ap()
    self.kts = nc.dram_tensor("kts", [NCH, 128, 1024], BF16).ap()
    self.ktok = nc.dram_tensor("ktok", [NT, RQK], BF16).ap()
    self.vtok = nc.dram_tensor("vtok", [NT, RV], BF16).ap()
    self.sgt = nc.dram_tensor("sgt", [NT, RV], BF16).ap()
    self.st = nc.dram_tensor("st", [2, NCH, RH, 128, 1024], BF16).ap()


def ret_consts_host():
    j = np.arange(128, dtype=np.float32)
    c = np.zeros((128, 6, 128), np.float32)
    diff = j[None, :] - j[:, None]
    c[:, 0, :] = np.maximum(diff, 0.0)
    c[:, 1, :] = np.maximum(-diff, 0.0)
    c[:, 2, :] = (diff >= 0).astype(np.float32) / 16.0
    c[:, 3, :] = (diff <= 0).astype(np.float32) / 16.0
    c[:, 4, :] = (j[None, :] + 1.0)
    c[:, 5, :] = (128.0 - j[None, :])
    return c


def _phase_ret(self, l):
    k = self.k
    nc = self.nc
    o = self.cfg.types[:l].count("ret")
    L = self.cfg.L
    NT = L + CTX
    NCH = NT // 128
    NCL = L // 128
    last = (l == self.cfg.depth - 1) and not getattr(self, "force_ctx_out", False)

    with k.phase():
        Wr = [k.sb([128, RIN], BF16, "Wr%d" % c) for c in range(8)]
        for c in range(8):
            k.dma("pool", Wr[c][:, :], self.ret_in[o, c * 128:(c + 1) * 128, :], writes=[Wr[c]])
        G, S, gate = self.load_mod_vecs(l, 1, 3, False)
        TS = 256
        xpool = k.pool_of(2, [128, 2, D], F32, "xt")
        xs_pool = k.pool_of(1, [128, 2, D], BF16, "xs")
        tp_pool = k.pool_of(2, [128, TS], BF16, "tp", psum=True)
        xnT = k.sb([128, 8, TS], BF16, "xnT")
        scr = k.sb([128, D], F32, "scr")
        ssq = k.sb([128, 2], F32, "ssq")
        rstd = k.sb([128, 2], F32, "rstd")
        pp = k.pool_of(3, [128, 512], F32, "pp", psum=True)
        tq_pool = k.pool_of(2, [128, 1024], BF16, "tq", psum=True)
        rope_pool = k.pool_of(2, [128, 2, 128], F32, "rope")
        qtok_pool = k.pool_of(2, [128, RQK], BF16, "qtok")
        ktok_pool = k.pool_of(2, [128, RQK], BF16, "ktok")
        v_pool = k.pool_of(2, [128, RV], BF16, "vst")
        g_pool = k.pool_of(2, [128, RV], BF16, "gst")
        qT_pool = k.pool_of(2, [128, 1024], BF16, "qTs")
        kT_pool = k.pool_of(2, [128, 1024], BF16, "kTs")
        tmp = [k.sb([128, 256], F32, "rt%d" % i) for i in range(4)]
        for (xap, ntok, row) in self.tiles(TS):
            xt = xpool.get()
            k.dma("sp", xt[:, :, :], xap.rearrange("(s p) d -> p s d", p=128), writes=[xt])
            self.norm_mod_T(xt, 2, row, G, S, xs_pool, tp_pool, xnT, scr, ssq, rstd)
            for s in range(2):
                tok0 = (self.tile_tok0(xap) + s * 128)
                ch = tok0 // 128
                if row == 0:
                    rp = rope_pool.get()
                    k.dma("sp", rp[:, :, :], self.rope[tok0:tok0 + 128, :, :], writes=[rp])
                qtok = qtok_pool.get()
                ktok = ktok_pool.get()
                vst = v_pool.get()
                gst = g_pool.get()
                for n in range(12):
                    p = pp.get()
                    for c in range(8):
                        k.mm(p[:, :], xnT[:, c, s * 128:(s + 1) * 128], Wr[c][:, n * 512:(n + 1) * 512], c == 0, c == 7,
                             [xnT, Wr[c]], [p])
                    if n < 4:
                        dst = qtok if n < 2 else ktok
                        cs = (n % 2) * 512
                        if row == 0:
                            pv = p[:, :].rearrange("p (h g f d) -> p h g f d", h=2, g=2, f=2)
                            dv = dst[:, cs:cs + 512].rearrange("p (h g f d) -> p h g f d", h=2, g=2, f=2)
                            ct = rp[:, 0, :].rearrange("p (g d) -> p g d", g=2).unsqueeze(1).broadcast_to([128, 2, 2, 64])
                            sn = rp[:, 1, :].rearrange("p (g d) -> p g d", g=2).unsqueeze(1).broadcast_to([128, 2, 2, 64])
                            tv = [t[:, :].rearrange("p (h g d) -> p h g d", h=2, g=2) for t in tmp]
                            k.tt("dve", tv[0], pv[:, :, :, 0, :], ct, ALU.mult, [p, rp], [tmp[0]])
                            k.tt("dve", tv[1], pv[:, :, :, 1, :], sn, ALU.mult, [p, rp], [tmp[1]])
                            k.tt("dve", tv[2], pv[:, :, :, 0, :], sn, ALU.mult, [p, rp], [tmp[2]])
                            k.tt("dve", tv[3], pv[:, :, :, 1, :], ct, ALU.mult, [p, rp], [tmp[3]])
                            k.tt("pool", dv[:, :, :, 0, :], tv[0], tv[1], ALU.subtract, [tmp[0], tmp[1]], [dst])
                            k.tt("pool", dv[:, :, :, 1, :], tv[2], tv[3], ALU.add, [tmp[2], tmp[3]], [dst])
                        else:
                            k.copy("act", dst[:, cs:cs + 512], p[:, :], [p], [dst])
                    elif n < 8:
                        k.copy("act", vst[:, (n - 4) * 512:(n - 3) * 512], p[:, :], [p], [vst])
                    else:
                        k.act(gst[:, (n - 8) * 512:(n - 7) * 512], p[:, :], AF.Silu, [p], [gst])
                for (src, dpool, dscr) in ((qtok, qT_pool, self.qts), (ktok, kT_pool, self.kts)):
                    tq = tq_pool.get()
                    for b in range(8):
                        k.tr(tq[:, b * 128:(b + 1) * 128], src[:, b * 128:(b + 1) * 128], self.ident[:, :],
                             [src, self.ident], [tq], inc=(b == 7))
                    dT = dpool.get()
                    k.copy("dve" if src is qtok else "act", dT[:, :], tq[:, :], [tq], [dT])
                    k.dma("sp", dscr[ch, :, :], dT[:, :], reads=[dT])
                k.dma("sp", self.ktok[tok0:tok0 + 128, :], ktok[:, :], reads=[ktok])
                k.dma("sp", self.vtok[tok0:tok0 + 128, :], vst[:, :], reads=[vst])
                k.dma("sp", self.sgt[tok0:tok0 + 128, :], gst[:, :], reads=[gst])

    with k.phase():
        rc = k.sb([128, 6, 128], F32, "rconst")
        k.dma("sp", rc[:, :, :], self.rconst, writes=[rc])
        lg = k.sb([128, 2 * RH], F32, "lg")
        k.dma("sp", lg[:, 0:RH], bcast_rows(self.ret_lf[o:o + 1, :]), writes=[lg])
        k.dma("sp", lg[:, RH:2 * RH], bcast_rows(self.ret_lb[o:o + 1, :]), writes=[lg])
        k.act(lg[:, :], lg[:, :], AF.Exp, [lg], [lg], scale=-1.0)
        k.act(lg[:, :], lg[:, :], AF.Ln, [lg], [lg], bias=1.0, scale=1.0)
        k.ts("dve", lg[:, :], lg[:, :], -1.0, None, ALU.mult, None, [lg], [lg])
        zeta = k.sb([128, 2 * RH], F32, "zeta")
        gch = k.sb([128, 2 * RH], F32, "gch")
        XI = k.sb([128, 2 * RH, 128], BF16, "XI")
        DcT = k.sb([128, RH, 128], BF16, "DcT")
        dtmp = k.sb([128, 2, 128], F32, "dtmp")
        for e in range(RH):
            f, b = e, RH + e
            k.act(XI[:, f, :], rc[:, 4, :], AF.Exp, [rc, lg], [XI], scale=lg[:, f:f + 1])
            k.act(XI[:, b, :], rc[:, 5, :], AF.Exp, [rc, lg], [XI], scale=lg[:, b:b + 1])
            k.act(dtmp[:, 0, :], rc[:, 0, :], AF.Exp, [rc, lg], [dtmp], scale=lg[:, f:f + 1])
            k.act(dtmp[:, 1, :], rc[:, 1, :], AF.Exp, [rc, lg], [dtmp], scale=lg[:, b:b + 1])
            k.tt("dve", dtmp[:, :, :], dtmp[:, :, :], rc[:, 2:4, :], ALU.mult, [dtmp, rc], [dtmp])
            k.tt("dve", DcT[:, e, :], dtmp[:, 0, :], dtmp[:, 1, :], ALU.add, [dtmp], [DcT])
        for e in range(RH):
            k.act(zeta[:, e:e + 1], rc[:, 0, 127:128], AF.Exp, [rc, lg], [zeta], scale=lg[:, e:e + 1])
            k.act(zeta[:, RH + e:RH + e + 1], rc[:, 1, 0:1], AF.Exp, [rc, lg], [zeta], scale=lg[:, RH + e:RH + e + 1])
        k.ts("dve", zeta[:, :], zeta[:, :], 1.0 / 16.0, None, ALU.mult, None, [zeta], [zeta])
        k.act(gch[:, :], lg[:, :], AF.Exp, [lg], [gch], scale=128.0)

        with k.phase():
            Sst = [[k.sb([128, 2, RDV], F32, "S%d%d" % (d, e)) for e in range(RH)] for d in range(2)]
            Sbf = [[k.pool_of(2, [128, 2 * RDV], BF16, "Sb%d%d" % (d, e)) for e in range(RH)] for d in range(2)]
            for d in range(2):
                for e in range(RH):
                    k.memset("pool", Sst[d][e][:, :, :], 0.0, [Sst[d][e]])
            kin = k.pool_of(4, [128, RQK], BF16, "kin")
            vin = k.pool_of(4, [128, RV], BF16, "vin")
            kz_pool = k.pool_of(3, [128, RQK], BF16, "kz")
            pd = k.pool_of(6, [128, RDV], F32, "pd", psum=True)
            order_f = [NCL, NCL + 1] + list(range(NCL))
            order_b = [NCL + 1, NCL] + list(range(NCL - 1, -1, -1))
            for step in range(NCH):
                for d, order in ((0, order_f), (1, order_b)):
                    ch = order[step]
                    kt = kin.get()
                    vt = vin.get()
                    k.dma("sp", kt[:, :], self.ktok[ch * 128:(ch + 1) * 128, :], writes=[kt])
                    k.dma("sp", vt[:, :], self.vtok[ch * 128:(ch + 1) * 128, :], writes=[vt])
                    if step < NCH - 1:
                        kz = kz_pool.get()
                        k.tt("dve", kz[:, :].rearrange("p (e x) -> p e x", e=RH), kt[:, :].rearrange("p (e x) -> p e x", e=RH),
                             zeta[:, d * RH:(d + 1) * RH].unsqueeze(2).broadcast_to([128, RH, RDK]), ALU.mult, [kt, zeta], [kz])
                    for e in range(RH):
                        S_ = Sst[d][e]
                        sb_ = Sbf[d][e].get()
                        k.copy("act", sb_[:, :], S_[:, :, :].rearrange("p a b -> p (a b)"), [S_], [sb_])
                        k.dma("sp", self.st[d, ch, e, :, :], sb_[:, :], reads=[sb_])
                        if step == NCH - 1:
                            continue
                        for dc in range(2):
                            p = pd.get()
                            k.mm(p[:, :], kz[:, e * RDK + dc * 128:e * RDK + (dc + 1) * 128], vt[:, e * RDV:(e + 1) * RDV], True, True, [kz, vt], [p])
                            k.stt(S_[:, dc, :], S_[:, dc, :], gch[:, d * RH + e:d * RH + e + 1], p[:, :], ALU.mult, ALU.add,
                                  [S_, gch, p], [S_])

        with k.phase():
            Wo = k.sb([128, 16, D], BF16, "Wo")
            k.dma("pool", Wo[:, :, :], self.ret_out[o, :, :].rearrange("(j p) d -> p j d", p=128), writes=[Wo])
            G, S, gate = self.load_mod_vecs(l, 1, 3, False)
            qT_pool = k.pool_of(2, [128, 8, 128], BF16, "qTc")
            kT_pool = k.pool_of(2, [128, 8, 128], BF16, "kTc")
            v_pool = k.pool_of(2, [128, RV], BF16, "vc")
            g_pool = k.pool_of(2, [128, RV], BF16, "gc")
            st_pool = k.pool_of(2, [128, 2, RH, 2, RDV], BF16, "stc")
            x_pool = k.pool_of(2, [128, D], F32, "xc")
            ps_pool = k.pool_of(2, [128, RH, 128], F32, "psc", psum=True)
            po = [k.ps([128, RDV], F32, "poc%d" % e) for e in range(RH)]
            pt_pool = k.pool_of(1, [128, 1024], BF16, "ptc", psum=True)
            pr_pool = k.pool_of(1, [128, 512], F32, "prc", psum=True)
            in_pool = k.pool_of(2, [128, RH, 128], BF16, "inT")
            qs_pool = k.pool_of(2, [128, 2, 2 * RH, 128], BF16, "qs")
            Y = k.sb([128, RV], BF16, "Y")
            YT = k.sb([128, 16, 128], BF16, "YT")
            stats = k.sb([128, RH, 6], F32, "stats")
            mv = k.sb([128, RH, 2], F32, "mv")
            rs = k.sb([128, RH], F32, "rs")
            nb = k.sb([128, RH], F32, "nb")
            yn = k.sb([128, RV], F32, "yn")
            scr2 = k.sb([128, D], F32, "scr2")
            gneps = k.sb([128, 1], F32, "gneps")
            k.memset("dve", gneps[:, :], GN_EPS, [gneps])
            chunks = list(range(NCL)) + ([] if last else [NCL, NCL + 1])
            for ch in chunks:
                row = 0 if ch < NCL else 1
                xap = self.out[ch * 128:(ch + 1) * 128, :] if row == 0 else self.xc[(ch - NCL) * 128:(ch - NCL + 1) * 128, :]
                qT = qT_pool.get(); kT = kT_pool.get(); vt = v_pool.get(); gt = g_pool.get(); stt_ = st_pool.get(); xt = x_pool.get()
                k.dma("sp", qT[:, :, :], self.qts[ch, :, :].rearrange("p (b t) -> p b t", b=8), writes=[qT])
                k.dma("sp", kT[:, :, :], self.kts[ch, :, :].rearrange("p (b t) -> p b t", b=8), writes=[kT])
                k.dma("sp", vt[:, :], self.vtok[ch * 128:(ch + 1) * 128, :], writes=[vt])
                k.dma("sp", gt[:, :], self.sgt[ch * 128:(ch + 1) * 128, :], writes=[gt])
                for d in range(2):
                    k.dma("sp", stt_[:, d, :, :, :], self.st[d, ch, :, :, :].rearrange("e p (a b) -> p e a b", a=2), writes=[stt_])
                k.dma("sp", xt[:, :], xap, writes=[xt])
                ps = ps_pool.get()
                for e in range(RH):
                    for dc in range(2):
                        k.mm(ps[:, e, :], kT[:, 2 * e + dc, :], qT[:, 2 * e + dc, :], dc == 0, dc == 1, [kT, qT], [ps],
                             inc=(e == RH - 1 and dc == 1))
                inT = in_pool.get()
                k.tt("dve", inT[:, :, :], ps[:, :, :], DcT[:, :, :], ALU.mult, [ps, DcT], [inT])
                qs = qs_pool.get()
                for d in range(2):
                    k.tt("pool", qs[:, d, :, :].rearrange("p (e c) t -> p e c t", e=RH), qT[:, :, :].rearrange("p (e c) t -> p e c t", e=RH),
                         XI[:, d * RH:(d + 1) * RH, :].unsqueeze(2).broadcast_to([128, RH, 2, 128]), ALU.mult, [qT, XI], [qs])
                for e in range(RH):
                    k.mm(po[e][:, :], inT[:, e, :], vt[:, e * RDV:(e + 1) * RDV], True, False, [inT, vt], [po[e]])
                    for d in range(2):
                        for dc in range(2):
                            k.mm(po[e][:, :], qs[:, d, 2 * e + dc, :], stt_[:, d, e, dc, :], False, (d == 1 and dc == 1), [qs, stt_], [po[e]])
                for e in range(RH):
                    k.op("dve", lambda g, e=e: g.bn_stats(stats[:, e, :], po[e][:, :]), [po[e]], [stats])
                for e in range(RH):
                    k.op("dve", lambda g, e=e: g.bn_aggr(mv[:, e, :], stats[:, e, :]), [stats], [mv])
                k.act(rs[:, :], mv[:, :, 1], AF.Sqrt, [mv, gneps], [rs], bias=gneps[:, 0:1], scale=1.0)
                k.op("dve", lambda g: g.reciprocal(rs[:, :], rs[:, :]), [rs], [rs])
                k.stt(nb[:, :], mv[:, :, 0], -1.0, rs[:, :], ALU.mult, ALU.mult, [mv, rs], [nb])
                for e in range(RH):
                    k.act(yn[:, e * RDV:(e + 1) * RDV], po[e][:, :], AF.Identity, [po[e], rs, nb], [yn], bias=nb[:, e:e + 1], scale=rs[:, e:e + 1])
                k.tt("pool", Y[:, :], yn[:, :], gt[:, :], ALU.mult, [yn, gt], [Y])
                for hf in range(2):
                    pt = pt_pool.get()
                    for b in range(8):
                        bb = hf * 8 + b
                        k.tr(pt[:, b * 128:(b + 1) * 128], Y[:, bb * 128:(bb + 1) * 128], self.ident[:, :], [Y, self.ident], [pt],
                             inc=(b == 7))
                    k.copy("dve" if hf == 0 else "act", YT[:, hf * 8:(hf + 1) * 8, :].rearrange("p b t -> p (b t)"), pt[:, :], [pt], [YT])
                for hf in range(2):
                    pr = pr_pool.get()
                    for b in range(16):
                        k.mm(pr[:, :], YT[:, b, :], Wo[:, b, hf * 512:(hf + 1) * 512], b == 0, b == 15, [YT, Wo], [pr])
                    sl = slice(hf * 512, (hf + 1) * 512)
                    k.tt("dve", scr2[:, sl], pr[:, :], gate[:, row, sl], ALU.mult, [pr, gate], [scr2])
                    k.tt("pool", xt[:, sl], xt[:, sl], scr2[:, sl], ALU.add, [xt, scr2], [xt])
                k.dma("sp", xap, xt[:, :], reads=[xt])


Prog.ret_setup = _ret_setup
Prog.phase_ret = _phase_ret


def host_consts(rows):
    L = rows * GRID_W
    out = {}
    out["ident"] = np.eye(128, dtype=np.float32).astype(ml_dtypes.bfloat16)
    t = np.arange(L)
    nf = RDK // 4
    inv = (10000.0 ** (-np.arange(nf, dtype=np.float32) / nf)).astype(np.float32)
    ang = np.concatenate([(t // GRID_W).astype(np.float32)[:, None] * inv, (t % GRID_W).astype(np.float32)[:, None] * inv], axis=-1)
    rope = np.stack([np.cos(ang), np.sin(ang)], axis=1).astype(np.float32)
    out["rope_tab"] = rope
    out["ret_const"] = ret_consts_host()
    for nm, Ls in (("lat", L), ("ctx", CTX)):
        hc = hyena_consts_host(Ls)
        out["dft_" + nm] = hc["dft"]; out["emb_" + nm] = hc["emb"]; out["negt_" + nm] = hc["negt"]; out["wk_" + nm] = hc["wk"]
    max_decay = math.log(1e-2) / 0.3
    min_decay = math.log(1e-2) / 1.5
    out["absdelta"] = np.abs(np.linspace(min_decay, max_decay, HYW, dtype=np.float32))[None, :].astype(np.float32)
    return out


NAH = 8
NAD = 64
NAW = 512
HYW = 512
EIN = 3 * NAW + 3 * HYW
HY_EMB = 17
HY_ORDER = 64
I32 = mybir.dt.int32
TWO_PI = 2.0 * math.pi


def _even_setup(self):
    nc = self.nc
    L = self.cfg.L
    NT = L + CTX
    ne = max(1, self.cfg.types.count("even"))
    a = self.inp_add
    a("even_in", [ne, D, EIN]); a("even_out", [ne, D, D]); a("na_q_gain", [ne, NAD]); a("na_k_gain", [ne, NAD])
    a("na_tab", [ne, NAH, 128, 2, 16, 64])
    a("hy_conv_w", [ne, 3, 3 * HYW]); a("hy_conv_b", [ne, 3 * HYW])
    a("hy_fw1", [ne, HY_EMB, HY_ORDER]); a("hy_fb1", [ne, HY_ORDER]); a("hy_fw2", [ne, HY_ORDER, HY_ORDER]); a("hy_fb2", [ne, HY_ORDER])
    a("hy_fw3", [ne, HY_ORDER, HY_ORDER]); a("hy_fb3", [ne, HY_ORDER]); a("hy_fw4", [ne, HY_ORDER, 2 * HYW]); a("hy_freq", [ne, HY_ORDER])
    a("hy_bias", [ne, HYW])
    for nm, Ls in (("lat", L), ("ctx", CTX)):
        KC = Ls // 128 + 1
        a("dft_" + nm, [2, KC, 128, KC, 128], BF16)
        a("emb_" + nm, [HY_EMB, Ls])
        a("negt_" + nm, [128, Ls // 128])
        a("wk_" + nm, [128, KC, 2])
    a("absdelta", [1, HYW])
    self.qtn = nc.dram_tensor("qtn", [4, 128, NT], BF16).ap()
    self.ktn = nc.dram_tensor("ktn", [4, 128, NT], BF16).ap()
    self.vn = nc.dram_tensor("vn", [NT, NAW], BF16).ap()
    self.u_lat = nc.dram_tensor("u_lat", [L + 2, 3 * HYW], F32).ap()
    self.u_ctx = nc.dram_tensor("u_ctx", [CTX + 2, 3 * HYW], F32).ap()
    self.cat = nc.dram_tensor("cat", [NT, D], BF16).ap()
    self.x0z = nc.dram_tensor("x0z", [NT, 2, HYW], BF16).ap()


def na_tab_host(rpb, rows):
    ne = rpb.shape[0]
    tab = np.full((ne, NAH, 128, 2, 16, 64), -30000.0, np.float32)
    c = np.arange(64)
    cs = np.clip(c - 8, 0, 48)
    cp = np.arange(64)
    colvalid = (cp[:, None] >= cs[None, :]) & (cp[:, None] < cs[None, :] + 16)
    dcidx = np.clip(cp[:, None] - c[None, :] + 15, 0, 30)
    for rk in range(2):
        for jr in range(16):
            dr = rk + 7 - jr
            if abs(dr) > 7:
                continue
            g = rpb[:, :, dr + 7, :][:, :, dcidx]
            g = np.where(colvalid[None, None], g, np.float32(-30000.0))
            tab[:, :, rk * 64:(rk + 1) * 64, 0, jr, :] = g
            if -4 <= dr <= 3:
                tab[:, :, rk * 64:(rk + 1) * 64, 1, jr, :] = g
    return tab


def hyena_consts_host(Ls):
    KC = Ls // 128 + 1
    N = 2 * Ls
    nn = KC * 128
    a = np.arange(nn, dtype=np.int64)
    m = (a[:, None] * a[None, :]) % N
    ang = (2.0 * np.pi / N) * m.astype(np.float64)
    out = {}
    tabs = np.stack([np.cos(ang), np.sin(ang)]).astype(np.float32)
    t5 = tabs.reshape(2, KC, 128, KC, 128).transpose(0, 3, 2, 1, 4)
    out["dft"] = np.ascontiguousarray(t5).astype(ml_dtypes.bfloat16)
    t = np.linspace(0.0, 1.0, Ls, dtype=np.float32)[:, None]
    w = (2.0 * np.float32(math.pi) * np.arange(Ls, dtype=np.float32)[:, None] / np.float32(Ls)).astype(np.float32)
    bands = np.linspace(1e-4, 8 - 1, 8, dtype=np.float32)
    emb = np.concatenate([t, np.cos(bands * w), -np.sin(bands * w)], axis=-1).astype(np.float32)
    out["emb"] = np.ascontiguousarray(emb.T)
    out["negt"] = np.ascontiguousarray((-t[:, 0]).reshape(Ls // 128, 128).T).astype(np.float32)
    k = np.arange(nn)
    wk = np.where((k == 0) | (k == Ls), 1.0, 2.0) / N
    wk = np.where(k <= Ls, wk, 0.0).astype(np.float32)
    wk2 = np.stack([wk, -wk], axis=-1).reshape(KC, 128, 2).transpose(1, 0, 2)
    out["wk"] = np.ascontiguousarray(wk2).astype(np.float32)
    return out


def _phase_even(self, l):
    k = self.k
    e = self.cfg.types[:l].count("even")
    L = self.cfg.L
    NT = L + CTX
    rows = self.cfg.rows
    last = (l == self.cfg.depth - 1) and not getattr(self, "force_ctx_out", False)
    inp = self.inp

    with k.phase():
        We = [k.sb([128, EIN], BF16, "We%d" % c) for c in range(8)]
        for c in range(8):
            k.dma("pool", We[c][:, :], inp["even_in"][e, c * 128:(c + 1) * 128, :], writes=[We[c]])
        G, S, gate = self.load_mod_vecs(l, 1, 3, False)
        zrow = k.sb([1, 3 * HYW], F32, "zrow")
        k.memset("dve", zrow[:, :], 0.0, [zrow])
        for (ut, Ls) in ((self.u_lat, L), (self.u_ctx, CTX)):
            k.dma("sp", ut[0:1, :], zrow[:, :], reads=[zrow])
            k.dma("sp", ut[Ls + 1:Ls + 2, :], zrow[:, :], reads=[zrow])
        gq = k.sb([128, 2, NAD], F32, "gq")
        k.dma("sp", gq[:, 0, :], bcast_rows(inp["na_q_gain"][e:e + 1, :]), writes=[gq])
        k.dma("sp", gq[:, 1, :], bcast_rows(inp["na_k_gain"][e:e + 1, :]), writes=[gq])
        TS = 256
        xpool = k.pool_of(2, [128, 2, D], F32, "xt")
        xs_pool = k.pool_of(1, [128, 2, D], BF16, "xs")
        tp_pool = k.pool_of(2, [128, TS], BF16, "tp", psum=True)
        xnT = k.sb([128, 8, TS], BF16, "xnT")
        scr = k.sb([128, D], F32, "scr")
        ssq = k.sb([128, 2], F32, "ssq")
        rstd = k.sb([128, 2], F32, "rstd")
        pp = k.pool_of(3, [128, 512], F32, "pp", psum=True)
        tq_pool = k.pool_of(2, [128, 512], BF16, "tq", psum=True)
        sq = k.sb([128, 512], F32, "sq")
        hs = k.sb([128, 2, NAH], F32, "hs")
        qn = k.sb([128, 512], F32, "qn")
        qk_tok = k.pool_of(2, [128, 512], BF16, "qktok")
        qkT = k.pool_of(4, [128, 4, 128], BF16, "qkT")
        vst = k.pool_of(2, [128, NAW], BF16, "vst")
        ust = k.pool_of(2, [128, 3 * HYW], F32, "ust")
        for (xap, ntok, row) in self.tiles(TS):
            xt = xpool.get()
            k.dma("sp", xt[:, :, :], xap.rearrange("(s p) d -> p s d", p=128), writes=[xt])
            self.norm_mod_T(xt, 2, row, G, S, xs_pool, tp_pool, xnT, scr, ssq, rstd)
            for s in range(2):
                tok0 = self.tile_tok0(xap) + s * 128
                us = ust.get()
                for n in range(6):
                    p = pp.get()
                    for c in range(8):
                        k.mm(p[:, :], xnT[:, c, s * 128:(s + 1) * 128], We[c][:, n * 512:(n + 1) * 512], c == 0, c == 7,
                             [xnT, We[c]], [p])
                    if n < 2:
                        k.act(sq[:, :], p[:, :], AF.Square, [p], [sq])
                        k.op("dve", lambda g, n=n: g.tensor_reduce(hs[:, n, :], sq[:, :].rearrange("p (h d) -> p h d", h=NAH),
                                                                  AX.X, ALU.add), [sq], [hs])
                        k.act(hs[:, n, :], hs[:, n, :], AF.Sqrt, [hs, self.epsb], [hs], bias=self.epsb[:, 0:1], scale=1.0 / NAD)
                        k.op("dve", lambda g, n=n: g.reciprocal(hs[:, n, :], hs[:, n, :]), [hs], [hs])
                        k.tt("dve", qn[:, :].rearrange("p (h d) -> p h d", h=NAH), p[:, :].rearrange("p (h d) -> p h d", h=NAH),
                             hs[:, n, :].unsqueeze(2).broadcast_to([128, NAH, NAD]), ALU.mult, [p, hs], [qn])
                        qt = qk_tok.get()
                        k.tt("pool", qt[:, :].rearrange("p (h d) -> p h d", h=NAH), qn[:, :].rearrange("p (h d) -> p h d", h=NAH),
                             gq[:, n, :].unsqueeze(1).broadcast_to([128, NAH, NAD]), ALU.mult, [qn, gq], [qt])
                        tq = tq_pool.get()
                        for b in range(4):
                            k.tr(tq[:, b * 128:(b + 1) * 128], qt[:, b * 128:(b + 1) * 128], self.ident[:, :], [qt, self.ident], [tq],
                                 inc=(b == 3))
                        dT = qkT.get()
                        k.copy("act", dT[:, :, :].rearrange("p b t -> p (b t)"), tq[:, :], [tq], [dT])
                        dst = self.qtn if n == 0 else self.ktn
                        k.dma("sp", dst[:, :, tok0:tok0 + 128].rearrange("b p t -> p b t"), dT[:, :, :], reads=[dT])
                    elif n == 2:
                        v_ = vst.get()
                        k.copy("act", v_[:, :], p[:, :], [p], [v_])
                        k.dma("sp", self.vn[tok0:tok0 + 128, :], v_[:, :], reads=[v_])
                    else:
                        k.copy("act" if n % 2 else "dve", us[:, (n - 3) * 512:(n - 2) * 512], p[:, :], [p], [us])
                ut, t0 = (self.u_lat, tok0) if row == 0 else (self.u_ctx, tok0 - L)
                k.dma("sp", ut[1 + t0:1 + t0 + 128, :], us[:, :], reads=[us])

    self.hyena(l, e, "lat", L, self.u_lat, 0)
    if not last:
        self.hyena(l, e, "ctx", CTX, self.u_ctx, L)
    self.na_attention(l, e, last)
    with k.phase():
        Wo = k.sb([128, 8, D], BF16, "Weo")
        k.dma("pool", Wo[:, :, :], inp["even_out"][e, :, :].rearrange("(j p) d -> p j d", p=128), writes=[Wo])
        G, S, gate = self.load_mod_vecs(l, 1, 3, False)
        c_pool = k.pool_of(2, [128, D], BF16, "catc")
        x_pool = k.pool_of(2, [128, D], F32, "xo")
        pt_pool = k.pool_of(2, [128, 1024], BF16, "pto", psum=True)
        pr_pool = k.pool_of(2, [128, 512], F32, "pro", psum=True)
        cT_pool = k.pool_of(2, [128, 8, 128], BF16, "cT")
        scr2 = k.sb([128, D], F32, "scr2")
        nchunks = (L // 128) + (0 if last else CTX // 128)
        for ch in range(nchunks):
            row = 0 if ch < L // 128 else 1
            xap = self.out[ch * 128:(ch + 1) * 128, :] if row == 0 else self.xc[(ch - L // 128) * 128:(ch - L // 128 + 1) * 128, :]
            ct = c_pool.get(); xt = x_pool.get()
            k.dma("sp", ct[:, :], self.cat[ch * 128:(ch + 1) * 128, :], writes=[ct])
            k.dma("sp", xt[:, :], xap, writes=[xt])
            pt = pt_pool.get()
            for b in range(8):
                k.tr(pt[:, b * 128:(b + 1) * 128], ct[:, b * 128:(b + 1) * 128], self.ident[:, :], [ct, self.ident], [pt], inc=(b == 7))
            cT = cT_pool.get()
            k.copy("act", cT[:, :, :].rearrange("p b t -> p (b t)"), pt[:, :], [pt], [cT])
            for hf in range(2):
                pr = pr_pool.get()
                for b in range(8):
                    k.mm(pr[:, :], cT[:, b, :], Wo[:, b, hf * 512:(hf + 1) * 512], b == 0, b == 7, [cT, Wo], [pr])
                sl = slice(hf * 512, (hf + 1) * 512)
                k.tt("dve", scr2[:, sl], pr[:, :], gate[:, row, sl], ALU.mult, [pr, gate], [scr2])
                k.tt("pool", xt[:, sl], xt[:, sl], scr2[:, sl], ALU.add, [xt, scr2], [xt])
            k.dma("sp", xap, xt[:, :], reads=[xt])


def _hyena(self, l, e, nm, Ls, ut, tokbase):
    k = self.k
    inp = self.inp
    NCn = Ls // 128
    KC = NCn + 1
    dft = inp["dft_" + nm]
    with k.phase():
        cw = k.sb([128, 3, 3 * HYW], F32, "cw")
        cb = k.sb([128, 3 * HYW], F32, "cb")
        for tpi in range(3):
            k.dma("sp", cw[:, tpi, :], bcast_rows(inp["hy_conv_w"][e, tpi:tpi + 1, :]), writes=[cw])
        k.dma("sp", cb[:, :], bcast_rows(inp["hy_conv_b"][e:e + 1, :]), writes=[cb])
        upool = k.pool_of(3, [128, 3, 3 * HYW], F32, "uabc")
        t1 = k.pool_of(3, [128, 3 * HYW], F32, "sc1")
        t2 = k.pool_of(3, [128, 3 * HYW], F32, "sc2")
        xz = k.pool_of(3, [128, 2, HYW], BF16, "xz")
        for n in range(NCn):
            u = upool.get()
            for tpi in range(3):
                k.dma("sp", u[:, tpi, :], ut[n * 128 + tpi:n * 128 + tpi + 128, :], writes=[u])
            a_ = t1.get(); b2 = t2.get()
            eg = "pool" if n % 3 == 2 else "dve"
            k.tt(eg, a_[:, :], u[:, 0, :], cw[:, 0, :], ALU.mult, [u, cw], [a_])
            k.tt(eg, b2[:, :], u[:, 1, :], cw[:, 1, :], ALU.mult, [u, cw], [b2])
            k.tt(eg, a_[:, :], a_[:, :], b2[:, :], ALU.add, [a_, b2], [a_])
            k.tt(eg, b2[:, :], u[:, 2, :], cw[:, 2, :], ALU.mult, [u, cw], [b2])
            k.tt(eg, a_[:, :], a_[:, :], b2[:, :], ALU.add, [a_, b2], [a_])
            k.tt(eg, a_[:, :], a_[:, :], cb[:, :], ALU.add, [a_, cb], [a_])
            o_ = xz.get()
            k.copy("act", o_[:, 0, :], a_[:, 0:HYW], [a_], [o_])
            k.tt(eg, o_[:, 1, :], a_[:, HYW:2 * HYW], a_[:, 2 * HYW:3 * HYW], ALU.mult, [a_], [o_])
            k.dma("sp", self.x0z[tokbase + n * 128:tokbase + (n + 1) * 128, :, :], o_[:, :, :], reads=[o_])
    with k.phase():
        wk = k.sb([128, KC, 2], F32, "wk")
        k.dma("sp", wk[:, :, :], inp["wk_" + nm], writes=[wk])
        negt = k.sb([128, NCn], F32, "negt")
        k.dma("sp", negt[:, :], inp["negt_" + nm], writes=[negt])
        KK = k.sb([128, KC, 2, HYW], BF16, "KK")
        KKtok = [Tok("kk%d" % i) for i in range(KC)]
        with k.phase():
            Hf = k.sb([128, NCn, HYW], BF16, "Hf")
            Hb = k.sb([128, NCn, HYW], BF16, "Hb")
            with k.phase():
                CW = min(512, Ls)
                embp = k.pool_of(2, [HY_EMB, CW], F32, "emb")
                fw = [k.sb([HY_EMB, HY_ORDER], F32, "fw1"), k.sb([HY_ORDER, HY_ORDER], F32, "fw2"), k.sb([HY_ORDER, HY_ORDER], F32, "fw3")]
                fw4 = k.sb([HY_ORDER, 2 * HYW], F32, "fw4")
                fbT = k.sb([HY_ORDER, 4], F32, "fbT")
                k.dma("sp", fw[0][:, :], inp["hy_fw1"][e], writes=[fw[0]])
                k.dma("sp", fw[1][:, :], inp["hy_fw2"][e], writes=[fw[1]])
                k.dma("sp", fw[2][:, :], inp["hy_fw3"][e], writes=[fw[2]])
                k.dma("sp", fw4[:, :], inp["hy_fw4"][e], writes=[fw4])
                with self.nc.allow_non_contiguous_dma("tiny"):
                    for i, nmv in enumerate(("hy_fb1", "hy_fb2", "hy_fb3", "hy_freq")):
                        k.dma("sp", fbT[:, i:i + 1], inp[nmv][e:e + 1, :].rearrange("o f -> f o"), writes=[fbT])
                fbias = k.sb([HY_ORDER, 3], F32, "fbias")
                for i in range(3):
                    k.tt("dve", fbias[:, i:i + 1], fbT[:, i:i + 1], fbT[:, 3:4], ALU.mult, [fbT], [fbias])
                adl = k.sb([128, HYW], F32, "adl")
                k.dma("sp", adl[:, :], bcast_rows(inp["absdelta"]), writes=[adl])
                hcur = [k.sb([HY_ORDER, CW], F32, "hmlp%d" % i) for i in range(2)]
                pm = k.pool_of(2, [HY_ORDER, 512], F32, "pm", psum=True)
                pre = k.sb([HY_ORDER, 512], F32, "pre")
                nfl = k.sb([HY_ORDER, 512], F32, "nfl")
                nin = k.sb([HY_ORDER, 512], I32, "nin")
                win = k.pool_of(2, [128, HYW], F32, "win")
                ph = k.pool_of(2, [128, 512], F32, "ph", psum=True)
                for cc in range(Ls // CW):
                    em = embp.get()
                    k.dma("sp", em[:, :], inp["emb_" + nm][:, cc * CW:(cc + 1) * CW], writes=[em])
                    for layer in range(3):
                        src = em if layer == 0 else hcur[(layer - 1) % 2]
                        dst = hcur[layer % 2]
                        p = pm.get()
                        k.mm(p[:, :CW], fw[layer][:, :], src[:, :], True, True, [fw[layer], src], [p])
                        k.act(pre[:, :CW], p[:, :CW], AF.Identity, [p, fbT, fbias], [pre], bias=fbias[:, layer:layer + 1], scale=fbT[:, 3:4])
                        k.ts("dve", nfl[:, :CW], pre[:, :CW], 1.0 / TWO_PI, None, ALU.mult, None, [pre], [nfl])
                        k.copy("dve", nin[:, :CW], nfl[:, :CW], [nfl], [nin])
                        k.copy("dve", nfl[:, :CW], nin[:, :CW], [nin], [nfl])
                        k.stt(pre[:, :CW], nfl[:, :CW], -TWO_PI, pre[:, :CW], ALU.mult, ALU.add, [nfl, pre], [pre])
                        k.ts("dve", pre[:, :CW], pre[:, :CW], 3.1415925, -3.1415925, ALU.min, ALU.max, [pre], [pre])
                        k.act(dst[:, :], pre[:, :CW], AF.Sin, [pre], [dst])
                    h3 = hcur[0]
                    for sub in range(CW // 128):
                        n = cc * (CW // 128) + sub
                        w_ = win.get()
                        k.act(w_[:, :], adl[:, :], AF.Exp, [adl, negt], [w_], scale=negt[:, n:n + 1])
                        for hf, dstH in ((0, Hf), (1, Hb)):
                            p = ph.get()
                            k.mm(p[:, :], h3[:, sub * 128:(sub + 1) * 128], fw4[:, hf * HYW:(hf + 1) * HYW], True, True, [h3, fw4], [p])
                            k.tt("dve", dstH[:, n, :], p[:, :], w_[:, :], ALU.mult, [p, w_], [dstH])
            tabp = k.pool_of(2, [128, 2, NCn, 128], BF16, "tabF")
            pacc = [k.ps([128, 512], F32, "pF%d" % i) for i in range(4)]
            bsb = k.pool_of(2, [128, 2, HYW], F32, "bsb")
            for kc in range(KC):
                tb = tabp.get()
                for cs_ in range(2):
                    k.dma("sp", tb[:, cs_, :, :], dft[cs_, kc, :, 0:NCn, :], writes=[tb])
                for n in range(NCn):
                    for cs_ in range(2):
                        for hi, Hsrc in ((0, Hf), (1, Hb)):
                            k.mm(pacc[cs_ * 2 + hi][:, :], tb[:, cs_, n, :], Hsrc[:, n, :], n == 0, n == NCn - 1, [tb, Hsrc], [pacc[cs_ * 2 + hi]])
                Fc, Bc, Fs, Bs = pacc[0], pacc[1], pacc[2], pacc[3]
                b_ = bsb.get()
                k.act(b_[:, 0, :], Bc[:, :], AF.Identity, [Bc, wk], [b_], scale=wk[:, kc, 0:1])
                k.act(b_[:, 1, :], Bs[:, :], AF.Identity, [Bs, wk], [b_], scale=wk[:, kc, 0:1])
                k.stt(KK[:, kc, 0, :], Fc[:, :], wk[:, kc, 0:1], b_[:, 0, :], ALU.mult, ALU.add, [Fc, wk, b_], [KKtok[kc]])
                k.stt(KK[:, kc, 1, :], Fs[:, :], wk[:, kc, 1:2], b_[:, 1, :], ALU.mult, ALU.add, [Fs, wk, b_], [KKtok[kc]])
        with k.phase():
            z = k.sb([128, NCn, HYW], BF16, "z")
            k.dma("sp", z[:, :, :], self.x0z[tokbase:tokbase + Ls, 1, :].rearrange("(n p) c -> p n c", p=128), writes=[z])
            tabp = k.pool_of(2, [128, 2, NCn, 128], BF16, "tabZ")
            pz = [k.pool_of(2, [128, 512], F32, "pZ%d" % i, psum=True) for i in range(2)]
            tm = [k.pool_of(2, [128, HYW], F32, "tmz%d" % i) for i in range(4)]
            for kc in range(KC):
                tb = tabp.get()
                for cs_ in range(2):
                    k.dma("sp", tb[:, cs_, :, :], dft[cs_, kc, :, 0:NCn, :], writes=[tb])
                Zc = pz[0].get(); Zs = pz[1].get()
                for n in range(NCn):
                    k.mm(Zc[:, :], tb[:, 0, n, :], z[:, n, :], n == 0, n == NCn - 1, [tb, z], [Zc])
                    k.mm(Zs[:, :], tb[:, 1, n, :], z[:, n, :], n == 0, n == NCn - 1, [tb, z], [Zs])
                a1 = tm[0].get(); a2 = tm[1].get(); a3 = tm[2].get(); a4 = tm[3].get()
                kt = KKtok[kc]
                k.tt("dve", a1[:, :], Zc[:, :], KK[:, kc, 0, :], ALU.mult, [Zc, kt], [a1])
                k.tt("dve", a2[:, :], Zs[:, :], KK[:, kc, 1, :], ALU.mult, [Zs, kt], [a2])
                k.tt("dve", a3[:, :], Zs[:, :], KK[:, kc, 0, :], ALU.mult, [Zs, kt], [a3])
                k.tt("dve", a4[:, :], Zc[:, :], KK[:, kc, 1, :], ALU.mult, [Zc, kt], [a4])
                k.tt("pool", KK[:, kc, 0, :], a1[:, :], a2[:, :], ALU.add, [a1, a2], [kt])
                k.tt("pool", KK[:, kc, 1, :], a3[:, :], a4[:, :], ALU.subtract, [a3, a4], [kt])
        with k.phase():
            tabp = k.pool_of(2, [128, 2, KC, 128], BF16, "tabI")
            py = k.pool_of(2, [128, 512], F32, "pY", psum=True)
            db = k.sb([128, HYW], F32, "dbias")
            k.dma("sp", db[:, :], bcast_rows(inp["hy_bias"][e:e + 1, :]), writes=[db])
            e1 = k.pool_of(2, [128, HYW], F32, "e1")
            bo = k.pool_of(2, [128, HYW], BF16, "bo")
            xzp = k.pool_of(2, [128, 2, HYW], BF16, "xzi")
            for tc_ in range(NCn):
                tb = tabp.get()
                for cs_ in range(2):
                    k.dma("sp", tb[:, cs_, :, :], dft[cs_, tc_, :, :, :], writes=[tb])
                xz_ = xzp.get()
                k.dma("sp", xz_[:, :, :], self.x0z[tokbase + tc_ * 128:tokbase + (tc_ + 1) * 128, :, :], writes=[xz_])
                y = py.get()
                for kc in range(KC):
                    k.mm(y[:, :], tb[:, 0, kc, :], KK[:, kc, 0, :], kc == 0, False, [tb, KKtok[kc]], [y])
                    k.mm(y[:, :], tb[:, 1, kc, :], KK[:, kc, 1, :], False, kc == KC - 1, [tb, KKtok[kc]], [y])
                t_ = e1.get()
                k.tt("pool", t_[:, :], xz_[:, 1, :], db[:, :], ALU.mult, [xz_, db], [t_])
                k.tt("dve", t_[:, :], y[:, :], t_[:, :], ALU.add, [y, t_], [t_])
                o_ = bo.get()
                k.tt("dve", o_[:, :], t_[:, :], xz_[:, 0, :], ALU.mult, [t_, xz_], [o_])
                k.dma("sp", self.cat[tokbase + tc_ * 128:tokbase + (tc_ + 1) * 128, NAW:D], o_[:, :], reads=[o_])


Prog.even_setup = _even_setup
Prog.phase_even = _phase_even
Prog.hyena = _hyena


def _na_attention(self, l, e, last):
    k = self.k
    inp = self.inp
    L = self.cfg.L
    NT = L + CTX
    rows = self.cfg.rows
    nP = rows // 2
    NB = L // 128
    with k.phase():
        Ve = k.sb([128, NB + 2, NAH, NAD + 1], BF16, "Ve")
        Vo = k.sb([128, NB - 1, NAH, NAD + 1], BF16, "Vo")
        k.memset("pool", Ve[:, :, :, NAD:NAD + 1], 1.0, [Ve])
        k.memset("pool", Vo[:, :, :, NAD:NAD + 1], 1.0, [Vo])
        for b in range(NB + 2):
            k.dma("sp", Ve[:, b, :, 0:NAD], self.vn[b * 128:(b + 1) * 128, :].rearrange("p (h d) -> p h d", h=NAH), writes=[Ve])
        for b in range(NB - 1):
            k.dma("sp", Vo[:, b, :, 0:NAD], self.vn[64 + b * 128:64 + (b + 1) * 128, :].rearrange("p (h d) -> p h d", h=NAH), writes=[Vo])
        A = k.sb([128, NB + 2, NAW], BF16, "Aall")
        qpool = k.pool_of(2, [128, NT], BF16, "qTn")
        kpool = k.pool_of(2, [128, NT], BF16, "kTn")
        tst = k.pool_of(2, [128, 2, 16, 64], F32, "tst")
        ttp = k.pool_of(2, [128, 2, 16 * 64], BF16, "TT")
        ps_pool = k.pool_of(2, [128, 1024], F32, "psS", psum=True)
        pv_pool = k.pool_of(2, [128, NAD + 1], F32, "psV", psum=True)
        pt_pool = k.pool_of(3, [128, 7 * 128], BF16, "PT")
        rec = k.pool_of(4, [128, 1], F32, "rec")
        for h in range(NAH):
            hp, pb = h // 2, (h % 2) * 64
            if h % 2 == 0:
                qT = qpool.get(); kT = kpool.get()
                k.dma("sp", qT[:, :], self.qtn[hp, :, :], writes=[qT])
                k.dma("sp", kT[:, :], self.ktn[hp, :, :], writes=[kT])
            ts_ = tst.get()
            k.dma("sp", ts_[:, :, :, :], inp["na_tab"][e, h, :, :, :, :], writes=[ts_])
            TT = ttp.get()
            k.act(TT[:, :, :], ts_[:, :, :, :].rearrange("p v j c -> p v (j c)"), AF.Exp, [ts_], [TT])
            units = []
            for i in range(nP):
                r0 = 2 * i
                if i < 2:
                    al, var = [0, 2, 4, 6], 0
                elif i >= nP - 2:
                    al, var = [rows - 8, rows - 6, rows - 4, rows - 2], 0
                else:
                    al, var = [r0 - 4, r0 - 2, r0, r0 + 2, r0 + 4], 1
                units.append((r0 * 64, al, var, i))
            if not last:
                units.append((L, [], 0, NB))
                units.append((L + 128, [], 0, NB + 1))
            for (q0, al, var, ablk) in units:
                M = len(al)
                nb = M + 2
                ps = ps_pool.get()
                for b in range(nb):
                    if b < M:
                        a_ = al[M - 1 - b]
                        ks = a_ * 64
                    else:
                        ks = L + (b - M) * 128
                    k.mm(ps[:, b * 128:(b + 1) * 128], kT[pb:pb + 64, ks:ks + 128], qT[pb:pb + 64, q0:q0 + 128], True, True,
                         [kT, qT], [ps], inc=(b == nb - 1))
                PT = pt_pool.get()
                for b0 in range(0, nb, 4):
                    b1 = min(nb, b0 + 4)
                    k.act(PT[:, b0 * 128:b1 * 128], ps[:, b0 * 128:b1 * 128], AF.Exp, [ps], [PT], scale=NAD ** -0.5)
                if M > 0:
                    r0 = q0 // 64
                    base = 7 - (al[0] - r0) - 2 * (M - 1)
                    k.tt("dve", PT[:, 0:M * 128], PT[:, 0:M * 128], TT[:, var, base * 64:(base + 2 * M) * 64], ALU.mult, [PT, TT], [PT])
                pv = pv_pool.get()
                for b in range(nb):
                    if b < M:
                        a_ = al[M - 1 - b]
                        vt = Ve[:, a_ // 2, h, :] if a_ % 2 == 0 else Vo[:, (a_ - 1) // 2, h, :]
                        vtok = Ve if a_ % 2 == 0 else Vo
                    else:
                        vt = Ve[:, NB + (b - M), h, :]
                        vtok = Ve
                    k.mm(pv[:, :], PT[:, b * 128:(b + 1) * 128], vt, b == 0, b == nb - 1, [PT, vtok], [pv])
                rc = rec.get()
                k.op("dve", lambda g, rc=rc, pv=pv: g.reciprocal(rc[:, :], pv[:, NAD:NAD + 1]), [pv], [rc])
                k.act(A[:, ablk, h * NAD:(h + 1) * NAD], pv[:, 0:NAD], AF.Identity, [pv, rc], [A], scale=rc[:, 0:1])
        nblk = NB + (0 if last else 2)
        for b in range(nblk):
            k.dma("sp", self.cat[b * 128:(b + 1) * 128, 0:NAW], A[:, b, :], reads=[A])


Prog.na_attention = _na_attention


_WEIGHTS = ["w_mod", "b_mod", "norm_gain", "ffn_a_in", "ffn_a_out", "ffn_b_in", "ffn_b_out", "even_in", "even_out",
            "na_q_gain", "na_k_gain", "hy_conv_w", "hy_conv_b", "hy_fw1", "hy_fb1", "hy_fw2", "hy_fb2", "hy_fw3", "hy_fb3",
            "hy_fw4", "hy_freq", "hy_bias", "ret_in", "ret_out", "ret_logit_f", "ret_logit_b"]


def kernel(**inputs):
    rows = 64
    B = inputs["x"].shape[0]
    P = Prog(Cfg(rows=rows, depth=4))
    nc = P.build()
    shared = {n: np.ascontiguousarray(np.asarray(inputs[n], dtype=np.float32)) for n in _WEIGHTS}
    shared.update(host_consts(rows))
    shared["na_tab"] = na_tab_host(np.asarray(inputs["na_rpb"], dtype=np.float32), rows)
    shared["c_ctx"] = np.ascontiguousarray(np.asarray(inputs["c_ctx"], dtype=np.float32)[None, :])
    shared = {n: v for n, v in shared.items() if n in P.inp}
    in_maps = []
    for b in range(B):
        m = dict(shared)
        m["x"] = np.ascontiguousarray(inputs["x"][b], dtype=np.float32)
        m["c"] = np.ascontiguousarray(inputs["c"][b:b + 1], dtype=np.float32)
        m["ctx"] = np.ascontiguousarray(inputs["ctx"][b], dtype=np.float32)
        in_maps.append(m)
    res = run_bass_kernel_spmd(nc, in_maps, core_ids=list(range(B)))
    return np.stack([np.asarray(r["out"], dtype=np.float32) for r in res.results], axis=0)
```

```python
import contextlib
import math
import numpy as np
import ml_dtypes
import concourse.bass as bass
import concourse.mybir as mybir
from concourse.bass_utils import run_bass_kernel_spmd

F32 = mybir.dt.float32
BF16 = mybir.dt.bfloat16
AF = mybir.ActivationFunctionType
ALU = mybir.AluOpType
AX = mybir.AxisListType

D = 1024
DFF = 2816
NMOD = 9
RMS_EPS = 1e-6
GN_EPS = 1e-6
GRID_W = 64
CTX = 256
SAME_ENGINE_SYNC = True


class Tok:
    __slots__ = ("w", "r", "name", "lane", "wb")

    def __init__(self, name=""):
        self.w = None
        self.r = {}
        self.name = name
        self.lane = None
        self.wb = False


class Lane:
    __slots__ = ("sem", "count", "sw")

    def __init__(self, sem):
        self.sem = sem
        self.count = 0
        self.sw = False


class T:
    def __init__(self, h, name):
        self.h = h
        self.tok = Tok(name)

    def __getitem__(self, idx):
        return self.h[idx]


class KB:
    def __init__(self):
        self.nc = bass.Bass("TRN2", target_bir_lowering=False)
        nc = self.nc
        self.es = contextlib.ExitStack()
        self.eng = dict(pe=nc.tensor, act=nc.scalar, dve=nc.vector, pool=nc.gpsimd, sp=nc.sync)
        self.sem = {e: self.es.enter_context(nc.semaphore("s_" + e)) for e in self.eng}
        self.cnt = {e: 0 for e in self.eng}
        self.seen = {e: {} for e in self.eng}
        self.free_lanes = []
        self.free_lanes_sw = []
        self.all_lanes = []
        self.nlanes = 0
        self.phase_stack = []
        self.ninst = 0
        self.uid = 0
        self.pending = []
        self.loads_since = 0

    def _name(self, p):
        self.uid += 1
        return "%s_%d" % (p, self.uid)

    def sb(self, shape, dt, name="t"):
        st = self.phase_stack[-1][0] if self.phase_stack else self.es
        h = st.enter_context(self.nc.sbuf_tensor(self._name(name), list(shape), dt))
        t = T(h, name)
        if self.phase_stack:
            self.phase_stack[-1][1].append(t)
        return t

    def ps(self, shape, dt, name="p"):
        st = self.phase_stack[-1][0] if self.phase_stack else self.es
        h = st.enter_context(self.nc.psum_tensor(self._name(name), list(shape), dt))
        t = T(h, name)
        if self.phase_stack:
            self.phase_stack[-1][1].append(t)
        return t

    def pool_of(self, n, shape, dt, name, psum=False):
        return Ring([(self.ps if psum else self.sb)(shape, dt, name) for _ in range(n)])

    @contextlib.contextmanager
    def phase(self):
        st = contextlib.ExitStack()
        toks = []
        self.phase_stack.append((st, toks))
        try:
            yield
        finally:
            self.barrier()
            self.phase_stack.pop()
            for t in toks:
                if t.tok.lane is not None:
                    (self.free_lanes_sw if t.tok.lane.sw else self.free_lanes).append(t.tok.lane)
                    t.tok.lane = None
            st.close()

    def lane_of(self, tok, sw=False):
        if tok.lane is None:
            fl = self.free_lanes_sw if sw else self.free_lanes
            if fl:
                tok.lane = fl.pop()
            else:
                sem = self.es.enter_context(self.nc.semaphore("l_%d" % self.nlanes))
                self.nlanes += 1
                tok.lane = Lane(sem)
                tok.lane.sw = sw
                self.all_lanes.append(tok.lane)
        assert tok.lane.sw == sw, "token %s mixes SW and HW DMA queues" % tok.name
        return tok.lane

    def _waits(self, e, reads, writes, bulk=False, is_dma=False, skip_sem=None):
        deps = {}
        own = None if is_dma else self.sem.get(e)

        def need(p):
            if p is None:
                return
            s, v = p
            k = id(s)
            if k not in deps or deps[k][1] < v:
                deps[k] = (s, v)

        for t in reads:
            if t.w is not None and t.w[0] is own:
                if e == "pe" or not SAME_ENGINE_SYNC or (bulk and t.wb):
                    continue
            need(t.w)
        for t in writes:
            if not (t.w is not None and (t.w[0] is own or (skip_sem is not None and t.w[0] is skip_sem))):
                need(t.w)
            for p in t.r.values():
                if p[0] is own:
                    continue
                need(p)
        for k, (s, v) in deps.items():
            if self.seen[e].get(k, 0) >= v:
                continue
            self.eng[e].wait_ge(s, v)
            self.seen[e][k] = v
            self.ninst += 1

    @staticmethod
    def _toks(xs):
        out = []
        for x in xs:
            if x is None:
                continue
            out.append(x.tok if isinstance(x, T) else x)
        return out

    def flush_stores(self):
        pend, self.pending = self.pending, []
        for (q, out, in_, reads, kw) in pend:
            self._dma_now(q, out, in_, reads, (), None, kw)
        self.loads_since = 0

    def _maybe_flush(self, writes, is_compute):
        if not self.pending:
            return
        if is_compute and self.loads_since > 0:
            self.flush_stores()
            return
        for (q, out, in_, reads, kw) in self.pending:
            for t in reads:
                if t in writes:
                    self.flush_stores()
                    return

    def op(self, e, emit, reads=(), writes=(), inc=True, bulk=False):
        reads = self._toks(reads)
        writes = self._toks(writes)
        self._maybe_flush(writes, e != "pe" or True)
        self._waits(e, reads, writes, bulk=bulk)
        ins = emit(self.eng[e])
        self.ninst += 1
        if inc:
            self.cnt[e] += 1
            ins.then_inc(self.sem[e], 1)
            me = (self.sem[e], self.cnt[e])
        else:
            me = (self.sem[e], self.cnt[e] + 1)
        for t in reads:
            t.r[e] = me
        for t in writes:
            t.w = me
            t.r = {}
            t.wb = bulk
        return ins

    def dma(self, q, out, in_, reads=(), writes=(), lane_tok=None, **kw):
        reads = self._toks(reads)
        writes = self._toks(writes)
        if not writes and reads and lane_tok is None and q == "sp":
            if len(self.pending) >= 6:
                self.flush_stores()
            self.pending.append((q, out, in_, reads, kw))
            return None
        self._maybe_flush(writes, False)
        self.loads_since += 1
        return self._dma_now(q, out, in_, reads, writes, lane_tok, kw)

    def _dma_now(self, q, out, in_, reads, writes, lane_tok, kw):
        lt = lane_tok.tok if isinstance(lane_tok, T) else lane_tok
        if lt is None:
            lt = writes[0] if writes else reads[0]
        lane = self.lane_of(lt, sw=(q == "pool"))
        self._waits(q, reads, writes, is_dma=True, skip_sem=lane.sem)
        if q == "pool" and lane.count > 0 and self.seen[q].get(id(lane.sem), 0) < lane.count:
            self.eng[q].wait_ge(lane.sem, lane.count)
            self.seen[q][id(lane.sem)] = lane.count
        ins = self.eng[q].dma_start(out=out, in_=in_, **kw)
        self.ninst += 1
        lane.count += 16
        ins.then_inc(lane.sem, 16)
        me = (lane.sem, lane.count)
        for t in reads:
            t.r["dma%d" % id(lane)] = me
        for t in writes:
            t.w = me
            t.r = {}
            t.wb = False
        return ins

    def barrier(self):
        self.flush_stores()
        for e in self.eng:
            for f in self.eng:
                if f == e or self.cnt[f] == 0:
                    continue
                if self.seen[e].get(id(self.sem[f]), 0) >= self.cnt[f]:
                    continue
                self.eng[e].wait_ge(self.sem[f], self.cnt[f])
                self.seen[e][id(self.sem[f])] = self.cnt[f]
            for ln in self.all_lanes:
                if ln.count == 0 or self.seen[e].get(id(ln.sem), 0) >= ln.count:
                    continue
                self.eng[e].wait_ge(ln.sem, ln.count)
                self.seen[e][id(ln.sem)] = ln.count

    def mm(self, out, lhsT, rhs, start, stop, reads, writes, inc=None):
        if inc is None:
            inc = stop
        return self.op("pe", lambda g: g.matmul(out, lhsT, rhs, start=start, stop=stop), reads, writes, inc=inc)

    def tr(self, out, in_, ident, reads, writes, inc=True):
        return self.op("pe", lambda g: g.transpose(out, in_, ident), reads, writes, inc=inc)

    @staticmethod
    def _bulk(out):
        try:
            return out.free_size() >= 256
        except Exception:
            return False

    def act(self, out, in_, func, reads, writes, bias=None, scale=None, accum_out=None, e="act"):
        kw = {}
        if bias is not None:
            kw["bias"] = bias
        if scale is not None:
            kw["scale"] = scale
        if accum_out is not None:
            kw["accum_out"] = accum_out
        return self.op(e, lambda g: g.activation(out, in_, func, **kw), reads, writes,
                       bulk=(accum_out is None and self._bulk(out)))

    def ts(self, e, out, in0, s1, s2, op0, op1, reads, writes, accum_out=None):
        kw = {}
        if op1 is not None:
            kw["op1"] = op1
        if accum_out is not None:
            kw["accum_out"] = accum_out
        return self.op(e, lambda g: g.tensor_scalar(out, in0, s1, s2, op0, **kw), reads, writes,
                       bulk=(accum_out is None and self._bulk(out)))

    def tt(self, e, out, in0, in1, op, reads, writes):
        return self.op(e, lambda g: g.tensor_tensor(out, in0, in1, op), reads, writes, bulk=self._bulk(out))

    def stt(self, out, in0, scalar, in1, op0, op1, reads, writes, e="dve"):
        return self.op(e, lambda g: g.scalar_tensor_tensor(out, in0, scalar, in1, op0, op1), reads, writes, bulk=self._bulk(out))

    def copy(self, e, out, in_, reads, writes):
        if e == "act":
            return self.op(e, lambda g: g.copy(out, in_), reads, writes, bulk=self._bulk(out))
        return self.op(e, lambda g: g.tensor_copy(out, in_), reads, writes, bulk=self._bulk(out))

    def memset(self, e, ap, val, writes):
        return self.op(e, lambda g: g.memset(ap, val), (), writes, bulk=self._bulk(ap))


class Ring:
    def __init__(self, items):
        self.items = items
        self.i = 0

    def get(self):
        t = self.items[self.i % len(self.items)]
        self.i += 1
        return t


class Cfg:
    def __init__(self, rows=64, depth=4, types=None):
        self.rows = rows
        self.L = rows * GRID_W
        self.depth = depth
        self.types = types if types is not None else [("even" if i % 2 == 0 else "ret") for i in range(depth)]
        self.layers = list(range(depth))


def bcast_rows(ap, n=128):
    return ap.partition_broadcast(n)


class Prog:
    def __init__(self, cfg, stages=None):
        self.cfg = cfg
        self.k = KB()
        self.nc = self.k.nc
        self.stages = stages
        nc = self.nc
        L = cfg.L
        nl = cfg.depth
        ne = (nl + 1) // 2
        no = nl // 2
        self.inp = {}

        def din(name, shape, dt=F32):
            self.inp[name] = nc.dram_tensor(name, list(shape), dt, kind="ExternalInput").ap()
            return self.inp[name]

        din("x", [L, D]); din("c", [1, D]); din("ctx", [CTX, D]); din("c_ctx", [1, D])
        din("w_mod", [nl, D, NMOD * D]); din("b_mod", [nl, NMOD * D]); din("norm_gain", [nl, 3, D])
        din("ffn_a_in", [nl, D, 2 * DFF]); din("ffn_a_out", [nl, DFF, D])
        din("ffn_b_in", [nl, D, 2 * DFF]); din("ffn_b_out", [nl, DFF, D])
        din("ident", [128, 128], BF16)
        self.out = nc.dram_tensor("out", [L, D], F32, kind="ExternalOutput").ap()
        self.xc = nc.dram_tensor("xc_scr", [CTX, D], F32, kind="ExternalOutput").ap()
        self.mod = nc.dram_tensor("mod_scr", [nl, 2, NMOD * D], F32).ap()
        if "ret" in cfg.types:
            self.ret_setup()
        if "even" in cfg.types:
            self.even_setup()

    def inp_add(self, name, shape, dt=F32):
        self.inp[name] = self.nc.dram_tensor(name, list(shape), dt, kind="ExternalInput").ap()
        return self.inp[name]

    def tile_tok0(self, xap):
        off = xap.offset // D
        return off if xap.tensor.name == self.out.tensor.name else self.cfg.L + off

    def consts(self):
        k = self.k
        self.ident = k.sb([128, 128], BF16, "ident")
        k.dma("sp", self.ident[:, :], self.inp["ident"], writes=[self.ident])
        self.epsb = k.sb([128, 1], F32, "eps")
        k.memset("dve", self.epsb[:, :], RMS_EPS, [self.epsb])

    def init_copy(self):
        k = self.k
        L = self.cfg.L
        self.t_xlat = Tok("xlat")
        self.t_xctx = Tok("xctx")
        nchunk = max(1, L // 1024)
        rows = L // nchunk
        for i in range(nchunk):
            k.dma("sp", self.out[i * rows:(i + 1) * rows, :], self.inp["x"][i * rows:(i + 1) * rows, :],
                  writes=[self.t_xlat])
        k.dma("sp", self.xc[:, :], self.inp["ctx"][:, :], writes=[self.t_xctx])
        k.barrier()

    def phase_mod(self, l):
        k = self.k
        with k.phase():
            cT = k.sb([128, 8, 2], F32, "cT")
            with self.nc.allow_non_contiguous_dma("tiny"):
                k.dma("sp", cT[:, :, 0], self.inp["c"].rearrange("o (c p) -> p (o c)", p=128), writes=[cT])
                k.dma("sp", cT[:, :, 1], self.inp["c_ctx"].rearrange("o (c p) -> p (o c)", p=128), writes=[cT])
            sc = k.sb([128, 8, 2], F32, "sc")
            k.act(sc[:, :, :], cT[:, :, :], AF.Silu, [cT], [sc])
            ones2 = k.sb([1, 2], F32, "ones2")
            k.memset("dve", ones2[:, :], 1.0, [ones2])
            brow = k.sb([1, NMOD * D], F32, "brow")
            k.dma("sp", brow[:, :], self.inp["b_mod"][l:l + 1, :], writes=[brow])
            wpool = k.pool_of(3, [128, 8, 512], F32, "wm")
            ppool = k.pool_of(2, [2, 512], F32, "pm", psum=True)
            spool = k.pool_of(2, [2, 512], F32, "sm")
            for n in range(NMOD * D // 512):
                w = wpool.get()
                k.dma("sp", w[:, :, :], self.inp["w_mod"][l, :, n * 512:(n + 1) * 512].rearrange("(c p) n -> p c n", p=128),
                      writes=[w])
                p = ppool.get()
                for c in range(8):
                    k.mm(p[:, :], sc[:, c, :], w[:, c, :], c == 0, False, [sc, w], [p])
                k.mm(p[:, :], ones2[:, :], brow[:, n * 512:(n + 1) * 512], False, True, [ones2, brow], [p])
                s = spool.get()
                k.copy("dve", s[:, :], p[:, :], [p], [s])
                k.dma("sp", self.mod[l, :, n * 512:(n + 1) * 512], s[:, :], reads=[s])

    def load_mod_vecs(self, l, nj, mv, half_gate):
        k = self.k
        G = k.sb([128, 8, 2], F32, "G")
        S = k.sb([128, 8, 2], F32, "S")
        gn = k.sb([128, 8], F32, "gn")
        gate = k.sb([128, 2, D], F32, "gate")
        with self.nc.allow_non_contiguous_dma("tiny"):
            k.dma("sp", gn[:, :], self.inp["norm_gain"][l, nj:nj + 1, :].rearrange("o (c p) -> p (o c)", p=128), writes=[gn])
            for r in range(2):
                k.dma("sp", S[:, :, r], self.mod[l, r:r + 1, mv * D:(mv + 1) * D].rearrange("o (c p) -> p (o c)", p=128), writes=[S])
                k.dma("sp", G[:, :, r], self.mod[l, r:r + 1, (mv + 1) * D:(mv + 2) * D].rearrange("o (c p) -> p (o c)", p=128), writes=[G])
                k.dma("sp", gate[:, r, :], bcast_rows(self.mod[l, r:r + 1, (mv + 2) * D:(mv + 3) * D]), writes=[gate])
        for r in range(2):
            k.stt(G[:, :, r], G[:, :, r], 1.0, gn[:, :], ALU.add, ALU.mult, [G, gn], [G])
        if half_gate:
            k.ts("dve", gate[:, :, :], gate[:, :, :], 0.5, None, ALU.mult, None, [gate], [gate])
        return G, S, gate

    def tiles(self, tsz):
        out = []
        for i in range(self.cfg.L // tsz):
            out.append((self.out[i * tsz:(i + 1) * tsz, :], tsz, 0))
        for i in range(max(1, CTX // tsz)):
            n = min(tsz, CTX)
            out.append((self.xc[i * n:(i + 1) * n, :], n, 1))
        return out

    def norm_part(self, xt, ns, xs_pool, scr, ssq, rstd):
        k = self.k
        for s in range(ns):
            k.act(scr[:, :], xt[:, s, :], AF.Square, [xt], [scr, ssq], accum_out=ssq[:, s:s + 1])
        k.act(rstd[:, :ns], ssq[:, :ns], AF.Sqrt, [ssq, self.epsb], [rstd], bias=self.epsb[:, 0:1], scale=1.0 / D)
        k.op("dve", lambda g: g.reciprocal(rstd[:, :ns], rstd[:, :ns]), [rstd], [rstd])
        xs = xs_pool.get()
        for s in range(ns):
            k.ts("dve", xs[:, s, :], xt[:, s, :], rstd[:, s:s + 1], None, ALU.mult, None, [xt, rstd], [xs])
        return xs

    def transp_part(self, xs, ns, row, G, S, tp_pool, xnT):
        k = self.k
        for c in range(8):
            tp = tp_pool.get()
            for s in range(ns):
                k.tr(tp[:, s * 128:(s + 1) * 128], xs[:, s, c * 128:(c + 1) * 128], self.ident[:, :],
                     [xs, self.ident], [tp], inc=(s == ns - 1))
            k.act(xnT[:, c, :ns * 128], tp[:, :ns * 128], AF.Identity, [tp, G, S], [xnT],
                  bias=S[:, c, row:row + 1], scale=G[:, c, row:row + 1])

    def norm_mod_T(self, xt, ns, row, G, S, xs_pool, tp_pool, xnT, scr, ssq, rstd):
        xs = self.norm_part(xt, ns, xs_pool, scr, ssq, rstd)
        self.transp_part(xs, ns, row, G, S, tp_pool, xnT)

    def phase_ffn(self, l, which):
        k = self.k
        TS = 256
        NS = TS // 128
        win_d = self.inp["ffn_a_in" if which == 0 else "ffn_b_in"]
        wout_d = self.inp["ffn_a_out" if which == 0 else "ffn_b_out"]
        nj, mv = (0, 0) if which == 0 else (2, 6)
        NF = DFF // 128
        with k.phase():
            Win = [k.sb([128, 2 * DFF], BF16, "Win%d" % c) for c in range(8)]
            Wout = k.sb([128, NF, D], BF16, "Wout")
            for c in range(8):
                k.dma("pool", Win[c][:, :], win_d[l, c * 128:(c + 1) * 128, :], writes=[Win[c]])
            k.dma("pool", Wout[:, :, :], wout_d[l, :, :].rearrange("(j p) d -> p j d", p=128), writes=[Wout])
            G, S, gate = self.load_mod_vecs(l, nj, mv, True)
            xpool = k.pool_of(2, [128, NS, D], F32, "xt")
            xs_pool = k.pool_of(1, [128, NS, D], BF16, "xs")
            tp_pool = k.pool_of(2, [128, TS], BF16, "tp", psum=True)
            xnT = k.sb([128, 8, TS], BF16, "xnT")
            hT = k.sb([128, NF, TS], BF16, "hT")
            scr = k.sb([128, D], F32, "scr")
            ssq = k.sb([128, NS], F32, "ssq")
            rstd = k.sb([128, NS], F32, "rstd")
            pa_pool = k.pool_of(2, [128, TS], F32, "pa", psum=True)
            pb_pool = k.pool_of(2, [128, TS], F32, "pb", psum=True)
            sa_pool = k.pool_of(2, [128, TS], BF16, "sa")
            po_pool = k.pool_of(2, [128, 512], F32, "po", psum=True)
            sqj = k.sb([128, D], F32, "sqj")
            tl = self.tiles(TS)

            def load(i):
                xap, ntok, row = tl[i]
                xt = xpool.get()
                k.dma("sp", xt[:, :ntok // 128, :], xap.rearrange("(s p) d -> p s d", p=128), writes=[xt])
                return xt

            xts = {0: load(0)}
            xss = {0: self.norm_part(xts[0], tl[0][1] // 128, xs_pool, sqj, ssq, rstd)}
            self.transp_part(xss[0], tl[0][1] // 128, tl[0][2], G, S, tp_pool, xnT)
            if len(tl) > 1:
                xts[1] = load(1)
            for i, (xap, ntok, row) in enumerate(tl):
                ns = ntok // 128
                xt = xts.pop(i)
                for j in range(NF):
                    pa = pa_pool.get()
                    pb = pb_pool.get()
                    for c in range(8):
                        k.mm(pa[:, :ntok], Win[c][:, j * 128:(j + 1) * 128], xnT[:, c, :ntok], c == 0, c == 7,
                             [Win[c], xnT], [pa])
                    for c in range(8):
                        k.mm(pb[:, :ntok], Win[c][:, DFF + j * 128:DFF + (j + 1) * 128], xnT[:, c, :ntok], c == 0, c == 7,
                             [Win[c], xnT], [pb])
                    sa = sa_pool.get()
                    k.act(sa[:, :ntok], pa[:, :ntok], AF.Silu, [pa], [sa])
                    k.tt("dve", hT[:, j, :ntok], sa[:, :ntok], pb[:, :ntok], ALU.mult, [sa, pb], [hT])
                    if j == NF // 2 and i + 1 < len(tl):
                        xss[i + 1] = self.norm_part(xts[i + 1], tl[i + 1][1] // 128, xs_pool, sqj, ssq, rstd)
                if i + 1 < len(tl):
                    self.transp_part(xss.pop(i + 1), tl[i + 1][1] // 128, tl[i + 1][2], G, S, tp_pool, xnT)
                for s in range(ns):
                    for hf in range(2):
                        po = po_pool.get()
                        for j in range(NF):
                            k.mm(po[:, :], hT[:, j, s * 128:(s + 1) * 128], Wout[:, j, hf * 512:(hf + 1) * 512],
                                 j == 0, j == NF - 1, [hT, Wout], [po])
                        sl = slice(hf * 512, (hf + 1) * 512)
                        k.tt("dve", scr[:, sl], po[:, :], gate[:, row, sl], ALU.mult, [po, gate], [scr])
                        k.tt("pool", xt[:, s, sl], xt[:, s, sl], scr[:, sl], ALU.add, [xt, scr], [xt])
                k.dma("sp", xap.rearrange("(s p) d -> p s d", p=128), xt[:, :ns, :], reads=[xt])
                if i + 2 < len(tl):
                    xts[i + 2] = load(i + 2)

    def build(self):
        k = self.k
        self.consts()
        self.init_copy()
        for l in self.cfg.layers:
            self.phase_mod(l)
            self.phase_ffn(l, 0)
            if self.stages == "ffn_a":
                continue
            if self.stages != "ffn_only":
                if self.cfg.types[l] == "ret":
                    self.phase_ret(l)
                else:
                    self.phase_even(l)
            if self.stages == "mix":
                continue
            if self.stages == "mix_only" and False:
                continue
            self.phase_ffn(l, 1)
        k.barrier()
        return self.nc


RH = 4
RDK = 256
RDV = 512
RQK = RH * RDK
RV = RH * RDV
RIN = 2 * RQK + 2 * RV


def _ret_setup(self):
    nc = self.nc
    L = self.cfg.L
    NT = L + CTX
    NCH = NT // 128
    self.ret_in = self.inp_add("ret_in", [max(1, self.cfg.types.count("ret")), D, RIN])
    self.ret_out = self.inp_add("ret_out", [max(1, self.cfg.types.count("ret")), RV, D])
    self.ret_lf = self.inp_add("ret_logit_f", [max(1, self.cfg.types.count("ret")), RH])
    self.ret_lb = self.inp_add("ret_logit_b", [max(1, self.cfg.types.count("ret")), RH])
    self.rope = self.inp_add("rope_tab", [L, 2, 128])
    self.rconst = self.inp_add("ret_const", [128, 6, 128])
    self.qts = nc.dram_tensor("qts", [NCH, 128, 1024], BF16).ap()
    self.kts = nc.dram_tensor("kts", [NCH, 128, 1024], BF16).ap()
    self.ktok = nc.dram_tensor("ktok", [NT, RQK], BF16).ap()
    self.vtok = nc.dram_tensor("vtok", [NT, RV], BF16).ap()
    self.sgt = nc.dram_tensor("sgt", [NT, RV], BF16).ap()
    self.st = nc.dram_tensor("st", [2, NCH, RH, 128, 1024], BF16).ap()


def ret_consts_host():
    j = np.arange(128, dtype=np.float32)
    c = np.zeros((128, 6, 128), np.float32)
    diff = j[None, :] - j[:, None]
    c[:, 0, :] = np.maximum(diff, 0.0)
    c[:, 1, :] = np.maximum(-diff, 0.0)
    c[:, 2, :] = (diff >= 0).astype(np.float32) / 16.0
    c[:, 3, :] = (diff <= 0).astype(np.float32) / 16.0
    c[:, 4, :] = (j[None, :] + 1.0)
    c[:, 5, :] = (128.0 - j[None, :])
    return c


def _phase_ret(self, l):
    k = self.k
    nc = self.nc
    o = self.cfg.types[:l].count("ret")
    L = self.cfg.L
    NT = L + CTX
    NCH = NT // 128
    NCL = L // 128
    last = (l == self.cfg.depth - 1) and not getattr(self, "force_ctx_out", False)

    with k.phase():
        Wr = [k.sb([128, RIN], BF16, "Wr%d" % c) for c in range(8)]
        for c in range(8):
            k.dma("pool", Wr[c][:, :], self.ret_in[o, c * 128:(c + 1) * 128, :], writes=[Wr[c]])
        G, S, gate = self.load_mod_vecs(l, 1, 3, False)
        TS = 256
        xpool = k.pool_of(2, [128, 2, D], F32, "xt")
        xs_pool = k.pool_of(1, [128, 2, D], BF16, "xs")
        tp_pool = k.pool_of(2, [128, TS], BF16, "tp", psum=True)
        xnT = k.sb([128, 8, TS], BF16, "xnT")
        scr = k.sb([128, D], F32, "scr")
        ssq = k.sb([128, 2], F32, "ssq")
        rstd = k.sb([128, 2], F32, "rstd")
        pp = k.pool_of(3, [128, 512], F32, "pp", psum=True)
        tq_pool = k.pool_of(2, [128, 1024], BF16, "tq", psum=True)
        rope_pool = k.pool_of(2, [128, 2, 128], F32, "rope")
        qtok_pool = k.pool_of(2, [128, RQK], BF16, "qtok")
        ktok_pool = k.pool_of(2, [128, RQK], BF16, "ktok")
        v_pool = k.pool_of(2, [128, RV], BF16, "vst")
        g_pool = k.pool_of(2, [128, RV], BF16, "gst")
        qT_pool = k.pool_of(2, [128, 1024], BF16, "qTs")
        kT_pool = k.pool_of(2, [128, 1024], BF16, "kTs")
        tmp = [k.sb([128, 256], F32, "rt%d" % i) for i in range(4)]
        for (xap, ntok, row) in self.tiles(TS):
            xt = xpool.get()
            k.dma("sp", xt[:, :, :], xap.rearrange("(s p) d -> p s d", p=128), writes=[xt])
            self.norm_mod_T(xt, 2, row, G, S, xs_pool, tp_pool, xnT, scr, ssq, rstd)
            for s in range(2):
                tok0 = (self.tile_tok0(xap) + s * 128)
                ch = tok0 // 128
                if row == 0:
                    rp = rope_pool.get()
                    k.dma("sp", rp[:, :, :], self.rope[tok0:tok0 + 128, :, :], writes=[rp])
                qtok = qtok_pool.get()
                ktok = ktok_pool.get()
                vst = v_pool.get()
                gst = g_pool.get()
                for n in range(12):
                    p = pp.get()
                    for c in range(8):
                        k.mm(p[:, :], xnT[:, c, s * 128:(s + 1) * 128], Wr[c][:, n * 512:(n + 1) * 512], c == 0, c == 7,
                             [xnT, Wr[c]], [p])
                    if n < 4:
                        dst = qtok if n < 2 else ktok
                        cs = (n % 2) * 512
                        if row == 0:
                            pv = p[:, :].rearrange("p (h g f d) -> p h g f d", h=2, g=2, f=2)
                            dv = dst[:, cs:cs + 512].rearrange("p (h g f d) -> p h g f d", h=2, g=2, f=2)
                            ct = rp[:, 0, :].rearrange("p (g d) -> p g d", g=2).unsqueeze(1).broadcast_to([128, 2, 2, 64])
                            sn = rp[:, 1, :].rearrange("p (g d) -> p g d", g=2).unsqueeze(1).broadcast_to([128, 2, 2, 64])
                            tv = [t[:, :].rearrange("p (h g d) -> p h g d", h=2, g=2) for t in tmp]
                            k.tt("dve", tv[0], pv[:, :, :, 0, :], ct, ALU.mult, [p, rp], [tmp[0]])
                            k.tt("dve", tv[1], pv[:, :, :, 1, :], sn, ALU.mult, [p, rp], [tmp[1]])
                            k.tt("dve", tv[2], pv[:, :, :, 0, :], sn, ALU.mult, [p, rp], [tmp[2]])
                            k.tt("dve", tv[3], pv[:, :, :, 1, :], ct, ALU.mult, [p, rp], [tmp[3]])
                            k.tt("pool", dv[:, :, :, 0, :], tv[0], tv[1], ALU.subtract, [tmp[0], tmp[1]], [dst])
                            k.tt("pool", dv[:, :, :, 1, :], tv[2], tv[3], ALU.add, [tmp[2], tmp[3]], [dst])
                        else:
                            k.copy("act", dst[:, cs:cs + 512], p[:, :], [p], [dst])
                    elif n < 8:
                        k.copy("act", vst[:, (n - 4) * 512:(n - 3) * 512], p[:, :], [p], [vst])
                    else:
                        k.act(gst[:, (n - 8) * 512:(n - 7) * 512], p[:, :], AF.Silu, [p], [gst])
                for (src, dpool, dscr) in ((qtok, qT_pool, self.qts), (ktok, kT_pool, self.kts)):
                    tq = tq_pool.get()
                    for b in range(8):
                        k.tr(tq[:, b * 128:(b + 1) * 128], src[:, b * 128:(b + 1) * 128], self.ident[:, :],
                             [src, self.ident], [tq], inc=(b == 7))
                    dT = dpool.get()
                    k.copy("dve" if src is qtok else "act", dT[:, :], tq[:, :], [tq], [dT])
                    k.dma("sp", dscr[ch, :, :], dT[:, :], reads=[dT])
                k.dma("sp", self.ktok[tok0:tok0 + 128, :], ktok[:, :], reads=[ktok])
                k.dma("sp", self.vtok[tok0:tok0 + 128, :], vst[:, :], reads=[vst])
                k.dma("sp", self.sgt[tok0:tok0 + 128, :], gst[:, :], reads=[gst])

    with k.phase():
        rc = k.sb([128, 6, 128], F32, "rconst")
        k.dma("sp", rc[:, :, :], self.rconst, writes=[rc])
        lg = k.sb([128, 2 * RH], F32, "lg")
        k.dma("sp", lg[:, 0:RH], bcast_rows(self.ret_lf[o:o + 1, :]), writes=[lg])
        k.dma("sp", lg[:, RH:2 * RH], bcast_rows(self.ret_lb[o:o + 1, :]), writes=[lg])
        k.act(lg[:, :], lg[:, :], AF.Exp, [lg], [lg], scale=-1.0)
        k.act(lg[:, :], lg[:, :], AF.Ln, [lg], [lg], bias=1.0, scale=1.0)
        k.ts("dve", lg[:, :], lg[:, :], -1.0, None, ALU.mult, None, [lg], [lg])
        zeta = k.sb([128, 2 * RH], F32, "zeta")
        gch = k.sb([128, 2 * RH], F32, "gch")
        XI = k.sb([128, 2 * RH, 128], BF16, "XI")
        DcT = k.sb([128, RH, 128], BF16, "DcT")
        dtmp = k.sb([128, 2, 128], F32, "dtmp")
        for e in range(RH):
            f, b = e, RH + e
            k.act(XI[:, f, :], rc[:, 4, :], AF.Exp, [rc, lg], [XI], scale=lg[:, f:f + 1])
            k.act(XI[:, b, :], rc[:, 5, :], AF.Exp, [rc, lg], [XI], scale=lg[:, b:b + 1])
            k.act(dtmp[:, 0, :], rc[:, 0, :], AF.Exp, [rc, lg], [dtmp], scale=lg[:, f:f + 1])
            k.act(dtmp[:, 1, :], rc[:, 1, :], AF.Exp, [rc, lg], [dtmp], scale=lg[:, b:b + 1])
            k.tt("dve", dtmp[:, :, :], dtmp[:, :, :], rc[:, 2:4, :], ALU.mult, [dtmp, rc], [dtmp])
            k.tt("dve", DcT[:, e, :], dtmp[:, 0, :], dtmp[:, 1, :], ALU.add, [dtmp], [DcT])
        for e in range(RH):
            k.act(zeta[:, e:e + 1], rc[:, 0, 127:128], AF.Exp, [rc, lg], [zeta], scale=lg[:, e:e + 1])
            k.act(zeta[:, RH + e:RH + e + 1], rc[:, 1, 0:1], AF.Exp, [rc, lg], [zeta], scale=lg[:, RH + e:RH + e + 1])
        k.ts("dve", zeta[:, :], zeta[:, :], 1.0 / 16.0, None, ALU.mult, None, [zeta], [zeta])
        k.act(gch[:, :], lg[:, :], AF.Exp, [lg], [gch], scale=128.0)

        with k.phase():
            Sst = [[k.sb([128, 2, RDV], F32, "S%d%d" % (d, e)) for e in range(RH)] for d in range(2)]
            Sbf = [[k.pool_of(2, [128, 2 * RDV], BF16, "Sb%d%d" % (d, e)) for e in range(RH)] for d in range(2)]
            for d in range(2):
                for e in range(RH):
                    k.memset("pool", Sst[d][e][:, :, :], 0.0, [Sst[d][e]])
            kin = k.pool_of(4, [128, RQK], BF16, "kin")
            vin = k.pool_of(4, [128, RV], BF16, "vin")
            kz_pool = k.pool_of(3, [128, RQK], BF16, "kz")
            pd = k.pool_of(6, [128, RDV], F32, "pd", psum=True)
            order_f = [NCL, NCL + 1] + list(range(NCL))
            order_b = [NCL + 1, NCL] + list(range(NCL - 1, -1, -1))
            for step in range(NCH):
                for d, order in ((0, order_f), (1, order_b)):
                    ch = order[step]
                    kt = kin.get()
                    vt = vin.get()
                    k.dma("sp", kt[:, :], self.ktok[ch * 128:(ch + 1) * 128, :], writes=[kt])
                    k.dma("sp", vt[:, :], self.vtok[ch * 128:(ch + 1) * 128, :], writes=[vt])
                    if step < NCH - 1:
                        kz = kz_pool.get()
                        k.tt("dve", kz[:, :].rearrange("p (e x) -> p e x", e=RH), kt[:, :].rearrange("p (e x) -> p e x", e=RH),
                             zeta[:, d * RH:(d + 1) * RH].unsqueeze(2).broadcast_to([128, RH, RDK]), ALU.mult, [kt, zeta], [kz])
                    for e in range(RH):
                        S_ = Sst[d][e]
                        sb_ = Sbf[d][e].get()
                        k.copy("act", sb_[:, :], S_[:, :, :].rearrange("p a b -> p (a b)"), [S_], [sb_])
                        k.dma("sp", self.st[d, ch, e, :, :], sb_[:, :], reads=[sb_])
                        if step == NCH - 1:
                            continue
                        for dc in range(2):
                            p = pd.get()
                            k.mm(p[:, :], kz[:, e * RDK + dc * 128:e * RDK + (dc + 1) * 128], vt[:, e * RDV:(e + 1) * RDV], True, True, [kz, vt], [p])
                            k.stt(S_[:, dc, :], S_[:, dc, :], gch[:, d * RH + e:d * RH + e + 1], p[:, :], ALU.mult, ALU.add,
                                  [S_, gch, p], [S_])

        with k.phase():
            Wo = k.sb([128, 16, D], BF16, "Wo")
            k.dma("pool", Wo[:, :, :], self.ret_out[o, :, :].rearrange("(j p) d -> p j d", p=128), writes=[Wo])
            G, S, gate = self.load_mod_vecs(l, 1, 3, False)
            qT_pool = k.pool_of(2, [128, 8, 128], BF16, "qTc")
            kT_pool = k.pool_of(2, [128, 8, 128], BF16, "kTc")
            v_pool = k.pool_of(2, [128, RV], BF16, "vc")
            g_pool = k.pool_of(2, [128, RV], BF16, "gc")
            st_pool = k.pool_of(2, [128, 2, RH, 2, RDV], BF16, "stc")
            x_pool = k.pool_of(2, [128, D], F32, "xc")
            ps_pool = k.pool_of(2, [128, RH, 128], F32, "psc", psum=True)
            po = [k.ps([128, RDV], F32, "poc%d" % e) for e in range(RH)]
            pt_pool = k.pool_of(1, [128, 1024], BF16, "ptc", psum=True)
            pr_pool = k.pool_of(1, [128, 512], F32, "prc", psum=True)
            in_pool = k.pool_of(2, [128, RH, 128], BF16, "inT")
            qs_pool = k.pool_of(2, [128, 2, 2 * RH, 128], BF16, "qs")
            Y = k.sb([128, RV], BF16, "Y")
            YT = k.sb([128, 16, 128], BF16, "YT")
            stats = k.sb([128, RH, 6], F32, "stats")
            mv = k.sb([128, RH, 2], F32, "mv")
            rs = k.sb([128, RH], F32, "rs")
            nb = k.sb([128, RH], F32, "nb")
            yn = k.sb([128, RV], F32, "yn")
            scr2 = k.sb([128, D], F32, "scr2")
            gneps = k.sb([128, 1], F32, "gneps")
            k.memset("dve", gneps[:, :], GN_EPS, [gneps])
            chunks = list(range(NCL)) + ([] if last else [NCL, NCL + 1])
            for ch in chunks:
                row = 0 if ch < NCL else 1
                xap = self.out[ch * 128:(ch + 1) * 128, :] if row == 0 else self.xc[(ch - NCL) * 128:(ch - NCL + 1) * 128, :]
                qT = qT_pool.get(); kT = kT_pool.get(); vt = v_pool.get(); gt = g_pool.get(); stt_ = st_pool.get(); xt = x_pool.get()
                k.dma("sp", qT[:, :, :], self.qts[ch, :, :].rearrange("p (b t) -> p b t", b=8), writes=[qT])
                k.dma("sp", kT[:, :, :], self.kts[ch, :, :].rearrange("p (b t) -> p b t", b=8), writes=[kT])
                k.dma("sp", vt[:, :], self.vtok[ch * 128:(ch + 1) * 128, :], writes=[vt])
                k.dma("sp", gt[:, :], self.sgt[ch * 128:(ch + 1) * 128, :], writes=[gt])
                for d in range(2):
                    k.dma("sp", stt_[:, d, :, :, :], self.st[d, ch, :, :, :].rearrange("e p (a b) -> p e a b", a=2), writes=[stt_])
                k.dma("sp", xt[:, :], xap, writes=[xt])
                ps = ps_pool.get()
                for e in range(RH):
                    for dc in range(2):
                        k.mm(ps[:, e, :], kT[:, 2 * e + dc, :], qT[:, 2 * e + dc, :], dc == 0, dc == 1, [kT, qT], [ps],
                             inc=(e == RH - 1 and dc == 1))
                inT = in_pool.get()
                k.tt("dve", inT[:, :, :], ps[:, :, :], DcT[:, :, :], ALU.mult, [ps, DcT], [inT])
                qs = qs_pool.get()
                for d in range(2):
                    k.tt("pool", qs[:, d, :, :].rearrange("p (e c) t -> p e c t", e=RH), qT[:, :, :].rearrange("p (e c) t -> p e c t", e=RH),
                         XI[:, d * RH:(d + 1) * RH, :].unsqueeze(2).broadcast_to([128, RH, 2, 128]), ALU.mult, [qT, XI], [qs])
                for e in range(RH):
                    k.mm(po[e][:, :], inT[:, e, :], vt[:, e * RDV:(e + 1) * RDV], True, False, [inT, vt], [po[e]])
                    for d in range(2):
                        for dc in range(2):
                            k.mm(po[e][:, :], qs[:, d, 2 * e + dc, :], stt_[:, d, e, dc, :], False, (d == 1 and dc == 1), [qs, stt_], [po[e]])
                for e in range(RH):
                    k.op("dve", lambda g, e=e: g.bn_stats(stats[:, e, :], po[e][:, :]), [po[e]], [stats])
                for e in range(RH):
                    k.op("dve", lambda g, e=e: g.bn_aggr(mv[:, e, :], stats[:, e, :]), [stats], [mv])
                k.act(rs[:, :], mv[:, :, 1], AF.Sqrt, [mv, gneps], [rs], bias=gneps[:, 0:1], scale=1.0)
                k.op("dve", lambda g: g.reciprocal(rs[:, :], rs[:, :]), [rs], [rs])
                k.stt(nb[:, :], mv[:, :, 0], -1.0, rs[:, :], ALU.mult, ALU.mult, [mv, rs], [nb])
                for e in range(RH):
                    k.act(yn[:, e * RDV:(e + 1) * RDV], po[e][:, :], AF.Identity, [po[e], rs, nb], [yn], bias=nb[:, e:e + 1], scale=rs[:, e:e + 1])
                k.tt("pool", Y[:, :], yn[:, :], gt[:, :], ALU.mult, [yn, gt], [Y])
                for hf in range(2):
                    pt = pt_pool.get()
                    for b in range(8):
                        bb = hf * 8 + b
                        k.tr(pt[:, b * 128:(b + 1) * 128], Y[:, bb * 128:(bb + 1) * 128], self.ident[:, :], [Y, self.ident], [pt],
                             inc=(b == 7))
                    k.copy("dve" if hf == 0 else "act", YT[:, hf * 8:(hf + 1) * 8, :].rearrange("p b t -> p (b t)"), pt[:, :], [pt], [YT])
                for hf in range(2):
                    pr = pr_pool.get()
                    for b in range(16):
                        k.mm(pr[:, :], YT[:, b, :], Wo[:, b, hf * 512:(hf + 1) * 512], b == 0, b == 15, [YT, Wo], [pr])
                    sl = slice(hf * 512, (hf + 1) * 512)
                    k.tt("dve", scr2[:, sl], pr[:, :], gate[:, row, sl], ALU.mult, [pr, gate], [scr2])
                    k.tt("pool", xt[:, sl], xt[:, sl], scr2[:, sl], ALU.add, [xt, scr2], [xt])
                k.dma("sp", xap, xt[:, :], reads=[xt])


Prog.ret_setup = _ret_setup
Prog.phase_ret = _phase_ret


def host_consts(rows):
    L = rows * GRID_W
    out = {}
    out["ident"] = np.eye(128, dtype=np.float32).astype(ml_dtypes.bfloat16)
    t = np.arange(L)
    nf = RDK // 4
    inv = (10000.0 ** (-np.arange(nf, dtype=np.float32) / nf)).astype(np.float32)
    ang = np.concatenate([(t // GRID_W).astype(np.float32)[:, None] * inv, (t % GRID_W).astype(np.float32)[:, None] * inv], axis=-1)
    rope = np.stack([np.cos(ang), np.sin(ang)], axis=1).astype(np.float32)
    out["rope_tab"] = rope
    out["ret_const"] = ret_consts_host()
    for nm, Ls in (("lat", L), ("ctx", CTX)):
        hc = hyena_consts_host(Ls)
        out["dft_" + nm] = hc["dft"]; out["emb_" + nm] = hc["emb"]; out["negt_" + nm] = hc["negt"]; out["wk_" + nm] = hc["wk"]
    max_decay = math.log(1e-2) / 0.3
    min_decay = math.log(1e-2) / 1.5
    out["absdelta"] = np.abs(np.linspace(min_decay, max_decay, HYW, dtype=np.float32))[None, :].astype(np.float32)
    return out


NAH = 8
NAD = 64
NAW = 512
HYW = 512
EIN = 3 * NAW + 3 * HYW
HY_EMB = 17
HY_ORDER = 64
I32 = mybir.dt.int32
TWO_PI = 2.0 * math.pi


def _even_setup(self):
    nc = self.nc
    L = self.cfg.L
    NT = L + CTX
    ne = max(1, self.cfg.types.count("even"))
    a = self.inp_add
    a("even_in", [ne, D, EIN]); a("even_out", [ne, D, D]); a("na_q_gain", [ne, NAD]); a("na_k_gain", [ne, NAD])
    a("na_tab", [ne, NAH, 128, 2, 16, 64])
    a("hy_conv_w", [ne, 3, 3 * HYW]); a("hy_conv_b", [ne, 3 * HYW])
    a("hy_fw1", [ne, HY_EMB, HY_ORDER]); a("hy_fb1", [ne, HY_ORDER]); a("hy_fw2", [ne, HY_ORDER, HY_ORDER]); a("hy_fb2", [ne, HY_ORDER])
    a("hy_fw3", [ne, HY_ORDER, HY_ORDER]); a("hy_fb3", [ne, HY_ORDER]); a("hy_fw4", [ne, HY_ORDER, 2 * HYW]); a("hy_freq", [ne, HY_ORDER])
    a("hy_bias", [ne, HYW])
    for nm, Ls in (("lat", L), ("ctx", CTX)):
        KC = Ls // 128 + 1
        a("dft_" + nm, [2, KC, 128, KC, 128], BF16)
        a("emb_" + nm, [HY_EMB, Ls])
        a("negt_" + nm, [128, Ls // 128])
        a("wk_" + nm, [128, KC, 2])
    a("absdelta", [1, HYW])
    self.qtn = nc.dram_tensor("qtn", [4, 128, NT], BF16).ap()
    self.ktn = nc.dram_tensor("ktn", [4, 128, NT], BF16).ap()
    self.vn = nc.dram_tensor("vn", [NT, NAW], BF16).ap()
    self.u_lat = nc.dram_tensor("u_lat", [L + 2, 3 * HYW], F32).ap()
    self.u_ctx = nc.dram_tensor("u_ctx", [CTX + 2, 3 * HYW], F32).ap()
    self.cat = nc.dram_tensor("cat", [NT, D], BF16).ap()
    self.x0z = nc.dram_tensor("x0z", [NT, 2, HYW], BF16).ap()


def na_tab_host(rpb, rows):
    ne = rpb.shape[0]
    tab = np.full((ne, NAH, 128, 2, 16, 64), -30000.0, np.float32)
    c = np.arange(64)
    cs = np.clip(c - 8, 0, 48)
    cp = np.arange(64)
    colvalid = (cp[:, None] >= cs[None, :]) & (cp[:, None] < cs[None, :] + 16)
    dcidx = np.clip(cp[:, None] - c[None, :] + 15, 0, 30)
    for rk in range(2):
        for jr in range(16):
            dr = rk + 7 - jr
            if abs(dr) > 7:
                continue
            g = rpb[:, :, dr + 7, :][:, :, dcidx]
            g = np.where(colvalid[None, None], g, np.float32(-30000.0))
            tab[:, :, rk * 64:(rk + 1) * 64, 0, jr, :] = g
            if -4 <= dr <= 3:
                tab[:, :, rk * 64:(rk + 1) * 64, 1, jr, :] = g
    return tab


def hyena_consts_host(Ls):
    KC = Ls // 128 + 1
    N = 2 * Ls
    nn = KC * 128
    a = np.arange(nn, dtype=np.int64)
    m = (a[:, None] * a[None, :]) % N
    ang = (2.0 * np.pi / N) * m.astype(np.float64)
    out = {}
    tabs = np.stack([np.cos(ang), np.sin(ang)]).astype(np.float32)
    t5 = tabs.reshape(2, KC, 128, KC, 128).transpose(0, 3, 2, 1, 4)
    out["dft"] = np.ascontiguousarray(t5).astype(ml_dtypes.bfloat16)
    t = np.linspace(0.0, 1.0, Ls, dtype=np.float32)[:, None]
    w = (2.0 * np.float32(math.pi) * np.arange(Ls, dtype=np.float32)[:, None] / np.float32(Ls)).astype(np.float32)
    bands = np.linspace(1e-4, 8 - 1, 8, dtype=np.float32)
    emb = np.concatenate([t, np.cos(bands * w), -np.sin(bands * w)], axis=-1).astype(np.float32)
    out["emb"] = np.ascontiguousarray(emb.T)
    out["negt"] = np.ascontiguousarray((-t[:, 0]).reshape(Ls // 128, 128).T).astype(np.float32)
    k = np.arange(nn)
    wk = np.where((k == 0) | (k == Ls), 1.0, 2.0) / N
    wk = np.where(k <= Ls, wk, 0.0).astype(np.float32)
    wk2 = np.stack([wk, -wk], axis=-1).reshape(KC, 128, 2).transpose(1, 0, 2)
    out["wk"] = np.ascontiguousarray(wk2).astype(np.float32)
    return out


def _phase_even(self, l):
    k = self.k
    e = self.cfg.types[:l].count("even")
    L = self.cfg.L
    NT = L + CTX
    rows = self.cfg.rows
    last = (l == self.cfg.depth - 1) and not getattr(self, "force_ctx_out", False)
    inp = self.inp

    with k.phase():
        We = [k.sb([128, EIN], BF16, "We%d" % c) for c in range(8)]
        for c in range(8):
            k.dma("pool", We[c][:, :], inp["even_in"][e, c * 128:(c + 1) * 128, :], writes=[We[c]])
        G, S, gate = self.load_mod_vecs(l, 1, 3, False)
        zrow = k.sb([1, 3 * HYW], F32, "zrow")
        k.memset("dve", zrow[:, :], 0.0, [zrow])
        for (ut, Ls) in ((self.u_lat, L), (self.u_ctx, CTX)):
            k.dma("sp", ut[0:1, :], zrow[:, :], reads=[zrow])
            k.dma("sp", ut[Ls + 1:Ls + 2, :], zrow[:, :], reads=[zrow])
        gq = k.sb([128, 2, NAD], F32, "gq")
        k.dma("sp", gq[:, 0, :], bcast_rows(inp["na_q_gain"][e:e + 1, :]), writes=[gq])
        k.dma("sp", gq[:, 1, :], bcast_rows(inp["na_k_gain"][e:e + 1, :]), writes=[gq])
        TS = 256
        xpool = k.pool_of(2, [128, 2, D], F32, "xt")
        xs_pool = k.pool_of(1, [128, 2, D], BF16, "xs")
        tp_pool = k.pool_of(2, [128, TS], BF16, "tp", psum=True)
        xnT = k.sb([128, 8, TS], BF16, "xnT")
        scr = k.sb([128, D], F32, "scr")
        ssq = k.sb([128, 2], F32, "ssq")
        rstd = k.sb([128, 2], F32, "rstd")
        pp = k.pool_of(3, [128, 512], F32, "pp", psum=True)
        tq_pool = k.pool_of(2, [128, 512], BF16, "tq", psum=True)
        sq = k.sb([128, 512], F32, "sq")
        hs = k.sb([128, 2, NAH], F32, "hs")
        qn = k.sb([128, 512], F32, "qn")
        qk_tok = k.pool_of(2, [128, 512], BF16, "qktok")
        qkT = k.pool_of(4, [128, 4, 128], BF16, "qkT")
        vst = k.pool_of(2, [128, NAW], BF16, "vst")
        ust = k.pool_of(2, [128, 3 * HYW], F32, "ust")
        for (xap, ntok, row) in self.tiles(TS):
            xt = xpool.get()
            k.dma("sp", xt[:, :, :], xap.rearrange("(s p) d -> p s d", p=128), writes=[xt])
            self.norm_mod_T(xt, 2, row, G, S, xs_pool, tp_pool, xnT, scr, ssq, rstd)
            for s in range(2):
                tok0 = self.tile_tok0(xap) + s * 128
                us = ust.get()
                pend_tr = []
                for n in range(6):
                    p = pp.get()
                    for c in range(8):
                        k.mm(p[:, :], xnT[:, c, s * 128:(s + 1) * 128], We[c][:, n * 512:(n + 1) * 512], c == 0, c == 7,
                             [xnT, We[c]], [p])
                    if n < 2:
                        k.act(sq[:, :], p[:, :], AF.Square, [p], [sq])
                        k.op("dve", lambda g, n=n: g.tensor_reduce(hs[:, n, :], sq[:, :].rearrange("p (h d) -> p h d", h=NAH),
                                                                  AX.X, ALU.add), [sq], [hs])
                        k.act(hs[:, n, :], hs[:, n, :], AF.Sqrt, [hs, self.epsb], [hs], bias=self.epsb[:, 0:1], scale=1.0 / NAD)
                        k.op("dve", lambda g, n=n: g.reciprocal(hs[:, n, :], hs[:, n, :]), [hs], [hs])
                        k.tt("dve", qn[:, :].rearrange("p (h d) -> p h d", h=NAH), p[:, :].rearrange("p (h d) -> p h d", h=NAH),
                             hs[:, n, :].unsqueeze(2).broadcast_to([128, NAH, NAD]), ALU.mult, [p, hs], [qn])
                        qt = qk_tok.get()
                        k.tt("pool", qt[:, :].rearrange("p (h d) -> p h d", h=NAH), qn[:, :].rearrange("p (h d) -> p h d", h=NAH),
                             gq[:, n, :].unsqueeze(1).broadcast_to([128, NAH, NAD]), ALU.mult, [qn, gq], [qt])
                        pend_tr.append((qt, self.qtn if n == 0 else self.ktn))
                    elif n == 2:
                        v_ = vst.get()
                        k.copy("act", v_[:, :], p[:, :], [p], [v_])
                        k.dma("sp", self.vn[tok0:tok0 + 128, :], v_[:, :], reads=[v_])
                    else:
                        k.copy("act" if n % 2 else "dve", us[:, (n - 3) * 512:(n - 2) * 512], p[:, :], [p], [us])
                for (qt, dst) in pend_tr:
                    tq = tq_pool.get()
                    for b in range(4):
                        k.tr(tq[:, b * 128:(b + 1) * 128], qt[:, b * 128:(b + 1) * 128], self.ident[:, :], [qt, self.ident], [tq],
                             inc=(b == 3))
                    dT = qkT.get()
                    k.copy("act", dT[:, :, :].rearrange("p b t -> p (b t)"), tq[:, :], [tq], [dT])
                    k.dma("sp", dst[:, :, tok0:tok0 + 128].rearrange("b p t -> p b t"), dT[:, :, :], reads=[dT])
                ut, t0 = (self.u_lat, tok0) if row == 0 else (self.u_ctx, tok0 - L)
                k.dma("sp", ut[1 + t0:1 + t0 + 128, :], us[:, :], reads=[us])

    self.hyena(l, e, "lat", L, self.u_lat, 0)
    if not last:
        self.hyena(l, e, "ctx", CTX, self.u_ctx, L)
    self.na_attention(l, e, last)
    with k.phase():
        Wo = k.sb([128, 8, D], BF16, "Weo")
        k.dma("pool", Wo[:, :, :], inp["even_out"][e, :, :].rearrange("(j p) d -> p j d", p=128), writes=[Wo])
        G, S, gate = self.load_mod_vecs(l, 1, 3, False)
        c_pool = k.pool_of(2, [128, D], BF16, "catc")
        x_pool = k.pool_of(2, [128, D], F32, "xo")
        pt_pool = k.pool_of(2, [128, 1024], BF16, "pto", psum=True)
        pr_pool = k.pool_of(2, [128, 512], F32, "pro", psum=True)
        cT_pool = k.pool_of(2, [128, 8, 128], BF16, "cT")
        scr2 = k.sb([128, D], F32, "scr2")
        nchunks = (L // 128) + (0 if last else CTX // 128)
        for ch in range(nchunks):
            row = 0 if ch < L // 128 else 1
            xap = self.out[ch * 128:(ch + 1) * 128, :] if row == 0 else self.xc[(ch - L // 128) * 128:(ch - L // 128 + 1) * 128, :]
            ct = c_pool.get(); xt = x_pool.get()
            k.dma("sp", ct[:, :], self.cat[ch * 128:(ch + 1) * 128, :], writes=[ct])
            k.dma("sp", xt[:, :], xap, writes=[xt])
            pt = pt_pool.get()
            for b in range(8):
                k.tr(pt[:, b * 128:(b + 1) * 128], ct[:, b * 128:(b + 1) * 128], self.ident[:, :], [ct, self.ident], [pt], inc=(b == 7))
            cT = cT_pool.get()
            k.copy("act", cT[:, :, :].rearrange("p b t -> p (b t)"), pt[:, :], [pt], [cT])
            for hf in range(2):
                pr = pr_pool.get()
                for b in range(8):
                    k.mm(pr[:, :], cT[:, b, :], Wo[:, b, hf * 512:(hf + 1) * 512], b == 0, b == 7, [cT, Wo], [pr])
                sl = slice(hf * 512, (hf + 1) * 512)
                k.tt("dve", scr2[:, sl], pr[:, :], gate[:, row, sl], ALU.mult, [pr, gate], [scr2])
                k.tt("pool", xt[:, sl], xt[:, sl], scr2[:, sl], ALU.add, [xt, scr2], [xt])
            k.dma("sp", xap, xt[:, :], reads=[xt])


def _hyena(self, l, e, nm, Ls, ut, tokbase):
    k = self.k
    inp = self.inp
    NCn = Ls // 128
    KC = NCn + 1
    dft = inp["dft_" + nm]
    with k.phase():
        cw = k.sb([128, 3, 3 * HYW], F32, "cw")
        cb = k.sb([128, 3 * HYW], F32, "cb")
        for tpi in range(3):
            k.dma("sp", cw[:, tpi, :], bcast_rows(inp["hy_conv_w"][e, tpi:tpi + 1, :]), writes=[cw])
        k.dma("sp", cb[:, :], bcast_rows(inp["hy_conv_b"][e:e + 1, :]), writes=[cb])
        upool = k.pool_of(3, [128, 3, 3 * HYW], F32, "uabc")
        t1 = k.pool_of(3, [128, 3 * HYW], F32, "sc1")
        t2 = k.pool_of(3, [128, 3 * HYW], F32, "sc2")
        xz = k.pool_of(3, [128, 2, HYW], BF16, "xz")
        for n in range(NCn):
            u = upool.get()
            for tpi in range(3):
                k.dma("sp", u[:, tpi, :], ut[n * 128 + tpi:n * 128 + tpi + 128, :], writes=[u])
            a_ = t1.get(); b2 = t2.get()
            eg = "pool" if n % 3 == 2 else "dve"
            k.tt(eg, a_[:, :], u[:, 0, :], cw[:, 0, :], ALU.mult, [u, cw], [a_])
            k.tt(eg, b2[:, :], u[:, 1, :], cw[:, 1, :], ALU.mult, [u, cw], [b2])
            k.tt(eg, a_[:, :], a_[:, :], b2[:, :], ALU.add, [a_, b2], [a_])
            k.tt(eg, b2[:, :], u[:, 2, :], cw[:, 2, :], ALU.mult, [u, cw], [b2])
            k.tt(eg, a_[:, :], a_[:, :], b2[:, :], ALU.add, [a_, b2], [a_])
            k.tt(eg, a_[:, :], a_[:, :], cb[:, :], ALU.add, [a_, cb], [a_])
            o_ = xz.get()
            k.copy("act", o_[:, 0, :], a_[:, 0:HYW], [a_], [o_])
            k.tt(eg, o_[:, 1, :], a_[:, HYW:2 * HYW], a_[:, 2 * HYW:3 * HYW], ALU.mult, [a_], [o_])
            k.dma("sp", self.x0z[tokbase + n * 128:tokbase + (n + 1) * 128, :, :], o_[:, :, :], reads=[o_])
    with k.phase():
        wk = k.sb([128, KC, 2], F32, "wk")
        k.dma("sp", wk[:, :, :], inp["wk_" + nm], writes=[wk])
        negt = k.sb([128, NCn], F32, "negt")
        k.dma("sp", negt[:, :], inp["negt_" + nm], writes=[negt])
        KK = k.sb([128, KC, 2, HYW], BF16, "KK")
        KKtok = [Tok("kk%d" % i) for i in range(KC)]
        with k.phase():
            Hf = k.sb([128, NCn, HYW], BF16, "Hf")
            Hb = k.sb([128, NCn, HYW], BF16, "Hb")
            with k.phase():
                CW = min(512, Ls)
                embp = k.pool_of(2, [HY_EMB, CW], F32, "emb")
                fw = [k.sb([HY_EMB, HY_ORDER], F32, "fw1"), k.sb([HY_ORDER, HY_ORDER], F32, "fw2"), k.sb([HY_ORDER, HY_ORDER], F32, "fw3")]
                fw4 = k.sb([HY_ORDER, 2 * HYW], F32, "fw4")
                fbT = k.sb([HY_ORDER, 4], F32, "fbT")
                k.dma("sp", fw[0][:, :], inp["hy_fw1"][e], writes=[fw[0]])
                k.dma("sp", fw[1][:, :], inp["hy_fw2"][e], writes=[fw[1]])
                k.dma("sp", fw[2][:, :], inp["hy_fw3"][e], writes=[fw[2]])
                k.dma("sp", fw4[:, :], inp["hy_fw4"][e], writes=[fw4])
                with self.nc.allow_non_contiguous_dma("tiny"):
                    for i, nmv in enumerate(("hy_fb1", "hy_fb2", "hy_fb3", "hy_freq")):
                        k.dma("sp", fbT[:, i:i + 1], inp[nmv][e:e + 1, :].rearrange("o f -> f o"), writes=[fbT])
                fbias = k.sb([HY_ORDER, 3], F32, "fbias")
                for i in range(3):
                    k.tt("dve", fbias[:, i:i + 1], fbT[:, i:i + 1], fbT[:, 3:4], ALU.mult, [fbT], [fbias])
                adl = k.sb([128, HYW], F32, "adl")
                k.dma("sp", adl[:, :], bcast_rows(inp["absdelta"]), writes=[adl])
                hcur = [k.sb([HY_ORDER, CW], F32, "hmlp%d" % i) for i in range(2)]
                pm = k.pool_of(2, [HY_ORDER, 512], F32, "pm", psum=True)
                pre = k.sb([HY_ORDER, 512], F32, "pre")
                nfl = k.sb([HY_ORDER, 512], F32, "nfl")
                nin = k.sb([HY_ORDER, 512], I32, "nin")
                win = k.pool_of(2, [128, HYW], F32, "win")
                ph = k.pool_of(2, [128, 512], F32, "ph", psum=True)
                for cc in range(Ls // CW):
                    em = embp.get()
                    k.dma("sp", em[:, :], inp["emb_" + nm][:, cc * CW:(cc + 1) * CW], writes=[em])
                    for layer in range(3):
                        src = em if layer == 0 else hcur[(layer - 1) % 2]
                        dst = hcur[layer % 2]
                        p = pm.get()
                        k.mm(p[:, :CW], fw[layer][:, :], src[:, :], True, True, [fw[layer], src], [p])
                        k.act(pre[:, :CW], p[:, :CW], AF.Identity, [p, fbT, fbias], [pre], bias=fbias[:, layer:layer + 1], scale=fbT[:, 3:4])
                        k.ts("dve", nfl[:, :CW], pre[:, :CW], 1.0 / TWO_PI, None, ALU.mult, None, [pre], [nfl])
                        k.copy("dve", nin[:, :CW], nfl[:, :CW], [nfl], [nin])
                        k.copy("dve", nfl[:, :CW], nin[:, :CW], [nin], [nfl])
                        k.stt(pre[:, :CW], nfl[:, :CW], -TWO_PI, pre[:, :CW], ALU.mult, ALU.add, [nfl, pre], [pre])
                        k.ts("dve", pre[:, :CW], pre[:, :CW], 3.1415925, -3.1415925, ALU.min, ALU.max, [pre], [pre])
                        k.act(dst[:, :], pre[:, :CW], AF.Sin, [pre], [dst])
                    h3 = hcur[0]
                    for sub in range(CW // 128):
                        n = cc * (CW // 128) + sub
                        w_ = win.get()
                        k.act(w_[:, :], adl[:, :], AF.Exp, [adl, negt], [w_], scale=negt[:, n:n + 1])
                        for hf, dstH in ((0, Hf), (1, Hb)):
                            p = ph.get()
                            k.mm(p[:, :], h3[:, sub * 128:(sub + 1) * 128], fw4[:, hf * HYW:(hf + 1) * HYW], True, True, [h3, fw4], [p])
                            k.tt("dve", dstH[:, n, :], p[:, :], w_[:, :], ALU.mult, [p, w_], [dstH])
            tabp = k.pool_of(2, [128, 2, NCn, 128], BF16, "tabF")
            pacc = [k.ps([128, 512], F32, "pF%d" % i) for i in range(4)]
            bsb = k.pool_of(2, [128, 2, HYW], F32, "bsb")
            for kc in range(KC):
                tb = tabp.get()
                for cs_ in range(2):
                    k.dma("sp", tb[:, cs_, :, :], dft[cs_, kc, :, 0:NCn, :], writes=[tb])
                for n in range(NCn):
                    for cs_ in range(2):
                        for hi, Hsrc in ((0, Hf), (1, Hb)):
                            k.mm(pacc[cs_ * 2 + hi][:, :], tb[:, cs_, n, :], Hsrc[:, n, :], n == 0, n == NCn - 1, [tb, Hsrc], [pacc[cs_ * 2 + hi]])
                Fc, Bc, Fs, Bs = pacc[0], pacc[1], pacc[2], pacc[3]
                b_ = bsb.get()
                k.act(b_[:, 0, :], Bc[:, :], AF.Identity, [Bc, wk], [b_], scale=wk[:, kc, 0:1])
                k.act(b_[:, 1, :], Bs[:, :], AF.Identity, [Bs, wk], [b_], scale=wk[:, kc, 0:1])
                k.stt(KK[:, kc, 0, :], Fc[:, :], wk[:, kc, 0:1], b_[:, 0, :], ALU.mult, ALU.add, [Fc, wk, b_], [KKtok[kc]])
                k.stt(KK[:, kc, 1, :], Fs[:, :], wk[:, kc, 1:2], b_[:, 1, :], ALU.mult, ALU.add, [Fs, wk, b_], [KKtok[kc]])
        with k.phase():
            z = k.sb([128, NCn, HYW], BF16, "z")
            k.dma("sp", z[:, :, :], self.x0z[tokbase:tokbase + Ls, 1, :].rearrange("(n p) c -> p n c", p=128), writes=[z])
            tabp = k.pool_of(2, [128, 2, NCn, 128], BF16, "tabZ")
            pz = [k.pool_of(2, [128, 512], F32, "pZ%d" % i, psum=True) for i in range(2)]
            tm = [k.pool_of(2, [128, HYW], F32, "tmz%d" % i) for i in range(4)]
            for kc in range(KC):
                tb = tabp.get()
                for cs_ in range(2):
                    k.dma("sp", tb[:, cs_, :, :], dft[cs_, kc, :, 0:NCn, :], writes=[tb])
                Zc = pz[0].get(); Zs = pz[1].get()
                for n in range(NCn):
                    k.mm(Zc[:, :], tb[:, 0, n, :], z[:, n, :], n == 0, n == NCn - 1, [tb, z], [Zc])
                    k.mm(Zs[:, :], tb[:, 1, n, :], z[:, n, :], n == 0, n == NCn - 1, [tb, z], [Zs])
                a1 = tm[0].get(); a2 = tm[1].get(); a3 = tm[2].get(); a4 = tm[3].get()
                kt = KKtok[kc]
                k.tt("dve", a1[:, :], Zc[:, :], KK[:, kc, 0, :], ALU.mult, [Zc, kt], [a1])
                k.tt("dve", a2[:, :], Zs[:, :], KK[:, kc, 1, :], ALU.mult, [Zs, kt], [a2])
                k.tt("dve", a3[:, :], Zs[:, :], KK[:, kc, 0, :], ALU.mult, [Zs, kt], [a3])
                k.tt("dve", a4[:, :], Zc[:, :], KK[:, kc, 1, :], ALU.mult, [Zc, kt], [a4])
                k.tt("pool", KK[:, kc, 0, :], a1[:, :], a2[:, :], ALU.add, [a1, a2], [kt])
                k.tt("pool", KK[:, kc, 1, :], a3[:, :], a4[:, :], ALU.subtract, [a3, a4], [kt])
        with k.phase():
            tabp = k.pool_of(2, [128, 2, KC, 128], BF16, "tabI")
            py = k.pool_of(2, [128, 512], F32, "pY", psum=True)
            db = k.sb([128, HYW], F32, "dbias")
            k.dma("sp", db[:, :], bcast_rows(inp["hy_bias"][e:e + 1, :]), writes=[db])
            e1 = k.pool_of(2, [128, HYW], F32, "e1")
            bo = k.pool_of(2, [128, HYW], BF16, "bo")
            xzp = k.pool_of(2, [128, 2, HYW], BF16, "xzi")
            for tc_ in range(NCn):
                tb = tabp.get()
                for cs_ in range(2):
                    k.dma("sp", tb[:, cs_, :, :], dft[cs_, tc_, :, :, :], writes=[tb])
                xz_ = xzp.get()
                k.dma("sp", xz_[:, :, :], self.x0z[tokbase + tc_ * 128:tokbase + (tc_ + 1) * 128, :, :], writes=[xz_])
                y = py.get()
                for kc in range(KC):
                    k.mm(y[:, :], tb[:, 0, kc, :], KK[:, kc, 0, :], kc == 0, False, [tb, KKtok[kc]], [y])
                    k.mm(y[:, :], tb[:, 1, kc, :], KK[:, kc, 1, :], False, kc == KC - 1, [tb, KKtok[kc]], [y])
                t_ = e1.get()
                k.tt("pool", t_[:, :], xz_[:, 1, :], db[:, :], ALU.mult, [xz_, db], [t_])
                k.tt("dve", t_[:, :], y[:, :], t_[:, :], ALU.add, [y, t_], [t_])
                o_ = bo.get()
                k.tt("dve", o_[:, :], t_[:, :], xz_[:, 0, :], ALU.mult, [t_, xz_], [o_])
                k.dma("sp", self.cat[tokbase + tc_ * 128:tokbase + (tc_ + 1) * 128, NAW:D], o_[:, :], reads=[o_])


Prog.even_setup = _even_setup
Prog.phase_even = _phase_even
Prog.hyena = _hyena


def _na_attention(self, l, e, last):
    k = self.k
    inp = self.inp
    L = self.cfg.L
    NT = L + CTX
    rows = self.cfg.rows
    nP = rows // 2
    NB = L // 128
    with k.phase():
        Ve = k.sb([128, NB + 2, NAH, NAD + 1], BF16, "Ve")
        Vo = k.sb([128, NB - 1, NAH, NAD + 1], BF16, "Vo")
        k.memset("pool", Ve[:, :, :, NAD:NAD + 1], 1.0, [Ve])
        k.memset("pool", Vo[:, :, :, NAD:NAD + 1], 1.0, [Vo])
        for b in range(NB + 2):
            k.dma("sp", Ve[:, b, :, 0:NAD], self.vn[b * 128:(b + 1) * 128, :].rearrange("p (h d) -> p h d", h=NAH), writes=[Ve])
        for b in range(NB - 1):
            k.dma("sp", Vo[:, b, :, 0:NAD], self.vn[64 + b * 128:64 + (b + 1) * 128, :].rearrange("p (h d) -> p h d", h=NAH), writes=[Vo])
        A = k.sb([128, NB + 2, NAW], BF16, "Aall")
        qpool = k.pool_of(2, [128, NT], BF16, "qTn")
        kpool = k.pool_of(2, [128, NT], BF16, "kTn")
        tst = k.pool_of(2, [128, 2, 16, 64], F32, "tst")
        ttp = k.pool_of(2, [128, 2, 16 * 64], BF16, "TT")
        ps_pool = k.pool_of(2, [128, 1024], F32, "psS", psum=True)
        pv_pool = k.pool_of(2, [128, NAD + 1], F32, "psV", psum=True)
        pt_pool = k.pool_of(3, [128, 7 * 128], BF16, "PT")
        rec = k.pool_of(4, [128, 1], F32, "rec")
        for h in range(NAH):
            hp, pb = h // 2, (h % 2) * 64
            if h % 2 == 0:
                qT = qpool.get(); kT = kpool.get()
                k.dma("sp", qT[:, :], self.qtn[hp, :, :], writes=[qT])
                k.dma("sp", kT[:, :], self.ktn[hp, :, :], writes=[kT])
            ts_ = tst.get()
            k.dma("sp", ts_[:, :, :, :], inp["na_tab"][e, h, :, :, :, :], writes=[ts_])
            TT = ttp.get()
            k.act(TT[:, :, :], ts_[:, :, :, :].rearrange("p v j c -> p v (j c)"), AF.Exp, [ts_], [TT])
            units = []
            for i in range(nP):
                r0 = 2 * i
                if i < 2:
                    al, var = [0, 2, 4, 6], 0
                elif i >= nP - 2:
                    al, var = [rows - 8, rows - 6, rows - 4, rows - 2], 0
                else:
                    al, var = [r0 - 4, r0 - 2, r0, r0 + 2, r0 + 4], 1
                units.append((r0 * 64, al, var, i))
            if not last:
                units.append((L, [], 0, NB))
                units.append((L + 128, [], 0, NB + 1))
            for (q0, al, var, ablk) in units:
                M = len(al)
                nb = M + 2
                ps = ps_pool.get()
                for b in range(nb):
                    if b < M:
                        a_ = al[M - 1 - b]
                        ks = a_ * 64
                    else:
                        ks = L + (b - M) * 128
                    k.mm(ps[:, b * 128:(b + 1) * 128], kT[pb:pb + 64, ks:ks + 128], qT[pb:pb + 64, q0:q0 + 128], True, True,
                         [kT, qT], [ps], inc=(b == nb - 1))
                PT = pt_pool.get()
                for b0 in range(0, nb, 4):
                    b1 = min(nb, b0 + 4)
                    k.act(PT[:, b0 * 128:b1 * 128], ps[:, b0 * 128:b1 * 128], AF.Exp, [ps], [PT], scale=NAD ** -0.5)
                if M > 0:
                    r0 = q0 // 64
                    base = 7 - (al[0] - r0) - 2 * (M - 1)
                    k.tt("dve", PT[:, 0:M * 128], PT[:, 0:M * 128], TT[:, var, base * 64:(base + 2 * M) * 64], ALU.mult, [PT, TT], [PT])
                pv = pv_pool.get()
                for b in range(nb):
                    if b < M:
                        a_ = al[M - 1 - b]
                        vt = Ve[:, a_ // 2, h, :] if a_ % 2 == 0 else Vo[:, (a_ - 1) // 2, h, :]
                        vtok = Ve if a_ % 2 == 0 else Vo
                    else:
                        vt = Ve[:, NB + (b - M), h, :]
                        vtok = Ve
                    k.mm(pv[:, :], PT[:, b * 128:(b + 1) * 128], vt, b == 0, b == nb - 1, [PT, vtok], [pv])
                rc = rec.get()
                k.op("dve", lambda g, rc=rc, pv=pv: g.reciprocal(rc[:, :], pv[:, NAD:NAD + 1]), [pv], [rc])
                k.act(A[:, ablk, h * NAD:(h + 1) * NAD], pv[:, 0:NAD], AF.Identity, [pv, rc], [A], scale=rc[:, 0:1])
        nblk = NB + (0 if last else 2)
        for b in range(nblk):
            k.dma("sp", self.cat[b * 128:(b + 1) * 128, 0:NAW], A[:, b, :], reads=[A])


Prog.na_attention = _na_attention


_WEIGHTS = ["w_mod", "b_mod", "norm_gain", "ffn_a_in", "ffn_a_out", "ffn_b_in", "ffn_b_out", "even_in", "even_out",
            "na_q_gain", "na_k_gain", "hy_conv_w", "hy_conv_b", "hy_fw1", "hy_fb1", "hy_fw2", "hy_fb2", "hy_fw3", "hy_fb3",
            "hy_fw4", "hy_freq", "hy_bias", "ret_in", "ret_out", "ret_logit_f", "ret_logit_b"]


def kernel(**inputs):
    rows = 64
    B = inputs["x"].shape[0]
    P = Prog(Cfg(rows=rows, depth=4))
    nc = P.build()
    shared = {n: np.ascontiguousarray(np.asarray(inputs[n], dtype=np.float32)) for n in _WEIGHTS}
    shared.update(host_consts(rows))
    shared["na_tab"] = na_tab_host(np.asarray(inputs["na_rpb"], dtype=np.float32), rows)
    shared["c_ctx"] = np.ascontiguousarray(np.asarray(inputs["c_ctx"], dtype=np.float32)[None, :])
    shared = {n: v for n, v in shared.items() if n in P.inp}
    in_maps = []
    for b in range(B):
        m = dict(shared)
        m["x"] = np.ascontiguousarray(inputs["x"][b], dtype=np.float32)
        m["c"] = np.ascontiguousarray(inputs["c"][b:b + 1], dtype=np.float32)
        m["ctx"] = np.ascontiguousarray(inputs["ctx"][b], dtype=np.float32)
        in_maps.append(m)
    res = run_bass_kernel_spmd(nc, in_maps, core_ids=list(range(B)))
    return np.stack([np.asarray(r["out"], dtype=np.float32) for r in res.results], axis=0)
```
